# Optimizing a Trainium2 kernel written in Bass

```python
import jax, jax.numpy as jnp
from jax import lax
import numpy as np

D_MODEL = 2048
BATCH = 2
SEQ = 4096
DEPTH = 2
DEC_BATCH = 8
DEC_SEQ = 32
PAST_LEN = 1024

CHUNK = 64
N_META = 16
HG_HEADS = 16
HG_DK = 128
HG_DV = 128
HG_DIM = HG_HEADS * HG_DK
HG_VDIM = HG_HEADS * HG_DV
SSM_HEADS = 32
SSM_HEAD_DIM = 64
SSM_INNER = SSM_HEADS * SSM_HEAD_DIM
SSM_GROUPS = 4
SSM_HPG = SSM_HEADS // SSM_GROUPS
SSM_STATE = 128
SSM_CONV = 4
SSM_XBC = SSM_INNER + 2 * SSM_GROUPS * SSM_STATE
D_FF = 5632
FFN_CONV = 3
GLA_BLOCK = 32
SSD_BLOCK = 64
ALPHA = (2.0 * DEPTH) ** 0.25
BETA = (8.0 * DEPTH) ** -0.25
LN_EPS = 1e-5
RMS_EPS = 1e-6
SPLIT_SIZES = (HG_DIM, HG_DIM, HG_VDIM, HG_VDIM, SSM_INNER, SSM_XBC, SSM_HEADS, D_MODEL, D_MODEL)
N_IN = HG_DIM * 2 + HG_VDIM * 2 + SSM_INNER + SSM_XBC + SSM_HEADS + D_MODEL * 2

kernel_name = "hgrn2_mamba2_gated_parallel_deepnorm_stream_step"


def split_cols(a, sizes):
    out, start = [], 0
    for s in sizes:
        out.append(a[..., start:start + s])
        start += s
    return out


def layer_norm(x, g, b):
    xf = x.astype(jnp.float32)
    mu = jnp.mean(xf, axis=-1, keepdims=True)
    var = jnp.mean(jnp.square(xf - mu), axis=-1, keepdims=True)
    return ((xf - mu) * lax.rsqrt(var + LN_EPS) * g.astype(jnp.float32) + b.astype(jnp.float32)).astype(x.dtype)


def rms_norm_groups(x, n_groups, w):
    shp = x.shape
    xf = x.astype(jnp.float32).reshape(shp[:-1] + (n_groups, shp[-1] // n_groups))
    xf = xf * lax.rsqrt(jnp.mean(jnp.square(xf), axis=-1, keepdims=True) + RMS_EPS)
    return xf.reshape(shp) * w.astype(jnp.float32)


def causal_dwconv(x, buf, w, b):
    width = w.shape[0]
    t = x.shape[1]
    xp = jnp.concatenate([buf.astype(x.dtype), x], axis=1)
    out = b + sum(xp[:, k:k + t] * w[k] for k in range(width))
    return out, xp[:, xp.shape[1] - (width - 1):]


def pad_blocks(a, block):
    bsz, t = a.shape[0], a.shape[1]
    nb = -(-t // block)
    a = jnp.pad(a, [(0, 0), (0, nb * block - t)] + [(0, 0)] * (a.ndim - 2))
    return jnp.moveaxis(a.reshape((bsz, nb, block) + a.shape[2:]), 1, 0)


def unblock(a, t):
    a = jnp.moveaxis(a, 0, 1)
    return a.reshape((a.shape[0], a.shape[1] * a.shape[2]) + a.shape[3:])[:, :t]


def gla_chunked(q, k, v, log_f, s0):
    t = q.shape[1]
    f32 = jnp.float32
    blocks = tuple(pad_blocks(a.astype(f32), GLA_BLOCK) for a in (q, k, v, log_f))
    causal = jnp.tril(jnp.ones((GLA_BLOCK, GLA_BLOCK), bool))[None, :, :, None, None]

    def step(s, blk):
        qb, kb, vb, gb = blk
        cum = jnp.cumsum(gb, axis=1)
        diff = cum[:, :, None] - cum[:, None, :]
        dec = jnp.where(causal, jnp.exp(jnp.where(causal, diff, 0.0)), 0.0)
        scores = jnp.sum(qb[:, :, None] * kb[:, None, :] * dec, axis=-1)
        o = (jnp.einsum('btsh,bshv->bthv', scores, vb)
             + jnp.einsum('bthd,bhdv->bthv', qb * jnp.exp(cum), s))
        c_last = cum[:, -1]
        k_dec = kb * jnp.exp(c_last[:, None] - cum)
        s = jnp.exp(c_last)[..., None] * s + jnp.einsum('bshd,bshv->bhdv', k_dec, vb)
        return s, o

    s_fin, o = lax.scan(step, s0.astype(f32), blocks)
    return unblock(o, t).astype(v.dtype), s_fin.astype(s0.dtype)


def ssd_chunked(x, dt, a_neg, bm, cm, h0):
    bsz, t = x.shape[0], x.shape[1]
    f32 = jnp.float32
    xg = x.astype(f32).reshape(bsz, t, SSM_GROUPS, SSM_HPG, SSM_HEAD_DIM)
    dtg = dt.astype(f32).reshape(bsz, t, SSM_GROUPS, SSM_HPG)
    a_g = a_neg.astype(f32).reshape(SSM_GROUPS, SSM_HPG)
    blocks = tuple(pad_blocks(a, SSD_BLOCK) for a in (xg, dtg, bm.astype(f32), cm.astype(f32)))
    causal = jnp.tril(jnp.ones((SSD_BLOCK, SSD_BLOCK), bool))[None, :, :, None, None]

    def step(h, blk):
        xb, dtb, bb, cb = blk
        cum = jnp.cumsum(dtb * a_g, axis=1)
        diff = cum[:, :, None] - cum[:, None, :]
        lmat = jnp.where(causal, jnp.exp(jnp.where(causal, diff, 0.0)), 0.0)
        cbm = jnp.einsum('btgn,bsgn->btsg', cb, bb)
        scores = cbm[..., None] * lmat * dtb[:, None]
        y = (jnp.einsum('btsgj,bsgjp->btgjp', scores, xb)
             + jnp.einsum('btgn,bgjpn->btgjp', cb, h) * jnp.exp(cum)[..., None])
        c_last = cum[:, -1]
        wgt = jnp.exp(c_last[:, None] - cum) * dtb
        h = jnp.exp(c_last)[..., None, None] * h + jnp.einsum('bsgj,bsgn,bsgjp->bgjpn', wgt, bb, xb)
        return h, y

    h_init = h0.astype(f32).reshape(bsz, SSM_GROUPS, SSM_HPG, SSM_HEAD_DIM, SSM_STATE)
    h_fin, y = lax.scan(step, h_init, blocks)
    y = unblock(y, t).reshape(bsz, t, SSM_HEADS, SSM_HEAD_DIM)
    return y, h_fin.reshape(bsz, SSM_HEADS, SSM_HEAD_DIM, SSM_STATE).astype(h0.dtype)


def parallel_mixer(x, s_hg, h_ssm, conv_buf, lb, w_in, hg_norm_w, w_proj_a, ssm_conv_w, ssm_conv_b,
                   ssm_dt_bias, ssm_a_log, ssm_d, ssm_norm_w, w_proj_b, w_out):
    bsz, t, _ = x.shape
    f32 = jnp.float32
    proj = x @ w_in
    q, f_raw, i_in, g_out, z, xbc, dt_raw, gate_a, gate_b = split_cols(proj, SPLIT_SIZES)

    qh = jax.nn.silu(q).reshape(bsz, t, HG_HEADS, HG_DK)
    lbf = lb.astype(f32)
    f = lbf + (1.0 - lbf) * jax.nn.sigmoid(f_raw.astype(f32))
    log_f = jnp.log(f).reshape(bsz, t, HG_HEADS, HG_DK)
    kh = (1.0 - f).reshape(bsz, t, HG_HEADS, HG_DK)
    vh = i_in.reshape(bsz, t, HG_HEADS, HG_DV)
    o, s_hg_new = gla_chunked(qh, kh, vh, log_f, s_hg)
    o = rms_norm_groups(o.reshape(bsz, t, HG_VDIM), HG_HEADS, hg_norm_w) * jax.nn.silu(g_out.astype(f32))
    u_a = o.astype(x.dtype) @ w_proj_a

    xbc, conv_new = causal_dwconv(xbc, conv_buf, ssm_conv_w, ssm_conv_b)
    xbc = jax.nn.silu(xbc)
    xs, bm, cm = split_cols(xbc, (SSM_INNER, SSM_GROUPS * SSM_STATE, SSM_GROUPS * SSM_STATE))
    xs = xs.reshape(bsz, t, SSM_HEADS, SSM_HEAD_DIM)
    bm = bm.reshape(bsz, t, SSM_GROUPS, SSM_STATE)
    cm = cm.reshape(bsz, t, SSM_GROUPS, SSM_STATE)
    dt = jax.nn.softplus(dt_raw.astype(f32) + ssm_dt_bias.astype(f32))
    a_neg = -jnp.exp(ssm_a_log.astype(f32))
    y, h_new = ssd_chunked(xs, dt, a_neg, bm, cm, h_ssm)
    y = y + ssm_d.astype(f32)[:, None] * xs.astype(f32)
    y = rms_norm_groups(y.reshape(bsz, t, SSM_INNER) * jax.nn.silu(z.astype(f32)), SSM_GROUPS, ssm_norm_w)
    u_b = y.astype(x.dtype) @ w_proj_b

    merged = jax.nn.sigmoid(gate_a) * u_a + jax.nn.sigmoid(gate_b) * u_b
    return merged @ w_out, s_hg_new, h_new, conv_new


def conv_ffn(x, buf, w_up, conv_w, conv_b, w_down):
    a, v = split_cols(x @ w_up, (D_FF, D_FF))
    a, buf_new = causal_dwconv(a, buf, conv_w, conv_b)
    return (jax.nn.gelu(a, approximate=False) * v) @ w_down, buf_new


def run_trunk(x, st_hg, st_ssm, st_conv, st_ffn, lbs, w_in, hgrn_norm_w, w_proj_a, ssm_conv_w, ssm_conv_b,
              ssm_dt_bias, ssm_a_log, ssm_d, ssm_norm_w, w_proj_b, w_out, ln1_g, ln1_b,
              ffn_w_up, ffn_conv_w, ffn_conv_b, ffn_w_down, ln2_g, ln2_b):
    new_hg, new_ssm, new_conv, new_ffn = [], [], [], []
    for l in range(DEPTH):
        m, s_hg, s_ssm, s_conv = parallel_mixer(
            x, st_hg[l], st_ssm[l], st_conv[l], lbs[l], w_in[l], hgrn_norm_w[l], w_proj_a[l],
            ssm_conv_w[l], ssm_conv_b[l], ssm_dt_bias[l], ssm_a_log[l], ssm_d[l], ssm_norm_w[l],
            w_proj_b[l], w_out[l])
        x = layer_norm(ALPHA * x + m, ln1_g[l], ln1_b[l])
        fo, s_ffn = conv_ffn(x, st_ffn[l], ffn_w_up[l], ffn_conv_w[l], ffn_conv_b[l], ffn_w_down[l])
        x = layer_norm(ALPHA * x + fo, ln2_g[l], ln2_b[l])
        new_hg.append(s_hg)
        new_ssm.append(s_ssm)
        new_conv.append(s_conv)
        new_ffn.append(s_ffn)
    return x, jnp.stack(new_hg), jnp.stack(new_ssm), jnp.stack(new_conv), jnp.stack(new_ffn)


def setup_inputs(seed: int = 0) -> dict:
    key = jax.random.key(seed)
    ks = jax.random.split(key, 32)
    nrm = jax.random.normal
    f32 = jnp.float32
    dt0 = jnp.exp(jax.random.uniform(ks[12], (DEPTH, SSM_HEADS), f32, np.log(1e-3), np.log(1e-1)))
    return {
        "x_prompt": nrm(ks[0], (BATCH, SEQ, D_MODEL), f32),
        "x_sample": nrm(ks[1], (DEC_BATCH, DEC_SEQ, D_MODEL), f32),
        "state_hgrn": 0.3 * nrm(ks[2], (DEPTH, DEC_BATCH, HG_HEADS, HG_DK, HG_DV), f32),
        "state_ssm": 0.1 * nrm(ks[3], (DEPTH, DEC_BATCH, SSM_HEADS, SSM_HEAD_DIM, SSM_STATE), f32),
        "state_ssm_conv": nrm(ks[4], (DEPTH, DEC_BATCH, SSM_CONV - 1, SSM_XBC), f32),
        "state_ffn_conv": nrm(ks[5], (DEPTH, DEC_BATCH, FFN_CONV - 1, D_FF), f32),
        "meta_tokens": nrm(ks[6], (N_META, D_MODEL), f32),
        "w_in": nrm(ks[7], (DEPTH, D_MODEL, N_IN), f32) * D_MODEL ** -0.5,
        "hgrn_lb_logits": nrm(ks[8], (DEPTH, HG_DIM), f32),
        "hgrn_norm_w": 1.0 + 0.02 * nrm(ks[9], (DEPTH, HG_VDIM), f32),
        "w_proj_a": nrm(ks[10], (DEPTH, HG_VDIM, D_MODEL), f32) * HG_VDIM ** -0.5,
        "ssm_conv_w": nrm(ks[11], (DEPTH, SSM_CONV, SSM_XBC), f32) * SSM_CONV ** -0.5,
        "ssm_conv_b": 0.02 * nrm(ks[13], (DEPTH, SSM_XBC), f32),
        "ssm_dt_bias": dt0 + jnp.log(-jnp.expm1(-dt0)),
        "ssm_a_log": jnp.log(jax.random.uniform(ks[14], (DEPTH, SSM_HEADS), f32, 1.0, 16.0)),
        "ssm_d": 1.0 + 0.1 * nrm(ks[15], (DEPTH, SSM_HEADS), f32),
        "ssm_norm_w": 1.0 + 0.02 * nrm(ks[16], (DEPTH, SSM_INNER), f32),
        "w_proj_b": nrm(ks[17], (DEPTH, SSM_INNER, D_MODEL), f32) * SSM_INNER ** -0.5,
        "w_out": nrm(ks[18], (DEPTH, D_MODEL, D_MODEL), f32) * (D_MODEL ** -0.5 * BETA),
        "ln1_g": 1.0 + 0.02 * nrm(ks[19], (DEPTH, D_MODEL), f32),
        "ln1_b": 0.02 * nrm(ks[20], (DEPTH, D_MODEL), f32),
        "ffn_w_up": nrm(ks[21], (DEPTH, D_MODEL, 2 * D_FF), f32) * D_MODEL ** -0.5,
        "ffn_conv_w": nrm(ks[22], (DEPTH, FFN_CONV, D_FF), f32) * FFN_CONV ** -0.5,
        "ffn_conv_b": 0.02 * nrm(ks[23], (DEPTH, D_FF), f32),
        "ffn_w_down": nrm(ks[24], (DEPTH, D_FF, D_MODEL), f32) * (D_FF ** -0.5 * BETA),
        "ln2_g": 1.0 + 0.02 * nrm(ks[25], (DEPTH, D_MODEL), f32),
        "ln2_b": 0.02 * nrm(ks[26], (DEPTH, D_MODEL), f32),
    }


def reference(x_prompt, x_sample, state_hgrn, state_ssm, state_ssm_conv, state_ffn_conv, meta_tokens,
              w_in, hgrn_lb_logits, hgrn_norm_w, w_proj_a, ssm_conv_w, ssm_conv_b, ssm_dt_bias, ssm_a_log,
              ssm_d, ssm_norm_w, w_proj_b, w_out, ln1_g, ln1_b, ffn_w_up, ffn_conv_w, ffn_conv_b,
              ffn_w_down, ln2_g, ln2_b):
    sm = jax.nn.softmax(hgrn_lb_logits.astype(jnp.float32), axis=0)
    lbs = jnp.cumsum(sm, axis=0) - sm[0]
    weights = (lbs, w_in, hgrn_norm_w, w_proj_a, ssm_conv_w, ssm_conv_b, ssm_dt_bias, ssm_a_log, ssm_d,
               ssm_norm_w, w_proj_b, w_out, ln1_g, ln1_b, ffn_w_up, ffn_conv_w, ffn_conv_b, ffn_w_down,
               ln2_g, ln2_b)

    dt_ = x_prompt.dtype
    xp = jnp.concatenate([jnp.broadcast_to(meta_tokens.astype(dt_), (BATCH, N_META, D_MODEL)), x_prompt], axis=1)
    z_hg = jnp.zeros((DEPTH, BATCH, HG_HEADS, HG_DK, HG_DV), dt_)
    z_ssm = jnp.zeros((DEPTH, BATCH, SSM_HEADS, SSM_HEAD_DIM, SSM_STATE), dt_)
    z_conv = jnp.zeros((DEPTH, BATCH, SSM_CONV - 1, SSM_XBC), dt_)
    z_ffn = jnp.zeros((DEPTH, BATCH, FFN_CONV - 1, D_FF), dt_)
    yp, hg_p, ssm_p, conv_p, ffn_p = run_trunk(xp, z_hg, z_ssm, z_conv, z_ffn, *weights)
    y_prompt = yp[:, N_META:]

    y_sample, hg_s, ssm_s, conv_s, ffn_s = run_trunk(x_sample, state_hgrn, state_ssm, state_ssm_conv,
                                                     state_ffn_conv, *weights)
    return (y_prompt, y_sample, hg_p, ssm_p, conv_p, ffn_p, hg_s, ssm_s, conv_s, ffn_s)
```

```python
import contextlib
import numpy as np
import concourse.bass as bass
import concourse.mybir as mybir
from concourse.bass_utils import run_bass_kernel_spmd

F32 = mybir.dt.float32
BF16 = mybir.dt.bfloat16
AF = mybir.ActivationFunctionType
ALU = mybir.AluOpType

D = 2048
NKC = 16
DEPTH = 2
SEQ = 4096
N_META = 16
DEC_SEQ = 32
HG_H = 16
SSM_H = 32
SSM_P = 64
SSM_N = 128
SSM_G = 4
XBC = 3072
DFF = 5632
NFF = 44
N_IN = 17440
O_Q, O_F, O_I, O_G, O_Z, O_XBC, O_DT, O_GA, O_GB = 0, 2048, 4096, 6144, 8192, 10240, 13312, 13344, 15392
ALPHA = (2.0 * DEPTH) ** 0.25
LN_EPS = 1e-5
RMS_EPS = 1e-6
BIG = 30000.0
TW = 560
TP = 512
NTILES = SEQ // TP
NTOK = N_META + SEQ + DEC_SEQ
MAXC = 30000


class Buf:
    __slots__ = ("t", "w", "r", "name")

    def __init__(self, t, name=""):
        self.t = t
        self.w = None
        self.r = {}
        self.name = name

    def __getitem__(self, k):
        return self.t[k]


class Eng:
    def __init__(self, name, handle):
        self.name = name
        self.h = handle
        self.sem = None
        self.count = 0
        self.known = {}


class KB:
    def __init__(self, nc, stack, n_dma_sems=16):
        self.nc = nc
        self.stack = stack
        self.sems = {}
        self.eng = {}
        self.nsem = 0
        for name, h in (("pe", nc.tensor), ("act", nc.scalar), ("dve", nc.vector),
                        ("pool", nc.gpsimd), ("sp", nc.sync)):
            E = Eng(name, h)
            self.eng[name] = E
            self._newsem(E)
        self.dma_slots = []
        for i in range(n_dma_sems):
            s = stack.enter_context(nc.semaphore("d%d" % i))
            self.sems[id(s)] = s
            self.dma_slots.append([s, 0])
        self.dma_i = 0
        self.n_wait = 0
        self.n_ins = 0
        self.names = set()

    def _newsem(self, E):
        s = self.stack.enter_context(self.nc.semaphore("s_%s_%d" % (E.name, self.nsem)))
        self.nsem += 1
        self.sems[id(s)] = s
        E.sem = s
        E.count = 0

    def uname(self, name):
        i = 0
        n = name
        while n in self.names:
            i += 1
            n = "%s_%d" % (name, i)
        self.names.add(n)
        return n

    def sb(self, name, shape, dt):
        name = self.uname(name)
        t = self.stack.enter_context(self.nc.sbuf_tensor(name, list(shape), dt))
        return Buf(t, name)

    def psum(self, name, shape, dt):
        t = self.stack.enter_context(self.nc.psum_tensor(name, list(shape), dt))
        return Buf(t, name)

    def _need(self, reads, writes):
        need = {}
        for b in reads:
            if b.w is not None:
                k, v = b.w
                if need.get(k, 0) < v:
                    need[k] = v
        for b in writes:
            if b.w is not None:
                k, v = b.w
                if need.get(k, 0) < v:
                    need[k] = v
            for k, v in b.r.items():
                if need.get(k, 0) < v:
                    need[k] = v
        return need

    def _wait(self, E, need, skip_own=False):
        for k, v in need.items():
            if skip_own and k == id(E.sem):
                continue
            if E.known.get(k, 0) < v:
                E.h.wait_ge(self.sems[k], v)
                E.known[k] = v
                self.n_wait += 1

    def _mark(self, tok, reads, writes):
        k, v = tok
        for b in reads:
            if b.r.get(k, 0) < v:
                b.r[k] = v
        for b in writes:
            b.w = tok
            b.r = {}

    def op(self, e, emit, reads=(), writes=(), inc=True):
        E = self.eng[e]
        if inc and E.count >= MAXC:
            self._newsem(E)
        self._wait(E, self._need(reads, writes), skip_own=(e == "pe"))
        ins = emit(E.h)
        self.n_ins += 1
        if inc:
            E.count += 1
            ins.then_inc(E.sem, 1)
            tok = (id(E.sem), E.count)
        else:
            tok = (id(E.sem), E.count + 1)
        self._mark(tok, reads, writes)
        return tok

    def dma(self, out, in_, reads=(), writes=(), q="sp", **kw):
        E = self.eng[q]
        slot = self.dma_slots[self.dma_i]
        self.dma_i = (self.dma_i + 1) % len(self.dma_slots)
        if slot[1] >= MAXC // 16:
            s = self.stack.enter_context(self.nc.semaphore("dx%d" % self.nsem))
            self.nsem += 1
            self.sems[id(s)] = s
            self._wait(E, {id(slot[0]): slot[1] * 16})
            slot[0] = s
            slot[1] = 0
        need = self._need(reads, writes)
        if slot[1] > 0:
            k = id(slot[0])
            need[k] = max(need.get(k, 0), slot[1] * 16)
        self._wait(E, need)
        E.h.dma_start(out=out, in_=in_, **kw).then_inc(slot[0], 16)
        self.n_ins += 1
        slot[1] += 1
        tok = (id(slot[0]), slot[1] * 16)
        self._mark(tok, reads, writes)
        return tok

    def finish(self, bufs=()):
        E = self.eng["sp"]
        need = {}
        for s, c in self.dma_slots:
            if c:
                need[id(s)] = c * 16
        for b in bufs:
            if b.w is not None:
                k, v = b.w
                need[k] = max(need.get(k, 0), v)
        self._wait(E, need)


def tile_defs(ntiles):
    tiles = []
    r = 0
    for ti in range(ntiles):
        if ti == 0:
            w = N_META + TP + DEC_SEQ
            segs = [(0, N_META + TP, 0), (N_META + TP, DEC_SEQ, 1)]
            chunks = [(N_META + TP, DEC_SEQ, 1), (0, N_META, 0)] + [(N_META + 64 * j, 64, 0) for j in range(TP // 64)]
        else:
            w = TP
            segs = [(0, TP, 0)]
            chunks = [(64 * j, 64, 0) for j in range(TP // 64)]
        pieces = []
        c = 0
        while c < w:
            pieces.append((c, min(c + 512, w)))
            c += 512
        tiles.append(dict(r0=r, w=w, segs=segs, chunks=chunks, pieces=pieces, idx=ti))
        r += w
    return tiles


def build_consts():
    c = {}
    c["identf"] = np.eye(128, dtype=np.float32)
    tri = (np.arange(128)[:, None] <= np.arange(128)[None, :]).astype(np.float32)
    c["tri"] = tri
    c["ones"] = np.ones((128, 128), np.float32)
    for L in (16, 32, 64):
        m = (np.arange(L)[None, :] < np.arange(L)[:, None]).astype(np.float32) * BIG
        c["bigm%d" % L] = np.ascontiguousarray(np.broadcast_to(m[:, None, :], (L, 8, L))).reshape(L, 8 * L)
    return c


class _Stop(Exception):
    pass


def build_program(ntiles=NTILES, nlayers=DEPTH, debug=False, stop_at=None):
    nc = bass.Bass("TRN2", target_bir_lowering=False)
    tiles = tile_defs(ntiles)
    ntok = sum(t["w"] for t in tiles)

    def din(name, shape):
        return nc.dram_tensor(name, list(shape), F32, kind="ExternalInput").ap()

    def dout(name, shape):
        return nc.dram_tensor(name, list(shape), F32, kind="ExternalOutput").ap()

    xin = din("xin", [NTOK, D])
    st_hg = din("st_hg", [DEPTH, HG_H, 128, 128])
    st_ssm = din("st_ssm", [DEPTH, SSM_H, SSM_P, SSM_N])
    st_conv = din("st_conv", [DEPTH, 3, XBC])
    st_ffn = din("st_ffn", [DEPTH, 2, DFF])
    w_in = din("w_in", [DEPTH, D, N_IN])
    lb_logits = din("hgrn_lb_logits", [DEPTH, D])
    hg_nw = din("hgrn_norm_w", [DEPTH, D])
    w_pa = din("w_proj_a", [DEPTH, D, D])
    cv_w = din("ssm_conv_w", [DEPTH, 4, XBC])
    cv_b = din("ssm_conv_b", [DEPTH, XBC])
    dt_bias = din("ssm_dt_bias", [DEPTH, SSM_H])
    a_log = din("ssm_a_log", [DEPTH, SSM_H])
    ssm_d = din("ssm_d", [DEPTH, SSM_H])
    ssm_nw = din("ssm_norm_w", [DEPTH, D])
    w_pb = din("w_proj_b", [DEPTH, D, D])
    w_out = din("w_out", [DEPTH, D, D])
    ln1_g = din("ln1_g", [DEPTH, D])
    ln1_b = din("ln1_b", [DEPTH, D])
    w_up = din("ffn_w_up", [DEPTH, D, 2 * DFF])
    fcv_w = din("ffn_conv_w", [DEPTH, 3, DFF])
    fcv_b = din("ffn_conv_b", [DEPTH, DFF])
    w_dn = din("ffn_w_down", [DEPTH, DFF, D])
    ln2_g = din("ln2_g", [DEPTH, D])
    ln2_b = din("ln2_b", [DEPTH, D])
    cdefs = build_consts()
    cin = {k: din("c_" + k, v.shape) for k, v in cdefs.items()}

    y_out = dout("y", [NTOK, D])
    hg_out = dout("hg_o", [DEPTH, 2, HG_H, 128, 128])
    ssm_out = dout("ssm_o", [DEPTH, 2, SSM_H, SSM_P, SSM_N])
    conv_out = dout("conv_o", [DEPTH, 2, 3, XBC])
    ffn_out = dout("ffn_o", [DEPTH, 2, 2, DFF])
    dbg = {}
    if debug:
        for nm in ("d_on", "d_mga", "d_yn", "d_mg", "d_xmid", "d_v2"):
            dbg[nm] = nc.dram_tensor(nm, [16, 128, TW], BF16, kind="ExternalOutput").ap()
        dbg["d_hm"] = nc.dram_tensor("d_hm", [NFF, 128, TW], BF16, kind="ExternalOutput").ap()

    WMAP = {"in": w_in, "pa": w_pa, "pb": w_pb, "out": w_out, "up": w_up, "dn": w_dn}

    with contextlib.ExitStack() as st:
        K = KB(nc, st)
        op = K.op

        xT = [K.sb("xT%d" % i, [128, TW], BF16) for i in range(NKC)]
        hm = [K.sb("hm%d" % i, [128, TW], BF16) for i in range(NFF)]
        on = hm[0:16]
        mg = hm[16:32]
        NT32 = 10
        tf = [K.sb("tf%d" % i, [128, TW + 40], F32) for i in range(NT32)]
        tb = [K.sb("tb%d" % i, [128, TW], BF16) for i in range(10)]
        NS, NB = 2, 6
        stage = [K.sb("stg%d" % i, [128, NKC, 128], F32) for i in range(NS)]
        wring = [K.sb("wb%d" % i, [128, NKC, 128], BF16) for i in range(NB)]
        psb = [K.psum("ps%d" % i, [128, 512], F32) for i in range(8)]
        ps_i = [0]

        def nps():
            b = psb[ps_i[0]]
            ps_i[0] = (ps_i[0] + 1) % 8
            return b

        Shg1 = K.sb("Shg", [128, 128], F32)
        Hss1 = K.sb("Hss", [128, 512], F32)
        Hsb1 = K.sb("Hsb", [128, 512], BF16)
        hg_c = Buf(nc.dram_tensor("hg_carry", [DEPTH, HG_H, 128, 128], F32).ap(), "hg_carry")
        ss_c = Buf(nc.dram_tensor("ss_carry", [DEPTH, SSM_G, 128, 512], F32).ap(), "ss_carry")
        halo_x = [K.sb("halox%d" % l, [128, 24, 3], F32) for l in range(DEPTH)]
        halo_f = [K.sb("halof%d" % l, [128, NFF, 2], F32) for l in range(DEPTH)]
        halo_xs = [K.sb("haloxs%d" % l, [128, 24, 3], F32) for l in range(DEPTH)]
        halo_fs = [K.sb("halofs%d" % l, [128, NFF, 2], F32) for l in range(DEPTH)]
        dram_y = Buf(y_out, "y")
        dram_hg = Buf(hg_out, "hgo")
        dram_ss = Buf(ssm_out, "sso")
        dram_cv = Buf(conv_out, "cvo")
        dram_ff = Buf(ffn_out, "ffo")

        cb = {}
        for k, v in cdefs.items():
            b = K.sb("sc_" + k, [128, v.shape[1]], F32)
            K.dma(b[0:v.shape[0], :], cin[k][:, :], writes=[b])
            cb[k] = b
        identb = K.sb("identb", [128, 128], BF16)
        onesb = K.sb("onesb", [128, 128], BF16)
        trib = K.sb("trib", [128, 128], BF16)
        negtri = K.sb("negtri", [128, 128], F32)
        op("dve", lambda e: e.tensor_copy(identb[:], cb["identf"][:]), reads=[cb["identf"]], writes=[identb])
        op("dve", lambda e: e.tensor_copy(onesb[:], cb["ones"][:]), reads=[cb["ones"]], writes=[onesb])
        op("dve", lambda e: e.tensor_copy(trib[:], cb["tri"][:]), reads=[cb["tri"]], writes=[trib])
        op("dve", lambda e: e.tensor_scalar(negtri[:], cb["tri"][:], -1.0, None, ALU.mult), reads=[cb["tri"]], writes=[negtri])
        identf, onesf, trif = cb["identf"], cb["ones"], cb["tri"]
        one_col = K.sb("one_col", [128, 1], F32)
        op("dve", lambda e: e.memset(one_col[:], 1.0), writes=[one_col])

        def load_vec(name, ap, nch, extra=None):
            b = K.sb("pv_" + name, [128, DEPTH, nch] if extra is None else [128, DEPTH, extra, nch], F32)
            for l in range(DEPTH):
                if extra is None:
                    K.dma(b[:, l, :], ap[l, :].rearrange("(c p) -> p c", p=128), writes=[b],
                          allow_slow_non_contiguous=True)
                else:
                    for k in range(extra):
                        K.dma(b[:, l, k, :], ap[l, k, :].rearrange("(c p) -> p c", p=128), writes=[b],
                              allow_slow_non_contiguous=True)
            return b

        pv_lbl = load_vec("lbl", lb_logits, 16)
        pv_hgnw = load_vec("hgnw", hg_nw, 16)
        pv_cvw = load_vec("cvw", cv_w, 24, extra=4)
        pv_cvb = load_vec("cvb", cv_b, 24)
        pv_snw = load_vec("snw", ssm_nw, 16)
        pv_l1g = load_vec("l1g", ln1_g, 16)
        pv_l1b = load_vec("l1b", ln1_b, 16)
        pv_fcw = load_vec("fcw", fcv_w, NFF, extra=3)
        pv_fcb = load_vec("fcb", fcv_b, NFF)
        pv_l2g = load_vec("l2g", ln2_g, 16)
        pv_l2b = load_vec("l2b", ln2_b, 16)
        hv = {}
        for name, ap in (("dtb", dt_bias), ("alog", a_log), ("dsk", ssm_d)):
            b = K.sb("hv_" + name, [SSM_H, DEPTH], F32)
            K.dma(b[:], ap.rearrange("l h -> h l"), writes=[b], allow_slow_non_contiguous=True)
            hv[name] = b
        hvA = K.sb("hv_A", [SSM_H, DEPTH], F32)
        op("act", lambda e: e.activation(hvA[:], hv["alog"][:], AF.Exp), reads=[hv["alog"]], writes=[hvA])
        pv_lb = K.sb("pv_lb", [128, DEPTH, 16], F32)
        pv_oml = K.sb("pv_oml", [128, DEPTH, 16], F32)
        lbe = K.sb("lbe", [128, DEPTH, 16], F32)
        lbs = K.sb("lbs", [128, 16], F32)
        op("act", lambda e: e.activation(lbe[:], pv_lbl[:], AF.Exp), reads=[pv_lbl], writes=[lbe])
        op("dve", lambda e: e.tensor_copy(lbs[:], lbe[:, 0, :]), reads=[lbe], writes=[lbs])
        for l in range(1, DEPTH):
            op("dve", lambda e: e.tensor_tensor(lbs[:], lbs[:], lbe[:, l, :], ALU.add), reads=[lbe, lbs], writes=[lbs])
        op("dve", lambda e: e.reciprocal(lbs[:], lbs[:]), reads=[lbs], writes=[lbs])
        op("dve", lambda e: e.memset(pv_lb[:, 0, :], 0.0), writes=[pv_lb])
        for l in range(1, DEPTH):
            op("dve", lambda e: e.tensor_tensor(pv_lb[:, l, :], lbe[:, l, :], lbs[:], ALU.mult), reads=[lbe, lbs], writes=[pv_lb])
            if l > 1:
                op("dve", lambda e: e.tensor_tensor(pv_lb[:, l, :], pv_lb[:, l, :], pv_lb[:, l - 1, :], ALU.add),
                   reads=[pv_lb], writes=[pv_lb])
        op("dve", lambda e: e.tensor_scalar(pv_oml[:], pv_lb[:], -1.0, 1.0, ALU.mult, ALU.add), reads=[pv_lb], writes=[pv_oml])
        pv_dsk = K.sb("pv_dsk", [128, DEPTH, 16], F32)
        for l in range(DEPTH):
            for two in range(2):
                src = ssm_d[l:l + 1, :].rearrange("o (c two) -> o c two", two=2)[:, :, two]
                K.dma(pv_dsk[two * 64:(two + 1) * 64, l, :], src.to_broadcast([64, 16]), writes=[pv_dsk],
                      allow_slow_non_contiguous=True)
        for l in range(DEPTH):
            op("dve", lambda e: e.memset(halo_x[l][:], 0.0), writes=[halo_x[l]])
            op("dve", lambda e: e.memset(halo_f[l][:], 0.0), writes=[halo_f[l]])
            for k in range(3):
                K.dma(halo_xs[l][:, :, k], st_conv[l, k, :].rearrange("(c p) -> p c", p=128), writes=[halo_xs[l]],
                      allow_slow_non_contiguous=True)
            for k in range(2):
                K.dma(halo_fs[l][:, :, k], st_ffn[l, k, :].rearrange("(c p) -> p c", p=128), writes=[halo_fs[l]],
                      allow_slow_non_contiguous=True)

        sched = []

        def plan():
            for ti in range(ntiles):
                for l in range(nlayers):
                    for h in range(HG_H):
                        for o in (O_Q, O_F, O_I, O_G):
                            yield ("in", l, 0, NKC, o + h * 128, 128)
                    for c in range(16):
                        yield ("in", l, 0, NKC, O_GA + c * 128, 128)
                        yield ("pa", l, 0, NKC, c * 128, 128)
                    yield ("in", l, 0, NKC, O_DT, 32)
                    for g in range(SSM_G):
                        yield ("in", l, 0, NKC, O_XBC + 2048 + g * 128, 128)
                        yield ("in", l, 0, NKC, O_XBC + 2560 + g * 128, 128)
                        for pr in range(4):
                            cc = g * 4 + pr
                            yield ("in", l, 0, NKC, O_XBC + cc * 128, 128)
                            yield ("in", l, 0, NKC, O_Z + cc * 128, 128)
                    for c in range(16):
                        yield ("in", l, 0, NKC, O_GB + c * 128, 128)
                        yield ("pb", l, 0, NKC, c * 128, 128)
                    for c in range(16):
                        yield ("out", l, 0, NKC, c * 128, 128)
                    for j in range(NFF):
                        yield ("up", l, 0, NKC, j * 128, 128)
                        yield ("up", l, 0, NKC, DFF + j * 128, 128)
                    for c in range(16):
                        yield ("dn", l, 0, 16, c * 128, 128)
                        yield ("dn", l, 16, 16, c * 128, 128)
                        yield ("dn", l, 32, 12, c * 128, 128)

        sched = list(plan())
        ws = dict(issued=0, got=0)
        cast_rr = [0]

        def ws_issue():
            i = ws["issued"]
            name, l, k0, nk, c0, ncol = sched[i]
            sg = stage[i % NS]
            wb = wring[i % NB]
            src = WMAP[name][l, k0 * 128:(k0 + nk) * 128, c0:c0 + ncol].rearrange("(k p) c -> p k c", p=128)
            K.dma(sg[:, 0:nk, 0:ncol], src, writes=[sg])
            op("act", lambda e: e.copy(wb[:, 0:nk, 0:ncol], sg[:, 0:nk, 0:ncol]), reads=[sg], writes=[wb])
            ws["issued"] += 1

        def wget(spec):
            i = ws["got"]
            assert sched[i] == spec, (i, sched[i], spec)
            while ws["issued"] < min(len(sched), i + NB - 2):
                ws_issue()
            ws["got"] += 1
            return wring[i % NB]

        def gemm(wspecs, srcs, tile, evac, m=128):
            wbs = [(wget(s), s[3]) for s in wspecs]
            for (c0, c1) in tile["pieces"]:
                p = nps()
                kk = 0
                nk_tot = sum(n for _, n in wbs)
                for wb, nk in wbs:
                    for k in range(nk):
                        src = srcs[kk]
                        last = (kk == nk_tot - 1)
                        op("pe", lambda e: e.matmul(p[0:m, 0:c1 - c0], lhsT=wb[:, k, 0:m], rhs=src[:, c0:c1],
                                                    start=(kk == 0), stop=last),
                           reads=[wb, src], writes=[p], inc=last)
                        kk += 1
                evac(p, c0, c1)

        def act(out, in_, func, reads, writes, **kw):
            op("act", lambda e: e.activation(out, in_, func, **kw), reads=reads, writes=writes)

        ATb = [K.sb("ATb%d" % i, [64, 64], BF16) for i in range(2)]
        vtok = [K.sb("vtok%d" % i, [64, 128], BF16) for i in range(2)]
        ktok = [K.sb("ktok%d" % i, [64, 128], BF16) for i in range(2)]
        Spb = [K.sb("Spb%d" % i, [128, 128], BF16) for i in range(2)]
        rr = K.sb("rr", [128, 3, 16], F32)
        rrd = K.sb("rrd", [128, 3, 16], F32)
        st_mean = K.sb("st_mean", [128, 512], F32)
        st_rstd = K.sb("st_rstd", [128, 512], F32)
        st_tmp = K.sb("st_tmp", [128, 512], F32)
        sqr = [K.sb("sqr%d" % i, [128, 512], BF16) for i in range(2)]
        pre_s = K.sb("pre_s", [128, 40], F32)
        acc_s = K.sb("acc_s", [128, 40], F32)
        dtT = tf[4]
        adtT = tf[5]
        NCH = 10
        dta = K.sb("dta", [64, NCH, 64], F32)
        dec_tok = K.sb("dec_tok", [64, NCH, 32], F32)
        w_tok = K.sb("w_tok", [64, NCH, 32], F32)
        rtot_bc = K.sb("rtot_bc", [128, NCH, 32], F32)
        gendS = K.sb("gendS", [128, 32], F32)
        wtmp = K.sb("wtmp", [64, 32], F32)
        gtokS = K.sb("gtokS", [64, 32], F32)
        xdt = K.sb("xdt", [64, 512], BF16)
        xw = K.sb("xw", [64, 512], BF16)
        Btok = K.sb("Btok", [64, 128], BF16)
        cbt = K.sb("cbt", [64, 64], F32)
        R1 = K.sb("R1", [64, 512], F32)
        R2 = K.sb("R2", [64, 512], F32)
        LT = R1
        scT = K.sb("scT", [64, 512], BF16)
        ytmp = R2
        ytok = K.sb("ytok", [64, 512], BF16)
        yT4 = tf[0:4]
        sttok = K.sb("sttok", [128, 128], F32)

        def seg_evac(tile, p, c0, c1, pre, hw):
            for (s0, sl, sq) in tile["segs"]:
                a, b = max(c0, s0), min(c1, s0 + sl)
                if a >= b:
                    continue
                dstb = pre if sq == 0 else pre_s
                dst = dstb[:, hw + a - s0: hw + b - s0]
                op("act", lambda e: e.copy(dst, p[:, a - c0:b - c0]), reads=[p], writes=[dstb])

        def conv_apply(tile, pre, halo, halo_sq, cidx, wfun, bap, width, dst, func):
            hw = width - 1
            for (s0, sl, sq) in tile["segs"]:
                pb = pre if sq == 0 else pre_s
                hb = halo if sq == 0 else halo_sq
                ab = tf[9] if sq == 0 else acc_s
                op("dve", lambda e: e.tensor_copy(pb[:, 0:hw], hb[:, cidx, :]), reads=[hb], writes=[pb])
                op("dve", lambda e: e.tensor_scalar(ab[:, 0:sl], pb[:, hw:hw + sl], wfun(hw), bap, ALU.mult, ALU.add),
                   reads=[pb], writes=[ab])
                for k in range(hw):
                    op("dve", lambda e: e.scalar_tensor_tensor(ab[:, 0:sl], pb[:, k:k + sl], wfun(k), ab[:, 0:sl],
                                                               ALU.mult, ALU.add), reads=[pb, ab], writes=[ab])
                act(dst[:, s0:s0 + sl], ab[:, 0:sl], func, [ab], [dst])
                op("dve", lambda e: e.tensor_copy(hb[:, cidx, :], pb[:, sl:sl + hw]), reads=[pb], writes=[hb])

        def bc_sum(srcs_fn, nsrc, c0, c1):
            p = nps()
            for i in range(nsrc):
                sbuf, sap = srcs_fn(i)
                op("pe", lambda e: e.matmul(p[:, 0:c1 - c0], lhsT=onesb[:], rhs=sap, start=(i == 0), stop=(i == nsrc - 1)),
                   reads=[onesb, sbuf], writes=[p], inc=(i == nsrc - 1))
            return p

        def rstd_from(p, w, scale, eps, dst):
            act(st_tmp[:, 0:w], p[:, 0:w], AF.Sqrt, [p], [st_tmp], bias=eps_col(eps), scale=scale)
            op("dve", lambda e: e.reciprocal(dst[:, 0:w], st_tmp[:, 0:w]), reads=[st_tmp], writes=[dst])

        eps_cols = {}

        def eps_col(v):
            if v not in eps_cols:
                b = K.sb("epsc%d" % len(eps_cols), [128, 1], F32)
                op("dve", lambda e: e.memset(b[:], float(v)), writes=[b])
                eps_cols[v] = b
            return eps_cols[v][:, 0:1]

        eps_col(RMS_EPS)
        eps_col(LN_EPS)

        def gemm_g(wspecs, srcs, tile, evac, m=128):
            wbs = [(wget(s), s[3]) for s in wspecs]
            for (c0, c1) in tile["pieces"]:
                p = nps()
                kk = 0
                nk_tot = sum(n for _, n in wbs)
                for wb, nk in wbs:
                    for k in range(nk):
                        src = srcs[kk]
                        last = (kk == nk_tot - 1)
                        op("pe", lambda e: e.matmul(p[0:m, 0:c1 - c0], lhsT=wb[:, k, 0:m], rhs=src[:, c0:c1],
                                                    start=(kk == 0), stop=last),
                           reads=[wb, src], writes=[p], inc=last)
                        kk += 1
                evac(p, c0, c1)
                yield

        HS = []
        for s_ in range(2):
            HS.append(dict(
                qs=tf[4 * s_ + 0], fk=tf[4 * s_ + 1], ln=tf[4 * s_ + 2], G=tf[4 * s_ + 3],
                vT=tb[4 * s_ + 0], gs=tb[4 * s_ + 1], qt=tb[4 * s_ + 2], kt=tb[4 * s_ + 3],
                S=K.sb("ShgS%d" % s_, [128, 128], F32), rr=K.sb("rrS%d" % s_, [128, 3, 16], F32),
                rrd=K.sb("rrdS%d" % s_, [128, 3, 16], F32),
                ATb=[K.sb("ATbS%d_%d" % (s_, i), [64, 64], BF16) for i in range(2)],
                vtok=[K.sb("vtokS%d_%d" % (s_, i), [64, 128], BF16) for i in range(2)],
                ktok=[K.sb("ktokS%d_%d" % (s_, i), [64, 128], BF16) for i in range(2)],
                Spb=[K.sb("SpbS%d_%d" % (s_, i), [128, 128], BF16) for i in range(2)],
                rstd=K.sb("rstdS%d" % s_, [128, 512], F32)))

        def hg_prep(tile, l, h, B):
            T = tile["w"]
            qs, fk, lnf, GnS = B["qs"], B["fk"], B["ln"], B["G"]
            Dd, E2 = B["ln"], B["G"]
            vT, gs, qt, kt = B["vT"], B["gs"], B["qt"], B["kt"]
            rr_, rrd_ = B["rr"], B["rrd"]
            specs = [("in", l, 0, NKC, o + h * 128, 128) for o in (O_Q, O_F, O_I, O_G)]
            yield from gemm_g([specs[0]], xT, tile, lambda p, c0, c1: act(qs[:, c0:c1], p[:, 0:c1 - c0], AF.Silu, [p], [qs]))
            yield from gemm_g([specs[1]], xT, tile, lambda p, c0, c1: act(fk[:, c0:c1], p[:, 0:c1 - c0], AF.Sigmoid, [p], [fk]))
            yield from gemm_g([specs[2]], xT, tile, lambda p, c0, c1: op("act", lambda e: e.copy(vT[:, c0:c1], p[:, 0:c1 - c0]),
                                                                         reads=[p], writes=[vT]))
            yield from gemm_g([specs[3]], xT, tile, lambda p, c0, c1: act(gs[:, c0:c1], p[:, 0:c1 - c0], AF.Silu, [p], [gs]))
            op("dve", lambda e: e.tensor_scalar(fk[:, 0:T], fk[:, 0:T], pv_oml[:, l, h:h + 1], pv_lb[:, l, h:h + 1],
                                                ALU.mult, ALU.add), reads=[fk, pv_oml, pv_lb], writes=[fk])
            act(lnf[:, 0:T], fk[:, 0:T], AF.Ln, [fk], [lnf])
            op("dve", lambda e: e.tensor_scalar(fk[:, 0:T], fk[:, 0:T], -1.0, 1.0, ALU.mult, ALU.add), reads=[fk], writes=[fk])
            op("dve", lambda e: e.memset(GnS[:, 0:1], 0.0), writes=[GnS])
            op("dve", lambda e: e.tensor_tensor_scan(GnS[:, 1:T + 1], one_col[:, 0:1].to_broadcast([128, T]), lnf[:, 0:T],
                                                     0.0, ALU.mult, ALU.add), reads=[one_col, lnf], writes=[GnS])
            yield
            runs = []
            for ci, (a, L, sq) in enumerate(tile["chunks"]):
                if runs and runs[-1][2] == L and runs[-1][0] + runs[-1][1] * L == a:
                    runs[-1][1] += 1
                else:
                    runs.append([a, 1, L, ci])
            for (a, n, L, ci0) in runs:
                hl = L // 2
                gv = lambda off: GnS[:, a + off:a + off + n * L].rearrange("p (j l) -> p j l", l=L)
                ref = gv(hl)[:, :, 0:1]
                sta = gv(0)[:, :, 0:1]
                end = gv(L)[:, :, 0:1]
                op("dve", lambda e: e.tensor_tensor(Dd[:, a:a + n * L].rearrange("p (j l) -> p j l", l=L), gv(1),
                                                    ref.to_broadcast([128, n, L]), ALU.subtract), reads=[GnS], writes=[Dd])
                r3 = lambda k: rrd_[:, k, ci0:ci0 + n].unsqueeze(2)
                op("dve", lambda e: e.tensor_tensor(r3(0), ref, sta, ALU.subtract), reads=[GnS], writes=[rrd_])
                op("dve", lambda e: e.tensor_tensor(r3(1), end, ref, ALU.subtract), reads=[GnS], writes=[rrd_])
                op("dve", lambda e: e.tensor_tensor(r3(2), end, sta, ALU.subtract), reads=[GnS], writes=[rrd_])
            nchk = len(tile["chunks"])
            act(rr_[:, :, 0:nchk], rrd_[:, :, 0:nchk], AF.Exp, [rrd_], [rr_])
            act(E2[:, 0:T], Dd[:, 0:T], AF.Exp, [Dd], [E2], scale=-1.0)
            act(Dd[:, 0:T], Dd[:, 0:T], AF.Exp, [Dd], [Dd])
            yield
            op("dve", lambda e: e.tensor_tensor(qt[:, 0:T], qs[:, 0:T], Dd[:, 0:T], ALU.mult), reads=[qs, Dd], writes=[qt])
            op("dve", lambda e: e.tensor_tensor(kt[:, 0:T], fk[:, 0:T], E2[:, 0:T], ALU.mult), reads=[fk, E2], writes=[kt])
            yield

        def hg_loop(tile, l, h, B, first_tile, last_tile):
            T = tile["w"]
            oT = B["qs"]
            vT, gs, qt, kt = B["vT"], B["gs"], B["qt"], B["kt"]
            rr_ = B["rr"]
            S = B["S"]
            cur_seq = None
            for ci, (a, L, sq) in enumerate(tile["chunks"]):
                if sq != cur_seq:
                    if cur_seq is not None:
                        store_hg(l, h, cur_seq, last_tile, S)
                    if sq == 0:
                        if first_tile:
                            op("dve", lambda e: e.memset(S[:], 0.0), writes=[S])
                        else:
                            K.dma(S[:], hg_c[l, h, :, :], reads=[hg_c], writes=[S])
                    else:
                        K.dma(S[:], st_hg[l, h, :, :], writes=[S])
                    cur_seq = sq
                i2 = ci % 2
                ATb_, vtok_, ktok_, Spb_ = B["ATb"][i2], B["vtok"][i2], B["ktok"][i2], B["Spb"][i2]
                p1 = nps()
                op("pe", lambda e: e.matmul(p1[0:L, 0:L], lhsT=kt[:, a:a + L], rhs=qt[:, a:a + L], start=True, stop=True),
                   reads=[kt, qt], writes=[p1])
                p2 = nps()
                p2b = p2[:, 0:256].bitcast(BF16)
                op("pe", lambda e: e.transpose(p2b[0:L, 0:128], vT[:, a:a + L], identb[:]), reads=[vT, identb], writes=[p2])
                p3 = nps()
                p3b = p3[:, 0:256].bitcast(BF16)
                op("pe", lambda e: e.transpose(p3b[0:L, 0:128], kt[:, a:a + L], identb[:]), reads=[kt, identb], writes=[p3])
                op("dve", lambda e: e.tensor_tensor(ATb_[0:L, 0:L], p1[0:L, 0:L], trif[0:L, 0:L], ALU.mult),
                   reads=[p1, trif], writes=[ATb_])
                op("act", lambda e: e.copy(vtok_[0:L, :], p2b[0:L, 0:128]), reads=[p2], writes=[vtok_])
                op("act", lambda e: e.copy(ktok_[0:L, :], p3b[0:L, 0:128]), reads=[p3], writes=[ktok_])
                op("dve", lambda e: e.tensor_scalar(Spb_[:], S[:], rr_[:, 0, ci:ci + 1], None, ALU.mult),
                   reads=[S, rr_], writes=[Spb_])
                yield
                p4 = nps()
                op("pe", lambda e: e.matmul(p4[:, 0:L], lhsT=vtok_[0:L, :], rhs=ATb_[0:L, 0:L], start=True, stop=False),
                   reads=[vtok_, ATb_], writes=[p4], inc=False)
                op("pe", lambda e: e.matmul(p4[:, 0:L], lhsT=Spb_[:], rhs=qt[:, a:a + L], start=False, stop=True),
                   reads=[Spb_, qt], writes=[p4])
                p5 = nps()
                op("pe", lambda e: e.matmul(p5[:, 0:128], lhsT=ktok_[0:L, :], rhs=vtok_[0:L, :], start=True, stop=True),
                   reads=[ktok_, vtok_], writes=[p5])
                op("act", lambda e: e.copy(oT[:, a:a + L], p4[:, 0:L]), reads=[p4], writes=[oT])
                op("dve", lambda e: e.tensor_scalar(S[:], S[:], rr_[:, 2, ci:ci + 1], None, ALU.mult), reads=[S, rr_], writes=[S])
                op("dve", lambda e: e.scalar_tensor_tensor(S[:], p5[:, 0:128], rr_[:, 1, ci:ci + 1], S[:], ALU.mult, ALU.add),
                   reads=[p5, rr_, S], writes=[S])
                yield
            store_hg(l, h, cur_seq, last_tile, S)
            sq_b = B["kt"]
            rstd_ = B["rstd"]
            act(sq_b[:, 0:T], oT[:, 0:T], AF.Square, [oT], [sq_b])
            yield
            for (c0, c1) in tile["pieces"]:
                w = c1 - c0
                p = bc_sum(lambda i: (sq_b, sq_b[:, c0:c1]), 1, c0, c1)
                rstd_from(p, w, 1.0 / 128, RMS_EPS, rstd_)
                yield
                op("dve", lambda e: e.scalar_tensor_tensor(oT[:, c0:c1], oT[:, c0:c1], pv_hgnw[:, l, h:h + 1], rstd_[:, 0:w],
                                                           ALU.mult, ALU.mult), reads=[oT, pv_hgnw, rstd_], writes=[oT])
                op("dve", lambda e: e.tensor_tensor(on[h][:, c0:c1], oT[:, c0:c1], gs[:, c0:c1], ALU.mult),
                   reads=[oT, gs], writes=[on[h]])
                yield

        def hgrn_all(tile, l, first_tile, last_tile):
            for _ in hg_prep(tile, l, 0, HS[0]):
                pass
            for h in range(HG_H):
                A = hg_loop(tile, l, h, HS[h % 2], first_tile, last_tile)
                Bg = hg_prep(tile, l, h + 1, HS[(h + 1) % 2]) if h + 1 < HG_H else None
                a_alive, b_alive = True, Bg is not None
                while a_alive or b_alive:
                    if a_alive:
                        try:
                            next(A)
                        except StopIteration:
                            a_alive = False
                    if b_alive:
                        try:
                            next(Bg)
                        except StopIteration:
                            b_alive = False

        def store_hg(l, h, sq, last_tile, S):
            if sq == 0:
                if last_tile:
                    K.dma(hg_out[l, 0, h, :, :], S[:], reads=[S], writes=[dram_hg])
                else:
                    K.dma(hg_c[l, h, :, :], S[:], reads=[S], writes=[hg_c])
            else:
                K.dma(hg_out[l, 1, h, :, :], S[:], reads=[S], writes=[dram_hg])

        def gated_proj(tile, l, srcs, gate_off, wname, accumulate):
            for c in range(16):
                sg = tf[0]
                gemm([("in", l, 0, NKC, gate_off + c * 128, 128)], xT, tile,
                     lambda p, c0, c1: act(sg[:, c0:c1], p[:, 0:c1 - c0], AF.Sigmoid, [p], [sg]))

                def ev(p, c0, c1):
                    if not accumulate:
                        op("dve", lambda e: e.tensor_tensor(mg[c][:, c0:c1], p[:, 0:c1 - c0], sg[:, c0:c1], ALU.mult),
                           reads=[p, sg], writes=[mg[c]])
                    else:
                        op("dve", lambda e: e.tensor_tensor(sg[:, c0:c1], p[:, 0:c1 - c0], sg[:, c0:c1], ALU.mult),
                           reads=[p, sg], writes=[sg])
                        op("dve", lambda e: e.tensor_tensor(mg[c][:, c0:c1], mg[c][:, c0:c1], sg[:, c0:c1], ALU.add),
                           reads=[mg[c], sg], writes=[mg[c]])
                gemm([(wname, l, 0, NKC, c * 128, 128)], srcs, tile, ev)

        def ss_load(l, g, sq, first_tile):
            H = Hss1
            if sq == 0:
                if first_tile:
                    op("dve", lambda e: e.memset(H[:], 0.0), writes=[H])
                else:
                    K.dma(H[:], ss_c[l, g, :, :], reads=[ss_c], writes=[H])
            else:
                for pr in range(4):
                    hh = (g * 4 + pr) * 2
                    K.dma(sttok[:], st_ssm[l, hh:hh + 2, :, :].rearrange("h p n -> (h p) n"), writes=[sttok])
                    p = nps()
                    op("pe", lambda e: e.transpose(p[:, 0:128], sttok[:], identf[:]), reads=[sttok, identf], writes=[p])
                    op("dve", lambda e: e.tensor_copy(H[:, pr * 128:(pr + 1) * 128], p[:, 0:128]), reads=[p], writes=[H])
            op("act", lambda e: e.copy(Hsb1[:], H[:]), reads=[H], writes=[Hsb1])

        def ss_store(l, g, sq, last_tile):
            H = Hss1
            if sq == 0 and not last_tile:
                K.dma(ss_c[l, g, :, :], H[:], reads=[H], writes=[ss_c])
                return
            for pr in range(4):
                hh = (g * 4 + pr) * 2
                p = nps()
                op("pe", lambda e: e.transpose(p[:, 0:128], H[:, pr * 128:(pr + 1) * 128], identf[:]),
                   reads=[H, identf], writes=[p])
                op("dve", lambda e: e.tensor_copy(sttok[:], p[:, 0:128]), reads=[p], writes=[sttok])
                K.dma(ssm_out[l, sq, hh:hh + 2, :, :].rearrange("h p n -> (h p) n"), sttok[:], reads=[sttok], writes=[dram_ss])

        def ssd(tile, l, first_tile, last_tile):
            T = tile["w"]
            chunks = tile["chunks"]
            wdt = ("in", l, 0, NKC, O_DT, 32)

            def ev_dt(p, c0, c1):
                act(dtT[0:32, c0:c1], p[0:32, 0:c1 - c0], AF.Exp, [p], [dtT], bias=hv["dtb"][:, l:l + 1])
            gemm([wdt], xT, tile, ev_dt, m=32)
            act(dtT[0:32, 0:T], dtT[0:32, 0:T], AF.Ln, [dtT], [dtT], bias=one_col[0:32, 0:1])
            op("dve", lambda e: e.tensor_scalar(adtT[0:32, 0:T], dtT[0:32, 0:T], hvA[:, l:l + 1], None, ALU.mult),
               reads=[dtT, hvA], writes=[adtT])
            if stop_at == "ssd_dt0":
                raise _Stop()
            for ci, (a, L, sq) in enumerate(chunks):
                p = nps()
                op("pe", lambda e: e.matmul(p[0:L, 0:32], lhsT=dtT[0:32, a:a + L], rhs=identf[0:32, 0:32], start=True, stop=True),
                   reads=[dtT, identf], writes=[p], inc=False)
                op("pe", lambda e: e.matmul(p[0:L, 32:64], lhsT=adtT[0:32, a:a + L], rhs=identf[0:32, 0:32], start=True, stop=True),
                   reads=[adtT, identf], writes=[p])
                op("dve", lambda e: e.tensor_copy(dta[0:L, ci, :], p[0:L, 0:64]), reads=[p], writes=[dta])
                p2 = nps()
                op("pe", lambda e: e.matmul(p2[0:L, 0:32], lhsT=trif[0:L, 0:L], rhs=dta[0:L, ci, 32:64], start=True, stop=True),
                   reads=[trif, dta], writes=[p2])
                p2x = nps()
                op("pe", lambda e: e.matmul(p2x[:, 0:32], lhsT=onesf[0:L, :], rhs=dta[0:L, ci, 32:64], start=True, stop=True),
                   reads=[onesf, dta], writes=[p2x])
                op("dve", lambda e: e.tensor_copy(gtokS[0:L, :], p2[0:L, 0:32]), reads=[p2], writes=[gtokS])
                op("dve", lambda e: e.tensor_copy(gendS[:], p2x[:, 0:32]), reads=[p2x], writes=[gendS])
                act(dec_tok[0:L, ci, :], gtokS[0:L, :], AF.Exp, [gtokS], [dec_tok], scale=-1.0)
                act(rtot_bc[:, ci, :], gendS[:], AF.Exp, [gendS], [rtot_bc], scale=-1.0)
                op("dve", lambda e: e.tensor_tensor(wtmp[0:L, :], gtokS[0:L, :], gendS[0:L, :], ALU.subtract),
                   reads=[gtokS, gendS], writes=[wtmp])
                act(wtmp[0:L, :], wtmp[0:L, :], AF.Exp, [wtmp], [wtmp])
                op("dve", lambda e: e.tensor_tensor(w_tok[0:L, ci, :], wtmp[0:L, :], dta[0:L, ci, 0:32], ALU.mult),
                   reads=[wtmp, dta], writes=[w_tok])
            if stop_at == "ssd_dt":
                raise _Stop()
            BT, CT = tb[0], tb[1]
            xc = tb[2:6]
            sz = tb[6:10]
            pre = tf[8]
            for g in range(SSM_G):
                for which, dstb in ((0, BT), (1, CT)):
                    cidx = 16 + which * 4 + g
                    gemm([("in", l, 0, NKC, O_XBC + 2048 + which * 512 + g * 128, 128)], xT, tile,
                         lambda p, c0, c1: seg_evac(tile, p, c0, c1, pre, 3))
                    conv_apply(tile, pre, halo_x[l], halo_xs[l], cidx, lambda k: pv_cvw[:, l, k, cidx:cidx + 1],
                               pv_cvb[:, l, cidx:cidx + 1], 4, dstb, AF.Silu)
                for pr in range(4):
                    cc = g * 4 + pr
                    gemm([("in", l, 0, NKC, O_XBC + cc * 128, 128)], xT, tile,
                         lambda p, c0, c1: seg_evac(tile, p, c0, c1, pre, 3))
                    conv_apply(tile, pre, halo_x[l], halo_xs[l], cc, lambda k: pv_cvw[:, l, k, cc:cc + 1],
                               pv_cvb[:, l, cc:cc + 1], 4, xc[pr], AF.Silu)
                    gemm([("in", l, 0, NKC, O_Z + cc * 128, 128)], xT, tile,
                         lambda p, c0, c1: act(sz[pr][:, c0:c1], p[:, 0:c1 - c0], AF.Silu, [p], [sz[pr]]))
                if stop_at == "ssd_conv":
                    raise _Stop()
                cur_seq = None
                H, Hb = Hss1, Hsb1
                for ci, (a, L, sq) in enumerate(chunks):
                    if sq != cur_seq:
                        if cur_seq is not None:
                            ss_store(l, g, cur_seq, last_tile)
                        ss_load(l, g, sq, first_tile)
                        cur_seq = sq
                    L8 = 8 * L
                    gs8 = slice(g * 8, (g + 1) * 8)
                    px = nps()
                    pxb = px[:, 0:256].bitcast(BF16)
                    for pr in range(4):
                        op("pe", lambda e: e.transpose(pxb[0:L, pr * 128:(pr + 1) * 128], xc[pr][:, a:a + L], identb[:]),
                           reads=[xc[pr], identb], writes=[px], inc=(pr == 3))
                    op("dve", lambda e: e.tensor_tensor(xdt[0:L, :].rearrange("s (h p) -> s h p", p=64),
                                                        pxb[0:L, 0:512].rearrange("s (h p) -> s h p", p=64),
                                                        dta[0:L, ci, gs8].unsqueeze(2).to_broadcast([L, 8, 64]), ALU.mult),
                       reads=[px, dta], writes=[xdt])
                    op("dve", lambda e: e.tensor_tensor(xw[0:L, :].rearrange("s (h p) -> s h p", p=64),
                                                        pxb[0:L, 0:512].rearrange("s (h p) -> s h p", p=64),
                                                        w_tok[0:L, ci, gs8].unsqueeze(2).to_broadcast([L, 8, 64]), ALU.mult),
                       reads=[px, w_tok], writes=[xw])
                    pB = nps()
                    pBb = pB[:, 0:256].bitcast(BF16)
                    op("pe", lambda e: e.transpose(pBb[0:L, 0:128], BT[:, a:a + L], identb[:]), reads=[BT, identb], writes=[pB])
                    op("act", lambda e: e.copy(Btok[0:L, :], pBb[0:L, 0:128]), reads=[pB], writes=[Btok])
                    pC = nps()
                    op("pe", lambda e: e.matmul(pC[0:L, 0:L], lhsT=BT[:, a:a + L], rhs=CT[:, a:a + L], start=True, stop=True),
                       reads=[BT, CT], writes=[pC])
                    op("act", lambda e: e.copy(cbt[0:L, 0:L], pC[0:L, 0:L]), reads=[pC], writes=[cbt])
                    adt8 = dta[0:L, ci, 32 + g * 8:32 + (g + 1) * 8].unsqueeze(2).to_broadcast([L, 8, L])
                    op("dve", lambda e: e.tensor_tensor(R1[0:L, 0:L8].rearrange("s (h t) -> s h t", t=L), adt8,
                                                        trif[0:L, 0:L].unsqueeze(1).to_broadcast([L, 8, L]), ALU.mult),
                       reads=[dta, trif], writes=[R1])
                    op("dve", lambda e: e.tensor_copy(R2[0:L, 0:L8].rearrange("s (h t) -> s h t", t=L), adt8),
                       reads=[dta], writes=[R2])
                    pL = nps()
                    bigm = cb["bigm%d" % L]
                    op("pe", lambda e: e.matmul(pL[0:L, 0:L8], lhsT=onesf[0:L, 0:L], rhs=R1[0:L, 0:L8], start=True, stop=False),
                       reads=[onesf, R1], writes=[pL], inc=False)
                    op("pe", lambda e: e.matmul(pL[0:L, 0:L8], lhsT=negtri[0:L, 0:L], rhs=R2[0:L, 0:L8], start=False, stop=False),
                       reads=[negtri, R2], writes=[pL], inc=False)
                    op("pe", lambda e: e.matmul(pL[0:L, 0:L8], lhsT=identf[0:L, 0:L], rhs=bigm[0:L, 0:L8], start=False, stop=True),
                       reads=[identf, bigm], writes=[pL])
                    op("dve", lambda e: e.tensor_copy(LT[0:L, 0:L8], pL[0:L, 0:L8]), reads=[pL], writes=[LT])
                    act(LT[0:L, 0:L8], LT[0:L, 0:L8], AF.Exp, [LT], [LT], scale=-1.0)
                    op("dve", lambda e: e.tensor_tensor(scT[0:L, 0:L8].rearrange("s (h t) -> s h t", t=L),
                                                        LT[0:L, 0:L8].rearrange("s (h t) -> s h t", t=L),
                                                        cbt[0:L, 0:L].unsqueeze(1).to_broadcast([L, 8, L]), ALU.mult),
                       reads=[LT, cbt], writes=[scT])
                    pY1 = nps()
                    for hh in range(8):
                        op("pe", lambda e: e.matmul(pY1[0:L, hh * 64:(hh + 1) * 64], lhsT=scT[0:L, hh * L:(hh + 1) * L],
                                                    rhs=xdt[0:L, hh * 64:(hh + 1) * 64], start=True, stop=True),
                           reads=[scT, xdt], writes=[pY1], inc=(hh == 7))
                    pY2 = nps()
                    op("pe", lambda e: e.matmul(pY2[0:L, 0:512], lhsT=CT[:, a:a + L], rhs=Hb[:, 0:512], start=True, stop=True),
                       reads=[CT, Hb], writes=[pY2])
                    op("dve", lambda e: e.tensor_tensor(ytmp[0:L, :].rearrange("s (h p) -> s h p", p=64),
                                                        pY2[0:L, 0:512].rearrange("s (h p) -> s h p", p=64),
                                                        dec_tok[0:L, ci, gs8].unsqueeze(2).to_broadcast([L, 8, 64]), ALU.mult),
                       reads=[pY2, dec_tok], writes=[ytmp])
                    op("dve", lambda e: e.tensor_tensor(ytok[0:L, :], pY1[0:L, 0:512], ytmp[0:L, :], ALU.add),
                       reads=[pY1, ytmp], writes=[ytok])
                    pT = nps()
                    pTb = pT[:, 0:256].bitcast(BF16)
                    for pr in range(4):
                        op("pe", lambda e: e.transpose(pTb[:, pr * 64:pr * 64 + L], ytok[0:L, pr * 128:(pr + 1) * 128],
                                                       identb[0:L, 0:L]), reads=[ytok, identb], writes=[pT], inc=(pr == 3))
                    for pr in range(4):
                        op("act", lambda e: e.copy(yT4[pr][:, a:a + L], pTb[:, pr * 64:pr * 64 + L]), reads=[pT], writes=[yT4[pr]])
                    pH = nps()
                    op("pe", lambda e: e.matmul(pH[:, 0:512], lhsT=Btok[0:L, :], rhs=xw[0:L, :], start=True, stop=True),
                       reads=[Btok, xw], writes=[pH])
                    op("dve", lambda e: e.tensor_tensor(H[:].rearrange("n (h p) -> n h p", p=64),
                                                        H[:].rearrange("n (h p) -> n h p", p=64),
                                                        rtot_bc[:, ci, gs8].unsqueeze(2).to_broadcast([128, 8, 64]), ALU.mult),
                       reads=[H, rtot_bc], writes=[H])
                    op("dve", lambda e: e.tensor_tensor(H[:], H[:], pH[:, 0:512], ALU.add), reads=[H, pH], writes=[H])
                    if stop_at == "ssd_c1":
                        raise _Stop()
                    op("act", lambda e: e.copy(Hb[:], H[:]), reads=[H], writes=[Hb])
                ss_store(l, g, cur_seq, last_tile)
                for pr in range(4):
                    cc = g * 4 + pr
                    yv = yT4[pr][:, 0:T]
                    op("dve", lambda e: e.scalar_tensor_tensor(yv, xc[pr][:, 0:T], pv_dsk[:, l, cc:cc + 1], yv, ALU.mult, ALU.add),
                       reads=[xc[pr], pv_dsk, yT4[pr]], writes=[yT4[pr]])
                    op("dve", lambda e: e.tensor_tensor(yv, yv, sz[pr][:, 0:T], ALU.mult), reads=[yT4[pr], sz[pr]], writes=[yT4[pr]])
                    act(xc[pr][:, 0:T], yv, AF.Square, [yT4[pr]], [xc[pr]])
                for (c0, c1) in tile["pieces"]:
                    w = c1 - c0
                    p = bc_sum(lambda i: (xc[i], xc[i][:, c0:c1]), 4, c0, c1)
                    rstd_from(p, w, 1.0 / 512, RMS_EPS, st_rstd)
                    for pr in range(4):
                        cc = g * 4 + pr
                        op("dve", lambda e: e.scalar_tensor_tensor(on[cc][:, c0:c1], yT4[pr][:, c0:c1], pv_snw[:, l, cc:cc + 1],
                                                                   st_rstd[:, 0:w], ALU.mult, ALU.mult),
                           reads=[yT4[pr], pv_snw, st_rstd], writes=[on[cc]])


        def layer_norm(tile, gvec, bvec, l, emit_out):
            T = tile["w"]
            for (c0, c1) in tile["pieces"]:
                w = c1 - c0
                p1 = bc_sum(lambda i: (xT[i], xT[i][:, c0:c1]), 16, c0, c1)
                p2 = nps()
                for c in range(16):
                    sb_ = sqr[c % 2]
                    act(sb_[:, 0:w], xT[c][:, c0:c1], AF.Square, [xT[c]], [sb_])
                    op("pe", lambda e: e.matmul(p2[:, 0:w], lhsT=onesb[:], rhs=sb_[:, 0:w], start=(c == 0), stop=(c == 15)),
                       reads=[onesb, sb_], writes=[p2], inc=True)
                op("dve", lambda e: e.tensor_scalar(st_mean[:, 0:w], p1[:, 0:w], 1.0 / D, None, ALU.mult), reads=[p1], writes=[st_mean])
                op("dve", lambda e: e.tensor_tensor(st_tmp[:, 0:w], st_mean[:, 0:w], st_mean[:, 0:w], ALU.mult),
                   reads=[st_mean], writes=[st_tmp])
                op("dve", lambda e: e.scalar_tensor_tensor(st_tmp[:, 0:w], p2[:, 0:w], 1.0 / D, st_tmp[:, 0:w], ALU.mult, ALU.subtract),
                   reads=[p2, st_tmp], writes=[st_tmp])
                act(st_tmp[:, 0:w], st_tmp[:, 0:w], AF.Sqrt, [st_tmp], [st_tmp], bias=eps_col(LN_EPS))
                op("dve", lambda e: e.reciprocal(st_rstd[:, 0:w], st_tmp[:, 0:w]), reads=[st_tmp], writes=[st_rstd])
                for c in range(16):
                    t32 = tf[0] if c % 2 == 0 else tf[1]
                    op("dve", lambda e: e.tensor_tensor(t32[:, 0:w], xT[c][:, c0:c1], st_mean[:, 0:w], ALU.subtract),
                       reads=[xT[c], st_mean], writes=[t32])
                    op("dve", lambda e: e.tensor_tensor(t32[:, 0:w], t32[:, 0:w], st_rstd[:, 0:w], ALU.mult),
                       reads=[t32, st_rstd], writes=[t32])
                    if not emit_out:
                        act(xT[c][:, c0:c1], t32[:, 0:w], AF.Identity, [t32, gvec, bvec], [xT[c]],
                            scale=gvec[:, l, c:c + 1], bias=bvec[:, l, c:c + 1])
                    else:
                        yb = tf[2 + (c % 4)]
                        act(yb[:, 0:w], t32[:, 0:w], AF.Identity, [t32, gvec, bvec], [yb],
                            scale=gvec[:, l, c:c + 1], bias=bvec[:, l, c:c + 1])
                        nb = (w + 127) // 128
                        for bi in range(nb):
                            nr = min(128, w - bi * 128)
                            if c % 4 == 0 and bi == 0:
                                pass
                            pt = nps()
                            op("pe", lambda e: e.transpose(pt[0:nr, 0:128], yb[:, bi * 128:bi * 128 + nr], identf[:]),
                               reads=[yb, identf], writes=[pt])
                            ob = ost[ost_i[0] % 4]
                            ost_i[0] += 1
                            op("dve", lambda e: e.tensor_copy(ob[0:nr, :], pt[0:nr, 0:128]), reads=[pt], writes=[ob])
                            r0 = tile["r0"] + c0 + bi * 128
                            K.dma(y_out[r0:r0 + nr, c * 128:(c + 1) * 128], ob[0:nr, :], reads=[ob], writes=[dram_y])

        ost = [K.sb("ost%d" % i, [128, 128], F32) for i in range(4)]
        ost_i = [0]
        xld = [K.sb("xld%d" % i, [128, 512], F32) for i in range(2)]
        xld_i = [0]

        def dump(nm, bufs, tile, l):
            if not debug or l != 0 or tile["idx"] != 0:
                return
            for i, b in enumerate(bufs):
                K.dma(dbg[nm][i, :, 0:tile["w"]], b[:, 0:tile["w"]], reads=[b])

        def mixer(tile, l, first_tile, last_tile):
            if stop_at in ("load", "load_dma", "load_1"):
                raise _Stop()
            hgrn_all(tile, l, first_tile, last_tile)
            dump("d_on", on, tile, l)
            if stop_at == "hgrn":
                raise _Stop()
            gated_proj(tile, l, on, O_GA, "pa", False)
            dump("d_mga", mg, tile, l)
            if stop_at == "pa":
                raise _Stop()
            ssd(tile, l, first_tile, last_tile)
            dump("d_yn", on, tile, l)
            if stop_at == "ssd":
                raise _Stop()
            gated_proj(tile, l, on, O_GB, "pb", True)
            dump("d_mg", mg, tile, l)
            for c in range(16):
                gemm([("out", l, 0, NKC, c * 128, 128)], mg, tile,
                     lambda p, c0, c1: op("dve", lambda e: e.scalar_tensor_tensor(xT[c][:, c0:c1], xT[c][:, c0:c1], ALPHA,
                                                                                  p[:, 0:c1 - c0], ALU.mult, ALU.add),
                                          reads=[xT[c], p], writes=[xT[c]]))
            layer_norm(tile, pv_l1g, pv_l1b, l, False)
            dump("d_xmid", xT, tile, l)
            if stop_at == "mixer":
                raise _Stop()

        def ffn(tile, l, last_layer):
            T = tile["w"]
            pre = tf[8]
            ga = tf[7]
            for j in range(NFF):
                gemm([("up", l, 0, NKC, j * 128, 128)], xT, tile, lambda p, c0, c1: seg_evac(tile, p, c0, c1, pre, 2))
                conv_apply(tile, pre, halo_f[l], halo_fs[l], j, lambda k: pv_fcw[:, l, k, j:j + 1], pv_fcb[:, l, j:j + 1], 3,
                           ga, AF.Gelu)
                gemm([("up", l, 0, NKC, DFF + j * 128, 128)], xT, tile,
                     lambda p, c0, c1: op("dve", lambda e: e.tensor_tensor(hm[j][:, c0:c1], p[:, 0:c1 - c0], ga[:, c0:c1], ALU.mult),
                                          reads=[p, ga], writes=[hm[j]]))
            for c in range(16):
                gemm([("dn", l, 0, 16, c * 128, 128), ("dn", l, 16, 16, c * 128, 128), ("dn", l, 32, 12, c * 128, 128)], hm, tile,
                     lambda p, c0, c1: op("dve", lambda e: e.scalar_tensor_tensor(xT[c][:, c0:c1], xT[c][:, c0:c1], ALPHA,
                                                                                  p[:, 0:c1 - c0], ALU.mult, ALU.add),
                                          reads=[xT[c], p], writes=[xT[c]]))
            dump("d_hm", hm, tile, l)
            dump("d_v2", xT, tile, l)
            layer_norm(tile, pv_l2g, pv_l2b, l, last_layer)

        if stop_at == "setup":
            tiles = []
        for tile in tiles:
            T = tile["w"]
            ti = tile["idx"]
            first_tile = (ti == 0)
            last_tile = (ti == NTILES - 1)
            nblk = (T + 127) // 128
            for bi in range(nblk):
                r0 = bi * 128
                nr = min(128, T - r0)
                for q4 in range(4):
                    xtok = xld[xld_i[0] % 2]
                    xld_i[0] += 1
                    K.dma(xtok[0:nr, :], xin[tile["r0"] + r0: tile["r0"] + r0 + nr, q4 * 512:(q4 + 1) * 512], writes=[xtok])
                    p = nps()
                    for j in range(4):
                        op("pe", lambda e: e.transpose(p[:, j * 128: j * 128 + nr], xtok[0:nr, j * 128:(j + 1) * 128],
                                                       identf[0:nr, 0:nr]), reads=[xtok, identf], writes=[p], inc=(j == 3))
                    for j in range(4):
                        cidx = q4 * 4 + j
                        if j % 2:
                            op("dve", lambda e: e.tensor_copy(xT[cidx][:, r0:r0 + nr], p[:, j * 128:j * 128 + nr]), reads=[p], writes=[xT[cidx]])
                        else:
                            op("dve", lambda e: e.tensor_copy(xT[cidx][:, r0:r0 + nr], p[:, j * 128:j * 128 + nr]), reads=[p], writes=[xT[cidx]])
            try:
                for l in range(nlayers):
                    mixer(tile, l, first_tile, last_tile)
                    ffn(tile, l, l == nlayers - 1)
            except _Stop:
                break
        for l in range(nlayers):
            for k in range(3):
                K.dma(conv_out[l, 0, k, :].rearrange("(c p) -> p c", p=128), halo_x[l][:, :, k], reads=[halo_x[l]], writes=[dram_cv],
                      allow_slow_non_contiguous=True)
                K.dma(conv_out[l, 1, k, :].rearrange("(c p) -> p c", p=128), halo_xs[l][:, :, k], reads=[halo_xs[l]], writes=[dram_cv],
                      allow_slow_non_contiguous=True)
            for k in range(2):
                K.dma(ffn_out[l, 0, k, :].rearrange("(c p) -> p c", p=128), halo_f[l][:, :, k], reads=[halo_f[l]], writes=[dram_ff],
                      allow_slow_non_contiguous=True)
                K.dma(ffn_out[l, 1, k, :].rearrange("(c p) -> p c", p=128), halo_fs[l][:, :, k], reads=[halo_fs[l]], writes=[dram_ff],
                      allow_slow_non_contiguous=True)
        K.finish([dram_y, dram_hg, dram_ss, dram_cv, dram_ff])
        print("instructions:", K.n_ins, "waits:", K.n_wait, "sems:", K.nsem, flush=True)
    return nc


_CACHE = {}


def kernel(**inp):
    f32 = lambda a: np.ascontiguousarray(np.asarray(a, dtype=np.float32))
    xp = f32(inp["x_prompt"])
    xs = f32(inp["x_sample"])
    meta = f32(inp["meta_tokens"])
    if "nc" not in _CACHE:
        _CACHE["nc"] = build_program()
    nc = _CACHE["nc"]
    consts = build_consts()
    wnames = ["w_in", "hgrn_lb_logits", "hgrn_norm_w", "w_proj_a", "ssm_conv_w", "ssm_conv_b", "ssm_dt_bias", "ssm_a_log",
              "ssm_d", "ssm_norm_w", "w_proj_b", "w_out", "ln1_g", "ln1_b", "ffn_w_up", "ffn_conv_w", "ffn_conv_b",
              "ffn_w_down", "ln2_g", "ln2_b"]
    shared = {n: f32(inp[n]) for n in wnames}
    for k, v in consts.items():
        shared["c_" + k] = v
    in_maps = []
    for c in range(8):
        b = c % 2
        xin = np.concatenate([meta, xp[b, 0:TP], xs[c], xp[b, TP:]], axis=0)
        m = dict(shared)
        m["xin"] = np.ascontiguousarray(xin)
        m["st_hg"] = f32(inp["state_hgrn"][:, c])
        m["st_ssm"] = f32(inp["state_ssm"][:, c])
        m["st_conv"] = f32(inp["state_ssm_conv"][:, c])
        m["st_ffn"] = f32(inp["state_ffn_conv"][:, c])
        in_maps.append(m)
    res = run_bass_kernel_spmd(nc, in_maps, core_ids=list(range(8)))
    R = res.results
    ys = [r["y"] for r in R]
    y_prompt = np.stack([np.concatenate([ys[b][N_META:N_META + TP], ys[b][N_META + TP + DEC_SEQ:]], axis=0) for b in range(2)])
    y_sample = np.stack([ys[c][N_META + TP:N_META + TP + DEC_SEQ] for c in range(8)])
    hg_p = np.stack([R[b]["hg_o"][:, 0] for b in range(2)], axis=1)
    ssm_p = np.stack([R[b]["ssm_o"][:, 0] for b in range(2)], axis=1)
    conv_p = np.stack([R[b]["conv_o"][:, 0] for b in range(2)], axis=1)
    ffn_p = np.stack([R[b]["ffn_o"][:, 0] for b in range(2)], axis=1)
    hg_s = np.stack([R[c]["hg_o"][:, 1] for c in range(8)], axis=1)
    ssm_s = np.stack([R[c]["ssm_o"][:, 1] for c in range(8)], axis=1)
    conv_s = np.stack([R[c]["conv_o"][:, 1] for c in range(8)], axis=1)
    ffn_s = np.stack([R[c]["ffn_o"][:, 1] for c in range(8)], axis=1)
    outs = (y_prompt, y_sample, hg_p, ssm_p, conv_p, ffn_p, hg_s, ssm_s, conv_s, ffn_s)
    return tuple(np.ascontiguousarray(o, dtype=np.float32) for o in outs)
DBG_NAMES = ["d_on", "d_mga", "d_yn", "d_mg", "d_xmid", "d_v2", "d_hm"]
```

```python
import contextlib
import numpy as np
import concourse.bass as bass
import concourse.mybir as mybir
from concourse.bass_utils import run_bass_kernel_spmd

F32 = mybir.dt.float32
BF16 = mybir.dt.bfloat16
AF = mybir.ActivationFunctionType
ALU = mybir.AluOpType

D = 2048
NKC = 16
DEPTH = 2
SEQ = 4096
N_META = 16
DEC_SEQ = 32
HG_H = 16
SSM_H = 32
SSM_P = 64
SSM_N = 128
SSM_G = 4
XBC = 3072
DFF = 5632
NFF = 44
N_IN = 17440
O_Q, O_F, O_I, O_G, O_Z, O_XBC, O_DT, O_GA, O_GB = 0, 2048, 4096, 6144, 8192, 10240, 13312, 13344, 15392
ALPHA = (2.0 * DEPTH) ** 0.25
LN_EPS = 1e-5
RMS_EPS = 1e-6
BIG = 30000.0
TW = 560
TP = 512
NTILES = SEQ // TP
NTOK = N_META + SEQ + DEC_SEQ
MAXC = 30000


class Buf:
    __slots__ = ("t", "w", "r", "name")

    def __init__(self, t, name=""):
        self.t = t
        self.w = None
        self.r = {}
        self.name = name

    def __getitem__(self, k):
        return self.t[k]


class Eng:
    def __init__(self, name, handle):
        self.name = name
        self.h = handle
        self.sem = None
        self.count = 0
        self.known = {}


class KB:
    def __init__(self, nc, stack, n_dma_sems=16):
        self.nc = nc
        self.stack = stack
        self.sems = {}
        self.eng = {}
        self.nsem = 0
        for name, h in (("pe", nc.tensor), ("act", nc.scalar), ("dve", nc.vector),
                        ("pool", nc.gpsimd), ("sp", nc.sync)):
            E = Eng(name, h)
            self.eng[name] = E
            self._newsem(E)
        self.dma_slots = []
        for i in range(n_dma_sems):
            s = stack.enter_context(nc.semaphore("d%d" % i))
            self.sems[id(s)] = s
            self.dma_slots.append([s, 0])
        self.dma_i = 0
        self.n_wait = 0
        self.n_ins = 0
        self.names = set()

    def _newsem(self, E):
        s = self.stack.enter_context(self.nc.semaphore("s_%s_%d" % (E.name, self.nsem)))
        self.nsem += 1
        self.sems[id(s)] = s
        E.sem = s
        E.count = 0

    def uname(self, name):
        i = 0
        n = name
        while n in self.names:
            i += 1
            n = "%s_%d" % (name, i)
        self.names.add(n)
        return n

    def sb(self, name, shape, dt):
        name = self.uname(name)
        t = self.stack.enter_context(self.nc.sbuf_tensor(name, list(shape), dt))
        return Buf(t, name)

    def psum(self, name, shape, dt):
        t = self.stack.enter_context(self.nc.psum_tensor(name, list(shape), dt))
        return Buf(t, name)

    def _need(self, reads, writes):
        need = {}
        for b in reads:
            if b.w is not None:
                k, v = b.w
                if need.get(k, 0) < v:
                    need[k] = v
        for b in writes:
            if b.w is not None:
                k, v = b.w
                if need.get(k, 0) < v:
                    need[k] = v
            for k, v in b.r.items():
                if need.get(k, 0) < v:
                    need[k] = v
        return need

    def _wait(self, E, need, skip_own=False):
        for k, v in need.items():
            if skip_own and k == id(E.sem):
                continue
            if E.known.get(k, 0) < v:
                E.h.wait_ge(self.sems[k], v)
                E.known[k] = v
                self.n_wait += 1

    def _mark(self, tok, reads, writes):
        k, v = tok
        for b in reads:
            if b.r.get(k, 0) < v:
                b.r[k] = v
        for b in writes:
            b.w = tok
            b.r = {}

    def op(self, e, emit, reads=(), writes=(), inc=True):
        E = self.eng[e]
        if inc and E.count >= MAXC:
            self._newsem(E)
        self._wait(E, self._need(reads, writes), skip_own=(e == "pe"))
        ins = emit(E.h)
        self.n_ins += 1
        if inc:
            E.count += 1
            ins.then_inc(E.sem, 1)
            tok = (id(E.sem), E.count)
        else:
            tok = (id(E.sem), E.count + 1)
        self._mark(tok, reads, writes)
        return tok

    def dma(self, out, in_, reads=(), writes=(), q="sp", **kw):
        E = self.eng[q]
        slot = self.dma_slots[self.dma_i]
        self.dma_i = (self.dma_i + 1) % len(self.dma_slots)
        if slot[1] >= MAXC // 16:
            s = self.stack.enter_context(self.nc.semaphore("dx%d" % self.nsem))
            self.nsem += 1
            self.sems[id(s)] = s
            self._wait(E, {id(slot[0]): slot[1] * 16})
            slot[0] = s
            slot[1] = 0
        need = self._need(reads, writes)
        if slot[1] > 0:
            k = id(slot[0])
            need[k] = max(need.get(k, 0), slot[1] * 16)
        self._wait(E, need)
        E.h.dma_start(out=out, in_=in_, **kw).then_inc(slot[0], 16)
        self.n_ins += 1
        slot[1] += 1
        tok = (id(slot[0]), slot[1] * 16)
        self._mark(tok, reads, writes)
        return tok

    def finish(self, bufs=()):
        E = self.eng["sp"]
        need = {}
        for s, c in self.dma_slots:
            if c:
                need[id(s)] = c * 16
        for b in bufs:
            if b.w is not None:
                k, v = b.w
                need[k] = max(need.get(k, 0), v)
        self._wait(E, need)


def tile_defs(ntiles):
    tiles = []
    r = 0
    for ti in range(ntiles):
        if ti == 0:
            w = N_META + TP + DEC_SEQ
            segs = [(0, N_META + TP, 0), (N_META + TP, DEC_SEQ, 1)]
            chunks = [(N_META + TP, DEC_SEQ, 1), (0, N_META, 0)] + [(N_META + 64 * j, 64, 0) for j in range(TP // 64)]
        else:
            w = TP
            segs = [(0, TP, 0)]
            chunks = [(64 * j, 64, 0) for j in range(TP // 64)]
        pieces = []
        c = 0
        while c < w:
            pieces.append((c, min(c + 512, w)))
            c += 512
        tiles.append(dict(r0=r, w=w, segs=segs, chunks=chunks, pieces=pieces, idx=ti))
        r += w
    return tiles


def build_consts():
    c = {}
    c["identf"] = np.eye(128, dtype=np.float32)
    tri = (np.arange(128)[:, None] <= np.arange(128)[None, :]).astype(np.float32)
    c["tri"] = tri
    c["ones"] = np.ones((128, 128), np.float32)
    for L in (16, 32, 64):
        m = (np.arange(L)[None, :] < np.arange(L)[:, None]).astype(np.float32) * BIG
        c["bigm%d" % L] = np.ascontiguousarray(np.broadcast_to(m[:, None, :], (L, 8, L))).reshape(L, 8 * L)
    return c


def plan_pass(l):
    for h in range(HG_H):
        for o in (O_Q, O_F, O_I, O_G):
            yield ("in", l, 0, NKC, o + h * 128, 128)
    for c in range(16):
        yield ("in", l, 0, NKC, O_GA + c * 128, 128)
        yield ("pa", l, 0, NKC, c * 128, 128)
    yield ("in", l, 0, NKC, O_DT, 32)
    for g in range(SSM_G):
        yield ("in", l, 0, NKC, O_XBC + 2048 + g * 128, 128)
        yield ("in", l, 0, NKC, O_XBC + 2560 + g * 128, 128)
        for pr in range(4):
            cc = g * 4 + pr
            yield ("in", l, 0, NKC, O_XBC + cc * 128, 128)
            yield ("in", l, 0, NKC, O_Z + cc * 128, 128)
    for c in range(16):
        yield ("in", l, 0, NKC, O_GB + c * 128, 128)
        yield ("pb", l, 0, NKC, c * 128, 128)
    for c in range(16):
        yield ("out", l, 0, NKC, c * 128, 128)
    for j in range(NFF):
        yield ("up", l, 0, NKC, j * 128, 128)
        yield ("up", l, 0, NKC, DFF + j * 128, 128)
    for c in range(16):
        yield ("dn", l, 0, 16, c * 128, 128)
        yield ("dn", l, 16, 16, c * 128, 128)
        yield ("dn", l, 32, 12, c * 128, 128)


NBL = len(list(plan_pass(0)))
WNAME = {"in": "w_in", "pa": "w_proj_a", "pb": "w_proj_b", "out": "w_out", "up": "ffn_w_up", "dn": "ffn_w_down"}


def build_wstream(inp):
    wst = np.zeros((DEPTH, NBL, 128, NKC, 128), np.float32)
    for l in range(DEPTH):
        for j, (name, _, k0, nk, c0, ncol) in enumerate(plan_pass(l)):
            W = inp[WNAME[name]][l]
            blk = np.asarray(W[k0 * 128:(k0 + nk) * 128, c0:c0 + ncol], dtype=np.float32).reshape(nk, 128, ncol)
            wst[l, j, :, 0:nk, 0:ncol] = blk.transpose(1, 0, 2)
    return wst.reshape(DEPTH, NBL, 128, NKC * 128)


class _Stop(Exception):
    pass


def build_program(ntiles=NTILES, nlayers=DEPTH, debug=False, stop_at=None):
    nc = bass.Bass("TRN2", target_bir_lowering=False)
    tiles = tile_defs(ntiles)
    ntok = sum(t["w"] for t in tiles)

    def din(name, shape):
        return nc.dram_tensor(name, list(shape), F32, kind="ExternalInput").ap()

    def dout(name, shape):
        return nc.dram_tensor(name, list(shape), F32, kind="ExternalOutput").ap()

    xin = din("xin", [NTOK, D])
    st_hg = din("st_hg", [DEPTH, HG_H, 128, 128])
    st_ssm = din("st_ssm", [DEPTH, SSM_H, SSM_P, SSM_N])
    st_conv = din("st_conv", [DEPTH, 3, XBC])
    st_ffn = din("st_ffn", [DEPTH, 2, DFF])
    lb_logits = din("hgrn_lb_logits", [DEPTH, D])
    hg_nw = din("hgrn_norm_w", [DEPTH, D])
    cv_w = din("ssm_conv_w", [DEPTH, 4, XBC])
    cv_b = din("ssm_conv_b", [DEPTH, XBC])
    dt_bias = din("ssm_dt_bias", [DEPTH, SSM_H])
    a_log = din("ssm_a_log", [DEPTH, SSM_H])
    ssm_d = din("ssm_d", [DEPTH, SSM_H])
    ssm_nw = din("ssm_norm_w", [DEPTH, D])
    ln1_g = din("ln1_g", [DEPTH, D])
    ln1_b = din("ln1_b", [DEPTH, D])
    fcv_w = din("ffn_conv_w", [DEPTH, 3, DFF])
    fcv_b = din("ffn_conv_b", [DEPTH, DFF])
    ln2_g = din("ln2_g", [DEPTH, D])
    ln2_b = din("ln2_b", [DEPTH, D])
    wst = din("wst", [DEPTH, NBL, 128, NKC * 128])
    cdefs = build_consts()
    cin = {k: din("c_" + k, v.shape) for k, v in cdefs.items()}

    y_out = dout("y", [NTOK, D])
    hg_out = dout("hg_o", [DEPTH, 2, HG_H, 128, 128])
    ssm_out = dout("ssm_o", [DEPTH, 2, SSM_H, SSM_P, SSM_N])
    conv_out = dout("conv_o", [DEPTH, 2, 3, XBC])
    ffn_out = dout("ffn_o", [DEPTH, 2, 2, DFF])
    dbg = {}
    if debug:
        for nm in ("d_on", "d_mga", "d_yn", "d_mg", "d_xmid", "d_v2"):
            dbg[nm] = nc.dram_tensor(nm, [16, 128, TW], BF16, kind="ExternalOutput").ap()
        dbg["d_hm"] = nc.dram_tensor("d_hm", [NFF, 128, TW], BF16, kind="ExternalOutput").ap()


    with contextlib.ExitStack() as st:
        K = KB(nc, st)
        op = K.op

        xT = [K.sb("xT%d" % i, [128, TW], BF16) for i in range(NKC)]
        hm = [K.sb("hm%d" % i, [128, TW], BF16) for i in range(NFF)]
        on = hm[0:16]
        mg = hm[16:32]
        NT32 = 10
        tf = [K.sb("tf%d" % i, [128, TW + 40], F32) for i in range(NT32)]
        tb = [K.sb("tb%d" % i, [128, TW], BF16) for i in range(10)]
        NS, NB = 2, 6
        stage = [K.sb("stg%d" % i, [128, NKC, 128], F32) for i in range(NS)]
        wring = [K.sb("wb%d" % i, [128, NKC, 128], BF16) for i in range(NB)]
        psb = [K.psum("ps%d" % i, [128, 512], F32) for i in range(8)]
        ps_i = [0]

        def nps():
            b = psb[ps_i[0]]
            ps_i[0] = (ps_i[0] + 1) % 8
            return b

        Shg1 = K.sb("Shg", [128, 128], F32)
        Hss1 = K.sb("Hss", [128, 512], F32)
        Hsb1 = K.sb("Hsb", [128, 512], BF16)
        hg_c = Buf(nc.dram_tensor("hg_carry", [DEPTH, HG_H, 128, 128], F32).ap(), "hg_carry")
        ss_c = Buf(nc.dram_tensor("ss_carry", [DEPTH, SSM_G, 128, 512], F32).ap(), "ss_carry")
        halo_x = [K.sb("halox%d" % l, [128, 24, 3], F32) for l in range(DEPTH)]
        halo_f = [K.sb("halof%d" % l, [128, NFF, 2], F32) for l in range(DEPTH)]
        halo_xs = [K.sb("haloxs%d" % l, [128, 24, 3], F32) for l in range(DEPTH)]
        halo_fs = [K.sb("halofs%d" % l, [128, NFF, 2], F32) for l in range(DEPTH)]
        dram_y = Buf(y_out, "y")
        dram_hg = Buf(hg_out, "hgo")
        dram_ss = Buf(ssm_out, "sso")
        dram_cv = Buf(conv_out, "cvo")
        dram_ff = Buf(ffn_out, "ffo")

        cb = {}
        for k, v in cdefs.items():
            b = K.sb("sc_" + k, [128, v.shape[1]], F32)
            K.dma(b[0:v.shape[0], :], cin[k][:, :], writes=[b])
            cb[k] = b
        identb = K.sb("identb", [128, 128], BF16)
        onesb = K.sb("onesb", [128, 128], BF16)
        trib = K.sb("trib", [128, 128], BF16)
        negtri = K.sb("negtri", [128, 128], F32)
        op("dve", lambda e: e.tensor_copy(identb[:], cb["identf"][:]), reads=[cb["identf"]], writes=[identb])
        op("dve", lambda e: e.tensor_copy(onesb[:], cb["ones"][:]), reads=[cb["ones"]], writes=[onesb])
        op("dve", lambda e: e.tensor_copy(trib[:], cb["tri"][:]), reads=[cb["tri"]], writes=[trib])
        op("dve", lambda e: e.tensor_scalar(negtri[:], cb["tri"][:], -1.0, None, ALU.mult), reads=[cb["tri"]], writes=[negtri])
        identf, onesf, trif = cb["identf"], cb["ones"], cb["tri"]
        one_col = K.sb("one_col", [128, 1], F32)
        op("dve", lambda e: e.memset(one_col[:], 1.0), writes=[one_col])

        def load_vec(name, ap, nch, extra=None):
            b = K.sb("pv_" + name, [128, DEPTH, nch] if extra is None else [128, DEPTH, extra, nch], F32)
            for l in range(DEPTH):
                if extra is None:
                    K.dma(b[:, l, :], ap[l, :].rearrange("(c p) -> p c", p=128), writes=[b],
                          allow_slow_non_contiguous=True)
                else:
                    for k in range(extra):
                        K.dma(b[:, l, k, :], ap[l, k, :].rearrange("(c p) -> p c", p=128), writes=[b],
                              allow_slow_non_contiguous=True)
            return b

        pv_lbl = load_vec("lbl", lb_logits, 16)
        pv_hgnw = load_vec("hgnw", hg_nw, 16)
        pv_cvw = load_vec("cvw", cv_w, 24, extra=4)
        pv_cvb = load_vec("cvb", cv_b, 24)
        pv_snw = load_vec("snw", ssm_nw, 16)
        pv_l1g = load_vec("l1g", ln1_g, 16)
        pv_l1b = load_vec("l1b", ln1_b, 16)
        pv_fcw = load_vec("fcw", fcv_w, NFF, extra=3)
        pv_fcb = load_vec("fcb", fcv_b, NFF)
        pv_l2g = load_vec("l2g", ln2_g, 16)
        pv_l2b = load_vec("l2b", ln2_b, 16)
        hv = {}
        for name, ap in (("dtb", dt_bias), ("alog", a_log), ("dsk", ssm_d)):
            b = K.sb("hv_" + name, [SSM_H, DEPTH], F32)
            K.dma(b[:], ap.rearrange("l h -> h l"), writes=[b], allow_slow_non_contiguous=True)
            hv[name] = b
        hvA = K.sb("hv_A", [SSM_H, DEPTH], F32)
        op("act", lambda e: e.activation(hvA[:], hv["alog"][:], AF.Exp), reads=[hv["alog"]], writes=[hvA])
        pv_lb = K.sb("pv_lb", [128, DEPTH, 16], F32)
        pv_oml = K.sb("pv_oml", [128, DEPTH, 16], F32)
        lbe = K.sb("lbe", [128, DEPTH, 16], F32)
        lbs = K.sb("lbs", [128, 16], F32)
        op("act", lambda e: e.activation(lbe[:], pv_lbl[:], AF.Exp), reads=[pv_lbl], writes=[lbe])
        op("dve", lambda e: e.tensor_copy(lbs[:], lbe[:, 0, :]), reads=[lbe], writes=[lbs])
        for l in range(1, DEPTH):
            op("dve", lambda e: e.tensor_tensor(lbs[:], lbs[:], lbe[:, l, :], ALU.add), reads=[lbe, lbs], writes=[lbs])
        op("dve", lambda e: e.reciprocal(lbs[:], lbs[:]), reads=[lbs], writes=[lbs])
        op("dve", lambda e: e.memset(pv_lb[:, 0, :], 0.0), writes=[pv_lb])
        for l in range(1, DEPTH):
            op("dve", lambda e: e.tensor_tensor(pv_lb[:, l, :], lbe[:, l, :], lbs[:], ALU.mult), reads=[lbe, lbs], writes=[pv_lb])
            if l > 1:
                op("dve", lambda e: e.tensor_tensor(pv_lb[:, l, :], pv_lb[:, l, :], pv_lb[:, l - 1, :], ALU.add),
                   reads=[pv_lb], writes=[pv_lb])
        op("dve", lambda e: e.tensor_scalar(pv_oml[:], pv_lb[:], -1.0, 1.0, ALU.mult, ALU.add), reads=[pv_lb], writes=[pv_oml])
        pv_dsk = K.sb("pv_dsk", [128, DEPTH, 16], F32)
        for l in range(DEPTH):
            for two in range(2):
                src = ssm_d[l:l + 1, :].rearrange("o (c two) -> o c two", two=2)[:, :, two]
                K.dma(pv_dsk[two * 64:(two + 1) * 64, l, :], src.to_broadcast([64, 16]), writes=[pv_dsk],
                      allow_slow_non_contiguous=True)
        for l in range(DEPTH):
            op("dve", lambda e: e.memset(halo_x[l][:], 0.0), writes=[halo_x[l]])
            op("dve", lambda e: e.memset(halo_f[l][:], 0.0), writes=[halo_f[l]])
            for k in range(3):
                K.dma(halo_xs[l][:, :, k], st_conv[l, k, :].rearrange("(c p) -> p c", p=128), writes=[halo_xs[l]],
                      allow_slow_non_contiguous=True)
            for k in range(2):
                K.dma(halo_fs[l][:, :, k], st_ffn[l, k, :].rearrange("(c p) -> p c", p=128), writes=[halo_fs[l]],
                      allow_slow_non_contiguous=True)

        sched = []

        def plan():
            for ti in range(ntiles):
                for l in range(nlayers):
                    yield from plan_pass(l)

        sched = list(plan())
        ws = dict(issued=0, got=0)
        cast_rr = [0]

        def ws_issue():
            i = ws["issued"]
            name, l, k0, nk, c0, ncol = sched[i]
            sg = stage[i % NS]
            wb = wring[i % NB]
            j = i % NBL
            K.dma(sg[:, 0:nk, :].rearrange("p k c -> p (k c)"), wst[l, j, :, 0:nk * 128], writes=[sg])
            op("act", lambda e: e.copy(wb[:, 0:nk, 0:ncol], sg[:, 0:nk, 0:ncol]), reads=[sg], writes=[wb])
            ws["issued"] += 1

        def wget(spec):
            i = ws["got"]
            assert sched[i] == spec, (i, sched[i], spec)
            while ws["issued"] < min(len(sched), i + NB - 2):
                ws_issue()
            ws["got"] += 1
            return wring[i % NB]

        def gemm(wspecs, srcs, tile, evac, m=128):
            wbs = [(wget(s), s[3]) for s in wspecs]
            for (c0, c1) in tile["pieces"]:
                p = nps()
                kk = 0
                nk_tot = sum(n for _, n in wbs)
                for wb, nk in wbs:
                    for k in range(nk):
                        src = srcs[kk]
                        last = (kk == nk_tot - 1)
                        op("pe", lambda e: e.matmul(p[0:m, 0:c1 - c0], lhsT=wb[:, k, 0:m], rhs=src[:, c0:c1],
                                                    start=(kk == 0), stop=last),
                           reads=[wb, src], writes=[p], inc=last)
                        kk += 1
                evac(p, c0, c1)

        def act(out, in_, func, reads, writes, **kw):
            op("act", lambda e: e.activation(out, in_, func, **kw), reads=reads, writes=writes)

        ATb = [K.sb("ATb%d" % i, [64, 64], BF16) for i in range(2)]
        vtok = [K.sb("vtok%d" % i, [64, 128], BF16) for i in range(2)]
        ktok = [K.sb("ktok%d" % i, [64, 128], BF16) for i in range(2)]
        Spb = [K.sb("Spb%d" % i, [128, 128], BF16) for i in range(2)]
        rr = K.sb("rr", [128, 3, 16], F32)
        rrd = K.sb("rrd", [128, 3, 16], F32)
        st_mean = K.sb("st_mean", [128, 512], F32)
        st_rstd = K.sb("st_rstd", [128, 512], F32)
        st_tmp = K.sb("st_tmp", [128, 512], F32)
        sqr = [K.sb("sqr%d" % i, [128, 512], BF16) for i in range(2)]
        pre_s = K.sb("pre_s", [128, 40], F32)
        acc_s = K.sb("acc_s", [128, 40], F32)
        dtT = tf[4]
        adtT = tf[5]
        NCH = 10
        dta = K.sb("dta", [64, NCH, 64], F32)
        dec_tok = K.sb("dec_tok", [64, NCH, 32], F32)
        w_tok = K.sb("w_tok", [64, NCH, 32], F32)
        rtot_bc = K.sb("rtot_bc", [128, NCH, 32], F32)
        gendS = K.sb("gendS", [128, 32], F32)
        wtmp = K.sb("wtmp", [64, 32], F32)
        gtokS = K.sb("gtokS", [64, 32], F32)
        xdt = K.sb("xdt", [64, 512], BF16)
        xw = K.sb("xw", [64, 512], BF16)
        Btok = K.sb("Btok", [64, 128], BF16)
        cbt = K.sb("cbt", [64, 64], F32)
        R1 = K.sb("R1", [64, 512], F32)
        R2 = K.sb("R2", [64, 512], F32)
        LT = R1
        scT = K.sb("scT", [64, 512], BF16)
        ytmp = R2
        ytok = K.sb("ytok", [64, 512], BF16)
        yT4 = tf[0:4]
        sttok = K.sb("sttok", [128, 128], F32)

        def seg_evac(tile, p, c0, c1, pre, hw):
            for (s0, sl, sq) in tile["segs"]:
                a, b = max(c0, s0), min(c1, s0 + sl)
                if a >= b:
                    continue
                dstb = pre if sq == 0 else pre_s
                dst = dstb[:, hw + a - s0: hw + b - s0]
                op("act", lambda e: e.copy(dst, p[:, a - c0:b - c0]), reads=[p], writes=[dstb])

        def conv_apply(tile, pre, halo, halo_sq, cidx, wfun, bap, width, dst, func):
            hw = width - 1
            for (s0, sl, sq) in tile["segs"]:
                pb = pre if sq == 0 else pre_s
                hb = halo if sq == 0 else halo_sq
                ab = tf[9] if sq == 0 else acc_s
                op("dve", lambda e: e.tensor_copy(pb[:, 0:hw], hb[:, cidx, :]), reads=[hb], writes=[pb])
                op("dve", lambda e: e.tensor_scalar(ab[:, 0:sl], pb[:, hw:hw + sl], wfun(hw), bap, ALU.mult, ALU.add),
                   reads=[pb], writes=[ab])
                for k in range(hw):
                    op("dve", lambda e: e.scalar_tensor_tensor(ab[:, 0:sl], pb[:, k:k + sl], wfun(k), ab[:, 0:sl],
                                                               ALU.mult, ALU.add), reads=[pb, ab], writes=[ab])
                act(dst[:, s0:s0 + sl], ab[:, 0:sl], func, [ab], [dst])
                op("dve", lambda e: e.tensor_copy(hb[:, cidx, :], pb[:, sl:sl + hw]), reads=[pb], writes=[hb])

        def bc_sum(srcs_fn, nsrc, c0, c1):
            p = nps()
            for i in range(nsrc):
                sbuf, sap = srcs_fn(i)
                op("pe", lambda e: e.matmul(p[:, 0:c1 - c0], lhsT=onesb[:], rhs=sap, start=(i == 0), stop=(i == nsrc - 1)),
                   reads=[onesb, sbuf], writes=[p], inc=(i == nsrc - 1))
            return p

        def rstd_from(p, w, scale, eps, dst):
            act(st_tmp[:, 0:w], p[:, 0:w], AF.Sqrt, [p], [st_tmp], bias=eps_col(eps), scale=scale)
            op("dve", lambda e: e.reciprocal(dst[:, 0:w], st_tmp[:, 0:w]), reads=[st_tmp], writes=[dst])

        eps_cols = {}

        def eps_col(v):
            if v not in eps_cols:
                b = K.sb("epsc%d" % len(eps_cols), [128, 1], F32)
                op("dve", lambda e: e.memset(b[:], float(v)), writes=[b])
                eps_cols[v] = b
            return eps_cols[v][:, 0:1]

        eps_col(RMS_EPS)
        eps_col(LN_EPS)

        def gemm_g(wspecs, srcs, tile, evac, m=128):
            wbs = [(wget(s), s[3]) for s in wspecs]
            for (c0, c1) in tile["pieces"]:
                p = nps()
                kk = 0
                nk_tot = sum(n for _, n in wbs)
                for wb, nk in wbs:
                    for k in range(nk):
                        src = srcs[kk]
                        last = (kk == nk_tot - 1)
                        op("pe", lambda e: e.matmul(p[0:m, 0:c1 - c0], lhsT=wb[:, k, 0:m], rhs=src[:, c0:c1],
                                                    start=(kk == 0), stop=last),
                           reads=[wb, src], writes=[p], inc=last)
                        kk += 1
                evac(p, c0, c1)
                yield

        HS = []
        for s_ in range(2):
            HS.append(dict(
                qs=tf[4 * s_ + 0], fk=tf[4 * s_ + 1], ln=tf[4 * s_ + 2], G=tf[4 * s_ + 3],
                vT=tb[4 * s_ + 0], gs=tb[4 * s_ + 1], qt=tb[4 * s_ + 2], kt=tb[4 * s_ + 3],
                S=K.sb("ShgS%d" % s_, [128, 128], F32), rr=K.sb("rrS%d" % s_, [128, 3, 16], F32),
                rrd=K.sb("rrdS%d" % s_, [128, 3, 16], F32),
                ATb=[K.sb("ATbS%d_%d" % (s_, i), [64, 64], BF16) for i in range(2)],
                vtok=[K.sb("vtokS%d_%d" % (s_, i), [64, 128], BF16) for i in range(2)],
                ktok=[K.sb("ktokS%d_%d" % (s_, i), [64, 128], BF16) for i in range(2)],
                Spb=[K.sb("SpbS%d_%d" % (s_, i), [128, 128], BF16) for i in range(2)],
                rstd=K.sb("rstdS%d" % s_, [128, 512], F32)))

        def hg_prep(tile, l, h, B):
            T = tile["w"]
            qs, fk, lnf, GnS = B["qs"], B["fk"], B["ln"], B["G"]
            Dd, E2 = B["ln"], B["G"]
            vT, gs, qt, kt = B["vT"], B["gs"], B["qt"], B["kt"]
            rr_, rrd_ = B["rr"], B["rrd"]
            specs = [("in", l, 0, NKC, o + h * 128, 128) for o in (O_Q, O_F, O_I, O_G)]
            yield from gemm_g([specs[0]], xT, tile, lambda p, c0, c1: act(qs[:, c0:c1], p[:, 0:c1 - c0], AF.Silu, [p], [qs]))
            yield from gemm_g([specs[1]], xT, tile, lambda p, c0, c1: act(fk[:, c0:c1], p[:, 0:c1 - c0], AF.Sigmoid, [p], [fk]))
            yield from gemm_g([specs[2]], xT, tile, lambda p, c0, c1: op("act", lambda e: e.copy(vT[:, c0:c1], p[:, 0:c1 - c0]),
                                                                         reads=[p], writes=[vT]))
            yield from gemm_g([specs[3]], xT, tile, lambda p, c0, c1: act(gs[:, c0:c1], p[:, 0:c1 - c0], AF.Silu, [p], [gs]))
            op("dve", lambda e: e.tensor_scalar(fk[:, 0:T], fk[:, 0:T], pv_oml[:, l, h:h + 1], pv_lb[:, l, h:h + 1],
                                                ALU.mult, ALU.add), reads=[fk, pv_oml, pv_lb], writes=[fk])
            act(lnf[:, 0:T], fk[:, 0:T], AF.Ln, [fk], [lnf])
            op("dve", lambda e: e.tensor_scalar(fk[:, 0:T], fk[:, 0:T], -1.0, 1.0, ALU.mult, ALU.add), reads=[fk], writes=[fk])
            op("dve", lambda e: e.memset(GnS[:, 0:1], 0.0), writes=[GnS])
            op("dve", lambda e: e.tensor_tensor_scan(GnS[:, 1:T + 1], one_col[:, 0:1].to_broadcast([128, T]), lnf[:, 0:T],
                                                     0.0, ALU.mult, ALU.add), reads=[one_col, lnf], writes=[GnS])
            yield
            runs = []
            for ci, (a, L, sq) in enumerate(tile["chunks"]):
                if runs and runs[-1][2] == L and runs[-1][0] + runs[-1][1] * L == a:
                    runs[-1][1] += 1
                else:
                    runs.append([a, 1, L, ci])
            for (a, n, L, ci0) in runs:
                hl = L // 2
                gv = lambda off: GnS[:, a + off:a + off + n * L].rearrange("p (j l) -> p j l", l=L)
                ref = gv(hl)[:, :, 0:1]
                sta = gv(0)[:, :, 0:1]
                end = gv(L)[:, :, 0:1]
                op("dve", lambda e: e.tensor_tensor(Dd[:, a:a + n * L].rearrange("p (j l) -> p j l", l=L), gv(1),
                                                    ref.to_broadcast([128, n, L]), ALU.subtract), reads=[GnS], writes=[Dd])
                r3 = lambda k: rrd_[:, k, ci0:ci0 + n].unsqueeze(2)
                op("dve", lambda e: e.tensor_tensor(r3(0), ref, sta, ALU.subtract), reads=[GnS], writes=[rrd_])
                op("dve", lambda e: e.tensor_tensor(r3(1), end, ref, ALU.subtract), reads=[GnS], writes=[rrd_])
                op("dve", lambda e: e.tensor_tensor(r3(2), end, sta, ALU.subtract), reads=[GnS], writes=[rrd_])
            nchk = len(tile["chunks"])
            act(rr_[:, :, 0:nchk], rrd_[:, :, 0:nchk], AF.Exp, [rrd_], [rr_])
            act(E2[:, 0:T], Dd[:, 0:T], AF.Exp, [Dd], [E2], scale=-1.0)
            act(Dd[:, 0:T], Dd[:, 0:T], AF.Exp, [Dd], [Dd])
            yield
            op("dve", lambda e: e.tensor_tensor(qt[:, 0:T], qs[:, 0:T], Dd[:, 0:T], ALU.mult), reads=[qs, Dd], writes=[qt])
            op("dve", lambda e: e.tensor_tensor(kt[:, 0:T], fk[:, 0:T], E2[:, 0:T], ALU.mult), reads=[fk, E2], writes=[kt])
            yield

        def hg_loop(tile, l, h, B, first_tile, last_tile):
            T = tile["w"]
            oT = B["qs"]
            vT, gs, qt, kt = B["vT"], B["gs"], B["qt"], B["kt"]
            rr_ = B["rr"]
            S = B["S"]
            cur_seq = None
            for ci, (a, L, sq) in enumerate(tile["chunks"]):
                if sq != cur_seq:
                    if cur_seq is not None:
                        store_hg(l, h, cur_seq, last_tile, S)
                    if sq == 0:
                        if first_tile:
                            op("dve", lambda e: e.memset(S[:], 0.0), writes=[S])
                        else:
                            K.dma(S[:], hg_c[l, h, :, :], reads=[hg_c], writes=[S])
                    else:
                        K.dma(S[:], st_hg[l, h, :, :], writes=[S])
                    cur_seq = sq
                i2 = ci % 2
                ATb_, vtok_, ktok_, Spb_ = B["ATb"][i2], B["vtok"][i2], B["ktok"][i2], B["Spb"][i2]
                p1 = nps()
                op("pe", lambda e: e.matmul(p1[0:L, 0:L], lhsT=kt[:, a:a + L], rhs=qt[:, a:a + L], start=True, stop=True),
                   reads=[kt, qt], writes=[p1])
                p2 = nps()
                p2b = p2[:, 0:256].bitcast(BF16)
                op("pe", lambda e: e.transpose(p2b[0:L, 0:128], vT[:, a:a + L], identb[:]), reads=[vT, identb], writes=[p2])
                p3 = nps()
                p3b = p3[:, 0:256].bitcast(BF16)
                op("pe", lambda e: e.transpose(p3b[0:L, 0:128], kt[:, a:a + L], identb[:]), reads=[kt, identb], writes=[p3])
                op("dve", lambda e: e.tensor_tensor(ATb_[0:L, 0:L], p1[0:L, 0:L], trif[0:L, 0:L], ALU.mult),
                   reads=[p1, trif], writes=[ATb_])
                op("act", lambda e: e.copy(vtok_[0:L, :], p2b[0:L, 0:128]), reads=[p2], writes=[vtok_])
                op("act", lambda e: e.copy(ktok_[0:L, :], p3b[0:L, 0:128]), reads=[p3], writes=[ktok_])
                op("dve", lambda e: e.tensor_scalar(Spb_[:], S[:], rr_[:, 0, ci:ci + 1], None, ALU.mult),
                   reads=[S, rr_], writes=[Spb_])
                yield
                p4 = nps()
                op("pe", lambda e: e.matmul(p4[:, 0:L], lhsT=vtok_[0:L, :], rhs=ATb_[0:L, 0:L], start=True, stop=False),
                   reads=[vtok_, ATb_], writes=[p4], inc=False)
                op("pe", lambda e: e.matmul(p4[:, 0:L], lhsT=Spb_[:], rhs=qt[:, a:a + L], start=False, stop=True),
                   reads=[Spb_, qt], writes=[p4])
                p5 = nps()
                op("pe", lambda e: e.matmul(p5[:, 0:128], lhsT=ktok_[0:L, :], rhs=vtok_[0:L, :], start=True, stop=True),
                   reads=[ktok_, vtok_], writes=[p5])
                op("act", lambda e: e.copy(oT[:, a:a + L], p4[:, 0:L]), reads=[p4], writes=[oT])
                op("dve", lambda e: e.tensor_scalar(S[:], S[:], rr_[:, 2, ci:ci + 1], None, ALU.mult), reads=[S, rr_], writes=[S])
                op("dve", lambda e: e.scalar_tensor_tensor(S[:], p5[:, 0:128], rr_[:, 1, ci:ci + 1], S[:], ALU.mult, ALU.add),
                   reads=[p5, rr_, S], writes=[S])
                yield
            store_hg(l, h, cur_seq, last_tile, S)
            sq_b = B["kt"]
            rstd_ = B["rstd"]
            act(sq_b[:, 0:T], oT[:, 0:T], AF.Square, [oT], [sq_b])
            yield
            for (c0, c1) in tile["pieces"]:
                w = c1 - c0
                p = bc_sum(lambda i: (sq_b, sq_b[:, c0:c1]), 1, c0, c1)
                rstd_from(p, w, 1.0 / 128, RMS_EPS, rstd_)
                yield
                op("dve", lambda e: e.scalar_tensor_tensor(oT[:, c0:c1], oT[:, c0:c1], pv_hgnw[:, l, h:h + 1], rstd_[:, 0:w],
                                                           ALU.mult, ALU.mult), reads=[oT, pv_hgnw, rstd_], writes=[oT])
                op("dve", lambda e: e.tensor_tensor(on[h][:, c0:c1], oT[:, c0:c1], gs[:, c0:c1], ALU.mult),
                   reads=[oT, gs], writes=[on[h]])
                yield

        def hgrn_all(tile, l, first_tile, last_tile):
            for _ in hg_prep(tile, l, 0, HS[0]):
                pass
            for h in range(HG_H):
                A = hg_loop(tile, l, h, HS[h % 2], first_tile, last_tile)
                Bg = hg_prep(tile, l, h + 1, HS[(h + 1) % 2]) if h + 1 < HG_H else None
                a_alive, b_alive = True, Bg is not None
                while a_alive or b_alive:
                    if a_alive:
                        try:
                            next(A)
                        except StopIteration:
                            a_alive = False
                    if b_alive:
                        try:
                            next(Bg)
                        except StopIteration:
                            b_alive = False

        def store_hg(l, h, sq, last_tile, S):
            if sq == 0:
                if last_tile:
                    K.dma(hg_out[l, 0, h, :, :], S[:], reads=[S], writes=[dram_hg])
                else:
                    K.dma(hg_c[l, h, :, :], S[:], reads=[S], writes=[hg_c])
            else:
                K.dma(hg_out[l, 1, h, :, :], S[:], reads=[S], writes=[dram_hg])

        def gated_proj(tile, l, srcs, gate_off, wname, accumulate):
            for c in range(16):
                sg = tf[0]
                gemm([("in", l, 0, NKC, gate_off + c * 128, 128)], xT, tile,
                     lambda p, c0, c1: act(sg[:, c0:c1], p[:, 0:c1 - c0], AF.Sigmoid, [p], [sg]))

                def ev(p, c0, c1):
                    if not accumulate:
                        op("dve", lambda e: e.tensor_tensor(mg[c][:, c0:c1], p[:, 0:c1 - c0], sg[:, c0:c1], ALU.mult),
                           reads=[p, sg], writes=[mg[c]])
                    else:
                        op("dve", lambda e: e.tensor_tensor(sg[:, c0:c1], p[:, 0:c1 - c0], sg[:, c0:c1], ALU.mult),
                           reads=[p, sg], writes=[sg])
                        op("dve", lambda e: e.tensor_tensor(mg[c][:, c0:c1], mg[c][:, c0:c1], sg[:, c0:c1], ALU.add),
                           reads=[mg[c], sg], writes=[mg[c]])
                gemm([(wname, l, 0, NKC, c * 128, 128)], srcs, tile, ev)

        def ss_load(l, g, sq, first_tile):
            H = Hss1
            if sq == 0:
                if first_tile:
                    op("dve", lambda e: e.memset(H[:], 0.0), writes=[H])
                else:
                    K.dma(H[:], ss_c[l, g, :, :], reads=[ss_c], writes=[H])
            else:
                for pr in range(4):
                    hh = (g * 4 + pr) * 2
                    K.dma(sttok[:], st_ssm[l, hh:hh + 2, :, :].rearrange("h p n -> (h p) n"), writes=[sttok])
                    p = nps()
                    op("pe", lambda e: e.transpose(p[:, 0:128], sttok[:], identf[:]), reads=[sttok, identf], writes=[p])
                    op("dve", lambda e: e.tensor_copy(H[:, pr * 128:(pr + 1) * 128], p[:, 0:128]), reads=[p], writes=[H])
            op("act", lambda e: e.copy(Hsb1[:], H[:]), reads=[H], writes=[Hsb1])

        def ss_store(l, g, sq, last_tile):
            H = Hss1
            if sq == 0 and not last_tile:
                K.dma(ss_c[l, g, :, :], H[:], reads=[H], writes=[ss_c])
                return
            for pr in range(4):
                hh = (g * 4 + pr) * 2
                p = nps()
                op("pe", lambda e: e.transpose(p[:, 0:128], H[:, pr * 128:(pr + 1) * 128], identf[:]),
                   reads=[H, identf], writes=[p])
                op("dve", lambda e: e.tensor_copy(sttok[:], p[:, 0:128]), reads=[p], writes=[sttok])
                K.dma(ssm_out[l, sq, hh:hh + 2, :, :].rearrange("h p n -> (h p) n"), sttok[:], reads=[sttok], writes=[dram_ss])

        def ssd(tile, l, first_tile, last_tile):
            T = tile["w"]
            chunks = tile["chunks"]
            wdt = ("in", l, 0, NKC, O_DT, 32)

            def ev_dt(p, c0, c1):
                act(dtT[0:32, c0:c1], p[0:32, 0:c1 - c0], AF.Exp, [p], [dtT], bias=hv["dtb"][:, l:l + 1])
            gemm([wdt], xT, tile, ev_dt, m=32)
            act(dtT[0:32, 0:T], dtT[0:32, 0:T], AF.Ln, [dtT], [dtT], bias=one_col[0:32, 0:1])
            op("dve", lambda e: e.tensor_scalar(adtT[0:32, 0:T], dtT[0:32, 0:T], hvA[:, l:l + 1], None, ALU.mult),
               reads=[dtT, hvA], writes=[adtT])
            if stop_at == "ssd_dt0":
                raise _Stop()
            for ci, (a, L, sq) in enumerate(chunks):
                p = nps()
                op("pe", lambda e: e.matmul(p[0:L, 0:32], lhsT=dtT[0:32, a:a + L], rhs=identf[0:32, 0:32], start=True, stop=True),
                   reads=[dtT, identf], writes=[p], inc=False)
                op("pe", lambda e: e.matmul(p[0:L, 32:64], lhsT=adtT[0:32, a:a + L], rhs=identf[0:32, 0:32], start=True, stop=True),
                   reads=[adtT, identf], writes=[p])
                op("dve", lambda e: e.tensor_copy(dta[0:L, ci, :], p[0:L, 0:64]), reads=[p], writes=[dta])
                p2 = nps()
                op("pe", lambda e: e.matmul(p2[0:L, 0:32], lhsT=trif[0:L, 0:L], rhs=dta[0:L, ci, 32:64], start=True, stop=True),
                   reads=[trif, dta], writes=[p2])
                p2x = nps()
                op("pe", lambda e: e.matmul(p2x[:, 0:32], lhsT=onesf[0:L, :], rhs=dta[0:L, ci, 32:64], start=True, stop=True),
                   reads=[onesf, dta], writes=[p2x])
                op("dve", lambda e: e.tensor_copy(gtokS[0:L, :], p2[0:L, 0:32]), reads=[p2], writes=[gtokS])
                op("dve", lambda e: e.tensor_copy(gendS[:], p2x[:, 0:32]), reads=[p2x], writes=[gendS])
                act(dec_tok[0:L, ci, :], gtokS[0:L, :], AF.Exp, [gtokS], [dec_tok], scale=-1.0)
                act(rtot_bc[:, ci, :], gendS[:], AF.Exp, [gendS], [rtot_bc], scale=-1.0)
                op("dve", lambda e: e.tensor_tensor(wtmp[0:L, :], gtokS[0:L, :], gendS[0:L, :], ALU.subtract),
                   reads=[gtokS, gendS], writes=[wtmp])
                act(wtmp[0:L, :], wtmp[0:L, :], AF.Exp, [wtmp], [wtmp])
                op("dve", lambda e: e.tensor_tensor(w_tok[0:L, ci, :], wtmp[0:L, :], dta[0:L, ci, 0:32], ALU.mult),
                   reads=[wtmp, dta], writes=[w_tok])
            if stop_at == "ssd_dt":
                raise _Stop()
            BT, CT = tb[0], tb[1]
            xc = tb[2:6]
            sz = tb[6:10]
            pre = tf[8]
            for g in range(SSM_G):
                for which, dstb in ((0, BT), (1, CT)):
                    cidx = 16 + which * 4 + g
                    gemm([("in", l, 0, NKC, O_XBC + 2048 + which * 512 + g * 128, 128)], xT, tile,
                         lambda p, c0, c1: seg_evac(tile, p, c0, c1, pre, 3))
                    conv_apply(tile, pre, halo_x[l], halo_xs[l], cidx, lambda k: pv_cvw[:, l, k, cidx:cidx + 1],
                               pv_cvb[:, l, cidx:cidx + 1], 4, dstb, AF.Silu)
                for pr in range(4):
                    cc = g * 4 + pr
                    gemm([("in", l, 0, NKC, O_XBC + cc * 128, 128)], xT, tile,
                         lambda p, c0, c1: seg_evac(tile, p, c0, c1, pre, 3))
                    conv_apply(tile, pre, halo_x[l], halo_xs[l], cc, lambda k: pv_cvw[:, l, k, cc:cc + 1],
                               pv_cvb[:, l, cc:cc + 1], 4, xc[pr], AF.Silu)
                    gemm([("in", l, 0, NKC, O_Z + cc * 128, 128)], xT, tile,
                         lambda p, c0, c1: act(sz[pr][:, c0:c1], p[:, 0:c1 - c0], AF.Silu, [p], [sz[pr]]))
                if stop_at == "ssd_conv":
                    raise _Stop()
                cur_seq = None
                H, Hb = Hss1, Hsb1
                for ci, (a, L, sq) in enumerate(chunks):
                    if sq != cur_seq:
                        if cur_seq is not None:
                            ss_store(l, g, cur_seq, last_tile)
                        ss_load(l, g, sq, first_tile)
                        cur_seq = sq
                    L8 = 8 * L
                    gs8 = slice(g * 8, (g + 1) * 8)
                    px = nps()
                    pxb = px[:, 0:256].bitcast(BF16)
                    for pr in range(4):
                        op("pe", lambda e: e.transpose(pxb[0:L, pr * 128:(pr + 1) * 128], xc[pr][:, a:a + L], identb[:]),
                           reads=[xc[pr], identb], writes=[px], inc=(pr == 3))
                    op("dve", lambda e: e.tensor_tensor(xdt[0:L, :].rearrange("s (h p) -> s h p", p=64),
                                                        pxb[0:L, 0:512].rearrange("s (h p) -> s h p", p=64),
                                                        dta[0:L, ci, gs8].unsqueeze(2).to_broadcast([L, 8, 64]), ALU.mult),
                       reads=[px, dta], writes=[xdt])
                    op("dve", lambda e: e.tensor_tensor(xw[0:L, :].rearrange("s (h p) -> s h p", p=64),
                                                        pxb[0:L, 0:512].rearrange("s (h p) -> s h p", p=64),
                                                        w_tok[0:L, ci, gs8].unsqueeze(2).to_broadcast([L, 8, 64]), ALU.mult),
                       reads=[px, w_tok], writes=[xw])
                    pB = nps()
                    pBb = pB[:, 0:256].bitcast(BF16)
                    op("pe", lambda e: e.transpose(pBb[0:L, 0:128], BT[:, a:a + L], identb[:]), reads=[BT, identb], writes=[pB])
                    op("act", lambda e: e.copy(Btok[0:L, :], pBb[0:L, 0:128]), reads=[pB], writes=[Btok])
                    pC = nps()
                    op("pe", lambda e: e.matmul(pC[0:L, 0:L], lhsT=BT[:, a:a + L], rhs=CT[:, a:a + L], start=True, stop=True),
                       reads=[BT, CT], writes=[pC])
                    op("act", lambda e: e.copy(cbt[0:L, 0:L], pC[0:L, 0:L]), reads=[pC], writes=[cbt])
                    adt8 = dta[0:L, ci, 32 + g * 8:32 + (g + 1) * 8].unsqueeze(2).to_broadcast([L, 8, L])
                    op("dve", lambda e: e.tensor_tensor(R1[0:L, 0:L8].rearrange("s (h t) -> s h t", t=L), adt8,
                                                        trif[0:L, 0:L].unsqueeze(1).to_broadcast([L, 8, L]), ALU.mult),
                       reads=[dta, trif], writes=[R1])
                    op("dve", lambda e: e.tensor_copy(R2[0:L, 0:L8].rearrange("s (h t) -> s h t", t=L), adt8),
                       reads=[dta], writes=[R2])
                    pL = nps()
                    bigm = cb["bigm%d" % L]
                    op("pe", lambda e: e.matmul(pL[0:L, 0:L8], lhsT=onesf[0:L, 0:L], rhs=R1[0:L, 0:L8], start=True, stop=False),
                       reads=[onesf, R1], writes=[pL], inc=False)
                    op("pe", lambda e: e.matmul(pL[0:L, 0:L8], lhsT=negtri[0:L, 0:L], rhs=R2[0:L, 0:L8], start=False, stop=False),
                       reads=[negtri, R2], writes=[pL], inc=False)
                    op("pe", lambda e: e.matmul(pL[0:L, 0:L8], lhsT=identf[0:L, 0:L], rhs=bigm[0:L, 0:L8], start=False, stop=True),
                       reads=[identf, bigm], writes=[pL])
                    op("dve", lambda e: e.tensor_copy(LT[0:L, 0:L8], pL[0:L, 0:L8]), reads=[pL], writes=[LT])
                    act(LT[0:L, 0:L8], LT[0:L, 0:L8], AF.Exp, [LT], [LT], scale=-1.0)
                    op("dve", lambda e: e.tensor_tensor(scT[0:L, 0:L8].rearrange("s (h t) -> s h t", t=L),
                                                        LT[0:L, 0:L8].rearrange("s (h t) -> s h t", t=L),
                                                        cbt[0:L, 0:L].unsqueeze(1).to_broadcast([L, 8, L]), ALU.mult),
                       reads=[LT, cbt], writes=[scT])
                    pY1 = nps()
                    for hh in range(8):
                        op("pe", lambda e: e.matmul(pY1[0:L, hh * 64:(hh + 1) * 64], lhsT=scT[0:L, hh * L:(hh + 1) * L],
                                                    rhs=xdt[0:L, hh * 64:(hh + 1) * 64], start=True, stop=True),
                           reads=[scT, xdt], writes=[pY1], inc=(hh == 7))
                    pY2 = nps()
                    op("pe", lambda e: e.matmul(pY2[0:L, 0:512], lhsT=CT[:, a:a + L], rhs=Hb[:, 0:512], start=True, stop=True),
                       reads=[CT, Hb], writes=[pY2])
                    op("dve", lambda e: e.tensor_tensor(ytmp[0:L, :].rearrange("s (h p) -> s h p", p=64),
                                                        pY2[0:L, 0:512].rearrange("s (h p) -> s h p", p=64),
                                                        dec_tok[0:L, ci, gs8].unsqueeze(2).to_broadcast([L, 8, 64]), ALU.mult),
                       reads=[pY2, dec_tok], writes=[ytmp])
                    op("dve", lambda e: e.tensor_tensor(ytok[0:L, :], pY1[0:L, 0:512], ytmp[0:L, :], ALU.add),
                       reads=[pY1, ytmp], writes=[ytok])
                    pT = nps()
                    pTb = pT[:, 0:256].bitcast(BF16)
                    for pr in range(4):
                        op("pe", lambda e: e.transpose(pTb[:, pr * 64:pr * 64 + L], ytok[0:L, pr * 128:(pr + 1) * 128],
                                                       identb[0:L, 0:L]), reads=[ytok, identb], writes=[pT], inc=(pr == 3))
                    for pr in range(4):
                        op("act", lambda e: e.copy(yT4[pr][:, a:a + L], pTb[:, pr * 64:pr * 64 + L]), reads=[pT], writes=[yT4[pr]])
                    pH = nps()
                    op("pe", lambda e: e.matmul(pH[:, 0:512], lhsT=Btok[0:L, :], rhs=xw[0:L, :], start=True, stop=True),
                       reads=[Btok, xw], writes=[pH])
                    op("dve", lambda e: e.tensor_tensor(H[:].rearrange("n (h p) -> n h p", p=64),
                                                        H[:].rearrange("n (h p) -> n h p", p=64),
                                                        rtot_bc[:, ci, gs8].unsqueeze(2).to_broadcast([128, 8, 64]), ALU.mult),
                       reads=[H, rtot_bc], writes=[H])
                    op("dve", lambda e: e.tensor_tensor(H[:], H[:], pH[:, 0:512], ALU.add), reads=[H, pH], writes=[H])
                    if stop_at == "ssd_c1":
                        raise _Stop()
                    op("act", lambda e: e.copy(Hb[:], H[:]), reads=[H], writes=[Hb])
                ss_store(l, g, cur_seq, last_tile)
                for pr in range(4):
                    cc = g * 4 + pr
                    yv = yT4[pr][:, 0:T]
                    op("dve", lambda e: e.scalar_tensor_tensor(yv, xc[pr][:, 0:T], pv_dsk[:, l, cc:cc + 1], yv, ALU.mult, ALU.add),
                       reads=[xc[pr], pv_dsk, yT4[pr]], writes=[yT4[pr]])
                    op("dve", lambda e: e.tensor_tensor(yv, yv, sz[pr][:, 0:T], ALU.mult), reads=[yT4[pr], sz[pr]], writes=[yT4[pr]])
                    act(xc[pr][:, 0:T], yv, AF.Square, [yT4[pr]], [xc[pr]])
                for (c0, c1) in tile["pieces"]:
                    w = c1 - c0
                    p = bc_sum(lambda i: (xc[i], xc[i][:, c0:c1]), 4, c0, c1)
                    rstd_from(p, w, 1.0 / 512, RMS_EPS, st_rstd)
                    for pr in range(4):
                        cc = g * 4 + pr
                        op("dve", lambda e: e.scalar_tensor_tensor(on[cc][:, c0:c1], yT4[pr][:, c0:c1], pv_snw[:, l, cc:cc + 1],
                                                                   st_rstd[:, 0:w], ALU.mult, ALU.mult),
                           reads=[yT4[pr], pv_snw, st_rstd], writes=[on[cc]])


        def layer_norm(tile, gvec, bvec, l, emit_out):
            T = tile["w"]
            for (c0, c1) in tile["pieces"]:
                w = c1 - c0
                p1 = bc_sum(lambda i: (xT[i], xT[i][:, c0:c1]), 16, c0, c1)
                p2 = nps()
                for c in range(16):
                    sb_ = sqr[c % 2]
                    act(sb_[:, 0:w], xT[c][:, c0:c1], AF.Square, [xT[c]], [sb_])
                    op("pe", lambda e: e.matmul(p2[:, 0:w], lhsT=onesb[:], rhs=sb_[:, 0:w], start=(c == 0), stop=(c == 15)),
                       reads=[onesb, sb_], writes=[p2], inc=True)
                op("dve", lambda e: e.tensor_scalar(st_mean[:, 0:w], p1[:, 0:w], 1.0 / D, None, ALU.mult), reads=[p1], writes=[st_mean])
                op("dve", lambda e: e.tensor_tensor(st_tmp[:, 0:w], st_mean[:, 0:w], st_mean[:, 0:w], ALU.mult),
                   reads=[st_mean], writes=[st_tmp])
                op("dve", lambda e: e.scalar_tensor_tensor(st_tmp[:, 0:w], p2[:, 0:w], 1.0 / D, st_tmp[:, 0:w], ALU.mult, ALU.subtract),
                   reads=[p2, st_tmp], writes=[st_tmp])
                act(st_tmp[:, 0:w], st_tmp[:, 0:w], AF.Sqrt, [st_tmp], [st_tmp], bias=eps_col(LN_EPS))
                op("dve", lambda e: e.reciprocal(st_rstd[:, 0:w], st_tmp[:, 0:w]), reads=[st_tmp], writes=[st_rstd])
                for c in range(16):
                    t32 = tf[0] if c % 2 == 0 else tf[1]
                    op("dve", lambda e: e.tensor_tensor(t32[:, 0:w], xT[c][:, c0:c1], st_mean[:, 0:w], ALU.subtract),
                       reads=[xT[c], st_mean], writes=[t32])
                    op("dve", lambda e: e.tensor_tensor(t32[:, 0:w], t32[:, 0:w], st_rstd[:, 0:w], ALU.mult),
                       reads=[t32, st_rstd], writes=[t32])
                    if not emit_out:
                        act(xT[c][:, c0:c1], t32[:, 0:w], AF.Identity, [t32, gvec, bvec], [xT[c]],
                            scale=gvec[:, l, c:c + 1], bias=bvec[:, l, c:c + 1])
                    else:
                        yb = tf[2 + (c % 4)]
                        act(yb[:, 0:w], t32[:, 0:w], AF.Identity, [t32, gvec, bvec], [yb],
                            scale=gvec[:, l, c:c + 1], bias=bvec[:, l, c:c + 1])
                        nb = (w + 127) // 128
                        for bi in range(nb):
                            nr = min(128, w - bi * 128)
                            if c % 4 == 0 and bi == 0:
                                pass
                            pt = nps()
                            op("pe", lambda e: e.transpose(pt[0:nr, 0:128], yb[:, bi * 128:bi * 128 + nr], identf[:]),
                               reads=[yb, identf], writes=[pt])
                            ob = ost[ost_i[0] % 4]
                            ost_i[0] += 1
                            op("dve", lambda e: e.tensor_copy(ob[0:nr, :], pt[0:nr, 0:128]), reads=[pt], writes=[ob])
                            r0 = tile["r0"] + c0 + bi * 128
                            K.dma(y_out[r0:r0 + nr, c * 128:(c + 1) * 128], ob[0:nr, :], reads=[ob], writes=[dram_y])

        ost = [K.sb("ost%d" % i, [128, 128], F32) for i in range(4)]
        ost_i = [0]
        xld = [K.sb("xld%d" % i, [128, 512], F32) for i in range(2)]
        xld_i = [0]

        def dump(nm, bufs, tile, l):
            if not debug or l != 0 or tile["idx"] != 0:
                return
            for i, b in enumerate(bufs):
                K.dma(dbg[nm][i, :, 0:tile["w"]], b[:, 0:tile["w"]], reads=[b])

        def mixer(tile, l, first_tile, last_tile):
            if stop_at in ("load", "load_dma", "load_1"):
                raise _Stop()
            hgrn_all(tile, l, first_tile, last_tile)
            dump("d_on", on, tile, l)
            if stop_at == "hgrn":
                raise _Stop()
            gated_proj(tile, l, on, O_GA, "pa", False)
            dump("d_mga", mg, tile, l)
            if stop_at == "pa":
                raise _Stop()
            ssd(tile, l, first_tile, last_tile)
            dump("d_yn", on, tile, l)
            if stop_at == "ssd":
                raise _Stop()
            gated_proj(tile, l, on, O_GB, "pb", True)
            dump("d_mg", mg, tile, l)
            for c in range(16):
                gemm([("out", l, 0, NKC, c * 128, 128)], mg, tile,
                     lambda p, c0, c1: op("dve", lambda e: e.scalar_tensor_tensor(xT[c][:, c0:c1], xT[c][:, c0:c1], ALPHA,
                                                                                  p[:, 0:c1 - c0], ALU.mult, ALU.add),
                                          reads=[xT[c], p], writes=[xT[c]]))
            layer_norm(tile, pv_l1g, pv_l1b, l, False)
            dump("d_xmid", xT, tile, l)
            if stop_at == "mixer":
                raise _Stop()

        def ffn(tile, l, last_layer):
            T = tile["w"]
            pre = tf[8]
            ga = tf[7]
            for j in range(NFF):
                gemm([("up", l, 0, NKC, j * 128, 128)], xT, tile, lambda p, c0, c1: seg_evac(tile, p, c0, c1, pre, 2))
                conv_apply(tile, pre, halo_f[l], halo_fs[l], j, lambda k: pv_fcw[:, l, k, j:j + 1], pv_fcb[:, l, j:j + 1], 3,
                           ga, AF.Gelu)
                gemm([("up", l, 0, NKC, DFF + j * 128, 128)], xT, tile,
                     lambda p, c0, c1: op("dve", lambda e: e.tensor_tensor(hm[j][:, c0:c1], p[:, 0:c1 - c0], ga[:, c0:c1], ALU.mult),
                                          reads=[p, ga], writes=[hm[j]]))
            for c in range(16):
                gemm([("dn", l, 0, 16, c * 128, 128), ("dn", l, 16, 16, c * 128, 128), ("dn", l, 32, 12, c * 128, 128)], hm, tile,
                     lambda p, c0, c1: op("dve", lambda e: e.scalar_tensor_tensor(xT[c][:, c0:c1], xT[c][:, c0:c1], ALPHA,
                                                                                  p[:, 0:c1 - c0], ALU.mult, ALU.add),
                                          reads=[xT[c], p], writes=[xT[c]]))
            dump("d_hm", hm, tile, l)
            dump("d_v2", xT, tile, l)
            layer_norm(tile, pv_l2g, pv_l2b, l, last_layer)

        if stop_at == "setup":
            tiles = []
        for tile in tiles:
            T = tile["w"]
            ti = tile["idx"]
            first_tile = (ti == 0)
            last_tile = (ti == NTILES - 1)
            nblk = (T + 127) // 128
            for bi in range(nblk):
                r0 = bi * 128
                nr = min(128, T - r0)
                for q4 in range(4):
                    xtok = xld[xld_i[0] % 2]
                    xld_i[0] += 1
                    K.dma(xtok[0:nr, :], xin[tile["r0"] + r0: tile["r0"] + r0 + nr, q4 * 512:(q4 + 1) * 512], writes=[xtok])
                    p = nps()
                    for j in range(4):
                        op("pe", lambda e: e.transpose(p[:, j * 128: j * 128 + nr], xtok[0:nr, j * 128:(j + 1) * 128],
                                                       identf[0:nr, 0:nr]), reads=[xtok, identf], writes=[p], inc=(j == 3))
                    for j in range(4):
                        cidx = q4 * 4 + j
                        if j % 2:
                            op("dve", lambda e: e.tensor_copy(xT[cidx][:, r0:r0 + nr], p[:, j * 128:j * 128 + nr]), reads=[p], writes=[xT[cidx]])
                        else:
                            op("dve", lambda e: e.tensor_copy(xT[cidx][:, r0:r0 + nr], p[:, j * 128:j * 128 + nr]), reads=[p], writes=[xT[cidx]])
            try:
                for l in range(nlayers):
                    mixer(tile, l, first_tile, last_tile)
                    ffn(tile, l, l == nlayers - 1)
            except _Stop:
                break
        for l in range(nlayers):
            for k in range(3):
                K.dma(conv_out[l, 0, k, :].rearrange("(c p) -> p c", p=128), halo_x[l][:, :, k], reads=[halo_x[l]], writes=[dram_cv],
                      allow_slow_non_contiguous=True)
                K.dma(conv_out[l, 1, k, :].rearrange("(c p) -> p c", p=128), halo_xs[l][:, :, k], reads=[halo_xs[l]], writes=[dram_cv],
                      allow_slow_non_contiguous=True)
            for k in range(2):
                K.dma(ffn_out[l, 0, k, :].rearrange("(c p) -> p c", p=128), halo_f[l][:, :, k], reads=[halo_f[l]], writes=[dram_ff],
                      allow_slow_non_contiguous=True)
                K.dma(ffn_out[l, 1, k, :].rearrange("(c p) -> p c", p=128), halo_fs[l][:, :, k], reads=[halo_fs[l]], writes=[dram_ff],
                      allow_slow_non_contiguous=True)
        K.finish([dram_y, dram_hg, dram_ss, dram_cv, dram_ff])
        print("instructions:", K.n_ins, "waits:", K.n_wait, "sems:", K.nsem, flush=True)
    return nc


_CACHE = {}


def kernel(**inp):
    f32 = lambda a: np.ascontiguousarray(np.asarray(a, dtype=np.float32))
    xp = f32(inp["x_prompt"])
    xs = f32(inp["x_sample"])
    meta = f32(inp["meta_tokens"])
    if "nc" not in _CACHE:
        _CACHE["nc"] = build_program()
    nc = _CACHE["nc"]
    consts = build_consts()
    wnames = ["hgrn_lb_logits", "hgrn_norm_w", "ssm_conv_w", "ssm_conv_b", "ssm_dt_bias", "ssm_a_log",
              "ssm_d", "ssm_norm_w", "ln1_g", "ln1_b", "ffn_conv_w", "ffn_conv_b", "ln2_g", "ln2_b"]
    shared = {n: f32(inp[n]) for n in wnames}
    shared["wst"] = build_wstream(inp)
    for k, v in consts.items():
        shared["c_" + k] = v
    in_maps = []
    for c in range(8):
        b = c % 2
        xin = np.concatenate([meta, xp[b, 0:TP], xs[c], xp[b, TP:]], axis=0)
        m = dict(shared)
        m["xin"] = np.ascontiguousarray(xin)
        m["st_hg"] = f32(inp["state_hgrn"][:, c])
        m["st_ssm"] = f32(inp["state_ssm"][:, c])
        m["st_conv"] = f32(inp["state_ssm_conv"][:, c])
        m["st_ffn"] = f32(inp["state_ffn_conv"][:, c])
        in_maps.append(m)
    res = run_bass_kernel_spmd(nc, in_maps, core_ids=list(range(8)))
    R = res.results
    ys = [r["y"] for r in R]
    y_prompt = np.stack([np.concatenate([ys[b][N_META:N_META + TP], ys[b][N_META + TP + DEC_SEQ:]], axis=0) for b in range(2)])
    y_sample = np.stack([ys[c][N_META + TP:N_META + TP + DEC_SEQ] for c in range(8)])
    hg_p = np.stack([R[b]["hg_o"][:, 0] for b in range(2)], axis=1)
    ssm_p = np.stack([R[b]["ssm_o"][:, 0] for b in range(2)], axis=1)
    conv_p = np.stack([R[b]["conv_o"][:, 0] for b in range(2)], axis=1)
    ffn_p = np.stack([R[b]["ffn_o"][:, 0] for b in range(2)], axis=1)
    hg_s = np.stack([R[c]["hg_o"][:, 1] for c in range(8)], axis=1)
    ssm_s = np.stack([R[c]["ssm_o"][:, 1] for c in range(8)], axis=1)
    conv_s = np.stack([R[c]["conv_o"][:, 1] for c in range(8)], axis=1)
    ffn_s = np.stack([R[c]["ffn_o"][:, 1] for c in range(8)], axis=1)
    outs = (y_prompt, y_sample, hg_p, ssm_p, conv_p, ffn_p, hg_s, ssm_s, conv_s, ffn_s)
    return tuple(np.ascontiguousarray(o, dtype=np.float32) for o in outs)
DBG_NAMES = ["d_on", "d_mga", "d_yn", "d_mg", "d_xmid", "d_v2", "d_hm"]
```

```python
import contextlib
import numpy as np
import concourse.bass as bass
import concourse.mybir as mybir
from concourse.bass_utils import run_bass_kernel_spmd

F32 = mybir.dt.float32
BF16 = mybir.dt.bfloat16
AF = mybir.ActivationFunctionType
ALU = mybir.AluOpType

D = 2048
NKC = 16
DEPTH = 2
SEQ = 4096
N_META = 16
DEC_SEQ = 32
HG_H = 16
SSM_H = 32
SSM_P = 64
SSM_N = 128
SSM_G = 4
XBC = 3072
DFF = 5632
NFF = 44
N_IN = 17440
O_Q, O_F, O_I, O_G, O_Z, O_XBC, O_DT, O_GA, O_GB = 0, 2048, 4096, 6144, 8192, 10240, 13312, 13344, 15392
ALPHA = (2.0 * DEPTH) ** 0.25
LN_EPS = 1e-5
RMS_EPS = 1e-6
BIG = 30000.0
TW = 560
TP = 512
NTILES = SEQ // TP
NTOK = N_META + SEQ + DEC_SEQ
MAXC = 30000


class Buf:
    __slots__ = ("t", "w", "r", "name")

    def __init__(self, t, name=""):
        self.t = t
        self.w = None
        self.r = {}
        self.name = name

    def __getitem__(self, k):
        return self.t[k]


class Eng:
    def __init__(self, name, handle):
        self.name = name
        self.h = handle
        self.sem = None
        self.count = 0
        self.known = {}


class KB:
    def __init__(self, nc, stack, n_dma_sems=16):
        self.nc = nc
        self.stack = stack
        self.sems = {}
        self.eng = {}
        self.nsem = 0
        for name, h in (("pe", nc.tensor), ("act", nc.scalar), ("dve", nc.vector),
                        ("pool", nc.gpsimd), ("sp", nc.sync)):
            E = Eng(name, h)
            self.eng[name] = E
            self._newsem(E)
        self.dma_slots = []
        for i in range(n_dma_sems):
            s = stack.enter_context(nc.semaphore("d%d" % i))
            self.sems[id(s)] = s
            self.dma_slots.append([s, 0])
        self.dma_i = 0
        self.n_wait = 0
        self.n_ins = 0
        self.names = set()

    def _newsem(self, E):
        s = self.stack.enter_context(self.nc.semaphore("s_%s_%d" % (E.name, self.nsem)))
        self.nsem += 1
        self.sems[id(s)] = s
        E.sem = s
        E.count = 0

    def uname(self, name):
        i = 0
        n = name
        while n in self.names:
            i += 1
            n = "%s_%d" % (name, i)
        self.names.add(n)
        return n

    def sb(self, name, shape, dt):
        name = self.uname(name)
        t = self.stack.enter_context(self.nc.sbuf_tensor(name, list(shape), dt))
        return Buf(t, name)

    def psum(self, name, shape, dt):
        t = self.stack.enter_context(self.nc.psum_tensor(name, list(shape), dt))
        return Buf(t, name)

    def _need(self, reads, writes):
        need = {}
        for b in reads:
            if b.w is not None:
                k, v = b.w
                if need.get(k, 0) < v:
                    need[k] = v
        for b in writes:
            if b.w is not None:
                k, v = b.w
                if need.get(k, 0) < v:
                    need[k] = v
            for k, v in b.r.items():
                if need.get(k, 0) < v:
                    need[k] = v
        return need

    def _wait(self, E, need, skip_own=False):
        for k, v in need.items():
            if skip_own and k == id(E.sem):
                continue
            if E.known.get(k, 0) < v:
                E.h.wait_ge(self.sems[k], v)
                E.known[k] = v
                self.n_wait += 1

    def _mark(self, tok, reads, writes):
        k, v = tok
        for b in reads:
            if b.r.get(k, 0) < v:
                b.r[k] = v
        for b in writes:
            b.w = tok
            b.r = {}

    def op(self, e, emit, reads=(), writes=(), inc=True):
        E = self.eng[e]
        if inc and E.count >= MAXC:
            self._newsem(E)
        self._wait(E, self._need(reads, writes), skip_own=(e == "pe"))
        ins = emit(E.h)
        self.n_ins += 1
        if inc:
            E.count += 1
            ins.then_inc(E.sem, 1)
            tok = (id(E.sem), E.count)
        else:
            tok = (id(E.sem), E.count + 1)
        self._mark(tok, reads, writes)
        return tok

    def dma(self, out, in_, reads=(), writes=(), q="sp", **kw):
        E = self.eng[q]
        slot = self.dma_slots[self.dma_i]
        self.dma_i = (self.dma_i + 1) % len(self.dma_slots)
        if slot[1] >= MAXC // 16:
            s = self.stack.enter_context(self.nc.semaphore("dx%d" % self.nsem))
            self.nsem += 1
            self.sems[id(s)] = s
            self._wait(E, {id(slot[0]): slot[1] * 16})
            slot[0] = s
            slot[1] = 0
        need = self._need(reads, writes)
        if slot[1] > 0:
            k = id(slot[0])
            need[k] = max(need.get(k, 0), slot[1] * 16)
        self._wait(E, need)
        E.h.dma_start(out=out, in_=in_, **kw).then_inc(slot[0], 16)
        self.n_ins += 1
        slot[1] += 1
        tok = (id(slot[0]), slot[1] * 16)
        self._mark(tok, reads, writes)
        return tok

    def finish(self, bufs=()):
        E = self.eng["sp"]
        need = {}
        for s, c in self.dma_slots:
            if c:
                need[id(s)] = c * 16
        for b in bufs:
            if b.w is not None:
                k, v = b.w
                need[k] = max(need.get(k, 0), v)
        self._wait(E, need)


def tile_defs(ntiles):
    tiles = []
    r = 0
    for ti in range(ntiles):
        if ti == 0:
            w = N_META + TP + DEC_SEQ
            segs = [(0, N_META + TP, 0), (N_META + TP, DEC_SEQ, 1)]
            chunks = [(N_META + TP, DEC_SEQ, 1), (0, N_META, 0)] + [(N_META + 64 * j, 64, 0) for j in range(TP // 64)]
        else:
            w = TP
            segs = [(0, TP, 0)]
            chunks = [(64 * j, 64, 0) for j in range(TP // 64)]
        pieces = []
        c = 0
        while c < w:
            pieces.append((c, min(c + 512, w)))
            c += 512
        tiles.append(dict(r0=r, w=w, segs=segs, chunks=chunks, pieces=pieces, idx=ti))
        r += w
    return tiles


def build_consts():
    c = {}
    c["identf"] = np.eye(128, dtype=np.float32)
    tri = (np.arange(128)[:, None] <= np.arange(128)[None, :]).astype(np.float32)
    c["tri"] = tri
    c["ones"] = np.ones((128, 128), np.float32)
    for L in (16, 32, 64):
        m = (np.arange(L)[None, :] < np.arange(L)[:, None]).astype(np.float32) * BIG
        c["bigm%d" % L] = np.ascontiguousarray(np.broadcast_to(m[:, None, :], (L, 8, L))).reshape(L, 8 * L)
    return c


def plan_pass(l):
    for h in range(HG_H):
        for o in (O_Q, O_F, O_I, O_G):
            yield ("in", l, 0, NKC, o + h * 128, 128)
    for c in range(16):
        yield ("in", l, 0, NKC, O_GA + c * 128, 128)
        yield ("pa", l, 0, NKC, c * 128, 128)
    yield ("in", l, 0, NKC, O_DT, 32)
    for g in range(SSM_G):
        yield ("in", l, 0, NKC, O_XBC + 2048 + g * 128, 128)
        yield ("in", l, 0, NKC, O_XBC + 2560 + g * 128, 128)
        for pr in range(4):
            cc = g * 4 + pr
            yield ("in", l, 0, NKC, O_XBC + cc * 128, 128)
            yield ("in", l, 0, NKC, O_Z + cc * 128, 128)
    for c in range(16):
        yield ("in", l, 0, NKC, O_GB + c * 128, 128)
        yield ("pb", l, 0, NKC, c * 128, 128)
    for c in range(16):
        yield ("out", l, 0, NKC, c * 128, 128)
    for j in range(NFF):
        yield ("up", l, 0, NKC, j * 128, 128)
        yield ("up", l, 0, NKC, DFF + j * 128, 128)
    for c in range(16):
        yield ("dn", l, 0, 16, c * 128, 128)
        yield ("dn", l, 16, 16, c * 128, 128)
        yield ("dn", l, 32, 12, c * 128, 128)


NBL = len(list(plan_pass(0)))
WNAME = {"in": "w_in", "pa": "w_proj_a", "pb": "w_proj_b", "out": "w_out", "up": "ffn_w_up", "dn": "ffn_w_down"}


def build_wstream(inp):
    wst = np.zeros((DEPTH, NBL, 128, NKC, 128), np.float32)
    for l in range(DEPTH):
        for j, (name, _, k0, nk, c0, ncol) in enumerate(plan_pass(l)):
            W = inp[WNAME[name]][l]
            blk = np.asarray(W[k0 * 128:(k0 + nk) * 128, c0:c0 + ncol], dtype=np.float32).reshape(nk, 128, ncol)
            wst[l, j, :, 0:nk, 0:ncol] = blk.transpose(1, 0, 2)
    return wst.reshape(DEPTH, NBL, 128, NKC * 128)


class _Stop(Exception):
    pass


def build_program(ntiles=NTILES, nlayers=DEPTH, debug=False, stop_at=None):
    nc = bass.Bass("TRN2", target_bir_lowering=False)
    tiles = tile_defs(ntiles)
    ntok = sum(t["w"] for t in tiles)

    def din(name, shape):
        return nc.dram_tensor(name, list(shape), F32, kind="ExternalInput").ap()

    def dout(name, shape):
        return nc.dram_tensor(name, list(shape), F32, kind="ExternalOutput").ap()

    xin = din("xin", [NTOK, D])
    st_hg = din("st_hg", [DEPTH, HG_H, 128, 128])
    st_ssm = din("st_ssm", [DEPTH, SSM_H, SSM_P, SSM_N])
    st_conv = din("st_conv", [DEPTH, 3, XBC])
    st_ffn = din("st_ffn", [DEPTH, 2, DFF])
    lb_logits = din("hgrn_lb_logits", [DEPTH, D])
    hg_nw = din("hgrn_norm_w", [DEPTH, D])
    cv_w = din("ssm_conv_w", [DEPTH, 4, XBC])
    cv_b = din("ssm_conv_b", [DEPTH, XBC])
    dt_bias = din("ssm_dt_bias", [DEPTH, SSM_H])
    a_log = din("ssm_a_log", [DEPTH, SSM_H])
    ssm_d = din("ssm_d", [DEPTH, SSM_H])
    ssm_nw = din("ssm_norm_w", [DEPTH, D])
    ln1_g = din("ln1_g", [DEPTH, D])
    ln1_b = din("ln1_b", [DEPTH, D])
    fcv_w = din("ffn_conv_w", [DEPTH, 3, DFF])
    fcv_b = din("ffn_conv_b", [DEPTH, DFF])
    ln2_g = din("ln2_g", [DEPTH, D])
    ln2_b = din("ln2_b", [DEPTH, D])
    wst = din("wst", [DEPTH, NBL, 128, NKC * 128])
    cdefs = build_consts()
    cin = {k: din("c_" + k, v.shape) for k, v in cdefs.items()}

    y_out = dout("y", [NTOK, D])
    hg_out = dout("hg_o", [DEPTH, 2, HG_H, 128, 128])
    ssm_out = dout("ssm_o", [DEPTH, 2, SSM_H, SSM_P, SSM_N])
    conv_out = dout("conv_o", [DEPTH, 2, 3, XBC])
    ffn_out = dout("ffn_o", [DEPTH, 2, 2, DFF])
    dbg = {}
    if debug:
        for nm in ("d_on", "d_mga", "d_yn", "d_mg", "d_xmid", "d_v2"):
            dbg[nm] = nc.dram_tensor(nm, [16, 128, TW], BF16, kind="ExternalOutput").ap()
        dbg["d_hm"] = nc.dram_tensor("d_hm", [NFF, 128, TW], BF16, kind="ExternalOutput").ap()


    with contextlib.ExitStack() as st:
        K = KB(nc, st)
        op = K.op

        xT = [K.sb("xT%d" % i, [128, TW], BF16) for i in range(NKC)]
        hm = [K.sb("hm%d" % i, [128, TW], BF16) for i in range(NFF)]
        on = hm[0:16]
        mg = hm[16:32]
        NT32 = 10
        tf = [K.sb("tf%d" % i, [128, TW + 40], F32) for i in range(NT32)]
        tb = [K.sb("tb%d" % i, [128, TW], BF16) for i in range(10)]
        NS, NB = 2, 6
        stage = [K.sb("stg%d" % i, [128, NKC, 128], F32) for i in range(NS)]
        wring = [K.sb("wb%d" % i, [128, NKC, 128], BF16) for i in range(NB)]
        psb = [K.psum("ps%d" % i, [128, 512], F32) for i in range(8)]
        ps_i = [0]

        def nps():
            b = psb[ps_i[0]]
            ps_i[0] = (ps_i[0] + 1) % 8
            return b

        Shg1 = K.sb("Shg", [128, 128], F32)
        Hss1 = K.sb("Hss", [128, 512], F32)
        Hsb1 = K.sb("Hsb", [128, 512], BF16)
        hg_c = Buf(nc.dram_tensor("hg_carry", [DEPTH, HG_H, 128, 128], F32).ap(), "hg_carry")
        ss_c = Buf(nc.dram_tensor("ss_carry", [DEPTH, SSM_G, 128, 512], F32).ap(), "ss_carry")
        halo_x = [K.sb("halox%d" % l, [128, 24, 3], F32) for l in range(DEPTH)]
        halo_f = [K.sb("halof%d" % l, [128, NFF, 2], F32) for l in range(DEPTH)]
        halo_xs = [K.sb("haloxs%d" % l, [128, 24, 3], F32) for l in range(DEPTH)]
        halo_fs = [K.sb("halofs%d" % l, [128, NFF, 2], F32) for l in range(DEPTH)]
        dram_y = Buf(y_out, "y")
        dram_hg = Buf(hg_out, "hgo")
        dram_ss = Buf(ssm_out, "sso")
        dram_cv = Buf(conv_out, "cvo")
        dram_ff = Buf(ffn_out, "ffo")

        cb = {}
        for k, v in cdefs.items():
            b = K.sb("sc_" + k, [128, v.shape[1]], F32)
            K.dma(b[0:v.shape[0], :], cin[k][:, :], writes=[b])
            cb[k] = b
        identb = K.sb("identb", [128, 128], BF16)
        onesb = K.sb("onesb", [128, 128], BF16)
        trib = K.sb("trib", [128, 128], BF16)
        negtri = K.sb("negtri", [128, 128], F32)
        op("dve", lambda e: e.tensor_copy(identb[:], cb["identf"][:]), reads=[cb["identf"]], writes=[identb])
        op("dve", lambda e: e.tensor_copy(onesb[:], cb["ones"][:]), reads=[cb["ones"]], writes=[onesb])
        op("dve", lambda e: e.tensor_copy(trib[:], cb["tri"][:]), reads=[cb["tri"]], writes=[trib])
        op("dve", lambda e: e.tensor_scalar(negtri[:], cb["tri"][:], -1.0, None, ALU.mult), reads=[cb["tri"]], writes=[negtri])
        identf, onesf, trif = cb["identf"], cb["ones"], cb["tri"]
        one_col = K.sb("one_col", [128, 1], F32)
        op("dve", lambda e: e.memset(one_col[:], 1.0), writes=[one_col])

        def load_vec(name, ap, nch, extra=None):
            b = K.sb("pv_" + name, [128, DEPTH, nch] if extra is None else [128, DEPTH, extra, nch], F32)
            for l in range(DEPTH):
                if extra is None:
                    K.dma(b[:, l, :], ap[l, :].rearrange("(c p) -> p c", p=128), writes=[b],
                          allow_slow_non_contiguous=True)
                else:
                    for k in range(extra):
                        K.dma(b[:, l, k, :], ap[l, k, :].rearrange("(c p) -> p c", p=128), writes=[b],
                              allow_slow_non_contiguous=True)
            return b

        pv_lbl = load_vec("lbl", lb_logits, 16)
        pv_hgnw = load_vec("hgnw", hg_nw, 16)
        pv_cvw = load_vec("cvw", cv_w, 24, extra=4)
        pv_cvb = load_vec("cvb", cv_b, 24)
        pv_snw = load_vec("snw", ssm_nw, 16)
        pv_l1g = load_vec("l1g", ln1_g, 16)
        pv_l1b = load_vec("l1b", ln1_b, 16)
        pv_fcw = load_vec("fcw", fcv_w, NFF, extra=3)
        pv_fcb = load_vec("fcb", fcv_b, NFF)
        pv_l2g = load_vec("l2g", ln2_g, 16)
        pv_l2b = load_vec("l2b", ln2_b, 16)
        hv = {}
        for name, ap in (("dtb", dt_bias), ("alog", a_log), ("dsk", ssm_d)):
            b = K.sb("hv_" + name, [SSM_H, DEPTH], F32)
            K.dma(b[:], ap.rearrange("l h -> h l"), writes=[b], allow_slow_non_contiguous=True)
            hv[name] = b
        hvA = K.sb("hv_A", [SSM_H, DEPTH], F32)
        op("act", lambda e: e.activation(hvA[:], hv["alog"][:], AF.Exp), reads=[hv["alog"]], writes=[hvA])
        pv_lb = K.sb("pv_lb", [128, DEPTH, 16], F32)
        pv_oml = K.sb("pv_oml", [128, DEPTH, 16], F32)
        lbe = K.sb("lbe", [128, DEPTH, 16], F32)
        lbs = K.sb("lbs", [128, 16], F32)
        op("act", lambda e: e.activation(lbe[:], pv_lbl[:], AF.Exp), reads=[pv_lbl], writes=[lbe])
        op("dve", lambda e: e.tensor_copy(lbs[:], lbe[:, 0, :]), reads=[lbe], writes=[lbs])
        for l in range(1, DEPTH):
            op("dve", lambda e: e.tensor_tensor(lbs[:], lbs[:], lbe[:, l, :], ALU.add), reads=[lbe, lbs], writes=[lbs])
        op("dve", lambda e: e.reciprocal(lbs[:], lbs[:]), reads=[lbs], writes=[lbs])
        op("dve", lambda e: e.memset(pv_lb[:, 0, :], 0.0), writes=[pv_lb])
        for l in range(1, DEPTH):
            op("dve", lambda e: e.tensor_tensor(pv_lb[:, l, :], lbe[:, l, :], lbs[:], ALU.mult), reads=[lbe, lbs], writes=[pv_lb])
            if l > 1:
                op("dve", lambda e: e.tensor_tensor(pv_lb[:, l, :], pv_lb[:, l, :], pv_lb[:, l - 1, :], ALU.add),
                   reads=[pv_lb], writes=[pv_lb])
        op("dve", lambda e: e.tensor_scalar(pv_oml[:], pv_lb[:], -1.0, 1.0, ALU.mult, ALU.add), reads=[pv_lb], writes=[pv_oml])
        pv_dsk = K.sb("pv_dsk", [128, DEPTH, 16], F32)
        for l in range(DEPTH):
            for two in range(2):
                src = ssm_d[l:l + 1, :].rearrange("o (c two) -> o c two", two=2)[:, :, two]
                K.dma(pv_dsk[two * 64:(two + 1) * 64, l, :], src.to_broadcast([64, 16]), writes=[pv_dsk],
                      allow_slow_non_contiguous=True)
        for l in range(DEPTH):
            op("dve", lambda e: e.memset(halo_x[l][:], 0.0), writes=[halo_x[l]])
            op("dve", lambda e: e.memset(halo_f[l][:], 0.0), writes=[halo_f[l]])
            for k in range(3):
                K.dma(halo_xs[l][:, :, k], st_conv[l, k, :].rearrange("(c p) -> p c", p=128), writes=[halo_xs[l]],
                      allow_slow_non_contiguous=True)
            for k in range(2):
                K.dma(halo_fs[l][:, :, k], st_ffn[l, k, :].rearrange("(c p) -> p c", p=128), writes=[halo_fs[l]],
                      allow_slow_non_contiguous=True)

        sched = []

        def plan():
            for ti in range(ntiles):
                for l in range(nlayers):
                    yield from plan_pass(l)

        sched = list(plan())
        ws = dict(issued=0, got=0)
        cast_rr = [0]

        def ws_issue():
            i = ws["issued"]
            name, l, k0, nk, c0, ncol = sched[i]
            sg = stage[i % NS]
            wb = wring[i % NB]
            j = i % NBL
            K.dma(sg[:, 0:nk, :].rearrange("p k c -> p (k c)"), wst[l, j, :, 0:nk * 128], writes=[sg])
            op("act", lambda e: e.copy(wb[:, 0:nk, 0:ncol], sg[:, 0:nk, 0:ncol]), reads=[sg], writes=[wb])
            ws["issued"] += 1

        def wget(spec):
            i = ws["got"]
            assert sched[i] == spec, (i, sched[i], spec)
            while ws["issued"] < min(len(sched), i + NB - 2):
                ws_issue()
            ws["got"] += 1
            return wring[i % NB]

        def gemm(wspecs, srcs, tile, evac, m=128):
            wbs = [(wget(s), s[3]) for s in wspecs]
            for (c0, c1) in tile["pieces"]:
                p = nps()
                kk = 0
                nk_tot = sum(n for _, n in wbs)
                for wb, nk in wbs:
                    for k in range(nk):
                        src = srcs[kk]
                        last = (kk == nk_tot - 1)
                        op("pe", lambda e: e.matmul(p[0:m, 0:c1 - c0], lhsT=wb[:, k, 0:m], rhs=src[:, c0:c1],
                                                    start=(kk == 0), stop=last),
                           reads=[wb, src], writes=[p], inc=last)
                        kk += 1
                evac(p, c0, c1)

        def act(out, in_, func, reads, writes, **kw):
            op("act", lambda e: e.activation(out, in_, func, **kw), reads=reads, writes=writes)

        ATb = [K.sb("ATb%d" % i, [64, 64], BF16) for i in range(2)]
        vtok = [K.sb("vtok%d" % i, [64, 128], BF16) for i in range(2)]
        ktok = [K.sb("ktok%d" % i, [64, 128], BF16) for i in range(2)]
        Spb = [K.sb("Spb%d" % i, [128, 128], BF16) for i in range(2)]
        rr = K.sb("rr", [128, 3, 16], F32)
        rrd = K.sb("rrd", [128, 3, 16], F32)
        st_mean = K.sb("st_mean", [128, 512], F32)
        st_rstd = K.sb("st_rstd", [128, 512], F32)
        st_tmp = K.sb("st_tmp", [128, 512], F32)
        sqr = [K.sb("sqr%d" % i, [128, 512], BF16) for i in range(2)]
        pre_s = K.sb("pre_s", [128, 40], F32)
        acc_s = K.sb("acc_s", [128, 40], F32)
        dtT = tf[4]
        adtT = tf[5]
        NCH = 10
        dta = K.sb("dta", [64, NCH, 64], F32)
        dec_tok = K.sb("dec_tok", [64, NCH, 32], F32)
        w_tok = K.sb("w_tok", [64, NCH, 32], F32)
        rtot_bc = K.sb("rtot_bc", [128, NCH, 32], F32)
        gendS = K.sb("gendS", [128, 32], F32)
        wtmp = K.sb("wtmp", [64, 32], F32)
        gtokS = K.sb("gtokS", [64, 32], F32)
        xdt = K.sb("xdt", [64, 512], BF16)
        xw = K.sb("xw", [64, 512], BF16)
        Btok = K.sb("Btok", [64, 128], BF16)
        cbt = K.sb("cbt", [64, 64], F32)
        R1 = K.sb("R1", [64, 512], F32)
        R2 = K.sb("R2", [64, 512], F32)
        LT = R1
        scT = K.sb("scT", [64, 512], BF16)
        ytmp = R2
        ytok = K.sb("ytok", [64, 512], BF16)
        yT4 = tf[0:4]
        sttok = K.sb("sttok", [128, 128], F32)

        def seg_evac(tile, p, c0, c1, pre, hw):
            for (s0, sl, sq) in tile["segs"]:
                a, b = max(c0, s0), min(c1, s0 + sl)
                if a >= b:
                    continue
                dstb = pre if sq == 0 else pre_s
                dst = dstb[:, hw + a - s0: hw + b - s0]
                op("act", lambda e: e.copy(dst, p[:, a - c0:b - c0]), reads=[p], writes=[dstb])

        def conv_apply(tile, pre, halo, halo_sq, cidx, wfun, bap, width, dst, func):
            hw = width - 1
            for (s0, sl, sq) in tile["segs"]:
                pb = pre if sq == 0 else pre_s
                hb = halo if sq == 0 else halo_sq
                ab = tf[9] if sq == 0 else acc_s
                op("dve", lambda e: e.tensor_copy(pb[:, 0:hw], hb[:, cidx, :]), reads=[hb], writes=[pb])
                op("dve", lambda e: e.tensor_scalar(ab[:, 0:sl], pb[:, hw:hw + sl], wfun(hw), bap, ALU.mult, ALU.add),
                   reads=[pb], writes=[ab])
                for k in range(hw):
                    op("dve", lambda e: e.scalar_tensor_tensor(ab[:, 0:sl], pb[:, k:k + sl], wfun(k), ab[:, 0:sl],
                                                               ALU.mult, ALU.add), reads=[pb, ab], writes=[ab])
                act(dst[:, s0:s0 + sl], ab[:, 0:sl], func, [ab], [dst])
                op("dve", lambda e: e.tensor_copy(hb[:, cidx, :], pb[:, sl:sl + hw]), reads=[pb], writes=[hb])

        def bc_sum(srcs_fn, nsrc, c0, c1):
            p = nps()
            for i in range(nsrc):
                sbuf, sap = srcs_fn(i)
                op("pe", lambda e: e.matmul(p[:, 0:c1 - c0], lhsT=onesb[:], rhs=sap, start=(i == 0), stop=(i == nsrc - 1)),
                   reads=[onesb, sbuf], writes=[p], inc=(i == nsrc - 1))
            return p

        def rstd_from(p, w, scale, eps, dst):
            act(st_tmp[:, 0:w], p[:, 0:w], AF.Sqrt, [p], [st_tmp], bias=eps_col(eps), scale=scale)
            op("dve", lambda e: e.reciprocal(dst[:, 0:w], st_tmp[:, 0:w]), reads=[st_tmp], writes=[dst])

        eps_cols = {}

        def eps_col(v):
            if v not in eps_cols:
                b = K.sb("epsc%d" % len(eps_cols), [128, 1], F32)
                op("dve", lambda e: e.memset(b[:], float(v)), writes=[b])
                eps_cols[v] = b
            return eps_cols[v][:, 0:1]

        eps_col(RMS_EPS)
        eps_col(LN_EPS)

        def gemm_g(wspecs, srcs, tile, evac, m=128):
            wbs = [(wget(s), s[3]) for s in wspecs]
            for (c0, c1) in tile["pieces"]:
                p = nps()
                kk = 0
                nk_tot = sum(n for _, n in wbs)
                for wb, nk in wbs:
                    for k in range(nk):
                        src = srcs[kk]
                        last = (kk == nk_tot - 1)
                        op("pe", lambda e: e.matmul(p[0:m, 0:c1 - c0], lhsT=wb[:, k, 0:m], rhs=src[:, c0:c1],
                                                    start=(kk == 0), stop=last),
                           reads=[wb, src], writes=[p], inc=last)
                        kk += 1
                evac(p, c0, c1)
                yield

        HS = []
        for s_ in range(2):
            HS.append(dict(
                qs=tf[4 * s_ + 0], fk=tf[4 * s_ + 1], ln=tf[4 * s_ + 2], G=tf[4 * s_ + 3],
                vT=tb[4 * s_ + 0], gs=tb[4 * s_ + 1], qt=tb[4 * s_ + 2], kt=tb[4 * s_ + 3],
                S=K.sb("ShgS%d" % s_, [128, 128], F32), rr=K.sb("rrS%d" % s_, [128, 3, 16], F32),
                rrd=K.sb("rrdS%d" % s_, [128, 3, 16], F32),
                ATb=[K.sb("ATbS%d_%d" % (s_, i), [64, 64], BF16) for i in range(2)],
                vtok=[K.sb("vtokS%d_%d" % (s_, i), [64, 128], BF16) for i in range(2)],
                ktok=[K.sb("ktokS%d_%d" % (s_, i), [64, 128], BF16) for i in range(2)],
                Spb=[K.sb("SpbS%d_%d" % (s_, i), [128, 128], BF16) for i in range(2)],
                rstd=K.sb("rstdS%d" % s_, [128, 512], F32)))

        def hg_prep(tile, l, h, B):
            T = tile["w"]
            qs, fk, lnf, GnS = B["qs"], B["fk"], B["ln"], B["G"]
            Dd, E2 = B["ln"], B["G"]
            vT, gs, qt, kt = B["vT"], B["gs"], B["qt"], B["kt"]
            rr_, rrd_ = B["rr"], B["rrd"]
            specs = [("in", l, 0, NKC, o + h * 128, 128) for o in (O_Q, O_F, O_I, O_G)]
            yield from gemm_g([specs[0]], xT, tile, lambda p, c0, c1: act(qs[:, c0:c1], p[:, 0:c1 - c0], AF.Silu, [p], [qs]))
            yield from gemm_g([specs[1]], xT, tile, lambda p, c0, c1: act(fk[:, c0:c1], p[:, 0:c1 - c0], AF.Sigmoid, [p], [fk]))
            yield from gemm_g([specs[2]], xT, tile, lambda p, c0, c1: op("act", lambda e: e.copy(vT[:, c0:c1], p[:, 0:c1 - c0]),
                                                                         reads=[p], writes=[vT]))
            yield from gemm_g([specs[3]], xT, tile, lambda p, c0, c1: act(gs[:, c0:c1], p[:, 0:c1 - c0], AF.Silu, [p], [gs]))
            op("dve", lambda e: e.tensor_scalar(fk[:, 0:T], fk[:, 0:T], pv_oml[:, l, h:h + 1], pv_lb[:, l, h:h + 1],
                                                ALU.mult, ALU.add), reads=[fk, pv_oml, pv_lb], writes=[fk])
            act(lnf[:, 0:T], fk[:, 0:T], AF.Ln, [fk], [lnf])
            op("dve", lambda e: e.tensor_scalar(fk[:, 0:T], fk[:, 0:T], -1.0, 1.0, ALU.mult, ALU.add), reads=[fk], writes=[fk])
            op("dve", lambda e: e.memset(GnS[:, 0:1], 0.0), writes=[GnS])
            op("dve", lambda e: e.tensor_tensor_scan(GnS[:, 1:T + 1], one_col[:, 0:1].to_broadcast([128, T]), lnf[:, 0:T],
                                                     0.0, ALU.mult, ALU.add), reads=[one_col, lnf], writes=[GnS])
            yield
            runs = []
            for ci, (a, L, sq) in enumerate(tile["chunks"]):
                if runs and runs[-1][2] == L and runs[-1][0] + runs[-1][1] * L == a:
                    runs[-1][1] += 1
                else:
                    runs.append([a, 1, L, ci])
            for (a, n, L, ci0) in runs:
                hl = L // 2
                gv = lambda off: GnS[:, a + off:a + off + n * L].rearrange("p (j l) -> p j l", l=L)
                ref = gv(hl)[:, :, 0:1]
                sta = gv(0)[:, :, 0:1]
                end = gv(L)[:, :, 0:1]
                op("dve", lambda e: e.tensor_tensor(Dd[:, a:a + n * L].rearrange("p (j l) -> p j l", l=L), gv(1),
                                                    ref.to_broadcast([128, n, L]), ALU.subtract), reads=[GnS], writes=[Dd])
                r3 = lambda k: rrd_[:, k, ci0:ci0 + n].unsqueeze(2)
                op("dve", lambda e: e.tensor_tensor(r3(0), ref, sta, ALU.subtract), reads=[GnS], writes=[rrd_])
                op("dve", lambda e: e.tensor_tensor(r3(1), end, ref, ALU.subtract), reads=[GnS], writes=[rrd_])
                op("dve", lambda e: e.tensor_tensor(r3(2), end, sta, ALU.subtract), reads=[GnS], writes=[rrd_])
            nchk = len(tile["chunks"])
            act(rr_[:, :, 0:nchk], rrd_[:, :, 0:nchk], AF.Exp, [rrd_], [rr_])
            act(E2[:, 0:T], Dd[:, 0:T], AF.Exp, [Dd], [E2], scale=-1.0)
            act(Dd[:, 0:T], Dd[:, 0:T], AF.Exp, [Dd], [Dd])
            yield
            op("dve", lambda e: e.tensor_tensor(qt[:, 0:T], qs[:, 0:T], Dd[:, 0:T], ALU.mult), reads=[qs, Dd], writes=[qt])
            op("dve", lambda e: e.tensor_tensor(kt[:, 0:T], fk[:, 0:T], E2[:, 0:T], ALU.mult), reads=[fk, E2], writes=[kt])
            yield

        def hg_loop(tile, l, h, B, first_tile, last_tile):
            T = tile["w"]
            oT = B["qs"]
            vT, gs, qt, kt = B["vT"], B["gs"], B["qt"], B["kt"]
            rr_ = B["rr"]
            S = B["S"]
            cur_seq = None
            for ci, (a, L, sq) in enumerate(tile["chunks"]):
                if sq != cur_seq:
                    if cur_seq is not None:
                        store_hg(l, h, cur_seq, last_tile, S)
                    if sq == 0:
                        if first_tile:
                            op("dve", lambda e: e.memset(S[:], 0.0), writes=[S])
                        else:
                            K.dma(S[:], hg_c[l, h, :, :], reads=[hg_c], writes=[S])
                    else:
                        K.dma(S[:], st_hg[l, h, :, :], writes=[S])
                    cur_seq = sq
                i2 = ci % 2
                ATb_, vtok_, ktok_, Spb_ = B["ATb"][i2], B["vtok"][i2], B["ktok"][i2], B["Spb"][i2]
                p1 = nps()
                op("pe", lambda e: e.matmul(p1[0:L, 0:L], lhsT=kt[:, a:a + L], rhs=qt[:, a:a + L], start=True, stop=True),
                   reads=[kt, qt], writes=[p1])
                p2 = nps()
                p2b = p2[:, 0:256].bitcast(BF16)
                op("pe", lambda e: e.transpose(p2b[0:L, 0:128], vT[:, a:a + L], identb[:]), reads=[vT, identb], writes=[p2])
                p3 = nps()
                p3b = p3[:, 0:256].bitcast(BF16)
                op("pe", lambda e: e.transpose(p3b[0:L, 0:128], kt[:, a:a + L], identb[:]), reads=[kt, identb], writes=[p3])
                op("dve", lambda e: e.tensor_tensor(ATb_[0:L, 0:L], p1[0:L, 0:L], trif[0:L, 0:L], ALU.mult),
                   reads=[p1, trif], writes=[ATb_])
                op("act", lambda e: e.copy(vtok_[0:L, :], p2b[0:L, 0:128]), reads=[p2], writes=[vtok_])
                op("act", lambda e: e.copy(ktok_[0:L, :], p3b[0:L, 0:128]), reads=[p3], writes=[ktok_])
                op("dve", lambda e: e.tensor_scalar(Spb_[:], S[:], rr_[:, 0, ci:ci + 1], None, ALU.mult),
                   reads=[S, rr_], writes=[Spb_])
                yield
                p4 = nps()
                op("pe", lambda e: e.matmul(p4[:, 0:L], lhsT=vtok_[0:L, :], rhs=ATb_[0:L, 0:L], start=True, stop=False),
                   reads=[vtok_, ATb_], writes=[p4], inc=False)
                op("pe", lambda e: e.matmul(p4[:, 0:L], lhsT=Spb_[:], rhs=qt[:, a:a + L], start=False, stop=True),
                   reads=[Spb_, qt], writes=[p4])
                p5 = nps()
                op("pe", lambda e: e.matmul(p5[:, 0:128], lhsT=ktok_[0:L, :], rhs=vtok_[0:L, :], start=True, stop=True),
                   reads=[ktok_, vtok_], writes=[p5])
                op("act", lambda e: e.copy(oT[:, a:a + L], p4[:, 0:L]), reads=[p4], writes=[oT])
                op("dve", lambda e: e.tensor_scalar(S[:], S[:], rr_[:, 2, ci:ci + 1], None, ALU.mult), reads=[S, rr_], writes=[S])
                op("dve", lambda e: e.scalar_tensor_tensor(S[:], p5[:, 0:128], rr_[:, 1, ci:ci + 1], S[:], ALU.mult, ALU.add),
                   reads=[p5, rr_, S], writes=[S])
                yield
            store_hg(l, h, cur_seq, last_tile, S)
            sq_b = B["kt"]
            rstd_ = B["rstd"]
            act(sq_b[:, 0:T], oT[:, 0:T], AF.Square, [oT], [sq_b])
            yield
            for (c0, c1) in tile["pieces"]:
                w = c1 - c0
                p = bc_sum(lambda i: (sq_b, sq_b[:, c0:c1]), 1, c0, c1)
                rstd_from(p, w, 1.0 / 128, RMS_EPS, rstd_)
                yield
                op("dve", lambda e: e.scalar_tensor_tensor(oT[:, c0:c1], oT[:, c0:c1], pv_hgnw[:, l, h:h + 1], rstd_[:, 0:w],
                                                           ALU.mult, ALU.mult), reads=[oT, pv_hgnw, rstd_], writes=[oT])
                op("dve", lambda e: e.tensor_tensor(on[h][:, c0:c1], oT[:, c0:c1], gs[:, c0:c1], ALU.mult),
                   reads=[oT, gs], writes=[on[h]])
                yield

        def hgrn_all(tile, l, first_tile, last_tile):
            for _ in hg_prep(tile, l, 0, HS[0]):
                pass
            for h in range(HG_H):
                A = hg_loop(tile, l, h, HS[h % 2], first_tile, last_tile)
                Bg = hg_prep(tile, l, h + 1, HS[(h + 1) % 2]) if h + 1 < HG_H else None
                a_alive, b_alive = True, Bg is not None
                while a_alive or b_alive:
                    if a_alive:
                        try:
                            next(A)
                        except StopIteration:
                            a_alive = False
                    if b_alive:
                        try:
                            next(Bg)
                        except StopIteration:
                            b_alive = False

        def store_hg(l, h, sq, last_tile, S):
            if sq == 0:
                if last_tile:
                    K.dma(hg_out[l, 0, h, :, :], S[:], reads=[S], writes=[dram_hg])
                else:
                    K.dma(hg_c[l, h, :, :], S[:], reads=[S], writes=[hg_c])
            else:
                K.dma(hg_out[l, 1, h, :, :], S[:], reads=[S], writes=[dram_hg])

        def gated_proj(tile, l, srcs, gate_off, wname, accumulate):
            for c in range(16):
                sg = tf[0]
                gemm([("in", l, 0, NKC, gate_off + c * 128, 128)], xT, tile,
                     lambda p, c0, c1: act(sg[:, c0:c1], p[:, 0:c1 - c0], AF.Sigmoid, [p], [sg]))

                def ev(p, c0, c1):
                    if not accumulate:
                        op("dve", lambda e: e.tensor_tensor(mg[c][:, c0:c1], p[:, 0:c1 - c0], sg[:, c0:c1], ALU.mult),
                           reads=[p, sg], writes=[mg[c]])
                    else:
                        op("dve", lambda e: e.tensor_tensor(sg[:, c0:c1], p[:, 0:c1 - c0], sg[:, c0:c1], ALU.mult),
                           reads=[p, sg], writes=[sg])
                        op("dve", lambda e: e.tensor_tensor(mg[c][:, c0:c1], mg[c][:, c0:c1], sg[:, c0:c1], ALU.add),
                           reads=[mg[c], sg], writes=[mg[c]])
                gemm([(wname, l, 0, NKC, c * 128, 128)], srcs, tile, ev)

        def ss_load(l, g, sq, first_tile):
            H = Hss1
            if sq == 0:
                if first_tile:
                    op("dve", lambda e: e.memset(H[:], 0.0), writes=[H])
                else:
                    K.dma(H[:], ss_c[l, g, :, :], reads=[ss_c], writes=[H])
            else:
                for pr in range(4):
                    hh = (g * 4 + pr) * 2
                    K.dma(sttok[:], st_ssm[l, hh:hh + 2, :, :].rearrange("h p n -> (h p) n"), writes=[sttok])
                    p = nps()
                    op("pe", lambda e: e.transpose(p[:, 0:128], sttok[:], identf[:]), reads=[sttok, identf], writes=[p])
                    op("dve", lambda e: e.tensor_copy(H[:, pr * 128:(pr + 1) * 128], p[:, 0:128]), reads=[p], writes=[H])
            op("act", lambda e: e.copy(Hsb1[:], H[:]), reads=[H], writes=[Hsb1])

        def ss_store(l, g, sq, last_tile):
            H = Hss1
            if sq == 0 and not last_tile:
                K.dma(ss_c[l, g, :, :], H[:], reads=[H], writes=[ss_c])
                return
            for pr in range(4):
                hh = (g * 4 + pr) * 2
                p = nps()
                op("pe", lambda e: e.transpose(p[:, 0:128], H[:, pr * 128:(pr + 1) * 128], identf[:]),
                   reads=[H, identf], writes=[p])
                op("dve", lambda e: e.tensor_copy(sttok[:], p[:, 0:128]), reads=[p], writes=[sttok])
                K.dma(ssm_out[l, sq, hh:hh + 2, :, :].rearrange("h p n -> (h p) n"), sttok[:], reads=[sttok], writes=[dram_ss])

        def ssd(tile, l, first_tile, last_tile):
            T = tile["w"]
            chunks = tile["chunks"]
            wdt = ("in", l, 0, NKC, O_DT, 32)

            def ev_dt(p, c0, c1):
                act(dtT[0:32, c0:c1], p[0:32, 0:c1 - c0], AF.Exp, [p], [dtT], bias=hv["dtb"][:, l:l + 1])
            gemm([wdt], xT, tile, ev_dt, m=32)
            act(dtT[0:32, 0:T], dtT[0:32, 0:T], AF.Ln, [dtT], [dtT], bias=one_col[0:32, 0:1])
            op("dve", lambda e: e.tensor_scalar(adtT[0:32, 0:T], dtT[0:32, 0:T], hvA[:, l:l + 1], None, ALU.mult),
               reads=[dtT, hvA], writes=[adtT])
            if stop_at == "ssd_dt0":
                raise _Stop()
            for ci, (a, L, sq) in enumerate(chunks):
                p = nps()
                op("pe", lambda e: e.matmul(p[0:L, 0:32], lhsT=dtT[0:32, a:a + L], rhs=identf[0:32, 0:32], start=True, stop=True),
                   reads=[dtT, identf], writes=[p], inc=False)
                op("pe", lambda e: e.matmul(p[0:L, 32:64], lhsT=adtT[0:32, a:a + L], rhs=identf[0:32, 0:32], start=True, stop=True),
                   reads=[adtT, identf], writes=[p])
                op("dve", lambda e: e.tensor_copy(dta[0:L, ci, :], p[0:L, 0:64]), reads=[p], writes=[dta])
                p2 = nps()
                op("pe", lambda e: e.matmul(p2[0:L, 0:32], lhsT=trif[0:L, 0:L], rhs=dta[0:L, ci, 32:64], start=True, stop=True),
                   reads=[trif, dta], writes=[p2])
                p2x = nps()
                op("pe", lambda e: e.matmul(p2x[:, 0:32], lhsT=onesf[0:L, :], rhs=dta[0:L, ci, 32:64], start=True, stop=True),
                   reads=[onesf, dta], writes=[p2x])
                op("dve", lambda e: e.tensor_copy(gtokS[0:L, :], p2[0:L, 0:32]), reads=[p2], writes=[gtokS])
                op("dve", lambda e: e.tensor_copy(gendS[:], p2x[:, 0:32]), reads=[p2x], writes=[gendS])
                act(dec_tok[0:L, ci, :], gtokS[0:L, :], AF.Exp, [gtokS], [dec_tok], scale=-1.0)
                act(rtot_bc[:, ci, :], gendS[:], AF.Exp, [gendS], [rtot_bc], scale=-1.0)
                op("dve", lambda e: e.tensor_tensor(wtmp[0:L, :], gtokS[0:L, :], gendS[0:L, :], ALU.subtract),
                   reads=[gtokS, gendS], writes=[wtmp])
                act(wtmp[0:L, :], wtmp[0:L, :], AF.Exp, [wtmp], [wtmp])
                op("dve", lambda e: e.tensor_tensor(w_tok[0:L, ci, :], wtmp[0:L, :], dta[0:L, ci, 0:32], ALU.mult),
                   reads=[wtmp, dta], writes=[w_tok])
            if stop_at == "ssd_dt":
                raise _Stop()
            BT, CT = tb[0], tb[1]
            xc = tb[2:6]
            sz = tb[6:10]
            pre = tf[8]
            for g in range(SSM_G):
                for which, dstb in ((0, BT), (1, CT)):
                    cidx = 16 + which * 4 + g
                    gemm([("in", l, 0, NKC, O_XBC + 2048 + which * 512 + g * 128, 128)], xT, tile,
                         lambda p, c0, c1: seg_evac(tile, p, c0, c1, pre, 3))
                    conv_apply(tile, pre, halo_x[l], halo_xs[l], cidx, lambda k: pv_cvw[:, l, k, cidx:cidx + 1],
                               pv_cvb[:, l, cidx:cidx + 1], 4, dstb, AF.Silu)
                for pr in range(4):
                    cc = g * 4 + pr
                    gemm([("in", l, 0, NKC, O_XBC + cc * 128, 128)], xT, tile,
                         lambda p, c0, c1: seg_evac(tile, p, c0, c1, pre, 3))
                    conv_apply(tile, pre, halo_x[l], halo_xs[l], cc, lambda k: pv_cvw[:, l, k, cc:cc + 1],
                               pv_cvb[:, l, cc:cc + 1], 4, xc[pr], AF.Silu)
                    gemm([("in", l, 0, NKC, O_Z + cc * 128, 128)], xT, tile,
                         lambda p, c0, c1: act(sz[pr][:, c0:c1], p[:, 0:c1 - c0], AF.Silu, [p], [sz[pr]]))
                if stop_at == "ssd_conv":
                    raise _Stop()
                cur_seq = None
                H, Hb = Hss1, Hsb1
                for ci, (a, L, sq) in enumerate(chunks):
                    if sq != cur_seq:
                        if cur_seq is not None:
                            ss_store(l, g, cur_seq, last_tile)
                        ss_load(l, g, sq, first_tile)
                        cur_seq = sq
                    L8 = 8 * L
                    gs8 = slice(g * 8, (g + 1) * 8)
                    px = nps()
                    pxb = px[:, 0:256].bitcast(BF16)
                    for pr in range(4):
                        op("pe", lambda e: e.transpose(pxb[0:L, pr * 128:(pr + 1) * 128], xc[pr][:, a:a + L], identb[:]),
                           reads=[xc[pr], identb], writes=[px], inc=(pr == 3))
                    op("dve", lambda e: e.tensor_tensor(xdt[0:L, :].rearrange("s (h p) -> s h p", p=64),
                                                        pxb[0:L, 0:512].rearrange("s (h p) -> s h p", p=64),
                                                        dta[0:L, ci, gs8].unsqueeze(2).to_broadcast([L, 8, 64]), ALU.mult),
                       reads=[px, dta], writes=[xdt])
                    op("dve", lambda e: e.tensor_tensor(xw[0:L, :].rearrange("s (h p) -> s h p", p=64),
                                                        pxb[0:L, 0:512].rearrange("s (h p) -> s h p", p=64),
                                                        w_tok[0:L, ci, gs8].unsqueeze(2).to_broadcast([L, 8, 64]), ALU.mult),
                       reads=[px, w_tok], writes=[xw])
                    pB = nps()
                    pBb = pB[:, 0:256].bitcast(BF16)
                    op("pe", lambda e: e.transpose(pBb[0:L, 0:128], BT[:, a:a + L], identb[:]), reads=[BT, identb], writes=[pB])
                    op("act", lambda e: e.copy(Btok[0:L, :], pBb[0:L, 0:128]), reads=[pB], writes=[Btok])
                    pC = nps()
                    op("pe", lambda e: e.matmul(pC[0:L, 0:L], lhsT=BT[:, a:a + L], rhs=CT[:, a:a + L], start=True, stop=True),
                       reads=[BT, CT], writes=[pC])
                    op("act", lambda e: e.copy(cbt[0:L, 0:L], pC[0:L, 0:L]), reads=[pC], writes=[cbt])
                    adt8 = dta[0:L, ci, 32 + g * 8:32 + (g + 1) * 8].unsqueeze(2).to_broadcast([L, 8, L])
                    op("dve", lambda e: e.tensor_tensor(R1[0:L, 0:L8].rearrange("s (h t) -> s h t", t=L), adt8,
                                                        trif[0:L, 0:L].unsqueeze(1).to_broadcast([L, 8, L]), ALU.mult),
                       reads=[dta, trif], writes=[R1])
                    op("dve", lambda e: e.tensor_copy(R2[0:L, 0:L8].rearrange("s (h t) -> s h t", t=L), adt8),
                       reads=[dta], writes=[R2])
                    pL = nps()
                    bigm = cb["bigm%d" % L]
                    op("pe", lambda e: e.matmul(pL[0:L, 0:L8], lhsT=onesf[0:L, 0:L], rhs=R1[0:L, 0:L8], start=True, stop=False),
                       reads=[onesf, R1], writes=[pL], inc=False)
                    op("pe", lambda e: e.matmul(pL[0:L, 0:L8], lhsT=negtri[0:L, 0:L], rhs=R2[0:L, 0:L8], start=False, stop=False),
                       reads=[negtri, R2], writes=[pL], inc=False)
                    op("pe", lambda e: e.matmul(pL[0:L, 0:L8], lhsT=identf[0:L, 0:L], rhs=bigm[0:L, 0:L8], start=False, stop=True),
                       reads=[identf, bigm], writes=[pL])
                    op("dve", lambda e: e.tensor_copy(LT[0:L, 0:L8], pL[0:L, 0:L8]), reads=[pL], writes=[LT])
                    act(LT[0:L, 0:L8], LT[0:L, 0:L8], AF.Exp, [LT], [LT], scale=-1.0)
                    op("dve", lambda e: e.tensor_tensor(scT[0:L, 0:L8].rearrange("s (h t) -> s h t", t=L),
                                                        LT[0:L, 0:L8].rearrange("s (h t) -> s h t", t=L),
                                                        cbt[0:L, 0:L].unsqueeze(1).to_broadcast([L, 8, L]), ALU.mult),
                       reads=[LT, cbt], writes=[scT])
                    pY1 = nps()
                    for hh in range(8):
                        op("pe", lambda e: e.matmul(pY1[0:L, hh * 64:(hh + 1) * 64], lhsT=scT[0:L, hh * L:(hh + 1) * L],
                                                    rhs=xdt[0:L, hh * 64:(hh + 1) * 64], start=True, stop=True),
                           reads=[scT, xdt], writes=[pY1], inc=(hh == 7))
                    pY2 = nps()
                    op("pe", lambda e: e.matmul(pY2[0:L, 0:512], lhsT=CT[:, a:a + L], rhs=Hb[:, 0:512], start=True, stop=True),
                       reads=[CT, Hb], writes=[pY2])
                    op("dve", lambda e: e.tensor_tensor(ytmp[0:L, :].rearrange("s (h p) -> s h p", p=64),
                                                        pY2[0:L, 0:512].rearrange("s (h p) -> s h p", p=64),
                                                        dec_tok[0:L, ci, gs8].unsqueeze(2).to_broadcast([L, 8, 64]), ALU.mult),
                       reads=[pY2, dec_tok], writes=[ytmp])
                    op("dve", lambda e: e.tensor_tensor(ytok[0:L, :], pY1[0:L, 0:512], ytmp[0:L, :], ALU.add),
                       reads=[pY1, ytmp], writes=[ytok])
                    pT = nps()
                    pTb = pT[:, 0:256].bitcast(BF16)
                    for pr in range(4):
                        op("pe", lambda e: e.transpose(pTb[:, pr * 64:pr * 64 + L], ytok[0:L, pr * 128:(pr + 1) * 128],
                                                       identb[0:L, 0:L]), reads=[ytok, identb], writes=[pT], inc=(pr == 3))
                    for pr in range(4):
                        op("act", lambda e: e.copy(yT4[pr][:, a:a + L], pTb[:, pr * 64:pr * 64 + L]), reads=[pT], writes=[yT4[pr]])
                    pH = nps()
                    op("pe", lambda e: e.matmul(pH[:, 0:512], lhsT=Btok[0:L, :], rhs=xw[0:L, :], start=True, stop=True),
                       reads=[Btok, xw], writes=[pH])
                    op("dve", lambda e: e.tensor_tensor(H[:].rearrange("n (h p) -> n h p", p=64),
                                                        H[:].rearrange("n (h p) -> n h p", p=64),
                                                        rtot_bc[:, ci, gs8].unsqueeze(2).to_broadcast([128, 8, 64]), ALU.mult),
                       reads=[H, rtot_bc], writes=[H])
                    op("dve", lambda e: e.tensor_tensor(H[:], H[:], pH[:, 0:512], ALU.add), reads=[H, pH], writes=[H])
                    if stop_at == "ssd_c1":
                        raise _Stop()
                    op("act", lambda e: e.copy(Hb[:], H[:]), reads=[H], writes=[Hb])
                ss_store(l, g, cur_seq, last_tile)
                for pr in range(4):
                    cc = g * 4 + pr
                    yv = yT4[pr][:, 0:T]
                    op("dve", lambda e: e.scalar_tensor_tensor(yv, xc[pr][:, 0:T], pv_dsk[:, l, cc:cc + 1], yv, ALU.mult, ALU.add),
                       reads=[xc[pr], pv_dsk, yT4[pr]], writes=[yT4[pr]])
                    op("dve", lambda e: e.tensor_tensor(yv, yv, sz[pr][:, 0:T], ALU.mult), reads=[yT4[pr], sz[pr]], writes=[yT4[pr]])
                    act(xc[pr][:, 0:T], yv, AF.Square, [yT4[pr]], [xc[pr]])
                for (c0, c1) in tile["pieces"]:
                    w = c1 - c0
                    p = bc_sum(lambda i: (xc[i], xc[i][:, c0:c1]), 4, c0, c1)
                    rstd_from(p, w, 1.0 / 512, RMS_EPS, st_rstd)
                    for pr in range(4):
                        cc = g * 4 + pr
                        op("dve", lambda e: e.scalar_tensor_tensor(on[cc][:, c0:c1], yT4[pr][:, c0:c1], pv_snw[:, l, cc:cc + 1],
                                                                   st_rstd[:, 0:w], ALU.mult, ALU.mult),
                           reads=[yT4[pr], pv_snw, st_rstd], writes=[on[cc]])


        def layer_norm(tile, gvec, bvec, l, emit_out):
            T = tile["w"]
            for (c0, c1) in tile["pieces"]:
                w = c1 - c0
                p1 = bc_sum(lambda i: (xT[i], xT[i][:, c0:c1]), 16, c0, c1)
                p2 = nps()
                for c in range(16):
                    sb_ = sqr[c % 2]
                    act(sb_[:, 0:w], xT[c][:, c0:c1], AF.Square, [xT[c]], [sb_])
                    op("pe", lambda e: e.matmul(p2[:, 0:w], lhsT=onesb[:], rhs=sb_[:, 0:w], start=(c == 0), stop=(c == 15)),
                       reads=[onesb, sb_], writes=[p2], inc=True)
                op("dve", lambda e: e.tensor_scalar(st_mean[:, 0:w], p1[:, 0:w], 1.0 / D, None, ALU.mult), reads=[p1], writes=[st_mean])
                op("dve", lambda e: e.tensor_tensor(st_tmp[:, 0:w], st_mean[:, 0:w], st_mean[:, 0:w], ALU.mult),
                   reads=[st_mean], writes=[st_tmp])
                op("dve", lambda e: e.scalar_tensor_tensor(st_tmp[:, 0:w], p2[:, 0:w], 1.0 / D, st_tmp[:, 0:w], ALU.mult, ALU.subtract),
                   reads=[p2, st_tmp], writes=[st_tmp])
                act(st_tmp[:, 0:w], st_tmp[:, 0:w], AF.Sqrt, [st_tmp], [st_tmp], bias=eps_col(LN_EPS))
                op("dve", lambda e: e.reciprocal(st_rstd[:, 0:w], st_tmp[:, 0:w]), reads=[st_tmp], writes=[st_rstd])
                for c in range(16):
                    t32 = tf[0] if c % 2 == 0 else tf[1]
                    op("dve", lambda e: e.tensor_tensor(t32[:, 0:w], xT[c][:, c0:c1], st_mean[:, 0:w], ALU.subtract),
                       reads=[xT[c], st_mean], writes=[t32])
                    op("dve", lambda e: e.tensor_tensor(t32[:, 0:w], t32[:, 0:w], st_rstd[:, 0:w], ALU.mult),
                       reads=[t32, st_rstd], writes=[t32])
                    if not emit_out:
                        act(xT[c][:, c0:c1], t32[:, 0:w], AF.Identity, [t32, gvec, bvec], [xT[c]],
                            scale=gvec[:, l, c:c + 1], bias=bvec[:, l, c:c + 1])
                    else:
                        yb = tf[2 + (c % 4)]
                        act(yb[:, 0:w], t32[:, 0:w], AF.Identity, [t32, gvec, bvec], [yb],
                            scale=gvec[:, l, c:c + 1], bias=bvec[:, l, c:c + 1])
                        nb = (w + 127) // 128
                        for bi in range(nb):
                            nr = min(128, w - bi * 128)
                            if c % 4 == 0 and bi == 0:
                                pass
                            pt = nps()
                            op("pe", lambda e: e.transpose(pt[0:nr, 0:128], yb[:, bi * 128:bi * 128 + nr], identf[:]),
                               reads=[yb, identf], writes=[pt])
                            ob = ost[ost_i[0] % 4]
                            ost_i[0] += 1
                            op("dve", lambda e: e.tensor_copy(ob[0:nr, :], pt[0:nr, 0:128]), reads=[pt], writes=[ob])
                            r0 = tile["r0"] + c0 + bi * 128
                            K.dma(y_out[r0:r0 + nr, c * 128:(c + 1) * 128], ob[0:nr, :], reads=[ob], writes=[dram_y])

        ost = [K.sb("ost%d" % i, [128, 128], F32) for i in range(4)]
        ost_i = [0]
        xld = [K.sb("xld%d" % i, [128, 512], F32) for i in range(2)]
        xld_i = [0]

        def dump(nm, bufs, tile, l):
            if not debug or l != 0 or tile["idx"] != 0:
                return
            for i, b in enumerate(bufs):
                K.dma(dbg[nm][i, :, 0:tile["w"]], b[:, 0:tile["w"]], reads=[b])

        def mixer(tile, l, first_tile, last_tile):
            if stop_at in ("load", "load_dma", "load_1"):
                raise _Stop()
            hgrn_all(tile, l, first_tile, last_tile)
            dump("d_on", on, tile, l)
            if stop_at == "hgrn":
                raise _Stop()
            gated_proj(tile, l, on, O_GA, "pa", False)
            dump("d_mga", mg, tile, l)
            if stop_at == "pa":
                raise _Stop()
            ssd(tile, l, first_tile, last_tile)
            dump("d_yn", on, tile, l)
            if stop_at == "ssd":
                raise _Stop()
            gated_proj(tile, l, on, O_GB, "pb", True)
            dump("d_mg", mg, tile, l)
            for c in range(16):
                gemm([("out", l, 0, NKC, c * 128, 128)], mg, tile,
                     lambda p, c0, c1: op("dve", lambda e: e.scalar_tensor_tensor(xT[c][:, c0:c1], xT[c][:, c0:c1], ALPHA,
                                                                                  p[:, 0:c1 - c0], ALU.mult, ALU.add),
                                          reads=[xT[c], p], writes=[xT[c]]))
            layer_norm(tile, pv_l1g, pv_l1b, l, False)
            dump("d_xmid", xT, tile, l)
            if stop_at == "mixer":
                raise _Stop()

        def ffn(tile, l, last_layer):
            T = tile["w"]
            pre = tf[8]
            ga = tf[7]
            for j in range(NFF):
                gemm([("up", l, 0, NKC, j * 128, 128)], xT, tile, lambda p, c0, c1: seg_evac(tile, p, c0, c1, pre, 2))
                conv_apply(tile, pre, halo_f[l], halo_fs[l], j, lambda k: pv_fcw[:, l, k, j:j + 1], pv_fcb[:, l, j:j + 1], 3,
                           ga, AF.Gelu)
                gemm([("up", l, 0, NKC, DFF + j * 128, 128)], xT, tile,
                     lambda p, c0, c1: op("dve", lambda e: e.tensor_tensor(hm[j][:, c0:c1], p[:, 0:c1 - c0], ga[:, c0:c1], ALU.mult),
                                          reads=[p, ga], writes=[hm[j]]))
            for c in range(16):
                gemm([("dn", l, 0, 16, c * 128, 128), ("dn", l, 16, 16, c * 128, 128), ("dn", l, 32, 12, c * 128, 128)], hm, tile,
                     lambda p, c0, c1: op("dve", lambda e: e.scalar_tensor_tensor(xT[c][:, c0:c1], xT[c][:, c0:c1], ALPHA,
                                                                                  p[:, 0:c1 - c0], ALU.mult, ALU.add),
                                          reads=[xT[c], p], writes=[xT[c]]))
            dump("d_hm", hm, tile, l)
            dump("d_v2", xT, tile, l)
            layer_norm(tile, pv_l2g, pv_l2b, l, last_layer)

        if stop_at == "setup":
            tiles = []
        for tile in tiles:
            T = tile["w"]
            ti = tile["idx"]
            first_tile = (ti == 0)
            last_tile = (ti == NTILES - 1)
            nblk = (T + 127) // 128
            for bi in range(nblk):
                r0 = bi * 128
                nr = min(128, T - r0)
                for q4 in range(4):
                    xtok = xld[xld_i[0] % 2]
                    xld_i[0] += 1
                    K.dma(xtok[0:nr, :], xin[tile["r0"] + r0: tile["r0"] + r0 + nr, q4 * 512:(q4 + 1) * 512], writes=[xtok])
                    p = nps()
                    for j in range(4):
                        op("pe", lambda e: e.transpose(p[:, j * 128: j * 128 + nr], xtok[0:nr, j * 128:(j + 1) * 128],
                                                       identf[0:nr, 0:nr]), reads=[xtok, identf], writes=[p], inc=(j == 3))
                    for j in range(4):
                        cidx = q4 * 4 + j
                        if j % 2:
                            op("dve", lambda e: e.tensor_copy(xT[cidx][:, r0:r0 + nr], p[:, j * 128:j * 128 + nr]), reads=[p], writes=[xT[cidx]])
                        else:
                            op("dve", lambda e: e.tensor_copy(xT[cidx][:, r0:r0 + nr], p[:, j * 128:j * 128 + nr]), reads=[p], writes=[xT[cidx]])
            try:
                for l in range(nlayers):
                    mixer(tile, l, first_tile, last_tile)
                    ffn(tile, l, l == nlayers - 1)
            except _Stop:
                break
        for l in range(nlayers):
            for k in range(3):
                K.dma(conv_out[l, 0, k, :].rearrange("(c p) -> p c", p=128), halo_x[l][:, :, k], reads=[halo_x[l]], writes=[dram_cv],
                      allow_slow_non_contiguous=True)
                K.dma(conv_out[l, 1, k, :].rearrange("(c p) -> p c", p=128), halo_xs[l][:, :, k], reads=[halo_xs[l]], writes=[dram_cv],
                      allow_slow_non_contiguous=True)
            for k in range(2):
                K.dma(ffn_out[l, 0, k, :].rearrange("(c p) -> p c", p=128), halo_f[l][:, :, k], reads=[halo_f[l]], writes=[dram_ff],
                      allow_slow_non_contiguous=True)
                K.dma(ffn_out[l, 1, k, :].rearrange("(c p) -> p c", p=128), halo_fs[l][:, :, k], reads=[halo_fs[l]], writes=[dram_ff],
                      allow_slow_non_contiguous=True)
        K.finish([dram_y, dram_hg, dram_ss, dram_cv, dram_ff])
        print("instructions:", K.n_ins, "waits:", K.n_wait, "sems:", K.nsem, flush=True)
    return nc


_CACHE = {}


def kernel(**inp):
    f32 = lambda a: np.ascontiguousarray(np.asarray(a, dtype=np.float32))
    xp = f32(inp["x_prompt"])
    xs = f32(inp["x_sample"])
    meta = f32(inp["meta_tokens"])
    if "nc" not in _CACHE:
        _CACHE["nc"] = build_program()
    nc = _CACHE["nc"]
    consts = build_consts()
    wnames = ["hgrn_lb_logits", "hgrn_norm_w", "ssm_conv_w", "ssm_conv_b", "ssm_dt_bias", "ssm_a_log",
              "ssm_d", "ssm_norm_w", "ln1_g", "ln1_b", "ffn_conv_w", "ffn_conv_b", "ln2_g", "ln2_b"]
    shared = {n: f32(inp[n]) for n in wnames}
    shared["wst"] = build_wstream(inp)
    for k, v in consts.items():
        shared["c_" + k] = v
    in_maps = []
    for c in range(8):
        b = c % 2
        if c < 2:
            xin = np.concatenate([meta, xp[b, 0:TP], xs[c], xp[b, TP:]], axis=0)
        else:
            xin = np.zeros((NTOK, D), np.float32)
            xin[N_META + TP:N_META + TP + DEC_SEQ] = xs[c]
        m = dict(shared)
        m["xin"] = np.ascontiguousarray(xin)
        m["st_hg"] = f32(inp["state_hgrn"][:, c])
        m["st_ssm"] = f32(inp["state_ssm"][:, c])
        m["st_conv"] = f32(inp["state_ssm_conv"][:, c])
        m["st_ffn"] = f32(inp["state_ffn_conv"][:, c])
        in_maps.append(m)
    res = run_bass_kernel_spmd(nc, in_maps, core_ids=list(range(8)))
    R = res.results
    ys = [r["y"] for r in R]
    y_prompt = np.stack([np.concatenate([ys[b][N_META:N_META + TP], ys[b][N_META + TP + DEC_SEQ:]], axis=0) for b in range(2)])
    y_sample = np.stack([ys[c][N_META + TP:N_META + TP + DEC_SEQ] for c in range(8)])
    hg_p = np.stack([R[b]["hg_o"][:, 0] for b in range(2)], axis=1)
    ssm_p = np.stack([R[b]["ssm_o"][:, 0] for b in range(2)], axis=1)
    conv_p = np.stack([R[b]["conv_o"][:, 0] for b in range(2)], axis=1)
    ffn_p = np.stack([R[b]["ffn_o"][:, 0] for b in range(2)], axis=1)
    hg_s = np.stack([R[c]["hg_o"][:, 1] for c in range(8)], axis=1)
    ssm_s = np.stack([R[c]["ssm_o"][:, 1] for c in range(8)], axis=1)
    conv_s = np.stack([R[c]["conv_o"][:, 1] for c in range(8)], axis=1)
    ffn_s = np.stack([R[c]["ffn_o"][:, 1] for c in range(8)], axis=1)
    outs = (y_prompt, y_sample, hg_p, ssm_p, conv_p, ffn_p, hg_s, ssm_s, conv_s, ffn_s)
    return tuple(np.ascontiguousarray(o, dtype=np.float32) for o in outs)
DBG_NAMES = ["d_on", "d_mga", "d_yn", "d_mg", "d_xmid", "d_v2", "d_hm"]
```

```python
import contextlib
import numpy as np
import concourse.bass as bass
import concourse.mybir as mybir
from concourse.bass_utils import run_bass_kernel_spmd

F32 = mybir.dt.float32
BF16 = mybir.dt.bfloat16
AF = mybir.ActivationFunctionType
ALU = mybir.AluOpType

D = 2048
NKC = 16
DEPTH = 2
SEQ = 4096
N_META = 16
DEC_SEQ = 32
HG_H = 16
SSM_H = 32
SSM_P = 64
SSM_N = 128
SSM_G = 4
XBC = 3072
DFF = 5632
NFF = 44
N_IN = 17440
O_Q, O_F, O_I, O_G, O_Z, O_XBC, O_DT, O_GA, O_GB = 0, 2048, 4096, 6144, 8192, 10240, 13312, 13344, 15392
ALPHA = (2.0 * DEPTH) ** 0.25
LN_EPS = 1e-5
RMS_EPS = 1e-6
BIG = 30000.0
TW = 560
TP = 512
NTILES = SEQ // TP
NTOK = N_META + SEQ + DEC_SEQ
MAXC = 30000


class Buf:
    __slots__ = ("t", "w", "r", "name")

    def __init__(self, t, name=""):
        self.t = t
        self.w = None
        self.r = {}
        self.name = name

    def __getitem__(self, k):
        return self.t[k]


class Eng:
    def __init__(self, name, handle):
        self.name = name
        self.h = handle
        self.sem = None
        self.count = 0
        self.known = {}


class KB:
    def __init__(self, nc, stack, n_dma_sems=16):
        self.nc = nc
        self.stack = stack
        self.sems = {}
        self.eng = {}
        self.nsem = 0
        for name, h in (("pe", nc.tensor), ("act", nc.scalar), ("dve", nc.vector),
                        ("pool", nc.gpsimd), ("sp", nc.sync)):
            E = Eng(name, h)
            self.eng[name] = E
            self._newsem(E)
        self.dma_slots = []
        for i in range(n_dma_sems):
            s = stack.enter_context(nc.semaphore("d%d" % i))
            self.sems[id(s)] = s
            self.dma_slots.append([s, 0])
        self.dma_i = 0
        self.n_wait = 0
        self.n_ins = 0
        self.names = set()

    def _newsem(self, E):
        s = self.stack.enter_context(self.nc.semaphore("s_%s_%d" % (E.name, self.nsem)))
        self.nsem += 1
        self.sems[id(s)] = s
        E.sem = s
        E.count = 0

    def uname(self, name):
        i = 0
        n = name
        while n in self.names:
            i += 1
            n = "%s_%d" % (name, i)
        self.names.add(n)
        return n

    def sb(self, name, shape, dt):
        name = self.uname(name)
        t = self.stack.enter_context(self.nc.sbuf_tensor(name, list(shape), dt))
        return Buf(t, name)

    def psum(self, name, shape, dt):
        t = self.stack.enter_context(self.nc.psum_tensor(name, list(shape), dt))
        return Buf(t, name)

    def _need(self, reads, writes):
        need = {}
        for b in reads:
            if b.w is not None:
                k, v = b.w
                if need.get(k, 0) < v:
                    need[k] = v
        for b in writes:
            if b.w is not None:
                k, v = b.w
                if need.get(k, 0) < v:
                    need[k] = v
            for k, v in b.r.items():
                if need.get(k, 0) < v:
                    need[k] = v
        return need

    def _wait(self, E, need, skip_own=False):
        for k, v in need.items():
            if skip_own and k == id(E.sem):
                continue
            if E.known.get(k, 0) < v:
                E.h.wait_ge(self.sems[k], v)
                E.known[k] = v
                self.n_wait += 1

    def _mark(self, tok, reads, writes):
        k, v = tok
        for b in reads:
            if b.r.get(k, 0) < v:
                b.r[k] = v
        for b in writes:
            b.w = tok
            b.r = {}

    def op(self, e, emit, reads=(), writes=(), inc=True):
        E = self.eng[e]
        if inc and E.count >= MAXC:
            self._newsem(E)
        self._wait(E, self._need(reads, writes), skip_own=(e == "pe"))
        ins = emit(E.h)
        self.n_ins += 1
        if inc:
            E.count += 1
            ins.then_inc(E.sem, 1)
            tok = (id(E.sem), E.count)
        else:
            tok = (id(E.sem), E.count + 1)
        self._mark(tok, reads, writes)
        return tok

    def dma(self, out, in_, reads=(), writes=(), q="sp", **kw):
        E = self.eng[q]
        slot = self.dma_slots[self.dma_i]
        self.dma_i = (self.dma_i + 1) % len(self.dma_slots)
        if slot[1] >= MAXC // 16:
            s = self.stack.enter_context(self.nc.semaphore("dx%d" % self.nsem))
            self.nsem += 1
            self.sems[id(s)] = s
            self._wait(E, {id(slot[0]): slot[1] * 16})
            slot[0] = s
            slot[1] = 0
        need = self._need(reads, writes)
        if slot[1] > 0:
            k = id(slot[0])
            need[k] = max(need.get(k, 0), slot[1] * 16)
        self._wait(E, need)
        E.h.dma_start(out=out, in_=in_, **kw).then_inc(slot[0], 16)
        self.n_ins += 1
        slot[1] += 1
        tok = (id(slot[0]), slot[1] * 16)
        self._mark(tok, reads, writes)
        return tok

    def finish(self, bufs=()):
        E = self.eng["sp"]
        need = {}
        for s, c in self.dma_slots:
            if c:
                need[id(s)] = c * 16
        for b in bufs:
            if b.w is not None:
                k, v = b.w
                need[k] = max(need.get(k, 0), v)
        self._wait(E, need)


def tile_defs(ntiles):
    tiles = []
    r = 0
    for ti in range(ntiles):
        if ti == 0:
            w = N_META + TP + DEC_SEQ
            segs = [(0, N_META + TP, 0), (N_META + TP, DEC_SEQ, 1)]
            chunks = [(N_META + TP, DEC_SEQ, 1), (0, N_META, 0)] + [(N_META + 64 * j, 64, 0) for j in range(TP // 64)]
        else:
            w = TP
            segs = [(0, TP, 0)]
            chunks = [(64 * j, 64, 0) for j in range(TP // 64)]
        pieces = []
        c = 0
        while c < w:
            pieces.append((c, min(c + 512, w)))
            c += 512
        tiles.append(dict(r0=r, w=w, segs=segs, chunks=chunks, pieces=pieces, idx=ti))
        r += w
    return tiles


def build_consts():
    c = {}
    c["identf"] = np.eye(128, dtype=np.float32)
    tri = (np.arange(128)[:, None] <= np.arange(128)[None, :]).astype(np.float32)
    c["tri"] = tri
    c["ones"] = np.ones((128, 128), np.float32)
    for L in (16, 32, 64):
        m = (np.arange(L)[None, :] < np.arange(L)[:, None]).astype(np.float32) * BIG
        c["bigm%d" % L] = np.ascontiguousarray(np.broadcast_to(m[:, None, :], (L, 8, L))).reshape(L, 8 * L)
    return c


def plan_pass(l):
    for h in range(HG_H):
        for o in (O_Q, O_F, O_I, O_G):
            yield ("in", l, 0, NKC, o + h * 128, 128)
    yield ("in", l, 0, NKC, O_DT, 32)
    for g in range(SSM_G):
        yield ("in", l, 0, NKC, O_XBC + 2048 + g * 128, 128)
        yield ("in", l, 0, NKC, O_XBC + 2560 + g * 128, 128)
        for pr in range(4):
            cc = g * 4 + pr
            yield ("in", l, 0, NKC, O_XBC + cc * 128, 128)
            yield ("in", l, 0, NKC, O_Z + cc * 128, 128)
        if g == 1:
            for c in range(16):
                yield ("in", l, 0, NKC, O_GA + c * 128, 128)
                yield ("pa", l, 0, NKC, c * 128, 128)
    for c in range(16):
        yield ("in", l, 0, NKC, O_GB + c * 128, 128)
        yield ("pb", l, 0, NKC, c * 128, 128)
    for c in range(16):
        yield ("out", l, 0, NKC, c * 128, 128)
    for j in range(NFF):
        yield ("up", l, 0, NKC, j * 128, 128)
        yield ("up", l, 0, NKC, DFF + j * 128, 128)
    for c in range(16):
        yield ("dn", l, 0, 16, c * 128, 128)
        yield ("dn", l, 16, 16, c * 128, 128)
        yield ("dn", l, 32, 12, c * 128, 128)


NBL = len(list(plan_pass(0)))
WNAME = {"in": "w_in", "pa": "w_proj_a", "pb": "w_proj_b", "out": "w_out", "up": "ffn_w_up", "dn": "ffn_w_down"}


def build_wstream(inp):
    wst = np.zeros((DEPTH, NBL, 128, NKC, 128), np.float32)
    for l in range(DEPTH):
        for j, (name, _, k0, nk, c0, ncol) in enumerate(plan_pass(l)):
            W = inp[WNAME[name]][l]
            blk = np.asarray(W[k0 * 128:(k0 + nk) * 128, c0:c0 + ncol], dtype=np.float32).reshape(nk, 128, ncol)
            wst[l, j, :, 0:nk, 0:ncol] = blk.transpose(1, 0, 2)
    return wst.reshape(DEPTH, NBL, 128, NKC * 128)


class _Stop(Exception):
    pass


def build_program(ntiles=NTILES, nlayers=DEPTH, debug=False, stop_at=None):
    nc = bass.Bass("TRN2", target_bir_lowering=False)
    tiles = tile_defs(ntiles)
    ntok = sum(t["w"] for t in tiles)

    def din(name, shape):
        return nc.dram_tensor(name, list(shape), F32, kind="ExternalInput").ap()

    def dout(name, shape):
        return nc.dram_tensor(name, list(shape), F32, kind="ExternalOutput").ap()

    xin = din("xin", [NTOK, D])
    st_hg = din("st_hg", [DEPTH, HG_H, 128, 128])
    st_ssm = din("st_ssm", [DEPTH, SSM_H, SSM_P, SSM_N])
    st_conv = din("st_conv", [DEPTH, 3, XBC])
    st_ffn = din("st_ffn", [DEPTH, 2, DFF])
    lb_logits = din("hgrn_lb_logits", [DEPTH, D])
    hg_nw = din("hgrn_norm_w", [DEPTH, D])
    cv_w = din("ssm_conv_w", [DEPTH, 4, XBC])
    cv_b = din("ssm_conv_b", [DEPTH, XBC])
    dt_bias = din("ssm_dt_bias", [DEPTH, SSM_H])
    a_log = din("ssm_a_log", [DEPTH, SSM_H])
    ssm_d = din("ssm_d", [DEPTH, SSM_H])
    ssm_nw = din("ssm_norm_w", [DEPTH, D])
    ln1_g = din("ln1_g", [DEPTH, D])
    ln1_b = din("ln1_b", [DEPTH, D])
    fcv_w = din("ffn_conv_w", [DEPTH, 3, DFF])
    fcv_b = din("ffn_conv_b", [DEPTH, DFF])
    ln2_g = din("ln2_g", [DEPTH, D])
    ln2_b = din("ln2_b", [DEPTH, D])
    wst = din("wst", [DEPTH, NBL, 128, NKC * 128])
    cdefs = build_consts()
    cin = {k: din("c_" + k, v.shape) for k, v in cdefs.items()}

    y_out = dout("y", [NTOK, D])
    hg_out = dout("hg_o", [DEPTH, 2, HG_H, 128, 128])
    ssm_out = dout("ssm_o", [DEPTH, 2, SSM_H, SSM_P, SSM_N])
    conv_out = dout("conv_o", [DEPTH, 2, 3, XBC])
    ffn_out = dout("ffn_o", [DEPTH, 2, 2, DFF])
    dbg = {}
    if debug:
        for nm in ("d_on", "d_mga", "d_yn", "d_mg", "d_xmid", "d_v2"):
            dbg[nm] = nc.dram_tensor(nm, [16, 128, TW], BF16, kind="ExternalOutput").ap()
        dbg["d_hm"] = nc.dram_tensor("d_hm", [NFF, 128, TW], BF16, kind="ExternalOutput").ap()


    with contextlib.ExitStack() as st:
        K = KB(nc, st)
        op = K.op

        xT = [K.sb("xT%d" % i, [128, TW], BF16) for i in range(NKC)]
        hm = [K.sb("hm%d" % i, [128, TW], BF16) for i in range(NFF)]
        on = hm[0:16]
        mg = hm[16:32]
        NT32 = 10
        tf = [K.sb("tf%d" % i, [128, TW + 40], F32) for i in range(NT32)]
        tb = [K.sb("tb%d" % i, [128, TW], BF16) for i in range(10)]
        NS, NB = 2, 6
        stage = [K.sb("stg%d" % i, [128, NKC, 128], F32) for i in range(NS)]
        wring = [K.sb("wb%d" % i, [128, NKC, 128], BF16) for i in range(NB)]
        psb = [K.psum("ps%d" % i, [128, 512], F32) for i in range(8)]
        ps_i = [0]

        def nps():
            b = psb[ps_i[0]]
            ps_i[0] = (ps_i[0] + 1) % 8
            return b

        Shg1 = K.sb("Shg", [128, 128], F32)
        Hss1 = K.sb("Hss", [128, 512], F32)
        Hsb1 = K.sb("Hsb", [128, 512], BF16)
        hg_c = Buf(nc.dram_tensor("hg_carry", [DEPTH, HG_H, 128, 128], F32).ap(), "hg_carry")
        ss_c = Buf(nc.dram_tensor("ss_carry", [DEPTH, SSM_G, 128, 512], F32).ap(), "ss_carry")
        halo_x = [K.sb("halox%d" % l, [128, 24, 3], F32) for l in range(DEPTH)]
        halo_f = [K.sb("halof%d" % l, [128, NFF, 2], F32) for l in range(DEPTH)]
        halo_xs = [K.sb("haloxs%d" % l, [128, 24, 3], F32) for l in range(DEPTH)]
        halo_fs = [K.sb("halofs%d" % l, [128, NFF, 2], F32) for l in range(DEPTH)]
        dram_y = Buf(y_out, "y")
        dram_hg = Buf(hg_out, "hgo")
        dram_ss = Buf(ssm_out, "sso")
        dram_cv = Buf(conv_out, "cvo")
        dram_ff = Buf(ffn_out, "ffo")

        cb = {}
        for k, v in cdefs.items():
            b = K.sb("sc_" + k, [128, v.shape[1]], F32)
            K.dma(b[0:v.shape[0], :], cin[k][:, :], writes=[b])
            cb[k] = b
        identb = K.sb("identb", [128, 128], BF16)
        onesb = K.sb("onesb", [128, 128], BF16)
        trib = K.sb("trib", [128, 128], BF16)
        negtri = K.sb("negtri", [128, 128], F32)
        op("dve", lambda e: e.tensor_copy(identb[:], cb["identf"][:]), reads=[cb["identf"]], writes=[identb])
        op("dve", lambda e: e.tensor_copy(onesb[:], cb["ones"][:]), reads=[cb["ones"]], writes=[onesb])
        op("dve", lambda e: e.tensor_copy(trib[:], cb["tri"][:]), reads=[cb["tri"]], writes=[trib])
        op("dve", lambda e: e.tensor_scalar(negtri[:], cb["tri"][:], -1.0, None, ALU.mult), reads=[cb["tri"]], writes=[negtri])
        identf, onesf, trif = cb["identf"], cb["ones"], cb["tri"]
        one_col = K.sb("one_col", [128, 1], F32)
        op("dve", lambda e: e.memset(one_col[:], 1.0), writes=[one_col])

        def load_vec(name, ap, nch, extra=None):
            b = K.sb("pv_" + name, [128, DEPTH, nch] if extra is None else [128, DEPTH, extra, nch], F32)
            for l in range(DEPTH):
                if extra is None:
                    K.dma(b[:, l, :], ap[l, :].rearrange("(c p) -> p c", p=128), writes=[b],
                          allow_slow_non_contiguous=True)
                else:
                    for k in range(extra):
                        K.dma(b[:, l, k, :], ap[l, k, :].rearrange("(c p) -> p c", p=128), writes=[b],
                              allow_slow_non_contiguous=True)
            return b

        pv_lbl = load_vec("lbl", lb_logits, 16)
        pv_hgnw = load_vec("hgnw", hg_nw, 16)
        pv_cvw = load_vec("cvw", cv_w, 24, extra=4)
        pv_cvb = load_vec("cvb", cv_b, 24)
        pv_snw = load_vec("snw", ssm_nw, 16)
        pv_l1g = load_vec("l1g", ln1_g, 16)
        pv_l1b = load_vec("l1b", ln1_b, 16)
        pv_fcw = load_vec("fcw", fcv_w, NFF, extra=3)
        pv_fcb = load_vec("fcb", fcv_b, NFF)
        pv_l2g = load_vec("l2g", ln2_g, 16)
        pv_l2b = load_vec("l2b", ln2_b, 16)
        hv = {}
        for name, ap in (("dtb", dt_bias), ("alog", a_log), ("dsk", ssm_d)):
            b = K.sb("hv_" + name, [SSM_H, DEPTH], F32)
            K.dma(b[:], ap.rearrange("l h -> h l"), writes=[b], allow_slow_non_contiguous=True)
            hv[name] = b
        hvA = K.sb("hv_A", [SSM_H, DEPTH], F32)
        op("act", lambda e: e.activation(hvA[:], hv["alog"][:], AF.Exp), reads=[hv["alog"]], writes=[hvA])
        pv_lb = K.sb("pv_lb", [128, DEPTH, 16], F32)
        pv_oml = K.sb("pv_oml", [128, DEPTH, 16], F32)
        lbe = K.sb("lbe", [128, DEPTH, 16], F32)
        lbs = K.sb("lbs", [128, 16], F32)
        op("act", lambda e: e.activation(lbe[:], pv_lbl[:], AF.Exp), reads=[pv_lbl], writes=[lbe])
        op("dve", lambda e: e.tensor_copy(lbs[:], lbe[:, 0, :]), reads=[lbe], writes=[lbs])
        for l in range(1, DEPTH):
            op("dve", lambda e: e.tensor_tensor(lbs[:], lbs[:], lbe[:, l, :], ALU.add), reads=[lbe, lbs], writes=[lbs])
        op("dve", lambda e: e.reciprocal(lbs[:], lbs[:]), reads=[lbs], writes=[lbs])
        op("dve", lambda e: e.memset(pv_lb[:, 0, :], 0.0), writes=[pv_lb])
        for l in range(1, DEPTH):
            op("dve", lambda e: e.tensor_tensor(pv_lb[:, l, :], lbe[:, l, :], lbs[:], ALU.mult), reads=[lbe, lbs], writes=[pv_lb])
            if l > 1:
                op("dve", lambda e: e.tensor_tensor(pv_lb[:, l, :], pv_lb[:, l, :], pv_lb[:, l - 1, :], ALU.add),
                   reads=[pv_lb], writes=[pv_lb])
        op("dve", lambda e: e.tensor_scalar(pv_oml[:], pv_lb[:], -1.0, 1.0, ALU.mult, ALU.add), reads=[pv_lb], writes=[pv_oml])
        pv_dsk = K.sb("pv_dsk", [128, DEPTH, 16], F32)
        for l in range(DEPTH):
            for two in range(2):
                src = ssm_d[l:l + 1, :].rearrange("o (c two) -> o c two", two=2)[:, :, two]
                K.dma(pv_dsk[two * 64:(two + 1) * 64, l, :], src.to_broadcast([64, 16]), writes=[pv_dsk],
                      allow_slow_non_contiguous=True)
        for l in range(DEPTH):
            op("dve", lambda e: e.memset(halo_x[l][:], 0.0), writes=[halo_x[l]])
            op("dve", lambda e: e.memset(halo_f[l][:], 0.0), writes=[halo_f[l]])
            for k in range(3):
                K.dma(halo_xs[l][:, :, k], st_conv[l, k, :].rearrange("(c p) -> p c", p=128), writes=[halo_xs[l]],
                      allow_slow_non_contiguous=True)
            for k in range(2):
                K.dma(halo_fs[l][:, :, k], st_ffn[l, k, :].rearrange("(c p) -> p c", p=128), writes=[halo_fs[l]],
                      allow_slow_non_contiguous=True)

        sched = []

        def plan():
            for ti in range(ntiles):
                for l in range(nlayers):
                    yield from plan_pass(l)

        sched = list(plan())
        ws = dict(issued=0, got=0)
        cast_rr = [0]

        def ws_issue():
            i = ws["issued"]
            name, l, k0, nk, c0, ncol = sched[i]
            sg = stage[i % NS]
            wb = wring[i % NB]
            j = i % NBL
            K.dma(sg[:, 0:nk, :].rearrange("p k c -> p (k c)"), wst[l, j, :, 0:nk * 128], writes=[sg])
            op("act", lambda e: e.copy(wb[:, 0:nk, 0:ncol], sg[:, 0:nk, 0:ncol]), reads=[sg], writes=[wb])
            ws["issued"] += 1

        def wget(spec):
            i = ws["got"]
            assert sched[i] == spec, (i, sched[i], spec)
            while ws["issued"] < min(len(sched), i + NB - 2):
                ws_issue()
            ws["got"] += 1
            return wring[i % NB]

        def gemm(wspecs, srcs, tile, evac, m=128):
            wbs = [(wget(s), s[3]) for s in wspecs]
            for (c0, c1) in tile["pieces"]:
                p = nps()
                kk = 0
                nk_tot = sum(n for _, n in wbs)
                for wb, nk in wbs:
                    for k in range(nk):
                        src = srcs[kk]
                        last = (kk == nk_tot - 1)
                        op("pe", lambda e: e.matmul(p[0:m, 0:c1 - c0], lhsT=wb[:, k, 0:m], rhs=src[:, c0:c1],
                                                    start=(kk == 0), stop=last),
                           reads=[wb, src], writes=[p], inc=last)
                        kk += 1
                evac(p, c0, c1)

        def act(out, in_, func, reads, writes, **kw):
            op("act", lambda e: e.activation(out, in_, func, **kw), reads=reads, writes=writes)

        ATb = [K.sb("ATb%d" % i, [64, 64], BF16) for i in range(2)]
        vtok = [K.sb("vtok%d" % i, [64, 128], BF16) for i in range(2)]
        ktok = [K.sb("ktok%d" % i, [64, 128], BF16) for i in range(2)]
        Spb = [K.sb("Spb%d" % i, [128, 128], BF16) for i in range(2)]
        rr = K.sb("rr", [128, 3, 16], F32)
        rrd = K.sb("rrd", [128, 3, 16], F32)
        st_mean = K.sb("st_mean", [128, 512], F32)
        st_rstd = K.sb("st_rstd", [128, 512], F32)
        st_tmp = K.sb("st_tmp", [128, 512], F32)
        sqr = [K.sb("sqr%d" % i, [128, 512], BF16) for i in range(2)]
        pre_s = K.sb("pre_s", [128, 40], F32)
        acc_s = K.sb("acc_s", [128, 40], F32)
        dtT = tf[4]
        adtT = tf[5]
        NCH = 10
        dta = K.sb("dta", [64, NCH, 64], F32)
        dec_tok = K.sb("dec_tok", [64, NCH, 32], F32)
        w_tok = K.sb("w_tok", [64, NCH, 32], F32)
        rtot_bc = K.sb("rtot_bc", [128, NCH, 32], F32)
        gendS = K.sb("gendS", [128, 32], F32)
        wtmp = K.sb("wtmp", [64, 32], F32)
        gtokS = K.sb("gtokS", [64, 32], F32)
        xdt = K.sb("xdt", [64, 512], BF16)
        xw = K.sb("xw", [64, 512], BF16)
        Btok = K.sb("Btok", [64, 128], BF16)
        cbt = K.sb("cbt", [64, 64], F32)
        R1 = K.sb("R1", [64, 512], F32)
        R2 = K.sb("R2", [64, 512], F32)
        LT = R1
        scT = K.sb("scT", [64, 512], BF16)
        ytmp = R2
        ytok = K.sb("ytok", [64, 512], BF16)
        yT4 = tf[0:4]
        sttok = K.sb("sttok", [128, 128], F32)

        def seg_evac(tile, p, c0, c1, pre, hw):
            for (s0, sl, sq) in tile["segs"]:
                a, b = max(c0, s0), min(c1, s0 + sl)
                if a >= b:
                    continue
                dstb = pre if sq == 0 else pre_s
                dst = dstb[:, hw + a - s0: hw + b - s0]
                op("act", lambda e: e.copy(dst, p[:, a - c0:b - c0]), reads=[p], writes=[dstb])

        def conv_apply(tile, pre, halo, halo_sq, cidx, wfun, bap, width, dst, func):
            hw = width - 1
            for (s0, sl, sq) in tile["segs"]:
                pb = pre if sq == 0 else pre_s
                hb = halo if sq == 0 else halo_sq
                ab = tf[9] if sq == 0 else acc_s
                op("dve", lambda e: e.tensor_copy(pb[:, 0:hw], hb[:, cidx, :]), reads=[hb], writes=[pb])
                op("dve", lambda e: e.tensor_scalar(ab[:, 0:sl], pb[:, hw:hw + sl], wfun(hw), bap, ALU.mult, ALU.add),
                   reads=[pb], writes=[ab])
                for k in range(hw):
                    op("dve", lambda e: e.scalar_tensor_tensor(ab[:, 0:sl], pb[:, k:k + sl], wfun(k), ab[:, 0:sl],
                                                               ALU.mult, ALU.add), reads=[pb, ab], writes=[ab])
                act(dst[:, s0:s0 + sl], ab[:, 0:sl], func, [ab], [dst])
                op("dve", lambda e: e.tensor_copy(hb[:, cidx, :], pb[:, sl:sl + hw]), reads=[pb], writes=[hb])

        def bc_sum(srcs_fn, nsrc, c0, c1):
            p = nps()
            for i in range(nsrc):
                sbuf, sap = srcs_fn(i)
                op("pe", lambda e: e.matmul(p[:, 0:c1 - c0], lhsT=onesb[:], rhs=sap, start=(i == 0), stop=(i == nsrc - 1)),
                   reads=[onesb, sbuf], writes=[p], inc=(i == nsrc - 1))
            return p

        def rstd_from(p, w, scale, eps, dst):
            act(st_tmp[:, 0:w], p[:, 0:w], AF.Sqrt, [p], [st_tmp], bias=eps_col(eps), scale=scale)
            op("dve", lambda e: e.reciprocal(dst[:, 0:w], st_tmp[:, 0:w]), reads=[st_tmp], writes=[dst])

        eps_cols = {}

        def eps_col(v):
            if v not in eps_cols:
                b = K.sb("epsc%d" % len(eps_cols), [128, 1], F32)
                op("dve", lambda e: e.memset(b[:], float(v)), writes=[b])
                eps_cols[v] = b
            return eps_cols[v][:, 0:1]

        eps_col(RMS_EPS)
        eps_col(LN_EPS)

        def gemm_g(wspecs, srcs, tile, evac, m=128):
            wbs = [(wget(s), s[3]) for s in wspecs]
            for (c0, c1) in tile["pieces"]:
                p = nps()
                kk = 0
                nk_tot = sum(n for _, n in wbs)
                for wb, nk in wbs:
                    for k in range(nk):
                        src = srcs[kk]
                        last = (kk == nk_tot - 1)
                        op("pe", lambda e: e.matmul(p[0:m, 0:c1 - c0], lhsT=wb[:, k, 0:m], rhs=src[:, c0:c1],
                                                    start=(kk == 0), stop=last),
                           reads=[wb, src], writes=[p], inc=last)
                        kk += 1
                evac(p, c0, c1)
                yield

        HS = []
        for s_ in range(2):
            HS.append(dict(
                qs=tf[4 * s_ + 0], fk=tf[4 * s_ + 1], ln=tf[4 * s_ + 2], G=tf[4 * s_ + 3],
                vT=tb[4 * s_ + 0], gs=tb[4 * s_ + 1], qt=tb[4 * s_ + 2], kt=tb[4 * s_ + 3],
                S=K.sb("ShgS%d" % s_, [128, 128], F32), rr=K.sb("rrS%d" % s_, [128, 3, 16], F32),
                rrd=K.sb("rrdS%d" % s_, [128, 3, 16], F32),
                ATb=[K.sb("ATbS%d_%d" % (s_, i), [64, 64], BF16) for i in range(2)],
                vtok=[K.sb("vtokS%d_%d" % (s_, i), [64, 128], BF16) for i in range(2)],
                ktok=[K.sb("ktokS%d_%d" % (s_, i), [64, 128], BF16) for i in range(2)],
                Spb=[K.sb("SpbS%d_%d" % (s_, i), [128, 128], BF16) for i in range(2)],
                rstd=K.sb("rstdS%d" % s_, [128, 512], F32)))

        def hg_prep(tile, l, h, B):
            T = tile["w"]
            qs, fk, lnf, GnS = B["qs"], B["fk"], B["ln"], B["G"]
            Dd, E2 = B["ln"], B["G"]
            vT, gs, qt, kt = B["vT"], B["gs"], B["qt"], B["kt"]
            rr_, rrd_ = B["rr"], B["rrd"]
            specs = [("in", l, 0, NKC, o + h * 128, 128) for o in (O_Q, O_F, O_I, O_G)]
            yield from gemm_g([specs[0]], xT, tile, lambda p, c0, c1: act(qs[:, c0:c1], p[:, 0:c1 - c0], AF.Silu, [p], [qs]))
            yield from gemm_g([specs[1]], xT, tile, lambda p, c0, c1: act(fk[:, c0:c1], p[:, 0:c1 - c0], AF.Sigmoid, [p], [fk]))
            yield from gemm_g([specs[2]], xT, tile, lambda p, c0, c1: op("act", lambda e: e.copy(vT[:, c0:c1], p[:, 0:c1 - c0]),
                                                                         reads=[p], writes=[vT]))
            yield from gemm_g([specs[3]], xT, tile, lambda p, c0, c1: act(gs[:, c0:c1], p[:, 0:c1 - c0], AF.Silu, [p], [gs]))
            op("dve", lambda e: e.tensor_scalar(fk[:, 0:T], fk[:, 0:T], pv_oml[:, l, h:h + 1], pv_lb[:, l, h:h + 1],
                                                ALU.mult, ALU.add), reads=[fk, pv_oml, pv_lb], writes=[fk])
            act(lnf[:, 0:T], fk[:, 0:T], AF.Ln, [fk], [lnf])
            op("dve", lambda e: e.tensor_scalar(fk[:, 0:T], fk[:, 0:T], -1.0, 1.0, ALU.mult, ALU.add), reads=[fk], writes=[fk])
            op("dve", lambda e: e.memset(GnS[:, 0:1], 0.0), writes=[GnS])
            op("dve", lambda e: e.tensor_tensor_scan(GnS[:, 1:T + 1], one_col[:, 0:1].to_broadcast([128, T]), lnf[:, 0:T],
                                                     0.0, ALU.mult, ALU.add), reads=[one_col, lnf], writes=[GnS])
            yield
            runs = []
            for ci, (a, L, sq) in enumerate(tile["chunks"]):
                if runs and runs[-1][2] == L and runs[-1][0] + runs[-1][1] * L == a:
                    runs[-1][1] += 1
                else:
                    runs.append([a, 1, L, ci])
            for (a, n, L, ci0) in runs:
                hl = L // 2
                gv = lambda off: GnS[:, a + off:a + off + n * L].rearrange("p (j l) -> p j l", l=L)
                ref = gv(hl)[:, :, 0:1]
                sta = gv(0)[:, :, 0:1]
                end = gv(L)[:, :, 0:1]
                op("dve", lambda e: e.tensor_tensor(Dd[:, a:a + n * L].rearrange("p (j l) -> p j l", l=L), gv(1),
                                                    ref.to_broadcast([128, n, L]), ALU.subtract), reads=[GnS], writes=[Dd])
                r3 = lambda k: rrd_[:, k, ci0:ci0 + n].unsqueeze(2)
                op("dve", lambda e: e.tensor_tensor(r3(0), ref, sta, ALU.subtract), reads=[GnS], writes=[rrd_])
                op("dve", lambda e: e.tensor_tensor(r3(1), end, ref, ALU.subtract), reads=[GnS], writes=[rrd_])
                op("dve", lambda e: e.tensor_tensor(r3(2), end, sta, ALU.subtract), reads=[GnS], writes=[rrd_])
            nchk = len(tile["chunks"])
            act(rr_[:, :, 0:nchk], rrd_[:, :, 0:nchk], AF.Exp, [rrd_], [rr_])
            act(E2[:, 0:T], Dd[:, 0:T], AF.Exp, [Dd], [E2], scale=-1.0)
            act(Dd[:, 0:T], Dd[:, 0:T], AF.Exp, [Dd], [Dd])
            yield
            op("dve", lambda e: e.tensor_tensor(qt[:, 0:T], qs[:, 0:T], Dd[:, 0:T], ALU.mult), reads=[qs, Dd], writes=[qt])
            op("dve", lambda e: e.tensor_tensor(kt[:, 0:T], fk[:, 0:T], E2[:, 0:T], ALU.mult), reads=[fk, E2], writes=[kt])
            yield

        def hg_loop(tile, l, h, B, first_tile, last_tile):
            T = tile["w"]
            oT = B["qs"]
            vT, gs, qt, kt = B["vT"], B["gs"], B["qt"], B["kt"]
            rr_ = B["rr"]
            S = B["S"]
            cur_seq = None
            for ci, (a, L, sq) in enumerate(tile["chunks"]):
                if sq != cur_seq:
                    if cur_seq is not None:
                        store_hg(l, h, cur_seq, last_tile, S)
                    if sq == 0:
                        if first_tile:
                            op("dve", lambda e: e.memset(S[:], 0.0), writes=[S])
                        else:
                            K.dma(S[:], hg_c[l, h, :, :], reads=[hg_c], writes=[S])
                    else:
                        K.dma(S[:], st_hg[l, h, :, :], writes=[S])
                    cur_seq = sq
                i2 = ci % 2
                ATb_, vtok_, ktok_, Spb_ = B["ATb"][i2], B["vtok"][i2], B["ktok"][i2], B["Spb"][i2]
                p1 = nps()
                op("pe", lambda e: e.matmul(p1[0:L, 0:L], lhsT=kt[:, a:a + L], rhs=qt[:, a:a + L], start=True, stop=True),
                   reads=[kt, qt], writes=[p1])
                p2 = nps()
                p2b = p2[:, 0:256].bitcast(BF16)
                op("pe", lambda e: e.transpose(p2b[0:L, 0:128], vT[:, a:a + L], identb[:]), reads=[vT, identb], writes=[p2])
                p3 = nps()
                p3b = p3[:, 0:256].bitcast(BF16)
                op("pe", lambda e: e.transpose(p3b[0:L, 0:128], kt[:, a:a + L], identb[:]), reads=[kt, identb], writes=[p3])
                op("dve", lambda e: e.tensor_tensor(ATb_[0:L, 0:L], p1[0:L, 0:L], trif[0:L, 0:L], ALU.mult),
                   reads=[p1, trif], writes=[ATb_])
                op("act", lambda e: e.copy(vtok_[0:L, :], p2b[0:L, 0:128]), reads=[p2], writes=[vtok_])
                op("act", lambda e: e.copy(ktok_[0:L, :], p3b[0:L, 0:128]), reads=[p3], writes=[ktok_])
                op("dve", lambda e: e.tensor_scalar(Spb_[:], S[:], rr_[:, 0, ci:ci + 1], None, ALU.mult),
                   reads=[S, rr_], writes=[Spb_])
                yield
                p4 = nps()
                op("pe", lambda e: e.matmul(p4[:, 0:L], lhsT=vtok_[0:L, :], rhs=ATb_[0:L, 0:L], start=True, stop=False),
                   reads=[vtok_, ATb_], writes=[p4], inc=False)
                op("pe", lambda e: e.matmul(p4[:, 0:L], lhsT=Spb_[:], rhs=qt[:, a:a + L], start=False, stop=True),
                   reads=[Spb_, qt], writes=[p4])
                p5 = nps()
                op("pe", lambda e: e.matmul(p5[:, 0:128], lhsT=ktok_[0:L, :], rhs=vtok_[0:L, :], start=True, stop=True),
                   reads=[ktok_, vtok_], writes=[p5])
                op("act", lambda e: e.copy(oT[:, a:a + L], p4[:, 0:L]), reads=[p4], writes=[oT])
                op("dve", lambda e: e.tensor_scalar(S[:], S[:], rr_[:, 2, ci:ci + 1], None, ALU.mult), reads=[S, rr_], writes=[S])
                op("dve", lambda e: e.scalar_tensor_tensor(S[:], p5[:, 0:128], rr_[:, 1, ci:ci + 1], S[:], ALU.mult, ALU.add),
                   reads=[p5, rr_, S], writes=[S])
                yield
            store_hg(l, h, cur_seq, last_tile, S)
            sq_b = B["kt"]
            rstd_ = B["rstd"]
            act(sq_b[:, 0:T], oT[:, 0:T], AF.Square, [oT], [sq_b])
            yield
            for (c0, c1) in tile["pieces"]:
                w = c1 - c0
                p = bc_sum(lambda i: (sq_b, sq_b[:, c0:c1]), 1, c0, c1)
                rstd_from(p, w, 1.0 / 128, RMS_EPS, rstd_)
                yield
                op("dve", lambda e: e.scalar_tensor_tensor(oT[:, c0:c1], oT[:, c0:c1], pv_hgnw[:, l, h:h + 1], rstd_[:, 0:w],
                                                           ALU.mult, ALU.mult), reads=[oT, pv_hgnw, rstd_], writes=[oT])
                op("dve", lambda e: e.tensor_tensor(on[h][:, c0:c1], oT[:, c0:c1], gs[:, c0:c1], ALU.mult),
                   reads=[oT, gs], writes=[on[h]])
                yield

        def hgrn_all(tile, l, first_tile, last_tile):
            for _ in hg_prep(tile, l, 0, HS[0]):
                pass
            for h in range(HG_H):
                A = hg_loop(tile, l, h, HS[h % 2], first_tile, last_tile)
                Bg = hg_prep(tile, l, h + 1, HS[(h + 1) % 2]) if h + 1 < HG_H else None
                a_alive, b_alive = True, Bg is not None
                while a_alive or b_alive:
                    if a_alive:
                        try:
                            next(A)
                        except StopIteration:
                            a_alive = False
                    if b_alive:
                        try:
                            next(Bg)
                        except StopIteration:
                            b_alive = False

        def store_hg(l, h, sq, last_tile, S):
            if sq == 0:
                if last_tile:
                    K.dma(hg_out[l, 0, h, :, :], S[:], reads=[S], writes=[dram_hg])
                else:
                    K.dma(hg_c[l, h, :, :], S[:], reads=[S], writes=[hg_c])
            else:
                K.dma(hg_out[l, 1, h, :, :], S[:], reads=[S], writes=[dram_hg])

        def gated_proj(tile, l, srcs, gate_off, wname, accumulate):
            for c in range(16):
                sg = tf[0]
                gemm([("in", l, 0, NKC, gate_off + c * 128, 128)], xT, tile,
                     lambda p, c0, c1: act(sg[:, c0:c1], p[:, 0:c1 - c0], AF.Sigmoid, [p], [sg]))

                def ev(p, c0, c1):
                    if not accumulate:
                        op("dve", lambda e: e.tensor_tensor(mg[c][:, c0:c1], p[:, 0:c1 - c0], sg[:, c0:c1], ALU.mult),
                           reads=[p, sg], writes=[mg[c]])
                    else:
                        op("dve", lambda e: e.tensor_tensor(sg[:, c0:c1], p[:, 0:c1 - c0], sg[:, c0:c1], ALU.mult),
                           reads=[p, sg], writes=[sg])
                        op("dve", lambda e: e.tensor_tensor(mg[c][:, c0:c1], mg[c][:, c0:c1], sg[:, c0:c1], ALU.add),
                           reads=[mg[c], sg], writes=[mg[c]])
                gemm([(wname, l, 0, NKC, c * 128, 128)], srcs, tile, ev)

        def ss_load(l, g, sq, first_tile):
            H = Hss1
            if sq == 0:
                if first_tile:
                    op("dve", lambda e: e.memset(H[:], 0.0), writes=[H])
                else:
                    K.dma(H[:], ss_c[l, g, :, :], reads=[ss_c], writes=[H])
            else:
                for pr in range(4):
                    hh = (g * 4 + pr) * 2
                    K.dma(sttok[:], st_ssm[l, hh:hh + 2, :, :].rearrange("h p n -> (h p) n"), writes=[sttok])
                    p = nps()
                    op("pe", lambda e: e.transpose(p[:, 0:128], sttok[:], identf[:]), reads=[sttok, identf], writes=[p])
                    op("dve", lambda e: e.tensor_copy(H[:, pr * 128:(pr + 1) * 128], p[:, 0:128]), reads=[p], writes=[H])
            op("act", lambda e: e.copy(Hsb1[:], H[:]), reads=[H], writes=[Hsb1])

        def ss_store(l, g, sq, last_tile):
            H = Hss1
            if sq == 0 and not last_tile:
                K.dma(ss_c[l, g, :, :], H[:], reads=[H], writes=[ss_c])
                return
            for pr in range(4):
                hh = (g * 4 + pr) * 2
                p = nps()
                op("pe", lambda e: e.transpose(p[:, 0:128], H[:, pr * 128:(pr + 1) * 128], identf[:]),
                   reads=[H, identf], writes=[p])
                op("dve", lambda e: e.tensor_copy(sttok[:], p[:, 0:128]), reads=[p], writes=[sttok])
                K.dma(ssm_out[l, sq, hh:hh + 2, :, :].rearrange("h p n -> (h p) n"), sttok[:], reads=[sttok], writes=[dram_ss])

        def ssd(tile, l, first_tile, last_tile):
            T = tile["w"]
            chunks = tile["chunks"]
            wdt = ("in", l, 0, NKC, O_DT, 32)

            def ev_dt(p, c0, c1):
                act(dtT[0:32, c0:c1], p[0:32, 0:c1 - c0], AF.Exp, [p], [dtT], bias=hv["dtb"][:, l:l + 1])
            gemm([wdt], xT, tile, ev_dt, m=32)
            act(dtT[0:32, 0:T], dtT[0:32, 0:T], AF.Ln, [dtT], [dtT], bias=one_col[0:32, 0:1])
            op("dve", lambda e: e.tensor_scalar(adtT[0:32, 0:T], dtT[0:32, 0:T], hvA[:, l:l + 1], None, ALU.mult),
               reads=[dtT, hvA], writes=[adtT])
            if stop_at == "ssd_dt0":
                raise _Stop()
            for ci, (a, L, sq) in enumerate(chunks):
                p = nps()
                op("pe", lambda e: e.matmul(p[0:L, 0:32], lhsT=dtT[0:32, a:a + L], rhs=identf[0:32, 0:32], start=True, stop=True),
                   reads=[dtT, identf], writes=[p], inc=False)
                op("pe", lambda e: e.matmul(p[0:L, 32:64], lhsT=adtT[0:32, a:a + L], rhs=identf[0:32, 0:32], start=True, stop=True),
                   reads=[adtT, identf], writes=[p])
                op("dve", lambda e: e.tensor_copy(dta[0:L, ci, :], p[0:L, 0:64]), reads=[p], writes=[dta])
                p2 = nps()
                op("pe", lambda e: e.matmul(p2[0:L, 0:32], lhsT=trif[0:L, 0:L], rhs=dta[0:L, ci, 32:64], start=True, stop=True),
                   reads=[trif, dta], writes=[p2])
                p2x = nps()
                op("pe", lambda e: e.matmul(p2x[:, 0:32], lhsT=onesf[0:L, :], rhs=dta[0:L, ci, 32:64], start=True, stop=True),
                   reads=[onesf, dta], writes=[p2x])
                op("dve", lambda e: e.tensor_copy(gtokS[0:L, :], p2[0:L, 0:32]), reads=[p2], writes=[gtokS])
                op("dve", lambda e: e.tensor_copy(gendS[:], p2x[:, 0:32]), reads=[p2x], writes=[gendS])
                act(dec_tok[0:L, ci, :], gtokS[0:L, :], AF.Exp, [gtokS], [dec_tok], scale=-1.0)
                act(rtot_bc[:, ci, :], gendS[:], AF.Exp, [gendS], [rtot_bc], scale=-1.0)
                op("dve", lambda e: e.tensor_tensor(wtmp[0:L, :], gtokS[0:L, :], gendS[0:L, :], ALU.subtract),
                   reads=[gtokS, gendS], writes=[wtmp])
                act(wtmp[0:L, :], wtmp[0:L, :], AF.Exp, [wtmp], [wtmp])
                op("dve", lambda e: e.tensor_tensor(w_tok[0:L, ci, :], wtmp[0:L, :], dta[0:L, ci, 0:32], ALU.mult),
                   reads=[wtmp, dta], writes=[w_tok])


        SSB = [dict(BT=tb[0], CT=tb[1], xc=tb[2:6], sz=tb[6:10]),
               dict(BT=hm[32], CT=hm[33], xc=hm[34:38], sz=hm[38:42])]

        def ssd_prep(tile, l, g, SB):
            BT, CT, xc, sz = SB["BT"], SB["CT"], SB["xc"], SB["sz"]
            pre = tf[8]
            for which, dstb in ((0, BT), (1, CT)):
                cidx = 16 + which * 4 + g
                yield from gemm_g([("in", l, 0, NKC, O_XBC + 2048 + which * 512 + g * 128, 128)], xT, tile,
                                  lambda p, c0, c1: seg_evac(tile, p, c0, c1, pre, 3))
                conv_apply(tile, pre, halo_x[l], halo_xs[l], cidx, lambda k: pv_cvw[:, l, k, cidx:cidx + 1],
                           pv_cvb[:, l, cidx:cidx + 1], 4, dstb, AF.Silu)
                yield
            for pr in range(4):
                cc = g * 4 + pr
                yield from gemm_g([("in", l, 0, NKC, O_XBC + cc * 128, 128)], xT, tile,
                                  lambda p, c0, c1: seg_evac(tile, p, c0, c1, pre, 3))
                conv_apply(tile, pre, halo_x[l], halo_xs[l], cc, lambda k: pv_cvw[:, l, k, cc:cc + 1],
                           pv_cvb[:, l, cc:cc + 1], 4, xc[pr], AF.Silu)
                yield
                yield from gemm_g([("in", l, 0, NKC, O_Z + cc * 128, 128)], xT, tile,
                                  lambda p, c0, c1: act(sz[pr][:, c0:c1], p[:, 0:c1 - c0], AF.Silu, [p], [sz[pr]]))

        def ssd_chunks(tile, l, g, SB, first_tile, last_tile):
            BT, CT, xc, sz = SB["BT"], SB["CT"], SB["xc"], SB["sz"]
            chunks = tile["chunks"]
            cur_seq = None
            H, Hb = Hss1, Hsb1
            for ci, (a, L, sq) in enumerate(chunks):
                if sq != cur_seq:
                    if cur_seq is not None:
                        ss_store(l, g, cur_seq, last_tile)
                    ss_load(l, g, sq, first_tile)
                    cur_seq = sq
                L8 = 8 * L
                gs8 = slice(g * 8, (g + 1) * 8)
                px = nps()
                pxb = px[:, 0:256].bitcast(BF16)
                for pr in range(4):
                    op("pe", lambda e: e.transpose(pxb[0:L, pr * 128:(pr + 1) * 128], xc[pr][:, a:a + L], identb[:]),
                       reads=[xc[pr], identb], writes=[px], inc=(pr == 3))
                op("dve", lambda e: e.tensor_tensor(xdt[0:L, :].rearrange("s (h p) -> s h p", p=64),
                                                    pxb[0:L, 0:512].rearrange("s (h p) -> s h p", p=64),
                                                    dta[0:L, ci, gs8].unsqueeze(2).to_broadcast([L, 8, 64]), ALU.mult),
                   reads=[px, dta], writes=[xdt])
                op("dve", lambda e: e.tensor_tensor(xw[0:L, :].rearrange("s (h p) -> s h p", p=64),
                                                    pxb[0:L, 0:512].rearrange("s (h p) -> s h p", p=64),
                                                    w_tok[0:L, ci, gs8].unsqueeze(2).to_broadcast([L, 8, 64]), ALU.mult),
                   reads=[px, w_tok], writes=[xw])
                pB = nps()
                pBb = pB[:, 0:256].bitcast(BF16)
                op("pe", lambda e: e.transpose(pBb[0:L, 0:128], BT[:, a:a + L], identb[:]), reads=[BT, identb], writes=[pB])
                op("act", lambda e: e.copy(Btok[0:L, :], pBb[0:L, 0:128]), reads=[pB], writes=[Btok])
                pC = nps()
                op("pe", lambda e: e.matmul(pC[0:L, 0:L], lhsT=BT[:, a:a + L], rhs=CT[:, a:a + L], start=True, stop=True),
                   reads=[BT, CT], writes=[pC])
                op("act", lambda e: e.copy(cbt[0:L, 0:L], pC[0:L, 0:L]), reads=[pC], writes=[cbt])
                adt8 = dta[0:L, ci, 32 + g * 8:32 + (g + 1) * 8].unsqueeze(2).to_broadcast([L, 8, L])
                op("dve", lambda e: e.tensor_tensor(R1[0:L, 0:L8].rearrange("s (h t) -> s h t", t=L), adt8,
                                                    trif[0:L, 0:L].unsqueeze(1).to_broadcast([L, 8, L]), ALU.mult),
                   reads=[dta, trif], writes=[R1])
                op("dve", lambda e: e.tensor_copy(R2[0:L, 0:L8].rearrange("s (h t) -> s h t", t=L), adt8),
                   reads=[dta], writes=[R2])
                yield
                pL = nps()
                bigm = cb["bigm%d" % L]
                op("pe", lambda e: e.matmul(pL[0:L, 0:L8], lhsT=onesf[0:L, 0:L], rhs=R1[0:L, 0:L8], start=True, stop=False),
                   reads=[onesf, R1], writes=[pL], inc=False)
                op("pe", lambda e: e.matmul(pL[0:L, 0:L8], lhsT=negtri[0:L, 0:L], rhs=R2[0:L, 0:L8], start=False, stop=False),
                   reads=[negtri, R2], writes=[pL], inc=False)
                op("pe", lambda e: e.matmul(pL[0:L, 0:L8], lhsT=identf[0:L, 0:L], rhs=bigm[0:L, 0:L8], start=False, stop=True),
                   reads=[identf, bigm], writes=[pL])
                op("dve", lambda e: e.tensor_copy(LT[0:L, 0:L8], pL[0:L, 0:L8]), reads=[pL], writes=[LT])
                act(LT[0:L, 0:L8], LT[0:L, 0:L8], AF.Exp, [LT], [LT], scale=-1.0)
                op("dve", lambda e: e.tensor_tensor(scT[0:L, 0:L8].rearrange("s (h t) -> s h t", t=L),
                                                    LT[0:L, 0:L8].rearrange("s (h t) -> s h t", t=L),
                                                    cbt[0:L, 0:L].unsqueeze(1).to_broadcast([L, 8, L]), ALU.mult),
                   reads=[LT, cbt], writes=[scT])
                yield
                pY1 = nps()
                for hh in range(8):
                    op("pe", lambda e: e.matmul(pY1[0:L, hh * 64:(hh + 1) * 64], lhsT=scT[0:L, hh * L:(hh + 1) * L],
                                                rhs=xdt[0:L, hh * 64:(hh + 1) * 64], start=True, stop=True),
                       reads=[scT, xdt], writes=[pY1], inc=(hh == 7))
                pY2 = nps()
                op("pe", lambda e: e.matmul(pY2[0:L, 0:512], lhsT=CT[:, a:a + L], rhs=Hb[:, 0:512], start=True, stop=True),
                   reads=[CT, Hb], writes=[pY2])
                op("dve", lambda e: e.tensor_tensor(ytmp[0:L, :].rearrange("s (h p) -> s h p", p=64),
                                                    pY2[0:L, 0:512].rearrange("s (h p) -> s h p", p=64),
                                                    dec_tok[0:L, ci, gs8].unsqueeze(2).to_broadcast([L, 8, 64]), ALU.mult),
                   reads=[pY2, dec_tok], writes=[ytmp])
                op("dve", lambda e: e.tensor_tensor(ytok[0:L, :], pY1[0:L, 0:512], ytmp[0:L, :], ALU.add),
                   reads=[pY1, ytmp], writes=[ytok])
                yield
                pT = nps()
                pTb = pT[:, 0:256].bitcast(BF16)
                for pr in range(4):
                    op("pe", lambda e: e.transpose(pTb[:, pr * 64:pr * 64 + L], ytok[0:L, pr * 128:(pr + 1) * 128],
                                                   identb[0:L, 0:L]), reads=[ytok, identb], writes=[pT], inc=(pr == 3))
                for pr in range(4):
                    op("act", lambda e: e.copy(yT4[pr][:, a:a + L], pTb[:, pr * 64:pr * 64 + L]), reads=[pT], writes=[yT4[pr]])
                pH = nps()
                op("pe", lambda e: e.matmul(pH[:, 0:512], lhsT=Btok[0:L, :], rhs=xw[0:L, :], start=True, stop=True),
                   reads=[Btok, xw], writes=[pH])
                op("dve", lambda e: e.tensor_tensor(H[:].rearrange("n (h p) -> n h p", p=64),
                                                    H[:].rearrange("n (h p) -> n h p", p=64),
                                                    rtot_bc[:, ci, gs8].unsqueeze(2).to_broadcast([128, 8, 64]), ALU.mult),
                   reads=[H, rtot_bc], writes=[H])
                op("dve", lambda e: e.tensor_tensor(H[:], H[:], pH[:, 0:512], ALU.add), reads=[H, pH], writes=[H])
                op("act", lambda e: e.copy(Hb[:], H[:]), reads=[H], writes=[Hb])
                yield
            ss_store(l, g, cur_seq, last_tile)

        def ssd_post(tile, l, g, SB):
            T = tile["w"]
            xc, sz = SB["xc"], SB["sz"]
            for pr in range(4):
                cc = g * 4 + pr
                yv = yT4[pr][:, 0:T]
                op("dve", lambda e: e.scalar_tensor_tensor(yv, xc[pr][:, 0:T], pv_dsk[:, l, cc:cc + 1], yv, ALU.mult, ALU.add),
                   reads=[xc[pr], pv_dsk, yT4[pr]], writes=[yT4[pr]])
                op("dve", lambda e: e.tensor_tensor(yv, yv, sz[pr][:, 0:T], ALU.mult), reads=[yT4[pr], sz[pr]], writes=[yT4[pr]])
                act(xc[pr][:, 0:T], yv, AF.Square, [yT4[pr]], [xc[pr]])
            for (c0, c1) in tile["pieces"]:
                w = c1 - c0
                p = bc_sum(lambda i: (xc[i], xc[i][:, c0:c1]), 4, c0, c1)
                rstd_from(p, w, 1.0 / 512, RMS_EPS, st_rstd)
                for pr in range(4):
                    cc = g * 4 + pr
                    op("dve", lambda e: e.scalar_tensor_tensor(on[cc][:, c0:c1], yT4[pr][:, c0:c1], pv_snw[:, l, cc:cc + 1],
                                                               st_rstd[:, 0:w], ALU.mult, ALU.mult),
                       reads=[yT4[pr], pv_snw, st_rstd], writes=[on[cc]])

        def gated_proj_g(tile, l, srcs, gate_off, wname, sg):
            for c in range(16):
                yield from gemm_g([("in", l, 0, NKC, gate_off + c * 128, 128)], xT, tile,
                                  lambda p, c0, c1: act(sg[:, c0:c1], p[:, 0:c1 - c0], AF.Sigmoid, [p], [sg]))
                yield from gemm_g([(wname, l, 0, NKC, c * 128, 128)], srcs, tile,
                                  lambda p, c0, c1: op("dve", lambda e: e.tensor_tensor(mg[c][:, c0:c1], p[:, 0:c1 - c0], sg[:, c0:c1],
                                                                                       ALU.mult), reads=[p, sg], writes=[mg[c]]))

        def chain_g(*gens):
            for g_ in gens:
                if g_ is not None:
                    yield from g_

        def interleave(A, Bg):
            a_alive, b_alive = A is not None, Bg is not None
            while a_alive or b_alive:
                if a_alive:
                    try:
                        next(A)
                    except StopIteration:
                        a_alive = False
                if b_alive:
                    try:
                        next(Bg)
                    except StopIteration:
                        b_alive = False

        def ssd_all(tile, l, first_tile, last_tile):
            interleave(ssd_prep(tile, l, 0, SSB[0]), None)
            for g in range(SSM_G):
                A = ssd_chunks(tile, l, g, SSB[g % 2], first_tile, last_tile)
                nxt = ssd_prep(tile, l, g + 1, SSB[(g + 1) % 2]) if g + 1 < SSM_G else None
                pa = gated_proj_g(tile, l, on, O_GA, "pa", tf[6]) if g == 0 else None
                interleave(A, chain_g(nxt, pa))
                if g == 0:
                    dump("d_mga", mg, tile, l)
                ssd_post(tile, l, g, SSB[g % 2])

        def layer_norm(tile, gvec, bvec, l, emit_out):
            T = tile["w"]
            for (c0, c1) in tile["pieces"]:
                w = c1 - c0
                p1 = bc_sum(lambda i: (xT[i], xT[i][:, c0:c1]), 16, c0, c1)
                p2 = nps()
                for c in range(16):
                    sb_ = sqr[c % 2]
                    act(sb_[:, 0:w], xT[c][:, c0:c1], AF.Square, [xT[c]], [sb_])
                    op("pe", lambda e: e.matmul(p2[:, 0:w], lhsT=onesb[:], rhs=sb_[:, 0:w], start=(c == 0), stop=(c == 15)),
                       reads=[onesb, sb_], writes=[p2], inc=True)
                op("dve", lambda e: e.tensor_scalar(st_mean[:, 0:w], p1[:, 0:w], 1.0 / D, None, ALU.mult), reads=[p1], writes=[st_mean])
                op("dve", lambda e: e.tensor_tensor(st_tmp[:, 0:w], st_mean[:, 0:w], st_mean[:, 0:w], ALU.mult),
                   reads=[st_mean], writes=[st_tmp])
                op("dve", lambda e: e.scalar_tensor_tensor(st_tmp[:, 0:w], p2[:, 0:w], 1.0 / D, st_tmp[:, 0:w], ALU.mult, ALU.subtract),
                   reads=[p2, st_tmp], writes=[st_tmp])
                act(st_tmp[:, 0:w], st_tmp[:, 0:w], AF.Sqrt, [st_tmp], [st_tmp], bias=eps_col(LN_EPS))
                op("dve", lambda e: e.reciprocal(st_rstd[:, 0:w], st_tmp[:, 0:w]), reads=[st_tmp], writes=[st_rstd])
                for c in range(16):
                    t32 = tf[0] if c % 2 == 0 else tf[1]
                    op("dve", lambda e: e.tensor_tensor(t32[:, 0:w], xT[c][:, c0:c1], st_mean[:, 0:w], ALU.subtract),
                       reads=[xT[c], st_mean], writes=[t32])
                    op("dve", lambda e: e.tensor_tensor(t32[:, 0:w], t32[:, 0:w], st_rstd[:, 0:w], ALU.mult),
                       reads=[t32, st_rstd], writes=[t32])
                    if not emit_out:
                        act(xT[c][:, c0:c1], t32[:, 0:w], AF.Identity, [t32, gvec, bvec], [xT[c]],
                            scale=gvec[:, l, c:c + 1], bias=bvec[:, l, c:c + 1])
                    else:
                        yb = tf[2 + (c % 4)]
                        act(yb[:, 0:w], t32[:, 0:w], AF.Identity, [t32, gvec, bvec], [yb],
                            scale=gvec[:, l, c:c + 1], bias=bvec[:, l, c:c + 1])
                        nb = (w + 127) // 128
                        for bi in range(nb):
                            nr = min(128, w - bi * 128)
                            if c % 4 == 0 and bi == 0:
                                pass
                            pt = nps()
                            op("pe", lambda e: e.transpose(pt[0:nr, 0:128], yb[:, bi * 128:bi * 128 + nr], identf[:]),
                               reads=[yb, identf], writes=[pt])
                            ob = ost[ost_i[0] % 4]
                            ost_i[0] += 1
                            op("dve", lambda e: e.tensor_copy(ob[0:nr, :], pt[0:nr, 0:128]), reads=[pt], writes=[ob])
                            r0 = tile["r0"] + c0 + bi * 128
                            K.dma(y_out[r0:r0 + nr, c * 128:(c + 1) * 128], ob[0:nr, :], reads=[ob], writes=[dram_y])

        ost = [K.sb("ost%d" % i, [128, 128], F32) for i in range(4)]
        ost_i = [0]
        xld = [K.sb("xld%d" % i, [128, 512], F32) for i in range(2)]
        xld_i = [0]

        def dump(nm, bufs, tile, l):
            if not debug or l != 0 or tile["idx"] != 0:
                return
            for i, b in enumerate(bufs):
                K.dma(dbg[nm][i, :, 0:tile["w"]], b[:, 0:tile["w"]], reads=[b])

        def mixer(tile, l, first_tile, last_tile):
            if stop_at in ("load", "load_dma", "load_1"):
                raise _Stop()
            hgrn_all(tile, l, first_tile, last_tile)
            dump("d_on", on, tile, l)
            ssd(tile, l, first_tile, last_tile)
            ssd_all(tile, l, first_tile, last_tile)
            dump("d_yn", on, tile, l)
            gated_proj(tile, l, on, O_GB, "pb", True)
            dump("d_mg", mg, tile, l)
            for c in range(16):
                gemm([("out", l, 0, NKC, c * 128, 128)], mg, tile,
                     lambda p, c0, c1: op("dve", lambda e: e.scalar_tensor_tensor(xT[c][:, c0:c1], xT[c][:, c0:c1], ALPHA,
                                                                                  p[:, 0:c1 - c0], ALU.mult, ALU.add),
                                          reads=[xT[c], p], writes=[xT[c]]))
            layer_norm(tile, pv_l1g, pv_l1b, l, False)
            dump("d_xmid", xT, tile, l)
            if stop_at == "mixer":
                raise _Stop()

        def ffn(tile, l, last_layer):
            T = tile["w"]
            pre = tf[8]
            ga = tf[7]
            for j in range(NFF):
                gemm([("up", l, 0, NKC, j * 128, 128)], xT, tile, lambda p, c0, c1: seg_evac(tile, p, c0, c1, pre, 2))
                conv_apply(tile, pre, halo_f[l], halo_fs[l], j, lambda k: pv_fcw[:, l, k, j:j + 1], pv_fcb[:, l, j:j + 1], 3,
                           ga, AF.Gelu)
                gemm([("up", l, 0, NKC, DFF + j * 128, 128)], xT, tile,
                     lambda p, c0, c1: op("dve", lambda e: e.tensor_tensor(hm[j][:, c0:c1], p[:, 0:c1 - c0], ga[:, c0:c1], ALU.mult),
                                          reads=[p, ga], writes=[hm[j]]))
            for c in range(16):
                gemm([("dn", l, 0, 16, c * 128, 128), ("dn", l, 16, 16, c * 128, 128), ("dn", l, 32, 12, c * 128, 128)], hm, tile,
                     lambda p, c0, c1: op("dve", lambda e: e.scalar_tensor_tensor(xT[c][:, c0:c1], xT[c][:, c0:c1], ALPHA,
                                                                                  p[:, 0:c1 - c0], ALU.mult, ALU.add),
                                          reads=[xT[c], p], writes=[xT[c]]))
            dump("d_hm", hm, tile, l)
            dump("d_v2", xT, tile, l)
            layer_norm(tile, pv_l2g, pv_l2b, l, last_layer)

        if stop_at == "setup":
            tiles = []
        for tile in tiles:
            T = tile["w"]
            ti = tile["idx"]
            first_tile = (ti == 0)
            last_tile = (ti == NTILES - 1)
            nblk = (T + 127) // 128
            for bi in range(nblk):
                r0 = bi * 128
                nr = min(128, T - r0)
                for q4 in range(4):
                    xtok = xld[xld_i[0] % 2]
                    xld_i[0] += 1
                    K.dma(xtok[0:nr, :], xin[tile["r0"] + r0: tile["r0"] + r0 + nr, q4 * 512:(q4 + 1) * 512], writes=[xtok])
                    p = nps()
                    for j in range(4):
                        op("pe", lambda e: e.transpose(p[:, j * 128: j * 128 + nr], xtok[0:nr, j * 128:(j + 1) * 128],
                                                       identf[0:nr, 0:nr]), reads=[xtok, identf], writes=[p], inc=(j == 3))
                    for j in range(4):
                        cidx = q4 * 4 + j
                        if j % 2:
                            op("dve", lambda e: e.tensor_copy(xT[cidx][:, r0:r0 + nr], p[:, j * 128:j * 128 + nr]), reads=[p], writes=[xT[cidx]])
                        else:
                            op("dve", lambda e: e.tensor_copy(xT[cidx][:, r0:r0 + nr], p[:, j * 128:j * 128 + nr]), reads=[p], writes=[xT[cidx]])
            try:
                for l in range(nlayers):
                    mixer(tile, l, first_tile, last_tile)
                    ffn(tile, l, l == nlayers - 1)
            except _Stop:
                break
        for l in range(nlayers):
            for k in range(3):
                K.dma(conv_out[l, 0, k, :].rearrange("(c p) -> p c", p=128), halo_x[l][:, :, k], reads=[halo_x[l]], writes=[dram_cv],
                      allow_slow_non_contiguous=True)
                K.dma(conv_out[l, 1, k, :].rearrange("(c p) -> p c", p=128), halo_xs[l][:, :, k], reads=[halo_xs[l]], writes=[dram_cv],
                      allow_slow_non_contiguous=True)
            for k in range(2):
                K.dma(ffn_out[l, 0, k, :].rearrange("(c p) -> p c", p=128), halo_f[l][:, :, k], reads=[halo_f[l]], writes=[dram_ff],
                      allow_slow_non_contiguous=True)
                K.dma(ffn_out[l, 1, k, :].rearrange("(c p) -> p c", p=128), halo_fs[l][:, :, k], reads=[halo_fs[l]], writes=[dram_ff],
                      allow_slow_non_contiguous=True)
        K.finish([dram_y, dram_hg, dram_ss, dram_cv, dram_ff])
        print("instructions:", K.n_ins, "waits:", K.n_wait, "sems:", K.nsem, flush=True)
    return nc


_CACHE = {}


def kernel(**inp):
    f32 = lambda a: np.ascontiguousarray(np.asarray(a, dtype=np.float32))
    xp = f32(inp["x_prompt"])
    xs = f32(inp["x_sample"])
    meta = f32(inp["meta_tokens"])
    if "nc" not in _CACHE:
        _CACHE["nc"] = build_program()
    nc = _CACHE["nc"]
    consts = build_consts()
    wnames = ["hgrn_lb_logits", "hgrn_norm_w", "ssm_conv_w", "ssm_conv_b", "ssm_dt_bias", "ssm_a_log",
              "ssm_d", "ssm_norm_w", "ln1_g", "ln1_b", "ffn_conv_w", "ffn_conv_b", "ln2_g", "ln2_b"]
    shared = {n: f32(inp[n]) for n in wnames}
    shared["wst"] = build_wstream(inp)
    for k, v in consts.items():
        shared["c_" + k] = v
    in_maps = []
    for c in range(8):
        b = c % 2
        xin = np.concatenate([meta, xp[b, 0:TP], xs[c], xp[b, TP:]], axis=0)
        m = dict(shared)
        m["xin"] = np.ascontiguousarray(xin)
        m["st_hg"] = f32(inp["state_hgrn"][:, c])
        m["st_ssm"] = f32(inp["state_ssm"][:, c])
        m["st_conv"] = f32(inp["state_ssm_conv"][:, c])
        m["st_ffn"] = f32(inp["state_ffn_conv"][:, c])
        in_maps.append(m)
    res = run_bass_kernel_spmd(nc, in_maps, core_ids=list(range(8)))
    R = res.results
    ys = [r["y"] for r in R]
    y_prompt = np.stack([np.concatenate([ys[b][N_META:N_META + TP], ys[b][N_META + TP + DEC_SEQ:]], axis=0) for b in range(2)])
    y_sample = np.stack([ys[c][N_META + TP:N_META + TP + DEC_SEQ] for c in range(8)])
    hg_p = np.stack([R[b]["hg_o"][:, 0] for b in range(2)], axis=1)
    ssm_p = np.stack([R[b]["ssm_o"][:, 0] for b in range(2)], axis=1)
    conv_p = np.stack([R[b]["conv_o"][:, 0] for b in range(2)], axis=1)
    ffn_p = np.stack([R[b]["ffn_o"][:, 0] for b in range(2)], axis=1)
    hg_s = np.stack([R[c]["hg_o"][:, 1] for c in range(8)], axis=1)
    ssm_s = np.stack([R[c]["ssm_o"][:, 1] for c in range(8)], axis=1)
    conv_s = np.stack([R[c]["conv_o"][:, 1] for c in range(8)], axis=1)
    ffn_s = np.stack([R[c]["ffn_o"][:, 1] for c in range(8)], axis=1)
    outs = (y_prompt, y_sample, hg_p, ssm_p, conv_p, ffn_p, hg_s, ssm_s, conv_s, ffn_s)
    return tuple(np.ascontiguousarray(o, dtype=np.float32) for o in outs)
DBG_NAMES = ["d_on", "d_mga", "d_yn", "d_mg", "d_xmid", "d_v2", "d_hm"]
```

```python
import contextlib
import numpy as np
import concourse.bass as bass
import concourse.mybir as mybir
from concourse.bass_utils import run_bass_kernel_spmd

F32 = mybir.dt.float32
BF16 = mybir.dt.bfloat16
AF = mybir.ActivationFunctionType
ALU = mybir.AluOpType

D = 2048
NKC = 16
DEPTH = 2
SEQ = 4096
N_META = 16
DEC_SEQ = 32
HG_H = 16
SSM_H = 32
SSM_P = 64
SSM_N = 128
SSM_G = 4
XBC = 3072
DFF = 5632
NFF = 44
N_IN = 17440
O_Q, O_F, O_I, O_G, O_Z, O_XBC, O_DT, O_GA, O_GB = 0, 2048, 4096, 6144, 8192, 10240, 13312, 13344, 15392
ALPHA = (2.0 * DEPTH) ** 0.25
LN_EPS = 1e-5
RMS_EPS = 1e-6
BIG = 30000.0
TW = 560
TP = 512
NTILES = SEQ // TP
NTOK = N_META + SEQ + DEC_SEQ
MAXC = 30000


class Buf:
    __slots__ = ("t", "w", "r", "name")

    def __init__(self, t, name=""):
        self.t = t
        self.w = None
        self.r = {}
        self.name = name

    def __getitem__(self, k):
        return self.t[k]


class Eng:
    def __init__(self, name, handle):
        self.name = name
        self.h = handle
        self.sem = None
        self.count = 0
        self.known = {}


class KB:
    def __init__(self, nc, stack, n_dma_sems=16):
        self.nc = nc
        self.stack = stack
        self.sems = {}
        self.eng = {}
        self.nsem = 0
        for name, h in (("pe", nc.tensor), ("act", nc.scalar), ("dve", nc.vector),
                        ("pool", nc.gpsimd), ("sp", nc.sync)):
            E = Eng(name, h)
            self.eng[name] = E
            self._newsem(E)
        self.dma_slots = []
        for i in range(n_dma_sems):
            s = stack.enter_context(nc.semaphore("d%d" % i))
            self.sems[id(s)] = s
            self.dma_slots.append([s, 0])
        self.dma_i = 0
        self.n_wait = 0
        self.n_ins = 0
        self.names = set()

    def _newsem(self, E):
        s = self.stack.enter_context(self.nc.semaphore("s_%s_%d" % (E.name, self.nsem)))
        self.nsem += 1
        self.sems[id(s)] = s
        E.sem = s
        E.count = 0

    def uname(self, name):
        i = 0
        n = name
        while n in self.names:
            i += 1
            n = "%s_%d" % (name, i)
        self.names.add(n)
        return n

    def sb(self, name, shape, dt):
        name = self.uname(name)
        t = self.stack.enter_context(self.nc.sbuf_tensor(name, list(shape), dt))
        return Buf(t, name)

    def psum(self, name, shape, dt):
        t = self.stack.enter_context(self.nc.psum_tensor(name, list(shape), dt))
        return Buf(t, name)

    def _need(self, reads, writes):
        need = {}
        for b in reads:
            if b.w is not None:
                k, v = b.w
                if need.get(k, 0) < v:
                    need[k] = v
        for b in writes:
            if b.w is not None:
                k, v = b.w
                if need.get(k, 0) < v:
                    need[k] = v
            for k, v in b.r.items():
                if need.get(k, 0) < v:
                    need[k] = v
        return need

    def _wait(self, E, need, skip_own=False):
        for k, v in need.items():
            if skip_own and k == id(E.sem):
                continue
            if E.known.get(k, 0) < v:
                E.h.wait_ge(self.sems[k], v)
                E.known[k] = v
                self.n_wait += 1

    def _mark(self, tok, reads, writes):
        k, v = tok
        for b in reads:
            if b.r.get(k, 0) < v:
                b.r[k] = v
        for b in writes:
            b.w = tok
            b.r = {}

    def op(self, e, emit, reads=(), writes=(), inc=True):
        E = self.eng[e]
        if inc and E.count >= MAXC:
            self._newsem(E)
        self._wait(E, self._need(reads, writes), skip_own=(e == "pe"))
        ins = emit(E.h)
        self.n_ins += 1
        if inc:
            E.count += 1
            ins.then_inc(E.sem, 1)
            tok = (id(E.sem), E.count)
        else:
            tok = (id(E.sem), E.count + 1)
        self._mark(tok, reads, writes)
        return tok

    def dma(self, out, in_, reads=(), writes=(), q="sp", **kw):
        E = self.eng[q]
        slot = self.dma_slots[self.dma_i]
        self.dma_i = (self.dma_i + 1) % len(self.dma_slots)
        if slot[1] >= MAXC // 16:
            s = self.stack.enter_context(self.nc.semaphore("dx%d" % self.nsem))
            self.nsem += 1
            self.sems[id(s)] = s
            self._wait(E, {id(slot[0]): slot[1] * 16})
            slot[0] = s
            slot[1] = 0
        need = self._need(reads, writes)
        if slot[1] > 0:
            k = id(slot[0])
            need[k] = max(need.get(k, 0), slot[1] * 16)
        self._wait(E, need)
        E.h.dma_start(out=out, in_=in_, **kw).then_inc(slot[0], 16)
        self.n_ins += 1
        slot[1] += 1
        tok = (id(slot[0]), slot[1] * 16)
        self._mark(tok, reads, writes)
        return tok

    def finish(self, bufs=()):
        E = self.eng["sp"]
        need = {}
        for s, c in self.dma_slots:
            if c:
                need[id(s)] = c * 16
        for b in bufs:
            if b.w is not None:
                k, v = b.w
                need[k] = max(need.get(k, 0), v)
        self._wait(E, need)


def tile_defs(ntiles):
    tiles = []
    r = 0
    for ti in range(ntiles):
        if ti == 0:
            w = N_META + TP + DEC_SEQ
            segs = [(0, N_META + TP, 0), (N_META + TP, DEC_SEQ, 1)]
            chunks = [(N_META + TP, DEC_SEQ, 1), (0, N_META, 0)] + [(N_META + 64 * j, 64, 0) for j in range(TP // 64)]
        else:
            w = TP
            segs = [(0, TP, 0)]
            chunks = [(64 * j, 64, 0) for j in range(TP // 64)]
        pieces = []
        c = 0
        while c < w:
            pieces.append((c, min(c + 512, w)))
            c += 512
        tiles.append(dict(r0=r, w=w, segs=segs, chunks=chunks, pieces=pieces, idx=ti))
        r += w
    return tiles


def build_consts():
    c = {}
    c["identf"] = np.eye(128, dtype=np.float32)
    tri = (np.arange(128)[:, None] <= np.arange(128)[None, :]).astype(np.float32)
    c["tri"] = tri
    c["ones"] = np.ones((128, 128), np.float32)
    for L in (16, 32, 64):
        m = (np.arange(L)[None, :] < np.arange(L)[:, None]).astype(np.float32) * BIG
        c["bigm%d" % L] = np.ascontiguousarray(np.broadcast_to(m[:, None, :], (L, 8, L))).reshape(L, 8 * L)
    return c


def plan_pass(l):
    for h in range(HG_H):
        for o in (O_Q, O_F, O_I, O_G):
            yield ("in", l, 0, NKC, o + h * 128, 128)
    yield ("in", l, 0, NKC, O_DT, 32)
    for g in range(SSM_G):
        yield ("in", l, 0, NKC, O_XBC + 2048 + g * 128, 128)
        yield ("in", l, 0, NKC, O_XBC + 2560 + g * 128, 128)
        for pr in range(4):
            cc = g * 4 + pr
            yield ("in", l, 0, NKC, O_XBC + cc * 128, 128)
            yield ("in", l, 0, NKC, O_Z + cc * 128, 128)
        if g == 1:
            for c in range(16):
                yield ("in", l, 0, NKC, O_GA + c * 128, 128)
                yield ("pa", l, 0, NKC, c * 128, 128)
    for c in range(16):
        yield ("in", l, 0, NKC, O_GB + c * 128, 128)
        yield ("pb", l, 0, NKC, c * 128, 128)
    for c in range(16):
        yield ("out", l, 0, NKC, c * 128, 128)
    for j in range(NFF):
        yield ("up", l, 0, NKC, j * 128, 128)
        yield ("up", l, 0, NKC, DFF + j * 128, 128)
    for c in range(16):
        yield ("dn", l, 0, 16, c * 128, 128)
        yield ("dn", l, 16, 16, c * 128, 128)
        yield ("dn", l, 32, 12, c * 128, 128)


NBL = len(list(plan_pass(0)))
WNAME = {"in": "w_in", "pa": "w_proj_a", "pb": "w_proj_b", "out": "w_out", "up": "ffn_w_up", "dn": "ffn_w_down"}


def build_wstream(inp):
    wst = np.zeros((DEPTH, NBL, 128, NKC, 128), np.float32)
    for l in range(DEPTH):
        for j, (name, _, k0, nk, c0, ncol) in enumerate(plan_pass(l)):
            W = inp[WNAME[name]][l]
            blk = np.asarray(W[k0 * 128:(k0 + nk) * 128, c0:c0 + ncol], dtype=np.float32).reshape(nk, 128, ncol)
            wst[l, j, :, 0:nk, 0:ncol] = blk.transpose(1, 0, 2)
    return wst.reshape(DEPTH, NBL, 128, NKC * 128)


class _Stop(Exception):
    pass


def build_program(ntiles=NTILES, nlayers=DEPTH, debug=False, stop_at=None):
    nc = bass.Bass("TRN2", target_bir_lowering=False)
    tiles = tile_defs(ntiles)
    ntok = sum(t["w"] for t in tiles)

    def din(name, shape):
        return nc.dram_tensor(name, list(shape), F32, kind="ExternalInput").ap()

    def dout(name, shape):
        return nc.dram_tensor(name, list(shape), F32, kind="ExternalOutput").ap()

    xin = din("xin", [NTOK, D])
    st_hg = din("st_hg", [DEPTH, HG_H, 128, 128])
    st_ssm = din("st_ssm", [DEPTH, SSM_H, SSM_P, SSM_N])
    st_conv = din("st_conv", [DEPTH, 3, XBC])
    st_ffn = din("st_ffn", [DEPTH, 2, DFF])
    lb_logits = din("hgrn_lb_logits", [DEPTH, D])
    hg_nw = din("hgrn_norm_w", [DEPTH, D])
    cv_w = din("ssm_conv_w", [DEPTH, 4, XBC])
    cv_b = din("ssm_conv_b", [DEPTH, XBC])
    dt_bias = din("ssm_dt_bias", [DEPTH, SSM_H])
    a_log = din("ssm_a_log", [DEPTH, SSM_H])
    ssm_d = din("ssm_d", [DEPTH, SSM_H])
    ssm_nw = din("ssm_norm_w", [DEPTH, D])
    ln1_g = din("ln1_g", [DEPTH, D])
    ln1_b = din("ln1_b", [DEPTH, D])
    fcv_w = din("ffn_conv_w", [DEPTH, 3, DFF])
    fcv_b = din("ffn_conv_b", [DEPTH, DFF])
    ln2_g = din("ln2_g", [DEPTH, D])
    ln2_b = din("ln2_b", [DEPTH, D])
    wst = din("wst", [DEPTH, NBL, 128, NKC * 128])
    cdefs = build_consts()
    cin = {k: din("c_" + k, v.shape) for k, v in cdefs.items()}

    y_out = dout("y", [NTOK, D])
    hg_out = dout("hg_o", [DEPTH, 2, HG_H, 128, 128])
    ssm_out = dout("ssm_o", [DEPTH, 2, SSM_H, SSM_P, SSM_N])
    conv_out = dout("conv_o", [DEPTH, 2, 3, XBC])
    ffn_out = dout("ffn_o", [DEPTH, 2, 2, DFF])
    dbg = {}
    if debug:
        for nm in ("d_on", "d_mga", "d_yn", "d_mg", "d_xmid", "d_v2"):
            dbg[nm] = nc.dram_tensor(nm, [16, 128, TW], BF16, kind="ExternalOutput").ap()
        dbg["d_hm"] = nc.dram_tensor("d_hm", [NFF, 128, TW], BF16, kind="ExternalOutput").ap()


    with contextlib.ExitStack() as st:
        K = KB(nc, st)
        op = K.op

        xT = [K.sb("xT%d" % i, [128, TW], BF16) for i in range(NKC)]
        hm = [K.sb("hm%d" % i, [128, TW], BF16) for i in range(NFF)]
        on = hm[0:16]
        mg = hm[16:32]
        NT32 = 10
        tf = [K.sb("tf%d" % i, [128, TW + 40], F32) for i in range(NT32)]
        tb = [K.sb("tb%d" % i, [128, TW], BF16) for i in range(10)]
        NS, NB = 0, 10
        stage = [K.sb("stg%d" % i, [128, NKC, 128], F32) for i in range(NS)]
        wring = [K.sb("wb%d" % i, [128, NKC, 128], BF16) for i in range(NB)]
        psb = [K.psum("ps%d" % i, [128, 512], F32) for i in range(8)]
        ps_i = [0]

        def nps():
            b = psb[ps_i[0]]
            ps_i[0] = (ps_i[0] + 1) % 8
            return b

        Shg1 = K.sb("Shg", [128, 128], F32)
        Hss1 = K.sb("Hss", [128, 512], F32)
        Hsb1 = K.sb("Hsb", [128, 512], BF16)
        hg_c = Buf(nc.dram_tensor("hg_carry", [DEPTH, HG_H, 128, 128], F32).ap(), "hg_carry")
        ss_c = Buf(nc.dram_tensor("ss_carry", [DEPTH, SSM_G, 128, 512], F32).ap(), "ss_carry")
        halo_x = [K.sb("halox%d" % l, [128, 24, 3], F32) for l in range(DEPTH)]
        halo_f = [K.sb("halof%d" % l, [128, NFF, 2], F32) for l in range(DEPTH)]
        halo_xs = [K.sb("haloxs%d" % l, [128, 24, 3], F32) for l in range(DEPTH)]
        halo_fs = [K.sb("halofs%d" % l, [128, NFF, 2], F32) for l in range(DEPTH)]
        dram_y = Buf(y_out, "y")
        dram_hg = Buf(hg_out, "hgo")
        dram_ss = Buf(ssm_out, "sso")
        dram_cv = Buf(conv_out, "cvo")
        dram_ff = Buf(ffn_out, "ffo")

        cb = {}
        for k, v in cdefs.items():
            b = K.sb("sc_" + k, [128, v.shape[1]], F32)
            K.dma(b[0:v.shape[0], :], cin[k][:, :], writes=[b])
            cb[k] = b
        identb = K.sb("identb", [128, 128], BF16)
        onesb = K.sb("onesb", [128, 128], BF16)
        trib = K.sb("trib", [128, 128], BF16)
        negtri = K.sb("negtri", [128, 128], F32)
        op("dve", lambda e: e.tensor_copy(identb[:], cb["identf"][:]), reads=[cb["identf"]], writes=[identb])
        op("dve", lambda e: e.tensor_copy(onesb[:], cb["ones"][:]), reads=[cb["ones"]], writes=[onesb])
        op("dve", lambda e: e.tensor_copy(trib[:], cb["tri"][:]), reads=[cb["tri"]], writes=[trib])
        op("dve", lambda e: e.tensor_scalar(negtri[:], cb["tri"][:], -1.0, None, ALU.mult), reads=[cb["tri"]], writes=[negtri])
        identf, onesf, trif = cb["identf"], cb["ones"], cb["tri"]
        one_col = K.sb("one_col", [128, 1], F32)
        op("dve", lambda e: e.memset(one_col[:], 1.0), writes=[one_col])

        def load_vec(name, ap, nch, extra=None):
            b = K.sb("pv_" + name, [128, DEPTH, nch] if extra is None else [128, DEPTH, extra, nch], F32)
            for l in range(DEPTH):
                if extra is None:
                    K.dma(b[:, l, :], ap[l, :].rearrange("(c p) -> p c", p=128), writes=[b],
                          allow_slow_non_contiguous=True)
                else:
                    for k in range(extra):
                        K.dma(b[:, l, k, :], ap[l, k, :].rearrange("(c p) -> p c", p=128), writes=[b],
                              allow_slow_non_contiguous=True)
            return b

        pv_lbl = load_vec("lbl", lb_logits, 16)
        pv_hgnw = load_vec("hgnw", hg_nw, 16)
        pv_cvw = load_vec("cvw", cv_w, 24, extra=4)
        pv_cvb = load_vec("cvb", cv_b, 24)
        pv_snw = load_vec("snw", ssm_nw, 16)
        pv_l1g = load_vec("l1g", ln1_g, 16)
        pv_l1b = load_vec("l1b", ln1_b, 16)
        pv_fcw = load_vec("fcw", fcv_w, NFF, extra=3)
        pv_fcb = load_vec("fcb", fcv_b, NFF)
        pv_l2g = load_vec("l2g", ln2_g, 16)
        pv_l2b = load_vec("l2b", ln2_b, 16)
        hv = {}
        for name, ap in (("dtb", dt_bias), ("alog", a_log), ("dsk", ssm_d)):
            b = K.sb("hv_" + name, [SSM_H, DEPTH], F32)
            K.dma(b[:], ap.rearrange("l h -> h l"), writes=[b], allow_slow_non_contiguous=True)
            hv[name] = b
        hvA = K.sb("hv_A", [SSM_H, DEPTH], F32)
        op("act", lambda e: e.activation(hvA[:], hv["alog"][:], AF.Exp), reads=[hv["alog"]], writes=[hvA])
        pv_lb = K.sb("pv_lb", [128, DEPTH, 16], F32)
        pv_oml = K.sb("pv_oml", [128, DEPTH, 16], F32)
        lbe = K.sb("lbe", [128, DEPTH, 16], F32)
        lbs = K.sb("lbs", [128, 16], F32)
        op("act", lambda e: e.activation(lbe[:], pv_lbl[:], AF.Exp), reads=[pv_lbl], writes=[lbe])
        op("dve", lambda e: e.tensor_copy(lbs[:], lbe[:, 0, :]), reads=[lbe], writes=[lbs])
        for l in range(1, DEPTH):
            op("dve", lambda e: e.tensor_tensor(lbs[:], lbs[:], lbe[:, l, :], ALU.add), reads=[lbe, lbs], writes=[lbs])
        op("dve", lambda e: e.reciprocal(lbs[:], lbs[:]), reads=[lbs], writes=[lbs])
        op("dve", lambda e: e.memset(pv_lb[:, 0, :], 0.0), writes=[pv_lb])
        for l in range(1, DEPTH):
            op("dve", lambda e: e.tensor_tensor(pv_lb[:, l, :], lbe[:, l, :], lbs[:], ALU.mult), reads=[lbe, lbs], writes=[pv_lb])
            if l > 1:
                op("dve", lambda e: e.tensor_tensor(pv_lb[:, l, :], pv_lb[:, l, :], pv_lb[:, l - 1, :], ALU.add),
                   reads=[pv_lb], writes=[pv_lb])
        op("dve", lambda e: e.tensor_scalar(pv_oml[:], pv_lb[:], -1.0, 1.0, ALU.mult, ALU.add), reads=[pv_lb], writes=[pv_oml])
        pv_dsk = K.sb("pv_dsk", [128, DEPTH, 16], F32)
        for l in range(DEPTH):
            for two in range(2):
                src = ssm_d[l:l + 1, :].rearrange("o (c two) -> o c two", two=2)[:, :, two]
                K.dma(pv_dsk[two * 64:(two + 1) * 64, l, :], src.to_broadcast([64, 16]), writes=[pv_dsk],
                      allow_slow_non_contiguous=True)
        for l in range(DEPTH):
            op("dve", lambda e: e.memset(halo_x[l][:], 0.0), writes=[halo_x[l]])
            op("dve", lambda e: e.memset(halo_f[l][:], 0.0), writes=[halo_f[l]])
            for k in range(3):
                K.dma(halo_xs[l][:, :, k], st_conv[l, k, :].rearrange("(c p) -> p c", p=128), writes=[halo_xs[l]],
                      allow_slow_non_contiguous=True)
            for k in range(2):
                K.dma(halo_fs[l][:, :, k], st_ffn[l, k, :].rearrange("(c p) -> p c", p=128), writes=[halo_fs[l]],
                      allow_slow_non_contiguous=True)

        sched = []

        def plan():
            for ti in range(ntiles):
                for l in range(nlayers):
                    yield from plan_pass(l)

        sched = list(plan())
        ws = dict(issued=0, got=0)
        cast_rr = [0]

        def ws_issue():
            i = ws["issued"]
            name, l, k0, nk, c0, ncol = sched[i]
            wb = wring[i % NB]
            j = i % NBL
            K.dma(wb[:, 0:nk, :].rearrange("p k c -> p (k c)"), wst[l, j, :, 0:nk * 128], writes=[wb], q="pool")
            ws["issued"] += 1

        def wget(spec):
            i = ws["got"]
            assert sched[i] == spec, (i, sched[i], spec)
            while ws["issued"] < min(len(sched), i + NB - 2):
                ws_issue()
            ws["got"] += 1
            return wring[i % NB]

        def gemm(wspecs, srcs, tile, evac, m=128):
            wbs = [(wget(s), s[3]) for s in wspecs]
            for (c0, c1) in tile["pieces"]:
                p = nps()
                kk = 0
                nk_tot = sum(n for _, n in wbs)
                for wb, nk in wbs:
                    for k in range(nk):
                        src = srcs[kk]
                        last = (kk == nk_tot - 1)
                        op("pe", lambda e: e.matmul(p[0:m, 0:c1 - c0], lhsT=wb[:, k, 0:m], rhs=src[:, c0:c1],
                                                    start=(kk == 0), stop=last),
                           reads=[wb, src], writes=[p], inc=last)
                        kk += 1
                evac(p, c0, c1)

        def act(out, in_, func, reads, writes, **kw):
            op("act", lambda e: e.activation(out, in_, func, **kw), reads=reads, writes=writes)

        ATb = [K.sb("ATb%d" % i, [64, 64], BF16) for i in range(2)]
        vtok = [K.sb("vtok%d" % i, [64, 128], BF16) for i in range(2)]
        ktok = [K.sb("ktok%d" % i, [64, 128], BF16) for i in range(2)]
        Spb = [K.sb("Spb%d" % i, [128, 128], BF16) for i in range(2)]
        rr = K.sb("rr", [128, 3, 16], F32)
        rrd = K.sb("rrd", [128, 3, 16], F32)
        st_mean = K.sb("st_mean", [128, 512], F32)
        st_rstd = K.sb("st_rstd", [128, 512], F32)
        st_tmp = K.sb("st_tmp", [128, 512], F32)
        sqr = [K.sb("sqr%d" % i, [128, 512], BF16) for i in range(2)]
        pre_s = K.sb("pre_s", [128, 40], F32)
        acc_s = K.sb("acc_s", [128, 40], F32)
        dtT = tf[4]
        adtT = tf[5]
        NCH = 10
        dta = K.sb("dta", [64, NCH, 64], F32)
        dec_tok = K.sb("dec_tok", [64, NCH, 32], F32)
        w_tok = K.sb("w_tok", [64, NCH, 32], F32)
        rtot_bc = K.sb("rtot_bc", [128, NCH, 32], F32)
        gendS = K.sb("gendS", [128, 32], F32)
        wtmp = K.sb("wtmp", [64, 32], F32)
        gtokS = K.sb("gtokS", [64, 32], F32)
        xdt = K.sb("xdt", [64, 512], BF16)
        xw = K.sb("xw", [64, 512], BF16)
        Btok = K.sb("Btok", [64, 128], BF16)
        cbt = K.sb("cbt", [64, 64], F32)
        R1 = K.sb("R1", [64, 512], F32)
        R2 = K.sb("R2", [64, 512], F32)
        LT = R1
        scT = K.sb("scT", [64, 512], BF16)
        ytmp = R2
        ytok = K.sb("ytok", [64, 512], BF16)
        yT4 = tf[0:4]
        sttok = K.sb("sttok", [128, 128], F32)

        def seg_evac(tile, p, c0, c1, pre, hw):
            for (s0, sl, sq) in tile["segs"]:
                a, b = max(c0, s0), min(c1, s0 + sl)
                if a >= b:
                    continue
                dstb = pre if sq == 0 else pre_s
                dst = dstb[:, hw + a - s0: hw + b - s0]
                op("act", lambda e: e.copy(dst, p[:, a - c0:b - c0]), reads=[p], writes=[dstb])

        def conv_apply(tile, pre, halo, halo_sq, cidx, wfun, bap, width, dst, func):
            hw = width - 1
            for (s0, sl, sq) in tile["segs"]:
                pb = pre if sq == 0 else pre_s
                hb = halo if sq == 0 else halo_sq
                ab = tf[9] if sq == 0 else acc_s
                op("dve", lambda e: e.tensor_copy(pb[:, 0:hw], hb[:, cidx, :]), reads=[hb], writes=[pb])
                op("dve", lambda e: e.tensor_scalar(ab[:, 0:sl], pb[:, hw:hw + sl], wfun(hw), bap, ALU.mult, ALU.add),
                   reads=[pb], writes=[ab])
                for k in range(hw):
                    op("dve", lambda e: e.scalar_tensor_tensor(ab[:, 0:sl], pb[:, k:k + sl], wfun(k), ab[:, 0:sl],
                                                               ALU.mult, ALU.add), reads=[pb, ab], writes=[ab])
                act(dst[:, s0:s0 + sl], ab[:, 0:sl], func, [ab], [dst])
                op("dve", lambda e: e.tensor_copy(hb[:, cidx, :], pb[:, sl:sl + hw]), reads=[pb], writes=[hb])

        def bc_sum(srcs_fn, nsrc, c0, c1):
            p = nps()
            for i in range(nsrc):
                sbuf, sap = srcs_fn(i)
                op("pe", lambda e: e.matmul(p[:, 0:c1 - c0], lhsT=onesb[:], rhs=sap, start=(i == 0), stop=(i == nsrc - 1)),
                   reads=[onesb, sbuf], writes=[p], inc=(i == nsrc - 1))
            return p

        def rstd_from(p, w, scale, eps, dst):
            act(st_tmp[:, 0:w], p[:, 0:w], AF.Sqrt, [p], [st_tmp], bias=eps_col(eps), scale=scale)
            op("dve", lambda e: e.reciprocal(dst[:, 0:w], st_tmp[:, 0:w]), reads=[st_tmp], writes=[dst])

        eps_cols = {}

        def eps_col(v):
            if v not in eps_cols:
                b = K.sb("epsc%d" % len(eps_cols), [128, 1], F32)
                op("dve", lambda e: e.memset(b[:], float(v)), writes=[b])
                eps_cols[v] = b
            return eps_cols[v][:, 0:1]

        eps_col(RMS_EPS)
        eps_col(LN_EPS)

        def gemm_g(wspecs, srcs, tile, evac, m=128):
            wbs = [(wget(s), s[3]) for s in wspecs]
            for (c0, c1) in tile["pieces"]:
                p = nps()
                kk = 0
                nk_tot = sum(n for _, n in wbs)
                for wb, nk in wbs:
                    for k in range(nk):
                        src = srcs[kk]
                        last = (kk == nk_tot - 1)
                        op("pe", lambda e: e.matmul(p[0:m, 0:c1 - c0], lhsT=wb[:, k, 0:m], rhs=src[:, c0:c1],
                                                    start=(kk == 0), stop=last),
                           reads=[wb, src], writes=[p], inc=last)
                        kk += 1
                evac(p, c0, c1)
                yield

        HS = []
        for s_ in range(2):
            HS.append(dict(
                qs=tf[4 * s_ + 0], fk=tf[4 * s_ + 1], ln=tf[4 * s_ + 2], G=tf[4 * s_ + 3],
                vT=tb[4 * s_ + 0], gs=tb[4 * s_ + 1], qt=tb[4 * s_ + 2], kt=tb[4 * s_ + 3],
                S=K.sb("ShgS%d" % s_, [128, 128], F32), rr=K.sb("rrS%d" % s_, [128, 3, 16], F32),
                rrd=K.sb("rrdS%d" % s_, [128, 3, 16], F32),
                ATb=[K.sb("ATbS%d_%d" % (s_, i), [64, 64], BF16) for i in range(2)],
                vtok=[K.sb("vtokS%d_%d" % (s_, i), [64, 128], BF16) for i in range(2)],
                ktok=[K.sb("ktokS%d_%d" % (s_, i), [64, 128], BF16) for i in range(2)],
                Spb=[K.sb("SpbS%d_%d" % (s_, i), [128, 128], BF16) for i in range(2)],
                rstd=K.sb("rstdS%d" % s_, [128, 512], F32)))

        def hg_prep(tile, l, h, B):
            T = tile["w"]
            qs, fk, lnf, GnS = B["qs"], B["fk"], B["ln"], B["G"]
            Dd, E2 = B["ln"], B["G"]
            vT, gs, qt, kt = B["vT"], B["gs"], B["qt"], B["kt"]
            rr_, rrd_ = B["rr"], B["rrd"]
            specs = [("in", l, 0, NKC, o + h * 128, 128) for o in (O_Q, O_F, O_I, O_G)]
            yield from gemm_g([specs[0]], xT, tile, lambda p, c0, c1: act(qs[:, c0:c1], p[:, 0:c1 - c0], AF.Silu, [p], [qs]))
            yield from gemm_g([specs[1]], xT, tile, lambda p, c0, c1: act(fk[:, c0:c1], p[:, 0:c1 - c0], AF.Sigmoid, [p], [fk]))
            yield from gemm_g([specs[2]], xT, tile, lambda p, c0, c1: op("act", lambda e: e.copy(vT[:, c0:c1], p[:, 0:c1 - c0]),
                                                                         reads=[p], writes=[vT]))
            yield from gemm_g([specs[3]], xT, tile, lambda p, c0, c1: act(gs[:, c0:c1], p[:, 0:c1 - c0], AF.Silu, [p], [gs]))
            op("dve", lambda e: e.tensor_scalar(fk[:, 0:T], fk[:, 0:T], pv_oml[:, l, h:h + 1], pv_lb[:, l, h:h + 1],
                                                ALU.mult, ALU.add), reads=[fk, pv_oml, pv_lb], writes=[fk])
            act(lnf[:, 0:T], fk[:, 0:T], AF.Ln, [fk], [lnf])
            op("dve", lambda e: e.tensor_scalar(fk[:, 0:T], fk[:, 0:T], -1.0, 1.0, ALU.mult, ALU.add), reads=[fk], writes=[fk])
            op("dve", lambda e: e.memset(GnS[:, 0:1], 0.0), writes=[GnS])
            op("dve", lambda e: e.tensor_tensor_scan(GnS[:, 1:T + 1], one_col[:, 0:1].to_broadcast([128, T]), lnf[:, 0:T],
                                                     0.0, ALU.mult, ALU.add), reads=[one_col, lnf], writes=[GnS])
            yield
            runs = []
            for ci, (a, L, sq) in enumerate(tile["chunks"]):
                if runs and runs[-1][2] == L and runs[-1][0] + runs[-1][1] * L == a:
                    runs[-1][1] += 1
                else:
                    runs.append([a, 1, L, ci])
            for (a, n, L, ci0) in runs:
                hl = L // 2
                gv = lambda off: GnS[:, a + off:a + off + n * L].rearrange("p (j l) -> p j l", l=L)
                ref = gv(hl)[:, :, 0:1]
                sta = gv(0)[:, :, 0:1]
                end = gv(L)[:, :, 0:1]
                op("dve", lambda e: e.tensor_tensor(Dd[:, a:a + n * L].rearrange("p (j l) -> p j l", l=L), gv(1),
                                                    ref.to_broadcast([128, n, L]), ALU.subtract), reads=[GnS], writes=[Dd])
                r3 = lambda k: rrd_[:, k, ci0:ci0 + n].unsqueeze(2)
                op("dve", lambda e: e.tensor_tensor(r3(0), ref, sta, ALU.subtract), reads=[GnS], writes=[rrd_])
                op("dve", lambda e: e.tensor_tensor(r3(1), end, ref, ALU.subtract), reads=[GnS], writes=[rrd_])
                op("dve", lambda e: e.tensor_tensor(r3(2), end, sta, ALU.subtract), reads=[GnS], writes=[rrd_])
            nchk = len(tile["chunks"])
            act(rr_[:, :, 0:nchk], rrd_[:, :, 0:nchk], AF.Exp, [rrd_], [rr_])
            act(E2[:, 0:T], Dd[:, 0:T], AF.Exp, [Dd], [E2], scale=-1.0)
            act(Dd[:, 0:T], Dd[:, 0:T], AF.Exp, [Dd], [Dd])
            yield
            op("dve", lambda e: e.tensor_tensor(qt[:, 0:T], qs[:, 0:T], Dd[:, 0:T], ALU.mult), reads=[qs, Dd], writes=[qt])
            op("dve", lambda e: e.tensor_tensor(kt[:, 0:T], fk[:, 0:T], E2[:, 0:T], ALU.mult), reads=[fk, E2], writes=[kt])
            yield

        def hg_loop(tile, l, h, B, first_tile, last_tile):
            T = tile["w"]
            oT = B["qs"]
            vT, gs, qt, kt = B["vT"], B["gs"], B["qt"], B["kt"]
            rr_ = B["rr"]
            S = B["S"]
            cur_seq = None
            for ci, (a, L, sq) in enumerate(tile["chunks"]):
                if sq != cur_seq:
                    if cur_seq is not None:
                        store_hg(l, h, cur_seq, last_tile, S)
                    if sq == 0:
                        if first_tile:
                            op("dve", lambda e: e.memset(S[:], 0.0), writes=[S])
                        else:
                            K.dma(S[:], hg_c[l, h, :, :], reads=[hg_c], writes=[S])
                    else:
                        K.dma(S[:], st_hg[l, h, :, :], writes=[S])
                    cur_seq = sq
                i2 = ci % 2
                ATb_, vtok_, ktok_, Spb_ = B["ATb"][i2], B["vtok"][i2], B["ktok"][i2], B["Spb"][i2]
                p1 = nps()
                op("pe", lambda e: e.matmul(p1[0:L, 0:L], lhsT=kt[:, a:a + L], rhs=qt[:, a:a + L], start=True, stop=True),
                   reads=[kt, qt], writes=[p1])
                p2 = nps()
                p2b = p2[:, 0:256].bitcast(BF16)
                op("pe", lambda e: e.transpose(p2b[0:L, 0:128], vT[:, a:a + L], identb[:]), reads=[vT, identb], writes=[p2])
                p3 = nps()
                p3b = p3[:, 0:256].bitcast(BF16)
                op("pe", lambda e: e.transpose(p3b[0:L, 0:128], kt[:, a:a + L], identb[:]), reads=[kt, identb], writes=[p3])
                op("dve", lambda e: e.tensor_tensor(ATb_[0:L, 0:L], p1[0:L, 0:L], trif[0:L, 0:L], ALU.mult),
                   reads=[p1, trif], writes=[ATb_])
                op("act", lambda e: e.copy(vtok_[0:L, :], p2b[0:L, 0:128]), reads=[p2], writes=[vtok_])
                op("act", lambda e: e.copy(ktok_[0:L, :], p3b[0:L, 0:128]), reads=[p3], writes=[ktok_])
                op("dve", lambda e: e.tensor_scalar(Spb_[:], S[:], rr_[:, 0, ci:ci + 1], None, ALU.mult),
                   reads=[S, rr_], writes=[Spb_])
                yield
                p4 = nps()
                op("pe", lambda e: e.matmul(p4[:, 0:L], lhsT=vtok_[0:L, :], rhs=ATb_[0:L, 0:L], start=True, stop=False),
                   reads=[vtok_, ATb_], writes=[p4], inc=False)
                op("pe", lambda e: e.matmul(p4[:, 0:L], lhsT=Spb_[:], rhs=qt[:, a:a + L], start=False, stop=True),
                   reads=[Spb_, qt], writes=[p4])
                p5 = nps()
                op("pe", lambda e: e.matmul(p5[:, 0:128], lhsT=ktok_[0:L, :], rhs=vtok_[0:L, :], start=True, stop=True),
                   reads=[ktok_, vtok_], writes=[p5])
                op("act", lambda e: e.copy(oT[:, a:a + L], p4[:, 0:L]), reads=[p4], writes=[oT])
                op("dve", lambda e: e.tensor_scalar(S[:], S[:], rr_[:, 2, ci:ci + 1], None, ALU.mult), reads=[S, rr_], writes=[S])
                op("dve", lambda e: e.scalar_tensor_tensor(S[:], p5[:, 0:128], rr_[:, 1, ci:ci + 1], S[:], ALU.mult, ALU.add),
                   reads=[p5, rr_, S], writes=[S])
                yield
            store_hg(l, h, cur_seq, last_tile, S)
            sq_b = B["kt"]
            rstd_ = B["rstd"]
            act(sq_b[:, 0:T], oT[:, 0:T], AF.Square, [oT], [sq_b])
            yield
            for (c0, c1) in tile["pieces"]:
                w = c1 - c0
                p = bc_sum(lambda i: (sq_b, sq_b[:, c0:c1]), 1, c0, c1)
                rstd_from(p, w, 1.0 / 128, RMS_EPS, rstd_)
                yield
                op("dve", lambda e: e.scalar_tensor_tensor(oT[:, c0:c1], oT[:, c0:c1], pv_hgnw[:, l, h:h + 1], rstd_[:, 0:w],
                                                           ALU.mult, ALU.mult), reads=[oT, pv_hgnw, rstd_], writes=[oT])
                op("dve", lambda e: e.tensor_tensor(on[h][:, c0:c1], oT[:, c0:c1], gs[:, c0:c1], ALU.mult),
                   reads=[oT, gs], writes=[on[h]])
                yield

        def hgrn_all(tile, l, first_tile, last_tile):
            for _ in hg_prep(tile, l, 0, HS[0]):
                pass
            for h in range(HG_H):
                A = hg_loop(tile, l, h, HS[h % 2], first_tile, last_tile)
                Bg = hg_prep(tile, l, h + 1, HS[(h + 1) % 2]) if h + 1 < HG_H else None
                a_alive, b_alive = True, Bg is not None
                while a_alive or b_alive:
                    if a_alive:
                        try:
                            next(A)
                        except StopIteration:
                            a_alive = False
                    if b_alive:
                        try:
                            next(Bg)
                        except StopIteration:
                            b_alive = False

        def store_hg(l, h, sq, last_tile, S):
            if sq == 0:
                if last_tile:
                    K.dma(hg_out[l, 0, h, :, :], S[:], reads=[S], writes=[dram_hg])
                else:
                    K.dma(hg_c[l, h, :, :], S[:], reads=[S], writes=[hg_c])
            else:
                K.dma(hg_out[l, 1, h, :, :], S[:], reads=[S], writes=[dram_hg])

        def gated_proj(tile, l, srcs, gate_off, wname, accumulate):
            for c in range(16):
                sg = tf[0]
                gemm([("in", l, 0, NKC, gate_off + c * 128, 128)], xT, tile,
                     lambda p, c0, c1: act(sg[:, c0:c1], p[:, 0:c1 - c0], AF.Sigmoid, [p], [sg]))

                def ev(p, c0, c1):
                    if not accumulate:
                        op("dve", lambda e: e.tensor_tensor(mg[c][:, c0:c1], p[:, 0:c1 - c0], sg[:, c0:c1], ALU.mult),
                           reads=[p, sg], writes=[mg[c]])
                    else:
                        op("dve", lambda e: e.tensor_tensor(sg[:, c0:c1], p[:, 0:c1 - c0], sg[:, c0:c1], ALU.mult),
                           reads=[p, sg], writes=[sg])
                        op("dve", lambda e: e.tensor_tensor(mg[c][:, c0:c1], mg[c][:, c0:c1], sg[:, c0:c1], ALU.add),
                           reads=[mg[c], sg], writes=[mg[c]])
                gemm([(wname, l, 0, NKC, c * 128, 128)], srcs, tile, ev)

        def ss_load(l, g, sq, first_tile):
            H = Hss1
            if sq == 0:
                if first_tile:
                    op("dve", lambda e: e.memset(H[:], 0.0), writes=[H])
                else:
                    K.dma(H[:], ss_c[l, g, :, :], reads=[ss_c], writes=[H])
            else:
                for pr in range(4):
                    hh = (g * 4 + pr) * 2
                    K.dma(sttok[:], st_ssm[l, hh:hh + 2, :, :].rearrange("h p n -> (h p) n"), writes=[sttok])
                    p = nps()
                    op("pe", lambda e: e.transpose(p[:, 0:128], sttok[:], identf[:]), reads=[sttok, identf], writes=[p])
                    op("dve", lambda e: e.tensor_copy(H[:, pr * 128:(pr + 1) * 128], p[:, 0:128]), reads=[p], writes=[H])
            op("act", lambda e: e.copy(Hsb1[:], H[:]), reads=[H], writes=[Hsb1])

        def ss_store(l, g, sq, last_tile):
            H = Hss1
            if sq == 0 and not last_tile:
                K.dma(ss_c[l, g, :, :], H[:], reads=[H], writes=[ss_c])
                return
            for pr in range(4):
                hh = (g * 4 + pr) * 2
                p = nps()
                op("pe", lambda e: e.transpose(p[:, 0:128], H[:, pr * 128:(pr + 1) * 128], identf[:]),
                   reads=[H, identf], writes=[p])
                op("dve", lambda e: e.tensor_copy(sttok[:], p[:, 0:128]), reads=[p], writes=[sttok])
                K.dma(ssm_out[l, sq, hh:hh + 2, :, :].rearrange("h p n -> (h p) n"), sttok[:], reads=[sttok], writes=[dram_ss])

        def ssd(tile, l, first_tile, last_tile):
            T = tile["w"]
            chunks = tile["chunks"]
            wdt = ("in", l, 0, NKC, O_DT, 32)

            def ev_dt(p, c0, c1):
                act(dtT[0:32, c0:c1], p[0:32, 0:c1 - c0], AF.Exp, [p], [dtT], bias=hv["dtb"][:, l:l + 1])
            gemm([wdt], xT, tile, ev_dt, m=32)
            act(dtT[0:32, 0:T], dtT[0:32, 0:T], AF.Ln, [dtT], [dtT], bias=one_col[0:32, 0:1])
            op("dve", lambda e: e.tensor_scalar(adtT[0:32, 0:T], dtT[0:32, 0:T], hvA[:, l:l + 1], None, ALU.mult),
               reads=[dtT, hvA], writes=[adtT])
            if stop_at == "ssd_dt0":
                raise _Stop()
            for ci, (a, L, sq) in enumerate(chunks):
                p = nps()
                op("pe", lambda e: e.matmul(p[0:L, 0:32], lhsT=dtT[0:32, a:a + L], rhs=identf[0:32, 0:32], start=True, stop=True),
                   reads=[dtT, identf], writes=[p], inc=False)
                op("pe", lambda e: e.matmul(p[0:L, 32:64], lhsT=adtT[0:32, a:a + L], rhs=identf[0:32, 0:32], start=True, stop=True),
                   reads=[adtT, identf], writes=[p])
                op("dve", lambda e: e.tensor_copy(dta[0:L, ci, :], p[0:L, 0:64]), reads=[p], writes=[dta])
                p2 = nps()
                op("pe", lambda e: e.matmul(p2[0:L, 0:32], lhsT=trif[0:L, 0:L], rhs=dta[0:L, ci, 32:64], start=True, stop=True),
                   reads=[trif, dta], writes=[p2])
                p2x = nps()
                op("pe", lambda e: e.matmul(p2x[:, 0:32], lhsT=onesf[0:L, :], rhs=dta[0:L, ci, 32:64], start=True, stop=True),
                   reads=[onesf, dta], writes=[p2x])
                op("dve", lambda e: e.tensor_copy(gtokS[0:L, :], p2[0:L, 0:32]), reads=[p2], writes=[gtokS])
                op("dve", lambda e: e.tensor_copy(gendS[:], p2x[:, 0:32]), reads=[p2x], writes=[gendS])
                act(dec_tok[0:L, ci, :], gtokS[0:L, :], AF.Exp, [gtokS], [dec_tok], scale=-1.0)
                act(rtot_bc[:, ci, :], gendS[:], AF.Exp, [gendS], [rtot_bc], scale=-1.0)
                op("dve", lambda e: e.tensor_tensor(wtmp[0:L, :], gtokS[0:L, :], gendS[0:L, :], ALU.subtract),
                   reads=[gtokS, gendS], writes=[wtmp])
                act(wtmp[0:L, :], wtmp[0:L, :], AF.Exp, [wtmp], [wtmp])
                op("dve", lambda e: e.tensor_tensor(w_tok[0:L, ci, :], wtmp[0:L, :], dta[0:L, ci, 0:32], ALU.mult),
                   reads=[wtmp, dta], writes=[w_tok])


        SSB = [dict(BT=tb[0], CT=tb[1], xc=tb[2:6], sz=tb[6:10]),
               dict(BT=hm[32], CT=hm[33], xc=hm[34:38], sz=hm[38:42])]

        def ssd_prep(tile, l, g, SB):
            BT, CT, xc, sz = SB["BT"], SB["CT"], SB["xc"], SB["sz"]
            pre = tf[8]
            for which, dstb in ((0, BT), (1, CT)):
                cidx = 16 + which * 4 + g
                yield from gemm_g([("in", l, 0, NKC, O_XBC + 2048 + which * 512 + g * 128, 128)], xT, tile,
                                  lambda p, c0, c1: seg_evac(tile, p, c0, c1, pre, 3))
                conv_apply(tile, pre, halo_x[l], halo_xs[l], cidx, lambda k: pv_cvw[:, l, k, cidx:cidx + 1],
                           pv_cvb[:, l, cidx:cidx + 1], 4, dstb, AF.Silu)
                yield
            for pr in range(4):
                cc = g * 4 + pr
                yield from gemm_g([("in", l, 0, NKC, O_XBC + cc * 128, 128)], xT, tile,
                                  lambda p, c0, c1: seg_evac(tile, p, c0, c1, pre, 3))
                conv_apply(tile, pre, halo_x[l], halo_xs[l], cc, lambda k: pv_cvw[:, l, k, cc:cc + 1],
                           pv_cvb[:, l, cc:cc + 1], 4, xc[pr], AF.Silu)
                yield
                yield from gemm_g([("in", l, 0, NKC, O_Z + cc * 128, 128)], xT, tile,
                                  lambda p, c0, c1: act(sz[pr][:, c0:c1], p[:, 0:c1 - c0], AF.Silu, [p], [sz[pr]]))

        def ssd_chunks(tile, l, g, SB, first_tile, last_tile):
            BT, CT, xc, sz = SB["BT"], SB["CT"], SB["xc"], SB["sz"]
            chunks = tile["chunks"]
            cur_seq = None
            H, Hb = Hss1, Hsb1
            for ci, (a, L, sq) in enumerate(chunks):
                if sq != cur_seq:
                    if cur_seq is not None:
                        ss_store(l, g, cur_seq, last_tile)
                    ss_load(l, g, sq, first_tile)
                    cur_seq = sq
                L8 = 8 * L
                gs8 = slice(g * 8, (g + 1) * 8)
                px = nps()
                pxb = px[:, 0:256].bitcast(BF16)
                for pr in range(4):
                    op("pe", lambda e: e.transpose(pxb[0:L, pr * 128:(pr + 1) * 128], xc[pr][:, a:a + L], identb[:]),
                       reads=[xc[pr], identb], writes=[px], inc=(pr == 3))
                op("dve", lambda e: e.tensor_tensor(xdt[0:L, :].rearrange("s (h p) -> s h p", p=64),
                                                    pxb[0:L, 0:512].rearrange("s (h p) -> s h p", p=64),
                                                    dta[0:L, ci, gs8].unsqueeze(2).to_broadcast([L, 8, 64]), ALU.mult),
                   reads=[px, dta], writes=[xdt])
                op("dve", lambda e: e.tensor_tensor(xw[0:L, :].rearrange("s (h p) -> s h p", p=64),
                                                    pxb[0:L, 0:512].rearrange("s (h p) -> s h p", p=64),
                                                    w_tok[0:L, ci, gs8].unsqueeze(2).to_broadcast([L, 8, 64]), ALU.mult),
                   reads=[px, w_tok], writes=[xw])
                pB = nps()
                pBb = pB[:, 0:256].bitcast(BF16)
                op("pe", lambda e: e.transpose(pBb[0:L, 0:128], BT[:, a:a + L], identb[:]), reads=[BT, identb], writes=[pB])
                op("act", lambda e: e.copy(Btok[0:L, :], pBb[0:L, 0:128]), reads=[pB], writes=[Btok])
                pC = nps()
                op("pe", lambda e: e.matmul(pC[0:L, 0:L], lhsT=BT[:, a:a + L], rhs=CT[:, a:a + L], start=True, stop=True),
                   reads=[BT, CT], writes=[pC])
                op("act", lambda e: e.copy(cbt[0:L, 0:L], pC[0:L, 0:L]), reads=[pC], writes=[cbt])
                adt8 = dta[0:L, ci, 32 + g * 8:32 + (g + 1) * 8].unsqueeze(2).to_broadcast([L, 8, L])
                op("dve", lambda e: e.tensor_tensor(R1[0:L, 0:L8].rearrange("s (h t) -> s h t", t=L), adt8,
                                                    trif[0:L, 0:L].unsqueeze(1).to_broadcast([L, 8, L]), ALU.mult),
                   reads=[dta, trif], writes=[R1])
                op("dve", lambda e: e.tensor_copy(R2[0:L, 0:L8].rearrange("s (h t) -> s h t", t=L), adt8),
                   reads=[dta], writes=[R2])
                yield
                pL = nps()
                bigm = cb["bigm%d" % L]
                op("pe", lambda e: e.matmul(pL[0:L, 0:L8], lhsT=onesf[0:L, 0:L], rhs=R1[0:L, 0:L8], start=True, stop=False),
                   reads=[onesf, R1], writes=[pL], inc=False)
                op("pe", lambda e: e.matmul(pL[0:L, 0:L8], lhsT=negtri[0:L, 0:L], rhs=R2[0:L, 0:L8], start=False, stop=False),
                   reads=[negtri, R2], writes=[pL], inc=False)
                op("pe", lambda e: e.matmul(pL[0:L, 0:L8], lhsT=identf[0:L, 0:L], rhs=bigm[0:L, 0:L8], start=False, stop=True),
                   reads=[identf, bigm], writes=[pL])
                op("dve", lambda e: e.tensor_copy(LT[0:L, 0:L8], pL[0:L, 0:L8]), reads=[pL], writes=[LT])
                act(LT[0:L, 0:L8], LT[0:L, 0:L8], AF.Exp, [LT], [LT], scale=-1.0)
                op("dve", lambda e: e.tensor_tensor(scT[0:L, 0:L8].rearrange("s (h t) -> s h t", t=L),
                                                    LT[0:L, 0:L8].rearrange("s (h t) -> s h t", t=L),
                                                    cbt[0:L, 0:L].unsqueeze(1).to_broadcast([L, 8, L]), ALU.mult),
                   reads=[LT, cbt], writes=[scT])
                yield
                pY1 = nps()
                for hh in range(8):
                    op("pe", lambda e: e.matmul(pY1[0:L, hh * 64:(hh + 1) * 64], lhsT=scT[0:L, hh * L:(hh + 1) * L],
                                                rhs=xdt[0:L, hh * 64:(hh + 1) * 64], start=True, stop=True),
                       reads=[scT, xdt], writes=[pY1], inc=(hh == 7))
                pY2 = nps()
                op("pe", lambda e: e.matmul(pY2[0:L, 0:512], lhsT=CT[:, a:a + L], rhs=Hb[:, 0:512], start=True, stop=True),
                   reads=[CT, Hb], writes=[pY2])
                op("dve", lambda e: e.tensor_tensor(ytmp[0:L, :].rearrange("s (h p) -> s h p", p=64),
                                                    pY2[0:L, 0:512].rearrange("s (h p) -> s h p", p=64),
                                                    dec_tok[0:L, ci, gs8].unsqueeze(2).to_broadcast([L, 8, 64]), ALU.mult),
                   reads=[pY2, dec_tok], writes=[ytmp])
                op("dve", lambda e: e.tensor_tensor(ytok[0:L, :], pY1[0:L, 0:512], ytmp[0:L, :], ALU.add),
                   reads=[pY1, ytmp], writes=[ytok])
                yield
                pT = nps()
                pTb = pT[:, 0:256].bitcast(BF16)
                for pr in range(4):
                    op("pe", lambda e: e.transpose(pTb[:, pr * 64:pr * 64 + L], ytok[0:L, pr * 128:(pr + 1) * 128],
                                                   identb[0:L, 0:L]), reads=[ytok, identb], writes=[pT], inc=(pr == 3))
                for pr in range(4):
                    op("act", lambda e: e.copy(yT4[pr][:, a:a + L], pTb[:, pr * 64:pr * 64 + L]), reads=[pT], writes=[yT4[pr]])
                pH = nps()
                op("pe", lambda e: e.matmul(pH[:, 0:512], lhsT=Btok[0:L, :], rhs=xw[0:L, :], start=True, stop=True),
                   reads=[Btok, xw], writes=[pH])
                op("dve", lambda e: e.tensor_tensor(H[:].rearrange("n (h p) -> n h p", p=64),
                                                    H[:].rearrange("n (h p) -> n h p", p=64),
                                                    rtot_bc[:, ci, gs8].unsqueeze(2).to_broadcast([128, 8, 64]), ALU.mult),
                   reads=[H, rtot_bc], writes=[H])
                op("dve", lambda e: e.tensor_tensor(H[:], H[:], pH[:, 0:512], ALU.add), reads=[H, pH], writes=[H])
                op("act", lambda e: e.copy(Hb[:], H[:]), reads=[H], writes=[Hb])
                yield
            ss_store(l, g, cur_seq, last_tile)

        def ssd_post(tile, l, g, SB):
            T = tile["w"]
            xc, sz = SB["xc"], SB["sz"]
            for pr in range(4):
                cc = g * 4 + pr
                yv = yT4[pr][:, 0:T]
                op("dve", lambda e: e.scalar_tensor_tensor(yv, xc[pr][:, 0:T], pv_dsk[:, l, cc:cc + 1], yv, ALU.mult, ALU.add),
                   reads=[xc[pr], pv_dsk, yT4[pr]], writes=[yT4[pr]])
                op("dve", lambda e: e.tensor_tensor(yv, yv, sz[pr][:, 0:T], ALU.mult), reads=[yT4[pr], sz[pr]], writes=[yT4[pr]])
                act(xc[pr][:, 0:T], yv, AF.Square, [yT4[pr]], [xc[pr]])
            for (c0, c1) in tile["pieces"]:
                w = c1 - c0
                p = bc_sum(lambda i: (xc[i], xc[i][:, c0:c1]), 4, c0, c1)
                rstd_from(p, w, 1.0 / 512, RMS_EPS, st_rstd)
                for pr in range(4):
                    cc = g * 4 + pr
                    op("dve", lambda e: e.scalar_tensor_tensor(on[cc][:, c0:c1], yT4[pr][:, c0:c1], pv_snw[:, l, cc:cc + 1],
                                                               st_rstd[:, 0:w], ALU.mult, ALU.mult),
                       reads=[yT4[pr], pv_snw, st_rstd], writes=[on[cc]])

        def gated_proj_g(tile, l, srcs, gate_off, wname, sg):
            for c in range(16):
                yield from gemm_g([("in", l, 0, NKC, gate_off + c * 128, 128)], xT, tile,
                                  lambda p, c0, c1: act(sg[:, c0:c1], p[:, 0:c1 - c0], AF.Sigmoid, [p], [sg]))
                yield from gemm_g([(wname, l, 0, NKC, c * 128, 128)], srcs, tile,
                                  lambda p, c0, c1: op("dve", lambda e: e.tensor_tensor(mg[c][:, c0:c1], p[:, 0:c1 - c0], sg[:, c0:c1],
                                                                                       ALU.mult), reads=[p, sg], writes=[mg[c]]))

        def chain_g(*gens):
            for g_ in gens:
                if g_ is not None:
                    yield from g_

        def interleave(A, Bg):
            a_alive, b_alive = A is not None, Bg is not None
            while a_alive or b_alive:
                if a_alive:
                    try:
                        next(A)
                    except StopIteration:
                        a_alive = False
                if b_alive:
                    try:
                        next(Bg)
                    except StopIteration:
                        b_alive = False

        def ssd_all(tile, l, first_tile, last_tile):
            interleave(ssd_prep(tile, l, 0, SSB[0]), None)
            for g in range(SSM_G):
                A = ssd_chunks(tile, l, g, SSB[g % 2], first_tile, last_tile)
                nxt = ssd_prep(tile, l, g + 1, SSB[(g + 1) % 2]) if g + 1 < SSM_G else None
                pa = gated_proj_g(tile, l, on, O_GA, "pa", tf[6]) if g == 0 else None
                interleave(A, chain_g(nxt, pa))
                if g == 0:
                    dump("d_mga", mg, tile, l)
                ssd_post(tile, l, g, SSB[g % 2])

        def layer_norm(tile, gvec, bvec, l, emit_out):
            T = tile["w"]
            for (c0, c1) in tile["pieces"]:
                w = c1 - c0
                p1 = bc_sum(lambda i: (xT[i], xT[i][:, c0:c1]), 16, c0, c1)
                p2 = nps()
                for c in range(16):
                    sb_ = sqr[c % 2]
                    act(sb_[:, 0:w], xT[c][:, c0:c1], AF.Square, [xT[c]], [sb_])
                    op("pe", lambda e: e.matmul(p2[:, 0:w], lhsT=onesb[:], rhs=sb_[:, 0:w], start=(c == 0), stop=(c == 15)),
                       reads=[onesb, sb_], writes=[p2], inc=True)
                op("dve", lambda e: e.tensor_scalar(st_mean[:, 0:w], p1[:, 0:w], 1.0 / D, None, ALU.mult), reads=[p1], writes=[st_mean])
                op("dve", lambda e: e.tensor_tensor(st_tmp[:, 0:w], st_mean[:, 0:w], st_mean[:, 0:w], ALU.mult),
                   reads=[st_mean], writes=[st_tmp])
                op("dve", lambda e: e.scalar_tensor_tensor(st_tmp[:, 0:w], p2[:, 0:w], 1.0 / D, st_tmp[:, 0:w], ALU.mult, ALU.subtract),
                   reads=[p2, st_tmp], writes=[st_tmp])
                act(st_tmp[:, 0:w], st_tmp[:, 0:w], AF.Sqrt, [st_tmp], [st_tmp], bias=eps_col(LN_EPS))
                op("dve", lambda e: e.reciprocal(st_rstd[:, 0:w], st_tmp[:, 0:w]), reads=[st_tmp], writes=[st_rstd])
                for c in range(16):
                    t32 = tf[0] if c % 2 == 0 else tf[1]
                    op("dve", lambda e: e.tensor_tensor(t32[:, 0:w], xT[c][:, c0:c1], st_mean[:, 0:w], ALU.subtract),
                       reads=[xT[c], st_mean], writes=[t32])
                    op("dve", lambda e: e.tensor_tensor(t32[:, 0:w], t32[:, 0:w], st_rstd[:, 0:w], ALU.mult),
                       reads=[t32, st_rstd], writes=[t32])
                    if not emit_out:
                        act(xT[c][:, c0:c1], t32[:, 0:w], AF.Identity, [t32, gvec, bvec], [xT[c]],
                            scale=gvec[:, l, c:c + 1], bias=bvec[:, l, c:c + 1])
                    else:
                        yb = tf[2 + (c % 4)]
                        act(yb[:, 0:w], t32[:, 0:w], AF.Identity, [t32, gvec, bvec], [yb],
                            scale=gvec[:, l, c:c + 1], bias=bvec[:, l, c:c + 1])
                        nb = (w + 127) // 128
                        for bi in range(nb):
                            nr = min(128, w - bi * 128)
                            if c % 4 == 0 and bi == 0:
                                pass
                            pt = nps()
                            op("pe", lambda e: e.transpose(pt[0:nr, 0:128], yb[:, bi * 128:bi * 128 + nr], identf[:]),
                               reads=[yb, identf], writes=[pt])
                            ob = ost[ost_i[0] % 4]
                            ost_i[0] += 1
                            op("dve", lambda e: e.tensor_copy(ob[0:nr, :], pt[0:nr, 0:128]), reads=[pt], writes=[ob])
                            r0 = tile["r0"] + c0 + bi * 128
                            K.dma(y_out[r0:r0 + nr, c * 128:(c + 1) * 128], ob[0:nr, :], reads=[ob], writes=[dram_y])

        ost = [K.sb("ost%d" % i, [128, 128], F32) for i in range(4)]
        ost_i = [0]
        xld = [K.sb("xld%d" % i, [128, 512], F32) for i in range(2)]
        xld_i = [0]

        def dump(nm, bufs, tile, l):
            if not debug or l != 0 or tile["idx"] != 0:
                return
            for i, b in enumerate(bufs):
                K.dma(dbg[nm][i, :, 0:tile["w"]], b[:, 0:tile["w"]], reads=[b])

        def mixer(tile, l, first_tile, last_tile):
            if stop_at in ("load", "load_dma", "load_1"):
                raise _Stop()
            hgrn_all(tile, l, first_tile, last_tile)
            dump("d_on", on, tile, l)
            ssd(tile, l, first_tile, last_tile)
            ssd_all(tile, l, first_tile, last_tile)
            dump("d_yn", on, tile, l)
            gated_proj(tile, l, on, O_GB, "pb", True)
            dump("d_mg", mg, tile, l)
            for c in range(16):
                gemm([("out", l, 0, NKC, c * 128, 128)], mg, tile,
                     lambda p, c0, c1: op("dve", lambda e: e.scalar_tensor_tensor(xT[c][:, c0:c1], xT[c][:, c0:c1], ALPHA,
                                                                                  p[:, 0:c1 - c0], ALU.mult, ALU.add),
                                          reads=[xT[c], p], writes=[xT[c]]))
            layer_norm(tile, pv_l1g, pv_l1b, l, False)
            dump("d_xmid", xT, tile, l)
            if stop_at == "mixer":
                raise _Stop()

        def ffn(tile, l, last_layer):
            T = tile["w"]
            pre = tf[8]
            ga = tf[7]
            for j in range(NFF):
                gemm([("up", l, 0, NKC, j * 128, 128)], xT, tile, lambda p, c0, c1: seg_evac(tile, p, c0, c1, pre, 2))
                conv_apply(tile, pre, halo_f[l], halo_fs[l], j, lambda k: pv_fcw[:, l, k, j:j + 1], pv_fcb[:, l, j:j + 1], 3,
                           ga, AF.Gelu)
                gemm([("up", l, 0, NKC, DFF + j * 128, 128)], xT, tile,
                     lambda p, c0, c1: op("dve", lambda e: e.tensor_tensor(hm[j][:, c0:c1], p[:, 0:c1 - c0], ga[:, c0:c1], ALU.mult),
                                          reads=[p, ga], writes=[hm[j]]))
            for c in range(16):
                gemm([("dn", l, 0, 16, c * 128, 128), ("dn", l, 16, 16, c * 128, 128), ("dn", l, 32, 12, c * 128, 128)], hm, tile,
                     lambda p, c0, c1: op("dve", lambda e: e.scalar_tensor_tensor(xT[c][:, c0:c1], xT[c][:, c0:c1], ALPHA,
                                                                                  p[:, 0:c1 - c0], ALU.mult, ALU.add),
                                          reads=[xT[c], p], writes=[xT[c]]))
            dump("d_hm", hm, tile, l)
            dump("d_v2", xT, tile, l)
            layer_norm(tile, pv_l2g, pv_l2b, l, last_layer)

        if stop_at == "setup":
            tiles = []
        for tile in tiles:
            T = tile["w"]
            ti = tile["idx"]
            first_tile = (ti == 0)
            last_tile = (ti == NTILES - 1)
            nblk = (T + 127) // 128
            for bi in range(nblk):
                r0 = bi * 128
                nr = min(128, T - r0)
                for q4 in range(4):
                    xtok = xld[xld_i[0] % 2]
                    xld_i[0] += 1
                    K.dma(xtok[0:nr, :], xin[tile["r0"] + r0: tile["r0"] + r0 + nr, q4 * 512:(q4 + 1) * 512], writes=[xtok])
                    p = nps()
                    for j in range(4):
                        op("pe", lambda e: e.transpose(p[:, j * 128: j * 128 + nr], xtok[0:nr, j * 128:(j + 1) * 128],
                                                       identf[0:nr, 0:nr]), reads=[xtok, identf], writes=[p], inc=(j == 3))
                    for j in range(4):
                        cidx = q4 * 4 + j
                        if j % 2:
                            op("dve", lambda e: e.tensor_copy(xT[cidx][:, r0:r0 + nr], p[:, j * 128:j * 128 + nr]), reads=[p], writes=[xT[cidx]])
                        else:
                            op("dve", lambda e: e.tensor_copy(xT[cidx][:, r0:r0 + nr], p[:, j * 128:j * 128 + nr]), reads=[p], writes=[xT[cidx]])
            try:
                for l in range(nlayers):
                    mixer(tile, l, first_tile, last_tile)
                    ffn(tile, l, l == nlayers - 1)
            except _Stop:
                break
        for l in range(nlayers):
            for k in range(3):
                K.dma(conv_out[l, 0, k, :].rearrange("(c p) -> p c", p=128), halo_x[l][:, :, k], reads=[halo_x[l]], writes=[dram_cv],
                      allow_slow_non_contiguous=True)
                K.dma(conv_out[l, 1, k, :].rearrange("(c p) -> p c", p=128), halo_xs[l][:, :, k], reads=[halo_xs[l]], writes=[dram_cv],
                      allow_slow_non_contiguous=True)
            for k in range(2):
                K.dma(ffn_out[l, 0, k, :].rearrange("(c p) -> p c", p=128), halo_f[l][:, :, k], reads=[halo_f[l]], writes=[dram_ff],
                      allow_slow_non_contiguous=True)
                K.dma(ffn_out[l, 1, k, :].rearrange("(c p) -> p c", p=128), halo_fs[l][:, :, k], reads=[halo_fs[l]], writes=[dram_ff],
                      allow_slow_non_contiguous=True)
        K.finish([dram_y, dram_hg, dram_ss, dram_cv, dram_ff])
        print("instructions:", K.n_ins, "waits:", K.n_wait, "sems:", K.nsem, flush=True)
    return nc


_CACHE = {}


def kernel(**inp):
    f32 = lambda a: np.ascontiguousarray(np.asarray(a, dtype=np.float32))
    xp = f32(inp["x_prompt"])
    xs = f32(inp["x_sample"])
    meta = f32(inp["meta_tokens"])
    if "nc" not in _CACHE:
        _CACHE["nc"] = build_program()
    nc = _CACHE["nc"]
    consts = build_consts()
    wnames = ["hgrn_lb_logits", "hgrn_norm_w", "ssm_conv_w", "ssm_conv_b", "ssm_dt_bias", "ssm_a_log",
              "ssm_d", "ssm_norm_w", "ln1_g", "ln1_b", "ffn_conv_w", "ffn_conv_b", "ln2_g", "ln2_b"]
    shared = {n: f32(inp[n]) for n in wnames}
    shared["wst"] = build_wstream(inp)
    for k, v in consts.items():
        shared["c_" + k] = v
    in_maps = []
    for c in range(8):
        b = c % 2
        xin = np.concatenate([meta, xp[b, 0:TP], xs[c], xp[b, TP:]], axis=0)
        m = dict(shared)
        m["xin"] = np.ascontiguousarray(xin)
        m["st_hg"] = f32(inp["state_hgrn"][:, c])
        m["st_ssm"] = f32(inp["state_ssm"][:, c])
        m["st_conv"] = f32(inp["state_ssm_conv"][:, c])
        m["st_ffn"] = f32(inp["state_ffn_conv"][:, c])
        in_maps.append(m)
    res = run_bass_kernel_spmd(nc, in_maps, core_ids=list(range(8)))
    R = res.results
    ys = [r["y"] for r in R]
    y_prompt = np.stack([np.concatenate([ys[b][N_META:N_META + TP], ys[b][N_META + TP + DEC_SEQ:]], axis=0) for b in range(2)])
    y_sample = np.stack([ys[c][N_META + TP:N_META + TP + DEC_SEQ] for c in range(8)])
    hg_p = np.stack([R[b]["hg_o"][:, 0] for b in range(2)], axis=1)
    ssm_p = np.stack([R[b]["ssm_o"][:, 0] for b in range(2)], axis=1)
    conv_p = np.stack([R[b]["conv_o"][:, 0] for b in range(2)], axis=1)
    ffn_p = np.stack([R[b]["ffn_o"][:, 0] for b in range(2)], axis=1)
    hg_s = np.stack([R[c]["hg_o"][:, 1] for c in range(8)], axis=1)
    ssm_s = np.stack([R[c]["ssm_o"][:, 1] for c in range(8)], axis=1)
    conv_s = np.stack([R[c]["conv_o"][:, 1] for c in range(8)], axis=1)
    ffn_s = np.stack([R[c]["ffn_o"][:, 1] for c in range(8)], axis=1)
    outs = (y_prompt, y_sample, hg_p, ssm_p, conv_p, ffn_p, hg_s, ssm_s, conv_s, ffn_s)
    return tuple(np.ascontiguousarray(o, dtype=np.float32) for o in outs)
DBG_NAMES = ["d_on", "d_mga", "d_yn", "d_mg", "d_xmid", "d_v2", "d_hm"]
```

```python
import contextlib
import numpy as np
import concourse.bass as bass
import concourse.mybir as mybir
from concourse.bass_utils import run_bass_kernel_spmd

F32 = mybir.dt.float32
BF16 = mybir.dt.bfloat16
AF = mybir.ActivationFunctionType
ALU = mybir.AluOpType

D = 2048
NKC = 16
DEPTH = 2
SEQ = 4096
N_META = 16
DEC_SEQ = 32
HG_H = 16
SSM_H = 32
SSM_P = 64
SSM_N = 128
SSM_G = 4
XBC = 3072
DFF = 5632
NFF = 44
N_IN = 17440
O_Q, O_F, O_I, O_G, O_Z, O_XBC, O_DT, O_GA, O_GB = 0, 2048, 4096, 6144, 8192, 10240, 13312, 13344, 15392
ALPHA = (2.0 * DEPTH) ** 0.25
LN_EPS = 1e-5
RMS_EPS = 1e-6
BIG = 30000.0
TW = 560
TP = 512
NTILES = SEQ // TP
NTOK = N_META + SEQ + DEC_SEQ
MAXC = 30000


class Buf:
    __slots__ = ("t", "w", "r", "name")

    def __init__(self, t, name=""):
        self.t = t
        self.w = None
        self.r = {}
        self.name = name

    def __getitem__(self, k):
        return self.t[k]


class Eng:
    def __init__(self, name, handle):
        self.name = name
        self.h = handle
        self.sem = None
        self.count = 0
        self.known = {}


class KB:
    def __init__(self, nc, stack, n_dma_sems=16):
        self.nc = nc
        self.stack = stack
        self.sems = {}
        self.eng = {}
        self.nsem = 0
        for name, h in (("pe", nc.tensor), ("act", nc.scalar), ("dve", nc.vector),
                        ("pool", nc.gpsimd), ("sp", nc.sync)):
            E = Eng(name, h)
            self.eng[name] = E
            self._newsem(E)
        self.dma_slots = []
        for i in range(n_dma_sems):
            s = stack.enter_context(nc.semaphore("d%d" % i))
            self.sems[id(s)] = s
            self.dma_slots.append([s, 0])
        self.dma_i = 0
        self.n_wait = 0
        self.n_ins = 0
        self.names = set()

    def _newsem(self, E):
        s = self.stack.enter_context(self.nc.semaphore("s_%s_%d" % (E.name, self.nsem)))
        self.nsem += 1
        self.sems[id(s)] = s
        E.sem = s
        E.count = 0

    def uname(self, name):
        i = 0
        n = name
        while n in self.names:
            i += 1
            n = "%s_%d" % (name, i)
        self.names.add(n)
        return n

    def sb(self, name, shape, dt):
        name = self.uname(name)
        t = self.stack.enter_context(self.nc.sbuf_tensor(name, list(shape), dt))
        return Buf(t, name)

    def psum(self, name, shape, dt):
        t = self.stack.enter_context(self.nc.psum_tensor(name, list(shape), dt))
        return Buf(t, name)

    def _need(self, reads, writes):
        need = {}
        for b in reads:
            if b.w is not None:
                k, v = b.w
                if need.get(k, 0) < v:
                    need[k] = v
        for b in writes:
            if b.w is not None:
                k, v = b.w
                if need.get(k, 0) < v:
                    need[k] = v
            for k, v in b.r.items():
                if need.get(k, 0) < v:
                    need[k] = v
        return need

    def _wait(self, E, need, skip_own=False):
        for k, v in need.items():
            if skip_own and k == id(E.sem):
                continue
            if E.known.get(k, 0) < v:
                E.h.wait_ge(self.sems[k], v)
                E.known[k] = v
                self.n_wait += 1

    def _mark(self, tok, reads, writes):
        k, v = tok
        for b in reads:
            if b.r.get(k, 0) < v:
                b.r[k] = v
        for b in writes:
            b.w = tok
            b.r = {}

    def op(self, e, emit, reads=(), writes=(), inc=True):
        E = self.eng[e]
        if inc and E.count >= MAXC:
            self._newsem(E)
        self._wait(E, self._need(reads, writes), skip_own=(e == "pe"))
        ins = emit(E.h)
        self.n_ins += 1
        if inc:
            E.count += 1
            ins.then_inc(E.sem, 1)
            tok = (id(E.sem), E.count)
        else:
            tok = (id(E.sem), E.count + 1)
        self._mark(tok, reads, writes)
        return tok

    def dma(self, out, in_, reads=(), writes=(), q="sp", **kw):
        E = self.eng[q]
        slot = self.dma_slots[self.dma_i]
        self.dma_i = (self.dma_i + 1) % len(self.dma_slots)
        if slot[1] >= MAXC // 16:
            s = self.stack.enter_context(self.nc.semaphore("dx%d" % self.nsem))
            self.nsem += 1
            self.sems[id(s)] = s
            self._wait(E, {id(slot[0]): slot[1] * 16})
            slot[0] = s
            slot[1] = 0
        need = self._need(reads, writes)
        if slot[1] > 0:
            k = id(slot[0])
            need[k] = max(need.get(k, 0), slot[1] * 16)
        self._wait(E, need)
        E.h.dma_start(out=out, in_=in_, **kw).then_inc(slot[0], 16)
        self.n_ins += 1
        slot[1] += 1
        tok = (id(slot[0]), slot[1] * 16)
        self._mark(tok, reads, writes)
        return tok

    def finish(self, bufs=()):
        E = self.eng["sp"]
        need = {}
        for s, c in self.dma_slots:
            if c:
                need[id(s)] = c * 16
        for b in bufs:
            if b.w is not None:
                k, v = b.w
                need[k] = max(need.get(k, 0), v)
        self._wait(E, need)


def tile_defs(ntiles):
    tiles = []
    r = 0
    for ti in range(ntiles):
        if ti == 0:
            w = N_META + TP + DEC_SEQ
            segs = [(0, N_META + TP, 0), (N_META + TP, DEC_SEQ, 1)]
            chunks = [(N_META + TP, DEC_SEQ, 1), (0, N_META, 0)] + [(N_META + 64 * j, 64, 0) for j in range(TP // 64)]
        else:
            w = TP
            segs = [(0, TP, 0)]
            chunks = [(64 * j, 64, 0) for j in range(TP // 64)]
        pieces = []
        c = 0
        while c < w:
            pieces.append((c, min(c + 512, w)))
            c += 512
        tiles.append(dict(r0=r, w=w, segs=segs, chunks=chunks, pieces=pieces, idx=ti))
        r += w
    return tiles


def build_consts():
    c = {}
    c["identf"] = np.eye(128, dtype=np.float32)
    tri = (np.arange(128)[:, None] <= np.arange(128)[None, :]).astype(np.float32)
    c["tri"] = tri
    c["ones"] = np.ones((128, 128), np.float32)
    for L in (16, 32, 64):
        m = (np.arange(L)[None, :] < np.arange(L)[:, None]).astype(np.float32) * BIG
        c["bigm%d" % L] = np.ascontiguousarray(np.broadcast_to(m[:, None, :], (L, 8, L))).reshape(L, 8 * L)
    return c


def plan_pass(l):
    for h in range(HG_H):
        for o in (O_Q, O_F, O_I, O_G):
            yield ("in", l, 0, NKC, o + h * 128, 128)
    yield ("in", l, 0, NKC, O_DT, 32)
    for g in range(SSM_G):
        yield ("in", l, 0, NKC, O_XBC + 2048 + g * 128, 128)
        yield ("in", l, 0, NKC, O_XBC + 2560 + g * 128, 128)
        for pr in range(4):
            cc = g * 4 + pr
            yield ("in", l, 0, NKC, O_XBC + cc * 128, 128)
            yield ("in", l, 0, NKC, O_Z + cc * 128, 128)
        if g == 1:
            for c in range(16):
                yield ("in", l, 0, NKC, O_GA + c * 128, 128)
                yield ("pa", l, 0, NKC, c * 128, 128)
    for c in range(16):
        yield ("in", l, 0, NKC, O_GB + c * 128, 128)
        yield ("pb", l, 0, NKC, c * 128, 128)
    for c in range(16):
        yield ("out", l, 0, NKC, c * 128, 128)
    for j in range(NFF):
        yield ("up", l, 0, NKC, j * 128, 128)
        yield ("up", l, 0, NKC, DFF + j * 128, 128)
    for c in range(16):
        yield ("dn", l, 0, 16, c * 128, 128)
        yield ("dn", l, 16, 16, c * 128, 128)
        yield ("dn", l, 32, 12, c * 128, 128)


NBL = len(list(plan_pass(0)))
WNAME = {"in": "w_in", "pa": "w_proj_a", "pb": "w_proj_b", "out": "w_out", "up": "ffn_w_up", "dn": "ffn_w_down"}


def build_wstream(inp):
    wst = np.zeros((DEPTH, NBL, 128, NKC, 128), np.float32)
    for l in range(DEPTH):
        for j, (name, _, k0, nk, c0, ncol) in enumerate(plan_pass(l)):
            W = inp[WNAME[name]][l]
            blk = np.asarray(W[k0 * 128:(k0 + nk) * 128, c0:c0 + ncol], dtype=np.float32).reshape(nk, 128, ncol)
            wst[l, j, :, 0:nk, 0:ncol] = blk.transpose(1, 0, 2)
    return wst.reshape(DEPTH, NBL, 128, NKC * 128)


class _Stop(Exception):
    pass


def build_program(ntiles=NTILES, nlayers=DEPTH, debug=False, stop_at=None):
    nc = bass.Bass("TRN2", target_bir_lowering=False)
    tiles = tile_defs(ntiles)
    ntok = sum(t["w"] for t in tiles)

    def din(name, shape):
        return nc.dram_tensor(name, list(shape), F32, kind="ExternalInput").ap()

    def dout(name, shape):
        return nc.dram_tensor(name, list(shape), F32, kind="ExternalOutput").ap()

    xin = din("xin", [NTOK, D])
    st_hg = din("st_hg", [DEPTH, HG_H, 128, 128])
    st_ssm = din("st_ssm", [DEPTH, SSM_H, SSM_P, SSM_N])
    st_conv = din("st_conv", [DEPTH, 3, XBC])
    st_ffn = din("st_ffn", [DEPTH, 2, DFF])
    lb_logits = din("hgrn_lb_logits", [DEPTH, D])
    hg_nw = din("hgrn_norm_w", [DEPTH, D])
    cv_w = din("ssm_conv_w", [DEPTH, 4, XBC])
    cv_b = din("ssm_conv_b", [DEPTH, XBC])
    dt_bias = din("ssm_dt_bias", [DEPTH, SSM_H])
    a_log = din("ssm_a_log", [DEPTH, SSM_H])
    ssm_d = din("ssm_d", [DEPTH, SSM_H])
    ssm_nw = din("ssm_norm_w", [DEPTH, D])
    ln1_g = din("ln1_g", [DEPTH, D])
    ln1_b = din("ln1_b", [DEPTH, D])
    fcv_w = din("ffn_conv_w", [DEPTH, 3, DFF])
    fcv_b = din("ffn_conv_b", [DEPTH, DFF])
    ln2_g = din("ln2_g", [DEPTH, D])
    ln2_b = din("ln2_b", [DEPTH, D])
    wst = din("wst", [DEPTH, NBL, 128, NKC * 128])
    cdefs = build_consts()
    cin = {k: din("c_" + k, v.shape) for k, v in cdefs.items()}

    y_out = dout("y", [NTOK, D])
    hg_out = dout("hg_o", [DEPTH, 2, HG_H, 128, 128])
    ssm_out = dout("ssm_o", [DEPTH, 2, SSM_H, SSM_P, SSM_N])
    conv_out = dout("conv_o", [DEPTH, 2, 3, XBC])
    ffn_out = dout("ffn_o", [DEPTH, 2, 2, DFF])
    dbg = {}
    if debug:
        for nm in ("d_on", "d_mga", "d_yn", "d_mg", "d_xmid", "d_v2"):
            dbg[nm] = nc.dram_tensor(nm, [16, 128, TW], BF16, kind="ExternalOutput").ap()
        dbg["d_hm"] = nc.dram_tensor("d_hm", [NFF, 128, TW], BF16, kind="ExternalOutput").ap()


    with contextlib.ExitStack() as st:
        K = KB(nc, st)
        op = K.op

        xT = [K.sb("xT%d" % i, [128, TW], BF16) for i in range(NKC)]
        hm = [K.sb("hm%d" % i, [128, TW], BF16) for i in range(NFF)]
        on = hm[0:16]
        mg = hm[16:32]
        NT32 = 10
        tf = [K.sb("tf%d" % i, [128, TW + 40], F32) for i in range(NT32)]
        tb = [K.sb("tb%d" % i, [128, TW], BF16) for i in range(10)]
        NS, NB = 0, 10
        stage = [K.sb("stg%d" % i, [128, NKC, 128], F32) for i in range(NS)]
        wring = [K.sb("wb%d" % i, [128, NKC, 128], BF16) for i in range(NB)]
        psb = [K.psum("ps%d" % i, [128, 512], F32) for i in range(8)]
        ps_i = [0]

        def nps():
            b = psb[ps_i[0]]
            ps_i[0] = (ps_i[0] + 1) % 8
            return b

        Shg1 = K.sb("Shg", [128, 128], F32)
        Hss1 = K.sb("Hss", [128, 512], F32)
        Hsb1 = K.sb("Hsb", [128, 512], BF16)
        hg_c = Buf(nc.dram_tensor("hg_carry", [DEPTH, HG_H, 128, 128], F32).ap(), "hg_carry")
        ss_c = Buf(nc.dram_tensor("ss_carry", [DEPTH, SSM_G, 128, 512], F32).ap(), "ss_carry")
        halo_x = [K.sb("halox%d" % l, [128, 24, 3], F32) for l in range(DEPTH)]
        halo_f = [K.sb("halof%d" % l, [128, NFF, 2], F32) for l in range(DEPTH)]
        halo_xs = [K.sb("haloxs%d" % l, [128, 24, 3], F32) for l in range(DEPTH)]
        halo_fs = [K.sb("halofs%d" % l, [128, NFF, 2], F32) for l in range(DEPTH)]
        dram_y = Buf(y_out, "y")
        dram_hg = Buf(hg_out, "hgo")
        dram_ss = Buf(ssm_out, "sso")
        dram_cv = Buf(conv_out, "cvo")
        dram_ff = Buf(ffn_out, "ffo")

        cb = {}
        for k, v in cdefs.items():
            b = K.sb("sc_" + k, [128, v.shape[1]], F32)
            K.dma(b[0:v.shape[0], :], cin[k][:, :], writes=[b])
            cb[k] = b
        identb = K.sb("identb", [128, 128], BF16)
        onesb = K.sb("onesb", [128, 128], BF16)
        trib = K.sb("trib", [128, 128], BF16)
        negtri = K.sb("negtri", [128, 128], F32)
        op("dve", lambda e: e.tensor_copy(identb[:], cb["identf"][:]), reads=[cb["identf"]], writes=[identb])
        op("dve", lambda e: e.tensor_copy(onesb[:], cb["ones"][:]), reads=[cb["ones"]], writes=[onesb])
        op("dve", lambda e: e.tensor_copy(trib[:], cb["tri"][:]), reads=[cb["tri"]], writes=[trib])
        op("dve", lambda e: e.tensor_scalar(negtri[:], cb["tri"][:], -1.0, None, ALU.mult), reads=[cb["tri"]], writes=[negtri])
        identf, onesf, trif = cb["identf"], cb["ones"], cb["tri"]
        one_col = K.sb("one_col", [128, 1], F32)
        op("dve", lambda e: e.memset(one_col[:], 1.0), writes=[one_col])

        def load_vec(name, ap, nch, extra=None):
            b = K.sb("pv_" + name, [128, DEPTH, nch] if extra is None else [128, DEPTH, extra, nch], F32)
            for l in range(DEPTH):
                if extra is None:
                    K.dma(b[:, l, :], ap[l, :].rearrange("(c p) -> p c", p=128), writes=[b],
                          allow_slow_non_contiguous=True)
                else:
                    for k in range(extra):
                        K.dma(b[:, l, k, :], ap[l, k, :].rearrange("(c p) -> p c", p=128), writes=[b],
                              allow_slow_non_contiguous=True)
            return b

        pv_lbl = load_vec("lbl", lb_logits, 16)
        pv_hgnw = load_vec("hgnw", hg_nw, 16)
        pv_cvw = load_vec("cvw", cv_w, 24, extra=4)
        pv_cvb = load_vec("cvb", cv_b, 24)
        pv_snw = load_vec("snw", ssm_nw, 16)
        pv_l1g = load_vec("l1g", ln1_g, 16)
        pv_l1b = load_vec("l1b", ln1_b, 16)
        pv_fcw = load_vec("fcw", fcv_w, NFF, extra=3)
        pv_fcb = load_vec("fcb", fcv_b, NFF)
        pv_l2g = load_vec("l2g", ln2_g, 16)
        pv_l2b = load_vec("l2b", ln2_b, 16)
        hv = {}
        for name, ap in (("dtb", dt_bias), ("alog", a_log), ("dsk", ssm_d)):
            b = K.sb("hv_" + name, [SSM_H, DEPTH], F32)
            K.dma(b[:], ap.rearrange("l h -> h l"), writes=[b], allow_slow_non_contiguous=True)
            hv[name] = b
        hvA = K.sb("hv_A", [SSM_H, DEPTH], F32)
        op("act", lambda e: e.activation(hvA[:], hv["alog"][:], AF.Exp), reads=[hv["alog"]], writes=[hvA])
        pv_lb = K.sb("pv_lb", [128, DEPTH, 16], F32)
        pv_oml = K.sb("pv_oml", [128, DEPTH, 16], F32)
        lbe = K.sb("lbe", [128, DEPTH, 16], F32)
        lbs = K.sb("lbs", [128, 16], F32)
        op("act", lambda e: e.activation(lbe[:], pv_lbl[:], AF.Exp), reads=[pv_lbl], writes=[lbe])
        op("dve", lambda e: e.tensor_copy(lbs[:], lbe[:, 0, :]), reads=[lbe], writes=[lbs])
        for l in range(1, DEPTH):
            op("dve", lambda e: e.tensor_tensor(lbs[:], lbs[:], lbe[:, l, :], ALU.add), reads=[lbe, lbs], writes=[lbs])
        op("dve", lambda e: e.reciprocal(lbs[:], lbs[:]), reads=[lbs], writes=[lbs])
        op("dve", lambda e: e.memset(pv_lb[:, 0, :], 0.0), writes=[pv_lb])
        for l in range(1, DEPTH):
            op("dve", lambda e: e.tensor_tensor(pv_lb[:, l, :], lbe[:, l, :], lbs[:], ALU.mult), reads=[lbe, lbs], writes=[pv_lb])
            if l > 1:
                op("dve", lambda e: e.tensor_tensor(pv_lb[:, l, :], pv_lb[:, l, :], pv_lb[:, l - 1, :], ALU.add),
                   reads=[pv_lb], writes=[pv_lb])
        op("dve", lambda e: e.tensor_scalar(pv_oml[:], pv_lb[:], -1.0, 1.0, ALU.mult, ALU.add), reads=[pv_lb], writes=[pv_oml])
        pv_dsk = K.sb("pv_dsk", [128, DEPTH, 16], F32)
        for l in range(DEPTH):
            for two in range(2):
                src = ssm_d[l:l + 1, :].rearrange("o (c two) -> o c two", two=2)[:, :, two]
                K.dma(pv_dsk[two * 64:(two + 1) * 64, l, :], src.to_broadcast([64, 16]), writes=[pv_dsk],
                      allow_slow_non_contiguous=True)
        for l in range(DEPTH):
            op("dve", lambda e: e.memset(halo_x[l][:], 0.0), writes=[halo_x[l]])
            op("dve", lambda e: e.memset(halo_f[l][:], 0.0), writes=[halo_f[l]])
            for k in range(3):
                K.dma(halo_xs[l][:, :, k], st_conv[l, k, :].rearrange("(c p) -> p c", p=128), writes=[halo_xs[l]],
                      allow_slow_non_contiguous=True)
            for k in range(2):
                K.dma(halo_fs[l][:, :, k], st_ffn[l, k, :].rearrange("(c p) -> p c", p=128), writes=[halo_fs[l]],
                      allow_slow_non_contiguous=True)

        sched = []

        def plan():
            for ti in range(ntiles):
                for l in range(nlayers):
                    yield from plan_pass(l)

        sched = list(plan())
        ws = dict(issued=0, got=0)
        cast_rr = [0]

        def ws_issue():
            i = ws["issued"]
            name, l, k0, nk, c0, ncol = sched[i]
            wb = wring[i % NB]
            j = i % NBL
            K.dma(wb[:, 0:nk, :].rearrange("p k c -> p (k c)"), wst[l, j, :, 0:nk * 128], writes=[wb], q="pool")
            ws["issued"] += 1

        def wget(spec):
            i = ws["got"]
            assert sched[i] == spec, (i, sched[i], spec)
            while ws["issued"] < min(len(sched), i + NB - 2):
                ws_issue()
            ws["got"] += 1
            return wring[i % NB]

        def gemm(wspecs, srcs, tile, evac, m=128):
            wbs = [(wget(s), s[3]) for s in wspecs]
            for (c0, c1) in tile["pieces"]:
                p = nps()
                kk = 0
                nk_tot = sum(n for _, n in wbs)
                for wb, nk in wbs:
                    for k in range(nk):
                        src = srcs[kk]
                        last = (kk == nk_tot - 1)
                        op("pe", lambda e: e.matmul(p[0:m, 0:c1 - c0], lhsT=wb[:, k, 0:m], rhs=src[:, c0:c1],
                                                    start=(kk == 0), stop=last),
                           reads=[wb, src], writes=[p], inc=last)
                        kk += 1
                evac(p, c0, c1)

        def act(out, in_, func, reads, writes, **kw):
            op("act", lambda e: e.activation(out, in_, func, **kw), reads=reads, writes=writes)

        ATb = [K.sb("ATb%d" % i, [64, 64], BF16) for i in range(2)]
        vtok = [K.sb("vtok%d" % i, [64, 128], BF16) for i in range(2)]
        ktok = [K.sb("ktok%d" % i, [64, 128], BF16) for i in range(2)]
        Spb = [K.sb("Spb%d" % i, [128, 128], BF16) for i in range(2)]
        rr = K.sb("rr", [128, 3, 16], F32)
        rrd = K.sb("rrd", [128, 3, 16], F32)
        st_mean = K.sb("st_mean", [128, 512], F32)
        st_rstd = K.sb("st_rstd", [128, 512], F32)
        st_tmp = K.sb("st_tmp", [128, 512], F32)
        sqr = [K.sb("sqr%d" % i, [128, 512], BF16) for i in range(2)]
        pre_s = K.sb("pre_s", [128, 40], F32)
        acc_s = K.sb("acc_s", [128, 40], F32)
        dtT = tf[4]
        adtT = tf[5]
        NCH = 10
        dta = K.sb("dta", [64, NCH, 64], F32)
        dec_tok = K.sb("dec_tok", [64, NCH, 32], F32)
        w_tok = K.sb("w_tok", [64, NCH, 32], F32)
        rtot_bc = K.sb("rtot_bc", [128, NCH, 32], F32)
        gendS = K.sb("gendS", [128, 32], F32)
        wtmp = K.sb("wtmp", [64, 32], F32)
        gtokS = K.sb("gtokS", [64, 32], F32)
        xdt = K.sb("xdt", [64, 512], BF16)
        xw = K.sb("xw", [64, 512], BF16)
        Btok = K.sb("Btok", [64, 128], BF16)
        cbt = K.sb("cbt", [64, 64], F32)
        R1 = K.sb("R1", [64, 512], F32)
        R2 = K.sb("R2", [64, 512], F32)
        LT = R1
        scT = K.sb("scT", [64, 512], BF16)
        ytmp = R2
        ytok = K.sb("ytok", [64, 512], BF16)
        yT4 = tf[0:4]
        sttok = K.sb("sttok", [128, 128], F32)

        def seg_evac(tile, p, c0, c1, pre, hw):
            for (s0, sl, sq) in tile["segs"]:
                a, b = max(c0, s0), min(c1, s0 + sl)
                if a >= b:
                    continue
                dstb = pre if sq == 0 else pre_s
                dst = dstb[:, hw + a - s0: hw + b - s0]
                op("act", lambda e: e.copy(dst, p[:, a - c0:b - c0]), reads=[p], writes=[dstb])

        def conv_apply(tile, pre, halo, halo_sq, cidx, wfun, bap, width, dst, func):
            hw = width - 1
            for (s0, sl, sq) in tile["segs"]:
                pb = pre if sq == 0 else pre_s
                hb = halo if sq == 0 else halo_sq
                ab = tf[9] if sq == 0 else acc_s
                op("dve", lambda e: e.tensor_copy(pb[:, 0:hw], hb[:, cidx, :]), reads=[hb], writes=[pb])
                op("dve", lambda e: e.tensor_scalar(ab[:, 0:sl], pb[:, hw:hw + sl], wfun(hw), bap, ALU.mult, ALU.add),
                   reads=[pb], writes=[ab])
                for k in range(hw):
                    op("dve", lambda e: e.scalar_tensor_tensor(ab[:, 0:sl], pb[:, k:k + sl], wfun(k), ab[:, 0:sl],
                                                               ALU.mult, ALU.add), reads=[pb, ab], writes=[ab])
                act(dst[:, s0:s0 + sl], ab[:, 0:sl], func, [ab], [dst])
                op("dve", lambda e: e.tensor_copy(hb[:, cidx, :], pb[:, sl:sl + hw]), reads=[pb], writes=[hb])

        def bc_sum(srcs_fn, nsrc, c0, c1):
            p = nps()
            for i in range(nsrc):
                sbuf, sap = srcs_fn(i)
                op("pe", lambda e: e.matmul(p[:, 0:c1 - c0], lhsT=onesb[:], rhs=sap, start=(i == 0), stop=(i == nsrc - 1)),
                   reads=[onesb, sbuf], writes=[p], inc=(i == nsrc - 1))
            return p

        def rstd_from(p, w, scale, eps, dst):
            act(st_tmp[:, 0:w], p[:, 0:w], AF.Sqrt, [p], [st_tmp], bias=eps_col(eps), scale=scale)
            op("dve", lambda e: e.reciprocal(dst[:, 0:w], st_tmp[:, 0:w]), reads=[st_tmp], writes=[dst])

        eps_cols = {}

        def eps_col(v):
            if v not in eps_cols:
                b = K.sb("epsc%d" % len(eps_cols), [128, 1], F32)
                op("dve", lambda e: e.memset(b[:], float(v)), writes=[b])
                eps_cols[v] = b
            return eps_cols[v][:, 0:1]

        eps_col(RMS_EPS)
        eps_col(LN_EPS)

        def gemm_g(wspecs, srcs, tile, evac, m=128):
            wbs = [(wget(s), s[3]) for s in wspecs]
            for (c0, c1) in tile["pieces"]:
                p = nps()
                kk = 0
                nk_tot = sum(n for _, n in wbs)
                for wb, nk in wbs:
                    for k in range(nk):
                        src = srcs[kk]
                        last = (kk == nk_tot - 1)
                        op("pe", lambda e: e.matmul(p[0:m, 0:c1 - c0], lhsT=wb[:, k, 0:m], rhs=src[:, c0:c1],
                                                    start=(kk == 0), stop=last),
                           reads=[wb, src], writes=[p], inc=last)
                        kk += 1
                evac(p, c0, c1)
                yield

        HS = []
        for s_ in range(2):
            HS.append(dict(
                qs=tf[4 * s_ + 0], fk=tf[4 * s_ + 1], ln=tf[4 * s_ + 2], G=tf[4 * s_ + 3],
                vT=tb[4 * s_ + 0], gs=tb[4 * s_ + 1], qt=tb[4 * s_ + 2], kt=tb[4 * s_ + 3],
                S=K.sb("ShgS%d" % s_, [128, 128], F32), rr=K.sb("rrS%d" % s_, [128, 3, 16], F32),
                rrd=K.sb("rrdS%d" % s_, [128, 3, 16], F32),
                ATb=[K.sb("ATbS%d_%d" % (s_, i), [64, 64], BF16) for i in range(2)],
                vtok=[K.sb("vtokS%d_%d" % (s_, i), [64, 128], BF16) for i in range(2)],
                ktok=[K.sb("ktokS%d_%d" % (s_, i), [64, 128], BF16) for i in range(2)],
                Spb=[K.sb("SpbS%d_%d" % (s_, i), [128, 128], BF16) for i in range(2)],
                rstd=K.sb("rstdS%d" % s_, [128, 512], F32)))

        def hg_prep(tile, l, h, B):
            T = tile["w"]
            qs, fk, lnf, GnS = B["qs"], B["fk"], B["ln"], B["G"]
            Dd, E2 = B["ln"], B["G"]
            vT, gs, qt, kt = B["vT"], B["gs"], B["qt"], B["kt"]
            rr_, rrd_ = B["rr"], B["rrd"]
            specs = [("in", l, 0, NKC, o + h * 128, 128) for o in (O_Q, O_F, O_I, O_G)]
            yield from gemm_g([specs[0]], xT, tile, lambda p, c0, c1: act(qs[:, c0:c1], p[:, 0:c1 - c0], AF.Silu, [p], [qs]))
            yield from gemm_g([specs[1]], xT, tile, lambda p, c0, c1: act(fk[:, c0:c1], p[:, 0:c1 - c0], AF.Sigmoid, [p], [fk]))
            yield from gemm_g([specs[2]], xT, tile, lambda p, c0, c1: op("act", lambda e: e.copy(vT[:, c0:c1], p[:, 0:c1 - c0]),
                                                                         reads=[p], writes=[vT]))
            yield from gemm_g([specs[3]], xT, tile, lambda p, c0, c1: act(gs[:, c0:c1], p[:, 0:c1 - c0], AF.Silu, [p], [gs]))
            op("dve", lambda e: e.tensor_scalar(fk[:, 0:T], fk[:, 0:T], pv_oml[:, l, h:h + 1], pv_lb[:, l, h:h + 1],
                                                ALU.mult, ALU.add), reads=[fk, pv_oml, pv_lb], writes=[fk])
            act(lnf[:, 0:T], fk[:, 0:T], AF.Ln, [fk], [lnf])
            op("dve", lambda e: e.tensor_scalar(fk[:, 0:T], fk[:, 0:T], -1.0, 1.0, ALU.mult, ALU.add), reads=[fk], writes=[fk])
            op("dve", lambda e: e.memset(GnS[:, 0:1], 0.0), writes=[GnS])
            op("dve", lambda e: e.tensor_tensor_scan(GnS[:, 1:T + 1], one_col[:, 0:1].to_broadcast([128, T]), lnf[:, 0:T],
                                                     0.0, ALU.mult, ALU.add), reads=[one_col, lnf], writes=[GnS])
            yield
            runs = []
            for ci, (a, L, sq) in enumerate(tile["chunks"]):
                if runs and runs[-1][2] == L and runs[-1][0] + runs[-1][1] * L == a:
                    runs[-1][1] += 1
                else:
                    runs.append([a, 1, L, ci])
            for (a, n, L, ci0) in runs:
                hl = L // 2
                gv = lambda off: GnS[:, a + off:a + off + n * L].rearrange("p (j l) -> p j l", l=L)
                ref = gv(hl)[:, :, 0:1]
                sta = gv(0)[:, :, 0:1]
                end = gv(L)[:, :, 0:1]
                op("dve", lambda e: e.tensor_tensor(Dd[:, a:a + n * L].rearrange("p (j l) -> p j l", l=L), gv(1),
                                                    ref.to_broadcast([128, n, L]), ALU.subtract), reads=[GnS], writes=[Dd])
                r3 = lambda k: rrd_[:, k, ci0:ci0 + n].unsqueeze(2)
                op("dve", lambda e: e.tensor_tensor(r3(0), ref, sta, ALU.subtract), reads=[GnS], writes=[rrd_])
                op("dve", lambda e: e.tensor_tensor(r3(1), end, ref, ALU.subtract), reads=[GnS], writes=[rrd_])
                op("dve", lambda e: e.tensor_tensor(r3(2), end, sta, ALU.subtract), reads=[GnS], writes=[rrd_])
            nchk = len(tile["chunks"])
            act(rr_[:, :, 0:nchk], rrd_[:, :, 0:nchk], AF.Exp, [rrd_], [rr_])
            act(E2[:, 0:T], Dd[:, 0:T], AF.Exp, [Dd], [E2], scale=-1.0)
            act(Dd[:, 0:T], Dd[:, 0:T], AF.Exp, [Dd], [Dd])
            yield
            op("dve", lambda e: e.tensor_tensor(qt[:, 0:T], qs[:, 0:T], Dd[:, 0:T], ALU.mult), reads=[qs, Dd], writes=[qt])
            op("dve", lambda e: e.tensor_tensor(kt[:, 0:T], fk[:, 0:T], E2[:, 0:T], ALU.mult), reads=[fk, E2], writes=[kt])
            yield

        def hg_loop(tile, l, h, B, first_tile, last_tile):
            T = tile["w"]
            oT = B["qs"]
            vT, gs, qt, kt = B["vT"], B["gs"], B["qt"], B["kt"]
            rr_ = B["rr"]
            S = B["S"]
            cur_seq = None
            for ci, (a, L, sq) in enumerate(tile["chunks"]):
                if sq != cur_seq:
                    if cur_seq is not None:
                        store_hg(l, h, cur_seq, last_tile, S)
                    if sq == 0:
                        if first_tile:
                            op("dve", lambda e: e.memset(S[:], 0.0), writes=[S])
                        else:
                            K.dma(S[:], hg_c[l, h, :, :], reads=[hg_c], writes=[S])
                    else:
                        K.dma(S[:], st_hg[l, h, :, :], writes=[S])
                    cur_seq = sq
                i2 = ci % 2
                ATb_, vtok_, ktok_, Spb_ = B["ATb"][i2], B["vtok"][i2], B["ktok"][i2], B["Spb"][i2]
                p1 = nps()
                op("pe", lambda e: e.matmul(p1[0:L, 0:L], lhsT=kt[:, a:a + L], rhs=qt[:, a:a + L], start=True, stop=True),
                   reads=[kt, qt], writes=[p1])
                p2 = nps()
                p2b = p2[:, 0:256].bitcast(BF16)
                op("pe", lambda e: e.transpose(p2b[0:L, 0:128], vT[:, a:a + L], identb[:]), reads=[vT, identb], writes=[p2])
                p3 = nps()
                p3b = p3[:, 0:256].bitcast(BF16)
                op("pe", lambda e: e.transpose(p3b[0:L, 0:128], kt[:, a:a + L], identb[:]), reads=[kt, identb], writes=[p3])
                op("dve", lambda e: e.tensor_tensor(ATb_[0:L, 0:L], p1[0:L, 0:L], trif[0:L, 0:L], ALU.mult),
                   reads=[p1, trif], writes=[ATb_])
                op("act", lambda e: e.copy(vtok_[0:L, :], p2b[0:L, 0:128]), reads=[p2], writes=[vtok_])
                op("act", lambda e: e.copy(ktok_[0:L, :], p3b[0:L, 0:128]), reads=[p3], writes=[ktok_])
                op("dve", lambda e: e.tensor_scalar(Spb_[:], S[:], rr_[:, 0, ci:ci + 1], None, ALU.mult),
                   reads=[S, rr_], writes=[Spb_])
                yield
                p4 = nps()
                op("pe", lambda e: e.matmul(p4[:, 0:L], lhsT=vtok_[0:L, :], rhs=ATb_[0:L, 0:L], start=True, stop=False),
                   reads=[vtok_, ATb_], writes=[p4], inc=False)
                op("pe", lambda e: e.matmul(p4[:, 0:L], lhsT=Spb_[:], rhs=qt[:, a:a + L], start=False, stop=True),
                   reads=[Spb_, qt], writes=[p4])
                p5 = nps()
                op("pe", lambda e: e.matmul(p5[:, 0:128], lhsT=ktok_[0:L, :], rhs=vtok_[0:L, :], start=True, stop=True),
                   reads=[ktok_, vtok_], writes=[p5])
                op("act", lambda e: e.copy(oT[:, a:a + L], p4[:, 0:L]), reads=[p4], writes=[oT])
                op("dve", lambda e: e.tensor_scalar(S[:], S[:], rr_[:, 2, ci:ci + 1], None, ALU.mult), reads=[S, rr_], writes=[S])
                op("dve", lambda e: e.scalar_tensor_tensor(S[:], p5[:, 0:128], rr_[:, 1, ci:ci + 1], S[:], ALU.mult, ALU.add),
                   reads=[p5, rr_, S], writes=[S])
                yield
            store_hg(l, h, cur_seq, last_tile, S)
            sq_b = B["kt"]
            rstd_ = B["rstd"]
            act(sq_b[:, 0:T], oT[:, 0:T], AF.Square, [oT], [sq_b])
            yield
            for (c0, c1) in tile["pieces"]:
                w = c1 - c0
                p = bc_sum(lambda i: (sq_b, sq_b[:, c0:c1]), 1, c0, c1)
                rstd_from(p, w, 1.0 / 128, RMS_EPS, rstd_)
                yield
                op("dve", lambda e: e.scalar_tensor_tensor(oT[:, c0:c1], oT[:, c0:c1], pv_hgnw[:, l, h:h + 1], rstd_[:, 0:w],
                                                           ALU.mult, ALU.mult), reads=[oT, pv_hgnw, rstd_], writes=[oT])
                op("dve", lambda e: e.tensor_tensor(on[h][:, c0:c1], oT[:, c0:c1], gs[:, c0:c1], ALU.mult),
                   reads=[oT, gs], writes=[on[h]])
                yield

        def hgrn_all(tile, l, first_tile, last_tile):
            for _ in hg_prep(tile, l, 0, HS[0]):
                pass
            for h in range(HG_H):
                A = hg_loop(tile, l, h, HS[h % 2], first_tile, last_tile)
                Bg = hg_prep(tile, l, h + 1, HS[(h + 1) % 2]) if h + 1 < HG_H else None
                a_alive, b_alive = True, Bg is not None
                while a_alive or b_alive:
                    if a_alive:
                        try:
                            next(A)
                        except StopIteration:
                            a_alive = False
                    if b_alive:
                        try:
                            next(Bg)
                        except StopIteration:
                            b_alive = False

        def store_hg(l, h, sq, last_tile, S):
            if sq == 0:
                if last_tile:
                    K.dma(hg_out[l, 0, h, :, :], S[:], reads=[S], writes=[dram_hg])
                else:
                    K.dma(hg_c[l, h, :, :], S[:], reads=[S], writes=[hg_c])
            else:
                K.dma(hg_out[l, 1, h, :, :], S[:], reads=[S], writes=[dram_hg])

        def gated_proj(tile, l, srcs, gate_off, wname, accumulate):
            for c in range(16):
                sg = tf[0]
                gemm([("in", l, 0, NKC, gate_off + c * 128, 128)], xT, tile,
                     lambda p, c0, c1: act(sg[:, c0:c1], p[:, 0:c1 - c0], AF.Sigmoid, [p], [sg]))

                def ev(p, c0, c1):
                    if not accumulate:
                        op("dve", lambda e: e.tensor_tensor(mg[c][:, c0:c1], p[:, 0:c1 - c0], sg[:, c0:c1], ALU.mult),
                           reads=[p, sg], writes=[mg[c]])
                    else:
                        op("dve", lambda e: e.tensor_tensor(sg[:, c0:c1], p[:, 0:c1 - c0], sg[:, c0:c1], ALU.mult),
                           reads=[p, sg], writes=[sg])
                        op("dve", lambda e: e.tensor_tensor(mg[c][:, c0:c1], mg[c][:, c0:c1], sg[:, c0:c1], ALU.add),
                           reads=[mg[c], sg], writes=[mg[c]])
                gemm([(wname, l, 0, NKC, c * 128, 128)], srcs, tile, ev)

        def ss_load(l, g, sq, first_tile):
            H = Hss1
            if sq == 0:
                if first_tile:
                    op("dve", lambda e: e.memset(H[:], 0.0), writes=[H])
                else:
                    K.dma(H[:], ss_c[l, g, :, :], reads=[ss_c], writes=[H])
            else:
                for pr in range(4):
                    hh = (g * 4 + pr) * 2
                    K.dma(sttok[:], st_ssm[l, hh:hh + 2, :, :].rearrange("h p n -> (h p) n"), writes=[sttok])
                    p = nps()
                    op("pe", lambda e: e.transpose(p[:, 0:128], sttok[:], identf[:]), reads=[sttok, identf], writes=[p])
                    op("dve", lambda e: e.tensor_copy(H[:, pr * 128:(pr + 1) * 128], p[:, 0:128]), reads=[p], writes=[H])
            op("act", lambda e: e.copy(Hsb1[:], H[:]), reads=[H], writes=[Hsb1])

        def ss_store(l, g, sq, last_tile):
            H = Hss1
            if sq == 0 and not last_tile:
                K.dma(ss_c[l, g, :, :], H[:], reads=[H], writes=[ss_c])
                return
            for pr in range(4):
                hh = (g * 4 + pr) * 2
                p = nps()
                op("pe", lambda e: e.transpose(p[:, 0:128], H[:, pr * 128:(pr + 1) * 128], identf[:]),
                   reads=[H, identf], writes=[p])
                op("dve", lambda e: e.tensor_copy(sttok[:], p[:, 0:128]), reads=[p], writes=[sttok])
                K.dma(ssm_out[l, sq, hh:hh + 2, :, :].rearrange("h p n -> (h p) n"), sttok[:], reads=[sttok], writes=[dram_ss])

        def ssd(tile, l, first_tile, last_tile):
            T = tile["w"]
            chunks = tile["chunks"]
            wdt = ("in", l, 0, NKC, O_DT, 32)

            def ev_dt(p, c0, c1):
                act(dtT[0:32, c0:c1], p[0:32, 0:c1 - c0], AF.Exp, [p], [dtT], bias=hv["dtb"][:, l:l + 1])
            gemm([wdt], xT, tile, ev_dt, m=32)
            act(dtT[0:32, 0:T], dtT[0:32, 0:T], AF.Ln, [dtT], [dtT], bias=one_col[0:32, 0:1])
            op("dve", lambda e: e.tensor_scalar(adtT[0:32, 0:T], dtT[0:32, 0:T], hvA[:, l:l + 1], None, ALU.mult),
               reads=[dtT, hvA], writes=[adtT])
            if stop_at == "ssd_dt0":
                raise _Stop()
            for ci, (a, L, sq) in enumerate(chunks):
                p = nps()
                op("pe", lambda e: e.matmul(p[0:L, 0:32], lhsT=dtT[0:32, a:a + L], rhs=identf[0:32, 0:32], start=True, stop=True),
                   reads=[dtT, identf], writes=[p], inc=False)
                op("pe", lambda e: e.matmul(p[0:L, 32:64], lhsT=adtT[0:32, a:a + L], rhs=identf[0:32, 0:32], start=True, stop=True),
                   reads=[adtT, identf], writes=[p])
                op("dve", lambda e: e.tensor_copy(dta[0:L, ci, :], p[0:L, 0:64]), reads=[p], writes=[dta])
                p2 = nps()
                op("pe", lambda e: e.matmul(p2[0:L, 0:32], lhsT=trif[0:L, 0:L], rhs=dta[0:L, ci, 32:64], start=True, stop=True),
                   reads=[trif, dta], writes=[p2])
                p2x = nps()
                op("pe", lambda e: e.matmul(p2x[:, 0:32], lhsT=onesf[0:L, :], rhs=dta[0:L, ci, 32:64], start=True, stop=True),
                   reads=[onesf, dta], writes=[p2x])
                op("dve", lambda e: e.tensor_copy(gtokS[0:L, :], p2[0:L, 0:32]), reads=[p2], writes=[gtokS])
                op("dve", lambda e: e.tensor_copy(gendS[:], p2x[:, 0:32]), reads=[p2x], writes=[gendS])
                act(dec_tok[0:L, ci, :], gtokS[0:L, :], AF.Exp, [gtokS], [dec_tok], scale=-1.0)
                act(rtot_bc[:, ci, :], gendS[:], AF.Exp, [gendS], [rtot_bc], scale=-1.0)
                op("dve", lambda e: e.tensor_tensor(wtmp[0:L, :], gtokS[0:L, :], gendS[0:L, :], ALU.subtract),
                   reads=[gtokS, gendS], writes=[wtmp])
                act(wtmp[0:L, :], wtmp[0:L, :], AF.Exp, [wtmp], [wtmp])
                op("dve", lambda e: e.tensor_tensor(w_tok[0:L, ci, :], wtmp[0:L, :], dta[0:L, ci, 0:32], ALU.mult),
                   reads=[wtmp, dta], writes=[w_tok])


        SSB = [dict(BT=tb[0], CT=tb[1], xc=tb[2:6], sz=tb[6:10]),
               dict(BT=hm[32], CT=hm[33], xc=hm[34:38], sz=hm[38:42])]

        def ssd_prep(tile, l, g, SB):
            BT, CT, xc, sz = SB["BT"], SB["CT"], SB["xc"], SB["sz"]
            pre = tf[8]
            for which, dstb in ((0, BT), (1, CT)):
                cidx = 16 + which * 4 + g
                yield from gemm_g([("in", l, 0, NKC, O_XBC + 2048 + which * 512 + g * 128, 128)], xT, tile,
                                  lambda p, c0, c1: seg_evac(tile, p, c0, c1, pre, 3))
                conv_apply(tile, pre, halo_x[l], halo_xs[l], cidx, lambda k: pv_cvw[:, l, k, cidx:cidx + 1],
                           pv_cvb[:, l, cidx:cidx + 1], 4, dstb, AF.Silu)
                yield
            for pr in range(4):
                cc = g * 4 + pr
                yield from gemm_g([("in", l, 0, NKC, O_XBC + cc * 128, 128)], xT, tile,
                                  lambda p, c0, c1: seg_evac(tile, p, c0, c1, pre, 3))
                conv_apply(tile, pre, halo_x[l], halo_xs[l], cc, lambda k: pv_cvw[:, l, k, cc:cc + 1],
                           pv_cvb[:, l, cc:cc + 1], 4, xc[pr], AF.Silu)
                yield
                yield from gemm_g([("in", l, 0, NKC, O_Z + cc * 128, 128)], xT, tile,
                                  lambda p, c0, c1: act(sz[pr][:, c0:c1], p[:, 0:c1 - c0], AF.Silu, [p], [sz[pr]]))

        def ssd_chunks(tile, l, g, SB, first_tile, last_tile):
            BT, CT, xc, sz = SB["BT"], SB["CT"], SB["xc"], SB["sz"]
            chunks = tile["chunks"]
            cur_seq = None
            H, Hb = Hss1, Hsb1
            for ci, (a, L, sq) in enumerate(chunks):
                if sq != cur_seq:
                    if cur_seq is not None:
                        ss_store(l, g, cur_seq, last_tile)
                    ss_load(l, g, sq, first_tile)
                    cur_seq = sq
                L8 = 8 * L
                gs8 = slice(g * 8, (g + 1) * 8)
                px = nps()
                pxb = px[:, 0:256].bitcast(BF16)
                for pr in range(4):
                    op("pe", lambda e: e.transpose(pxb[0:L, pr * 128:(pr + 1) * 128], xc[pr][:, a:a + L], identb[:]),
                       reads=[xc[pr], identb], writes=[px], inc=(pr == 3))
                op("dve", lambda e: e.tensor_tensor(xdt[0:L, :].rearrange("s (h p) -> s h p", p=64),
                                                    pxb[0:L, 0:512].rearrange("s (h p) -> s h p", p=64),
                                                    dta[0:L, ci, gs8].unsqueeze(2).to_broadcast([L, 8, 64]), ALU.mult),
                   reads=[px, dta], writes=[xdt])
                op("dve", lambda e: e.tensor_tensor(xw[0:L, :].rearrange("s (h p) -> s h p", p=64),
                                                    pxb[0:L, 0:512].rearrange("s (h p) -> s h p", p=64),
                                                    w_tok[0:L, ci, gs8].unsqueeze(2).to_broadcast([L, 8, 64]), ALU.mult),
                   reads=[px, w_tok], writes=[xw])
                pB = nps()
                pBb = pB[:, 0:256].bitcast(BF16)
                op("pe", lambda e: e.transpose(pBb[0:L, 0:128], BT[:, a:a + L], identb[:]), reads=[BT, identb], writes=[pB])
                op("act", lambda e: e.copy(Btok[0:L, :], pBb[0:L, 0:128]), reads=[pB], writes=[Btok])
                pC = nps()
                op("pe", lambda e: e.matmul(pC[0:L, 0:L], lhsT=BT[:, a:a + L], rhs=CT[:, a:a + L], start=True, stop=True),
                   reads=[BT, CT], writes=[pC])
                op("act", lambda e: e.copy(cbt[0:L, 0:L], pC[0:L, 0:L]), reads=[pC], writes=[cbt])
                adt8 = dta[0:L, ci, 32 + g * 8:32 + (g + 1) * 8].unsqueeze(2).to_broadcast([L, 8, L])
                op("dve", lambda e: e.tensor_tensor(R1[0:L, 0:L8].rearrange("s (h t) -> s h t", t=L), adt8,
                                                    trif[0:L, 0:L].unsqueeze(1).to_broadcast([L, 8, L]), ALU.mult),
                   reads=[dta, trif], writes=[R1])
                op("dve", lambda e: e.tensor_copy(R2[0:L, 0:L8].rearrange("s (h t) -> s h t", t=L), adt8),
                   reads=[dta], writes=[R2])
                yield
                pL = nps()
                bigm = cb["bigm%d" % L]
                op("pe", lambda e: e.matmul(pL[0:L, 0:L8], lhsT=onesf[0:L, 0:L], rhs=R1[0:L, 0:L8], start=True, stop=False),
                   reads=[onesf, R1], writes=[pL], inc=False)
                op("pe", lambda e: e.matmul(pL[0:L, 0:L8], lhsT=negtri[0:L, 0:L], rhs=R2[0:L, 0:L8], start=False, stop=False),
                   reads=[negtri, R2], writes=[pL], inc=False)
                op("pe", lambda e: e.matmul(pL[0:L, 0:L8], lhsT=identf[0:L, 0:L], rhs=bigm[0:L, 0:L8], start=False, stop=True),
                   reads=[identf, bigm], writes=[pL])
                op("dve", lambda e: e.tensor_copy(LT[0:L, 0:L8], pL[0:L, 0:L8]), reads=[pL], writes=[LT])
                act(LT[0:L, 0:L8], LT[0:L, 0:L8], AF.Exp, [LT], [LT], scale=-1.0)
                op("dve", lambda e: e.tensor_tensor(scT[0:L, 0:L8].rearrange("s (h t) -> s h t", t=L),
                                                    LT[0:L, 0:L8].rearrange("s (h t) -> s h t", t=L),
                                                    cbt[0:L, 0:L].unsqueeze(1).to_broadcast([L, 8, L]), ALU.mult),
                   reads=[LT, cbt], writes=[scT])
                yield
                pY1 = nps()
                for hh in range(8):
                    op("pe", lambda e: e.matmul(pY1[0:L, hh * 64:(hh + 1) * 64], lhsT=scT[0:L, hh * L:(hh + 1) * L],
                                                rhs=xdt[0:L, hh * 64:(hh + 1) * 64], start=True, stop=True),
                       reads=[scT, xdt], writes=[pY1], inc=(hh == 7))
                pY2 = nps()
                op("pe", lambda e: e.matmul(pY2[0:L, 0:512], lhsT=CT[:, a:a + L], rhs=Hb[:, 0:512], start=True, stop=True),
                   reads=[CT, Hb], writes=[pY2])
                op("dve", lambda e: e.tensor_tensor(ytmp[0:L, :].rearrange("s (h p) -> s h p", p=64),
                                                    pY2[0:L, 0:512].rearrange("s (h p) -> s h p", p=64),
                                                    dec_tok[0:L, ci, gs8].unsqueeze(2).to_broadcast([L, 8, 64]), ALU.mult),
                   reads=[pY2, dec_tok], writes=[ytmp])
                op("dve", lambda e: e.tensor_tensor(ytok[0:L, :], pY1[0:L, 0:512], ytmp[0:L, :], ALU.add),
                   reads=[pY1, ytmp], writes=[ytok])
                yield
                pT = nps()
                pTb = pT[:, 0:256].bitcast(BF16)
                for pr in range(4):
                    op("pe", lambda e: e.transpose(pTb[:, pr * 64:pr * 64 + L], ytok[0:L, pr * 128:(pr + 1) * 128],
                                                   identb[0:L, 0:L]), reads=[ytok, identb], writes=[pT], inc=(pr == 3))
                for pr in range(4):
                    op("act", lambda e: e.copy(yT4[pr][:, a:a + L], pTb[:, pr * 64:pr * 64 + L]), reads=[pT], writes=[yT4[pr]])
                pH = nps()
                op("pe", lambda e: e.matmul(pH[:, 0:512], lhsT=Btok[0:L, :], rhs=xw[0:L, :], start=True, stop=True),
                   reads=[Btok, xw], writes=[pH])
                op("dve", lambda e: e.tensor_tensor(H[:].rearrange("n (h p) -> n h p", p=64),
                                                    H[:].rearrange("n (h p) -> n h p", p=64),
                                                    rtot_bc[:, ci, gs8].unsqueeze(2).to_broadcast([128, 8, 64]), ALU.mult),
                   reads=[H, rtot_bc], writes=[H])
                op("dve", lambda e: e.tensor_tensor(H[:], H[:], pH[:, 0:512], ALU.add), reads=[H, pH], writes=[H])
                op("act", lambda e: e.copy(Hb[:], H[:]), reads=[H], writes=[Hb])
                yield
            ss_store(l, g, cur_seq, last_tile)

        def ssd_post(tile, l, g, SB):
            T = tile["w"]
            xc, sz = SB["xc"], SB["sz"]
            for pr in range(4):
                cc = g * 4 + pr
                yv = yT4[pr][:, 0:T]
                op("dve", lambda e: e.scalar_tensor_tensor(yv, xc[pr][:, 0:T], pv_dsk[:, l, cc:cc + 1], yv, ALU.mult, ALU.add),
                   reads=[xc[pr], pv_dsk, yT4[pr]], writes=[yT4[pr]])
                op("dve", lambda e: e.tensor_tensor(yv, yv, sz[pr][:, 0:T], ALU.mult), reads=[yT4[pr], sz[pr]], writes=[yT4[pr]])
                act(xc[pr][:, 0:T], yv, AF.Square, [yT4[pr]], [xc[pr]])
            for (c0, c1) in tile["pieces"]:
                w = c1 - c0
                p = bc_sum(lambda i: (xc[i], xc[i][:, c0:c1]), 4, c0, c1)
                rstd_from(p, w, 1.0 / 512, RMS_EPS, st_rstd)
                for pr in range(4):
                    cc = g * 4 + pr
                    op("dve", lambda e: e.scalar_tensor_tensor(on[cc][:, c0:c1], yT4[pr][:, c0:c1], pv_snw[:, l, cc:cc + 1],
                                                               st_rstd[:, 0:w], ALU.mult, ALU.mult),
                       reads=[yT4[pr], pv_snw, st_rstd], writes=[on[cc]])

        def gated_proj_g(tile, l, srcs, gate_off, wname, sg):
            for c in range(16):
                yield from gemm_g([("in", l, 0, NKC, gate_off + c * 128, 128)], xT, tile,
                                  lambda p, c0, c1: act(sg[:, c0:c1], p[:, 0:c1 - c0], AF.Sigmoid, [p], [sg]))
                yield from gemm_g([(wname, l, 0, NKC, c * 128, 128)], srcs, tile,
                                  lambda p, c0, c1: op("dve", lambda e: e.tensor_tensor(mg[c][:, c0:c1], p[:, 0:c1 - c0], sg[:, c0:c1],
                                                                                       ALU.mult), reads=[p, sg], writes=[mg[c]]))

        def chain_g(*gens):
            for g_ in gens:
                if g_ is not None:
                    yield from g_

        def interleave(A, Bg):
            a_alive, b_alive = A is not None, Bg is not None
            while a_alive or b_alive:
                if a_alive:
                    try:
                        next(A)
                    except StopIteration:
                        a_alive = False
                if b_alive:
                    try:
                        next(Bg)
                    except StopIteration:
                        b_alive = False

        def ssd_all(tile, l, first_tile, last_tile):
            interleave(ssd_prep(tile, l, 0, SSB[0]), None)
            for g in range(SSM_G):
                A = ssd_chunks(tile, l, g, SSB[g % 2], first_tile, last_tile)
                nxt = ssd_prep(tile, l, g + 1, SSB[(g + 1) % 2]) if g + 1 < SSM_G else None
                pa = gated_proj_g(tile, l, on, O_GA, "pa", tf[6]) if g == 0 else None
                interleave(A, chain_g(nxt, pa))
                if g == 0:
                    dump("d_mga", mg, tile, l)
                ssd_post(tile, l, g, SSB[g % 2])

        def layer_norm(tile, gvec, bvec, l, emit_out):
            T = tile["w"]
            for (c0, c1) in tile["pieces"]:
                w = c1 - c0
                p1 = bc_sum(lambda i: (xT[i], xT[i][:, c0:c1]), 16, c0, c1)
                p2 = nps()
                for c in range(16):
                    sb_ = sqr[c % 2]
                    act(sb_[:, 0:w], xT[c][:, c0:c1], AF.Square, [xT[c]], [sb_])
                    op("pe", lambda e: e.matmul(p2[:, 0:w], lhsT=onesb[:], rhs=sb_[:, 0:w], start=(c == 0), stop=(c == 15)),
                       reads=[onesb, sb_], writes=[p2], inc=True)
                op("dve", lambda e: e.tensor_scalar(st_mean[:, 0:w], p1[:, 0:w], 1.0 / D, None, ALU.mult), reads=[p1], writes=[st_mean])
                op("dve", lambda e: e.tensor_tensor(st_tmp[:, 0:w], st_mean[:, 0:w], st_mean[:, 0:w], ALU.mult),
                   reads=[st_mean], writes=[st_tmp])
                op("dve", lambda e: e.scalar_tensor_tensor(st_tmp[:, 0:w], p2[:, 0:w], 1.0 / D, st_tmp[:, 0:w], ALU.mult, ALU.subtract),
                   reads=[p2, st_tmp], writes=[st_tmp])
                act(st_tmp[:, 0:w], st_tmp[:, 0:w], AF.Sqrt, [st_tmp], [st_tmp], bias=eps_col(LN_EPS))
                op("dve", lambda e: e.reciprocal(st_rstd[:, 0:w], st_tmp[:, 0:w]), reads=[st_tmp], writes=[st_rstd])
                for c in range(16):
                    t32 = tf[0] if c % 2 == 0 else tf[1]
                    op("dve", lambda e: e.tensor_tensor(t32[:, 0:w], xT[c][:, c0:c1], st_mean[:, 0:w], ALU.subtract),
                       reads=[xT[c], st_mean], writes=[t32])
                    op("dve", lambda e: e.tensor_tensor(t32[:, 0:w], t32[:, 0:w], st_rstd[:, 0:w], ALU.mult),
                       reads=[t32, st_rstd], writes=[t32])
                    if not emit_out:
                        act(xT[c][:, c0:c1], t32[:, 0:w], AF.Identity, [t32, gvec, bvec], [xT[c]],
                            scale=gvec[:, l, c:c + 1], bias=bvec[:, l, c:c + 1])
                    else:
                        yb = tf[2 + (c % 4)]
                        act(yb[:, 0:w], t32[:, 0:w], AF.Identity, [t32, gvec, bvec], [yb],
                            scale=gvec[:, l, c:c + 1], bias=bvec[:, l, c:c + 1])
                        nb = (w + 127) // 128
                        for bi in range(nb):
                            nr = min(128, w - bi * 128)
                            if c % 4 == 0 and bi == 0:
                                pass
                            pt = nps()
                            op("pe", lambda e: e.transpose(pt[0:nr, 0:128], yb[:, bi * 128:bi * 128 + nr], identf[:]),
                               reads=[yb, identf], writes=[pt])
                            ob = ost[ost_i[0] % 4]
                            ost_i[0] += 1
                            op("dve", lambda e: e.tensor_copy(ob[0:nr, :], pt[0:nr, 0:128]), reads=[pt], writes=[ob])
                            r0 = tile["r0"] + c0 + bi * 128
                            K.dma(y_out[r0:r0 + nr, c * 128:(c + 1) * 128], ob[0:nr, :], reads=[ob], writes=[dram_y])

        ost = [K.sb("ost%d" % i, [128, 128], F32) for i in range(4)]
        ost_i = [0]
        xld = [K.sb("xld%d" % i, [128, 512], F32) for i in range(2)]
        xld_i = [0]

        def dump(nm, bufs, tile, l):
            if not debug or l != 0 or tile["idx"] != 0:
                return
            for i, b in enumerate(bufs):
                K.dma(dbg[nm][i, :, 0:tile["w"]], b[:, 0:tile["w"]], reads=[b])

        def mixer(tile, l, first_tile, last_tile):
            if stop_at in ("load", "load_dma", "load_1"):
                raise _Stop()
            hgrn_all(tile, l, first_tile, last_tile)
            dump("d_on", on, tile, l)
            ssd(tile, l, first_tile, last_tile)
            ssd_all(tile, l, first_tile, last_tile)
            dump("d_yn", on, tile, l)
            gated_proj(tile, l, on, O_GB, "pb", True)
            dump("d_mg", mg, tile, l)
            for c in range(16):
                gemm([("out", l, 0, NKC, c * 128, 128)], mg, tile,
                     lambda p, c0, c1: op("dve", lambda e: e.scalar_tensor_tensor(xT[c][:, c0:c1], xT[c][:, c0:c1], ALPHA,
                                                                                  p[:, 0:c1 - c0], ALU.mult, ALU.add),
                                          reads=[xT[c], p], writes=[xT[c]]))
            layer_norm(tile, pv_l1g, pv_l1b, l, False)
            dump("d_xmid", xT, tile, l)
            if stop_at == "mixer":
                raise _Stop()

        def ffn(tile, l, last_layer):
            T = tile["w"]
            pre = tf[8]
            ga = tf[7]
            for j in range(NFF):
                gemm([("up", l, 0, NKC, j * 128, 128)], xT, tile, lambda p, c0, c1: seg_evac(tile, p, c0, c1, pre, 2))
                conv_apply(tile, pre, halo_f[l], halo_fs[l], j, lambda k: pv_fcw[:, l, k, j:j + 1], pv_fcb[:, l, j:j + 1], 3,
                           ga, AF.Gelu)
                gemm([("up", l, 0, NKC, DFF + j * 128, 128)], xT, tile,
                     lambda p, c0, c1: op("dve", lambda e: e.tensor_tensor(hm[j][:, c0:c1], p[:, 0:c1 - c0], ga[:, c0:c1], ALU.mult),
                                          reads=[p, ga], writes=[hm[j]]))
            for c in range(16):
                gemm([("dn", l, 0, 16, c * 128, 128), ("dn", l, 16, 16, c * 128, 128), ("dn", l, 32, 12, c * 128, 128)], hm, tile,
                     lambda p, c0, c1: op("dve", lambda e: e.scalar_tensor_tensor(xT[c][:, c0:c1], xT[c][:, c0:c1], ALPHA,
                                                                                  p[:, 0:c1 - c0], ALU.mult, ALU.add),
                                          reads=[xT[c], p], writes=[xT[c]]))
            dump("d_hm", hm, tile, l)
            dump("d_v2", xT, tile, l)
            layer_norm(tile, pv_l2g, pv_l2b, l, last_layer)

        if stop_at == "setup":
            tiles = []
        for tile in tiles:
            T = tile["w"]
            ti = tile["idx"]
            first_tile = (ti == 0)
            last_tile = (ti == NTILES - 1)
            nblk = (T + 127) // 128
            for bi in range(nblk):
                r0 = bi * 128
                nr = min(128, T - r0)
                for q4 in range(4):
                    xtok = xld[xld_i[0] % 2]
                    xld_i[0] += 1
                    K.dma(xtok[0:nr, :], xin[tile["r0"] + r0: tile["r0"] + r0 + nr, q4 * 512:(q4 + 1) * 512], writes=[xtok])
                    p = nps()
                    for j in range(4):
                        op("pe", lambda e: e.transpose(p[:, j * 128: j * 128 + nr], xtok[0:nr, j * 128:(j + 1) * 128],
                                                       identf[0:nr, 0:nr]), reads=[xtok, identf], writes=[p], inc=(j == 3))
                    for j in range(4):
                        cidx = q4 * 4 + j
                        if j % 2:
                            op("dve", lambda e: e.tensor_copy(xT[cidx][:, r0:r0 + nr], p[:, j * 128:j * 128 + nr]), reads=[p], writes=[xT[cidx]])
                        else:
                            op("dve", lambda e: e.tensor_copy(xT[cidx][:, r0:r0 + nr], p[:, j * 128:j * 128 + nr]), reads=[p], writes=[xT[cidx]])
            try:
                for l in range(nlayers):
                    mixer(tile, l, first_tile, last_tile)
                    ffn(tile, l, l == nlayers - 1)
            except _Stop:
                break
        for l in range(nlayers):
            for k in range(3):
                K.dma(conv_out[l, 0, k, :].rearrange("(c p) -> p c", p=128), halo_x[l][:, :, k], reads=[halo_x[l]], writes=[dram_cv],
                      allow_slow_non_contiguous=True)
                K.dma(conv_out[l, 1, k, :].rearrange("(c p) -> p c", p=128), halo_xs[l][:, :, k], reads=[halo_xs[l]], writes=[dram_cv],
                      allow_slow_non_contiguous=True)
            for k in range(2):
                K.dma(ffn_out[l, 0, k, :].rearrange("(c p) -> p c", p=128), halo_f[l][:, :, k], reads=[halo_f[l]], writes=[dram_ff],
                      allow_slow_non_contiguous=True)
                K.dma(ffn_out[l, 1, k, :].rearrange("(c p) -> p c", p=128), halo_fs[l][:, :, k], reads=[halo_fs[l]], writes=[dram_ff],
                      allow_slow_non_contiguous=True)
        K.finish([dram_y, dram_hg, dram_ss, dram_cv, dram_ff])
        print("instructions:", K.n_ins, "waits:", K.n_wait, "sems:", K.nsem, flush=True)
    return nc


_CACHE = {}


def kernel(**inp):
    f32 = lambda a: np.ascontiguousarray(np.asarray(a, dtype=np.float32))
    xp = f32(inp["x_prompt"])
    xs = f32(inp["x_sample"])
    meta = f32(inp["meta_tokens"])
    if "nc" not in _CACHE:
        _CACHE["nc"] = build_program()
    nc = _CACHE["nc"]
    consts = build_consts()
    wnames = ["hgrn_lb_logits", "hgrn_norm_w", "ssm_conv_w", "ssm_conv_b", "ssm_dt_bias", "ssm_a_log",
              "ssm_d", "ssm_norm_w", "ln1_g", "ln1_b", "ffn_conv_w", "ffn_conv_b", "ln2_g", "ln2_b"]
    shared = {n: f32(inp[n]) for n in wnames}
    shared["wst"] = build_wstream(inp)
    for k, v in consts.items():
        shared["c_" + k] = v
    in_maps = []
    for c in range(8):
        b = c % 2
        if c < 2:
            xin = np.concatenate([meta, xp[b, 0:TP], xs[c], xp[b, TP:]], axis=0)
        else:
            xin = np.zeros((NTOK, D), np.float32)
            xin[N_META + TP:N_META + TP + DEC_SEQ] = xs[c]
        m = dict(shared)
        m["xin"] = np.ascontiguousarray(xin)
        m["st_hg"] = f32(inp["state_hgrn"][:, c])
        m["st_ssm"] = f32(inp["state_ssm"][:, c])
        m["st_conv"] = f32(inp["state_ssm_conv"][:, c])
        m["st_ffn"] = f32(inp["state_ffn_conv"][:, c])
        in_maps.append(m)
    res = run_bass_kernel_spmd(nc, in_maps, core_ids=list(range(8)))
    R = res.results
    ys = [r["y"] for r in R]
    y_prompt = np.stack([np.concatenate([ys[b][N_META:N_META + TP], ys[b][N_META + TP + DEC_SEQ:]], axis=0) for b in range(2)])
    y_sample = np.stack([ys[c][N_META + TP:N_META + TP + DEC_SEQ] for c in range(8)])
    hg_p = np.stack([R[b]["hg_o"][:, 0] for b in range(2)], axis=1)
    ssm_p = np.stack([R[b]["ssm_o"][:, 0] for b in range(2)], axis=1)
    conv_p = np.stack([R[b]["conv_o"][:, 0] for b in range(2)], axis=1)
    ffn_p = np.stack([R[b]["ffn_o"][:, 0] for b in range(2)], axis=1)
    hg_s = np.stack([R[c]["hg_o"][:, 1] for c in range(8)], axis=1)
    ssm_s = np.stack([R[c]["ssm_o"][:, 1] for c in range(8)], axis=1)
    conv_s = np.stack([R[c]["conv_o"][:, 1] for c in range(8)], axis=1)
    ffn_s = np.stack([R[c]["ffn_o"][:, 1] for c in range(8)], axis=1)
    outs = (y_prompt, y_sample, hg_p, ssm_p, conv_p, ffn_p, hg_s, ssm_s, conv_s, ffn_s)
    return tuple(np.ascontiguousarray(o, dtype=np.float32) for o in outs)
DBG_NAMES = ["d_on", "d_mga", "d_yn", "d_mg", "d_xmid", "d_v2", "d_hm"]
```

```python
import contextlib
import numpy as np
import concourse.bass as bass
import concourse.mybir as mybir
from concourse.bass_utils import run_bass_kernel_spmd

F32 = mybir.dt.float32
BF16 = mybir.dt.bfloat16
AF = mybir.ActivationFunctionType
ALU = mybir.AluOpType

D = 2048
NKC = 16
DEPTH = 2
SEQ = 4096
N_META = 16
DEC_SEQ = 32
HG_H = 16
SSM_H = 32
SSM_P = 64
SSM_N = 128
SSM_G = 4
XBC = 3072
DFF = 5632
NFF = 44
N_IN = 17440
O_Q, O_F, O_I, O_G, O_Z, O_XBC, O_DT, O_GA, O_GB = 0, 2048, 4096, 6144, 8192, 10240, 13312, 13344, 15392
ALPHA = (2.0 * DEPTH) ** 0.25
LN_EPS = 1e-5
RMS_EPS = 1e-6
BIG = 30000.0
TW = 560
TP = 512
NTILES = SEQ // TP
NTOK = N_META + SEQ + DEC_SEQ
MAXC = 30000


class Buf:
    __slots__ = ("t", "w", "r", "name")

    def __init__(self, t, name=""):
        self.t = t
        self.w = None
        self.r = {}
        self.name = name

    def __getitem__(self, k):
        return self.t[k]


class Eng:
    def __init__(self, name, handle):
        self.name = name
        self.h = handle
        self.sem = None
        self.count = 0
        self.known = {}


class KB:
    def __init__(self, nc, stack, n_dma_sems=16):
        self.nc = nc
        self.stack = stack
        self.sems = {}
        self.eng = {}
        self.nsem = 0
        for name, h in (("pe", nc.tensor), ("act", nc.scalar), ("dve", nc.vector),
                        ("pool", nc.gpsimd), ("sp", nc.sync)):
            E = Eng(name, h)
            self.eng[name] = E
            self._newsem(E)
        self.dma_slots = []
        for i in range(n_dma_sems):
            s = stack.enter_context(nc.semaphore("d%d" % i))
            self.sems[id(s)] = s
            self.dma_slots.append([s, 0])
        self.dma_i = 0
        self.n_wait = 0
        self.n_ins = 0
        self.names = set()

    def _newsem(self, E):
        s = self.stack.enter_context(self.nc.semaphore("s_%s_%d" % (E.name, self.nsem)))
        self.nsem += 1
        self.sems[id(s)] = s
        E.sem = s
        E.count = 0

    def uname(self, name):
        i = 0
        n = name
        while n in self.names:
            i += 1
            n = "%s_%d" % (name, i)
        self.names.add(n)
        return n

    def sb(self, name, shape, dt):
        name = self.uname(name)
        t = self.stack.enter_context(self.nc.sbuf_tensor(name, list(shape), dt))
        return Buf(t, name)

    def psum(self, name, shape, dt):
        t = self.stack.enter_context(self.nc.psum_tensor(name, list(shape), dt))
        return Buf(t, name)

    def _need(self, reads, writes):
        need = {}
        for b in reads:
            if b.w is not None:
                k, v = b.w
                if need.get(k, 0) < v:
                    need[k] = v
        for b in writes:
            if b.w is not None:
                k, v = b.w
                if need.get(k, 0) < v:
                    need[k] = v
            for k, v in b.r.items():
                if need.get(k, 0) < v:
                    need[k] = v
        return need

    def _wait(self, E, need, skip_own=False):
        for k, v in need.items():
            if skip_own and k == id(E.sem):
                continue
            if E.known.get(k, 0) < v:
                E.h.wait_ge(self.sems[k], v)
                E.known[k] = v
                self.n_wait += 1

    def _mark(self, tok, reads, writes):
        k, v = tok
        for b in reads:
            if b.r.get(k, 0) < v:
                b.r[k] = v
        for b in writes:
            b.w = tok
            b.r = {}

    def op(self, e, emit, reads=(), writes=(), inc=True):
        E = self.eng[e]
        if inc and E.count >= MAXC:
            self._newsem(E)
        self._wait(E, self._need(reads, writes), skip_own=(e == "pe"))
        ins = emit(E.h)
        self.n_ins += 1
        if inc:
            E.count += 1
            ins.then_inc(E.sem, 1)
            tok = (id(E.sem), E.count)
        else:
            tok = (id(E.sem), E.count + 1)
        self._mark(tok, reads, writes)
        return tok

    def dma(self, out, in_, reads=(), writes=(), q="sp", **kw):
        E = self.eng[q]
        slot = self.dma_slots[self.dma_i]
        self.dma_i = (self.dma_i + 1) % len(self.dma_slots)
        if slot[1] >= MAXC // 16:
            s = self.stack.enter_context(self.nc.semaphore("dx%d" % self.nsem))
            self.nsem += 1
            self.sems[id(s)] = s
            self._wait(E, {id(slot[0]): slot[1] * 16})
            slot[0] = s
            slot[1] = 0
        need = self._need(reads, writes)
        if slot[1] > 0:
            k = id(slot[0])
            need[k] = max(need.get(k, 0), slot[1] * 16)
        self._wait(E, need)
        E.h.dma_start(out=out, in_=in_, **kw).then_inc(slot[0], 16)
        self.n_ins += 1
        slot[1] += 1
        tok = (id(slot[0]), slot[1] * 16)
        self._mark(tok, reads, writes)
        return tok

    def finish(self, bufs=()):
        E = self.eng["sp"]
        need = {}
        for s, c in self.dma_slots:
            if c:
                need[id(s)] = c * 16
        for b in bufs:
            if b.w is not None:
                k, v = b.w
                need[k] = max(need.get(k, 0), v)
        self._wait(E, need)


def tile_defs(ntiles):
    tiles = []
    r = 0
    for ti in range(ntiles):
        if ti == 0:
            w = N_META + TP + DEC_SEQ
            segs = [(0, N_META + TP, 0), (N_META + TP, DEC_SEQ, 1)]
            chunks = [(N_META + TP, DEC_SEQ, 1), (0, N_META, 0)] + [(N_META + 64 * j, 64, 0) for j in range(TP // 64)]
        else:
            w = TP
            segs = [(0, TP, 0)]
            chunks = [(64 * j, 64, 0) for j in range(TP // 64)]
        pieces = []
        c = 0
        while c < w:
            pieces.append((c, min(c + 512, w)))
            c += 512
        tiles.append(dict(r0=r, w=w, segs=segs, chunks=chunks, pieces=pieces, idx=ti))
        r += w
    return tiles


def build_consts():
    c = {}
    c["identf"] = np.eye(128, dtype=np.float32)
    tri = (np.arange(128)[:, None] <= np.arange(128)[None, :]).astype(np.float32)
    c["tri"] = tri
    c["ones"] = np.ones((128, 128), np.float32)
    for L in (16, 32, 64):
        m = (np.arange(L)[None, :] < np.arange(L)[:, None]).astype(np.float32) * BIG
        c["bigm%d" % L] = np.ascontiguousarray(np.broadcast_to(m[:, None, :], (L, 8, L))).reshape(L, 8 * L)
    return c


def plan_pass(l):
    for h in range(HG_H):
        for o in (O_Q, O_F, O_I, O_G):
            yield ("in", l, 0, NKC, o + h * 128, 128)
    yield ("in", l, 0, NKC, O_DT, 32)
    for g in range(SSM_G):
        yield ("in", l, 0, NKC, O_XBC + 2048 + g * 128, 128)
        yield ("in", l, 0, NKC, O_XBC + 2560 + g * 128, 128)
        for pr in range(4):
            cc = g * 4 + pr
            yield ("in", l, 0, NKC, O_XBC + cc * 128, 128)
            yield ("in", l, 0, NKC, O_Z + cc * 128, 128)
        if g == 1:
            for c in range(16):
                yield ("in", l, 0, NKC, O_GA + c * 128, 128)
                yield ("pa", l, 0, NKC, c * 128, 128)
    for c in range(16):
        yield ("in", l, 0, NKC, O_GB + c * 128, 128)
        yield ("pb", l, 0, NKC, c * 128, 128)
    for c in range(16):
        yield ("out", l, 0, NKC, c * 128, 128)
    for j in range(NFF):
        yield ("up", l, 0, NKC, j * 128, 128)
        yield ("up", l, 0, NKC, DFF + j * 128, 128)
    for c in range(16):
        yield ("dn", l, 0, 16, c * 128, 128)
        yield ("dn", l, 16, 16, c * 128, 128)
        yield ("dn", l, 32, 12, c * 128, 128)


NBL = len(list(plan_pass(0)))
WNAME = {"in": "w_in", "pa": "w_proj_a", "pb": "w_proj_b", "out": "w_out", "up": "ffn_w_up", "dn": "ffn_w_down"}


def build_wstream(inp):
    wst = np.zeros((DEPTH, NBL, 128, NKC, 128), np.float32)
    for l in range(DEPTH):
        for j, (name, _, k0, nk, c0, ncol) in enumerate(plan_pass(l)):
            W = inp[WNAME[name]][l]
            blk = np.asarray(W[k0 * 128:(k0 + nk) * 128, c0:c0 + ncol], dtype=np.float32).reshape(nk, 128, ncol)
            wst[l, j, :, 0:nk, 0:ncol] = blk.transpose(1, 0, 2)
    return wst.reshape(DEPTH, NBL, 128, NKC * 128)


class _Stop(Exception):
    pass


def build_program(ntiles=NTILES, nlayers=DEPTH, debug=False, stop_at=None):
    nc = bass.Bass("TRN2", target_bir_lowering=False)
    tiles = tile_defs(ntiles)
    ntok = sum(t["w"] for t in tiles)

    def din(name, shape):
        return nc.dram_tensor(name, list(shape), F32, kind="ExternalInput").ap()

    def dout(name, shape):
        return nc.dram_tensor(name, list(shape), F32, kind="ExternalOutput").ap()

    xin = din("xin", [NTOK, D])
    st_hg = din("st_hg", [DEPTH, HG_H, 128, 128])
    st_ssm = din("st_ssm", [DEPTH, SSM_H, SSM_P, SSM_N])
    st_conv = din("st_conv", [DEPTH, 3, XBC])
    st_ffn = din("st_ffn", [DEPTH, 2, DFF])
    lb_logits = din("hgrn_lb_logits", [DEPTH, D])
    hg_nw = din("hgrn_norm_w", [DEPTH, D])
    cv_w = din("ssm_conv_w", [DEPTH, 4, XBC])
    cv_b = din("ssm_conv_b", [DEPTH, XBC])
    dt_bias = din("ssm_dt_bias", [DEPTH, SSM_H])
    a_log = din("ssm_a_log", [DEPTH, SSM_H])
    ssm_d = din("ssm_d", [DEPTH, SSM_H])
    ssm_nw = din("ssm_norm_w", [DEPTH, D])
    ln1_g = din("ln1_g", [DEPTH, D])
    ln1_b = din("ln1_b", [DEPTH, D])
    fcv_w = din("ffn_conv_w", [DEPTH, 3, DFF])
    fcv_b = din("ffn_conv_b", [DEPTH, DFF])
    ln2_g = din("ln2_g", [DEPTH, D])
    ln2_b = din("ln2_b", [DEPTH, D])
    wst = din("wst", [DEPTH, NBL, 128, NKC * 128])
    cdefs = build_consts()
    cin = {k: din("c_" + k, v.shape) for k, v in cdefs.items()}

    y_out = dout("y", [NTOK, D])
    hg_out = dout("hg_o", [DEPTH, 2, HG_H, 128, 128])
    ssm_out = dout("ssm_o", [DEPTH, 2, SSM_H, SSM_P, SSM_N])
    conv_out = dout("conv_o", [DEPTH, 2, 3, XBC])
    ffn_out = dout("ffn_o", [DEPTH, 2, 2, DFF])
    dbg = {}
    if debug:
        for nm in ("d_on", "d_mga", "d_yn", "d_mg", "d_xmid", "d_v2"):
            dbg[nm] = nc.dram_tensor(nm, [16, 128, TW], BF16, kind="ExternalOutput").ap()
        dbg["d_hm"] = nc.dram_tensor("d_hm", [NFF, 128, TW], BF16, kind="ExternalOutput").ap()


    with contextlib.ExitStack() as st:
        K = KB(nc, st)
        op = K.op

        xT = [K.sb("xT%d" % i, [128, TW], BF16) for i in range(NKC)]
        hm = [K.sb("hm%d" % i, [128, TW], BF16) for i in range(NFF)]
        on = hm[0:16]
        mg = hm[16:32]
        NT32 = 10
        tf = [K.sb("tf%d" % i, [128, TW + 40], F32) for i in range(NT32)]
        tb = [K.sb("tb%d" % i, [128, TW], BF16) for i in range(10)]
        NS, NB = 0, 10
        stage = [K.sb("stg%d" % i, [128, NKC, 128], F32) for i in range(NS)]
        wring = [K.sb("wb%d" % i, [128, NKC, 128], BF16) for i in range(NB)]
        psb = [K.psum("ps%d" % i, [128, 512], F32) for i in range(8)]
        ps_i = [0]

        def nps():
            b = psb[ps_i[0]]
            ps_i[0] = (ps_i[0] + 1) % 8
            return b

        Hss1 = K.sb("Hss", [128, 512], F32)
        Hsb1 = K.sb("Hsb", [128, 512], BF16)
        hg_c = Buf(nc.dram_tensor("hg_carry", [DEPTH, HG_H, 128, 128], F32).ap(), "hg_carry")
        ss_c = Buf(nc.dram_tensor("ss_carry", [DEPTH, SSM_G, 128, 512], F32).ap(), "ss_carry")
        halo_x = [K.sb("halox%d" % l, [128, 24, 3], F32) for l in range(DEPTH)]
        halo_f = [K.sb("halof%d" % l, [128, NFF, 2], F32) for l in range(DEPTH)]
        halo_xs = [K.sb("haloxs%d" % l, [128, 24, 3], F32) for l in range(DEPTH)]
        halo_fs = [K.sb("halofs%d" % l, [128, NFF, 2], F32) for l in range(DEPTH)]
        dram_y = Buf(y_out, "y")
        dram_hg = Buf(hg_out, "hgo")
        dram_ss = Buf(ssm_out, "sso")
        dram_cv = Buf(conv_out, "cvo")
        dram_ff = Buf(ffn_out, "ffo")

        cb = {}
        for k, v in cdefs.items():
            b = K.sb("sc_" + k, [128, v.shape[1]], F32)
            K.dma(b[0:v.shape[0], :], cin[k][:, :], writes=[b])
            cb[k] = b
        identb = K.sb("identb", [128, 128], BF16)
        onesb = K.sb("onesb", [128, 128], BF16)
        trib = K.sb("trib", [128, 128], BF16)
        negtri = K.sb("negtri", [128, 128], F32)
        op("dve", lambda e: e.tensor_copy(identb[:], cb["identf"][:]), reads=[cb["identf"]], writes=[identb])
        op("dve", lambda e: e.tensor_copy(onesb[:], cb["ones"][:]), reads=[cb["ones"]], writes=[onesb])
        op("dve", lambda e: e.tensor_copy(trib[:], cb["tri"][:]), reads=[cb["tri"]], writes=[trib])
        op("dve", lambda e: e.tensor_scalar(negtri[:], cb["tri"][:], -1.0, None, ALU.mult), reads=[cb["tri"]], writes=[negtri])
        identf, onesf, trif = cb["identf"], cb["ones"], cb["tri"]
        one_col = K.sb("one_col", [128, 1], F32)
        op("dve", lambda e: e.memset(one_col[:], 1.0), writes=[one_col])

        def load_vec(name, ap, nch, extra=None):
            b = K.sb("pv_" + name, [128, DEPTH, nch] if extra is None else [128, DEPTH, extra, nch], F32)
            for l in range(DEPTH):
                if extra is None:
                    K.dma(b[:, l, :], ap[l, :].rearrange("(c p) -> p c", p=128), writes=[b],
                          allow_slow_non_contiguous=True)
                else:
                    for k in range(extra):
                        K.dma(b[:, l, k, :], ap[l, k, :].rearrange("(c p) -> p c", p=128), writes=[b],
                              allow_slow_non_contiguous=True)
            return b

        pv_lbl = load_vec("lbl", lb_logits, 16)
        pv_hgnw = load_vec("hgnw", hg_nw, 16)
        pv_cvw = load_vec("cvw", cv_w, 24, extra=4)
        pv_cvb = load_vec("cvb", cv_b, 24)
        pv_snw = load_vec("snw", ssm_nw, 16)
        pv_l1g = load_vec("l1g", ln1_g, 16)
        pv_l1b = load_vec("l1b", ln1_b, 16)
        pv_fcw = load_vec("fcw", fcv_w, NFF, extra=3)
        pv_fcb = load_vec("fcb", fcv_b, NFF)
        pv_l2g = load_vec("l2g", ln2_g, 16)
        pv_l2b = load_vec("l2b", ln2_b, 16)
        hv = {}
        for name, ap in (("dtb", dt_bias), ("alog", a_log), ("dsk", ssm_d)):
            b = K.sb("hv_" + name, [SSM_H, DEPTH], F32)
            K.dma(b[:], ap.rearrange("l h -> h l"), writes=[b], allow_slow_non_contiguous=True)
            hv[name] = b
        hvA = K.sb("hv_A", [SSM_H, DEPTH], F32)
        op("act", lambda e: e.activation(hvA[:], hv["alog"][:], AF.Exp), reads=[hv["alog"]], writes=[hvA])
        pv_lb = K.sb("pv_lb", [128, DEPTH, 16], F32)
        pv_oml = K.sb("pv_oml", [128, DEPTH, 16], F32)
        lbe = K.sb("lbe", [128, DEPTH, 16], F32)
        lbs = K.sb("lbs", [128, 16], F32)
        op("act", lambda e: e.activation(lbe[:], pv_lbl[:], AF.Exp), reads=[pv_lbl], writes=[lbe])
        op("dve", lambda e: e.tensor_copy(lbs[:], lbe[:, 0, :]), reads=[lbe], writes=[lbs])
        for l in range(1, DEPTH):
            op("dve", lambda e: e.tensor_tensor(lbs[:], lbs[:], lbe[:, l, :], ALU.add), reads=[lbe, lbs], writes=[lbs])
        op("dve", lambda e: e.reciprocal(lbs[:], lbs[:]), reads=[lbs], writes=[lbs])
        op("dve", lambda e: e.memset(pv_lb[:, 0, :], 0.0), writes=[pv_lb])
        for l in range(1, DEPTH):
            op("dve", lambda e: e.tensor_tensor(pv_lb[:, l, :], lbe[:, l, :], lbs[:], ALU.mult), reads=[lbe, lbs], writes=[pv_lb])
            if l > 1:
                op("dve", lambda e: e.tensor_tensor(pv_lb[:, l, :], pv_lb[:, l, :], pv_lb[:, l - 1, :], ALU.add),
                   reads=[pv_lb], writes=[pv_lb])
        op("dve", lambda e: e.tensor_scalar(pv_oml[:], pv_lb[:], -1.0, 1.0, ALU.mult, ALU.add), reads=[pv_lb], writes=[pv_oml])
        pv_dsk = K.sb("pv_dsk", [128, DEPTH, 16], F32)
        for l in range(DEPTH):
            for two in range(2):
                src = ssm_d[l:l + 1, :].rearrange("o (c two) -> o c two", two=2)[:, :, two]
                K.dma(pv_dsk[two * 64:(two + 1) * 64, l, :], src.to_broadcast([64, 16]), writes=[pv_dsk],
                      allow_slow_non_contiguous=True)
        for l in range(DEPTH):
            op("dve", lambda e: e.memset(halo_x[l][:], 0.0), writes=[halo_x[l]])
            op("dve", lambda e: e.memset(halo_f[l][:], 0.0), writes=[halo_f[l]])
            for k in range(3):
                K.dma(halo_xs[l][:, :, k], st_conv[l, k, :].rearrange("(c p) -> p c", p=128), writes=[halo_xs[l]],
                      allow_slow_non_contiguous=True)
            for k in range(2):
                K.dma(halo_fs[l][:, :, k], st_ffn[l, k, :].rearrange("(c p) -> p c", p=128), writes=[halo_fs[l]],
                      allow_slow_non_contiguous=True)

        sched = []

        def plan():
            for ti in range(ntiles):
                for l in range(nlayers):
                    yield from plan_pass(l)

        sched = list(plan())
        ws = dict(issued=0, got=0)
        cast_rr = [0]

        def ws_issue():
            i = ws["issued"]
            name, l, k0, nk, c0, ncol = sched[i]
            wb = wring[i % NB]
            j = i % NBL
            K.dma(wb[:, 0:nk, :].rearrange("p k c -> p (k c)"), wst[l, j, :, 0:nk * 128], writes=[wb], q="pool")
            ws["issued"] += 1

        def wget(spec):
            i = ws["got"]
            assert sched[i] == spec, (i, sched[i], spec)
            while ws["issued"] < min(len(sched), i + NB - 2):
                ws_issue()
            ws["got"] += 1
            return wring[i % NB]

        def gemm(wspecs, srcs, tile, evac, m=128):
            wbs = [(wget(s), s[3]) for s in wspecs]
            for (c0, c1) in tile["pieces"]:
                p = nps()
                kk = 0
                nk_tot = sum(n for _, n in wbs)
                for wb, nk in wbs:
                    for k in range(nk):
                        src = srcs[kk]
                        last = (kk == nk_tot - 1)
                        op("pe", lambda e: e.matmul(p[0:m, 0:c1 - c0], lhsT=wb[:, k, 0:m], rhs=src[:, c0:c1],
                                                    start=(kk == 0), stop=last),
                           reads=[wb, src], writes=[p], inc=last)
                        kk += 1
                evac(p, c0, c1)

        def act(out, in_, func, reads, writes, **kw):
            op("act", lambda e: e.activation(out, in_, func, **kw), reads=reads, writes=writes)

        st_mean = K.sb("st_mean", [128, 512], F32)
        st_rstd = K.sb("st_rstd", [128, 512], F32)
        st_tmp = K.sb("st_tmp", [128, 512], F32)
        sqr = [K.sb("sqr%d" % i, [128, 512], BF16) for i in range(2)]
        pre_s = K.sb("pre_s", [128, 40], F32)
        acc_s = K.sb("acc_s", [128, 40], F32)
        dtT = tf[4]
        adtT = tf[5]
        NCH = 10
        dta = K.sb("dta", [64, NCH, 64], F32)
        dec_tok = K.sb("dec_tok", [64, NCH, 32], F32)
        w_tok = K.sb("w_tok", [64, NCH, 32], F32)
        rtot_bc = K.sb("rtot_bc", [128, NCH, 32], F32)
        gendS = K.sb("gendS", [128, 32], F32)
        wtmp = K.sb("wtmp", [64, 32], F32)
        gtok_all = K.sb("gtok_all", [64, NCH, 32], F32)
        bigmb = {}
        for L_ in (16, 32, 64):
            bigmb[L_] = K.sb("bigmb%d" % L_, [64, 8 * L_], BF16)
            op("dve", lambda e: e.tensor_copy(bigmb[L_][0:L_, :], cb["bigm%d" % L_][0:L_, :]), reads=[cb["bigm%d" % L_]], writes=[bigmb[L_]])
        xdt = K.sb("xdt", [64, 512], BF16)
        xw = K.sb("xw", [64, 512], BF16)
        Btok = K.sb("Btok", [64, 128], BF16)
        cbt = K.sb("cbt", [64, 64], F32)
        R1 = K.sb("R1", [64, 512], F32)
        R2 = K.sb("R2", [64, 512], F32)
        LT = R1
        scT = K.sb("scT", [64, 512], BF16)
        ytmp = R2
        ytok = K.sb("ytok", [64, 512], BF16)
        yT4 = tf[0:4]
        sttok = K.sb("sttok", [128, 128], F32)

        def seg_evac(tile, p, c0, c1, pre, hw):
            for (s0, sl, sq) in tile["segs"]:
                a, b = max(c0, s0), min(c1, s0 + sl)
                if a >= b:
                    continue
                dstb = pre if sq == 0 else pre_s
                dst = dstb[:, hw + a - s0: hw + b - s0]
                op("act", lambda e: e.copy(dst, p[:, a - c0:b - c0]), reads=[p], writes=[dstb])

        def conv_apply(tile, pre, halo, halo_sq, cidx, wfun, bap, width, dst, func):
            hw = width - 1
            for (s0, sl, sq) in tile["segs"]:
                pb = pre if sq == 0 else pre_s
                hb = halo if sq == 0 else halo_sq
                ab = tf[9] if sq == 0 else acc_s
                op("dve", lambda e: e.tensor_copy(pb[:, 0:hw], hb[:, cidx, :]), reads=[hb], writes=[pb])
                op("dve", lambda e: e.tensor_scalar(ab[:, 0:sl], pb[:, hw:hw + sl], wfun(hw), bap, ALU.mult, ALU.add),
                   reads=[pb], writes=[ab])
                for k in range(hw):
                    op("dve", lambda e: e.scalar_tensor_tensor(ab[:, 0:sl], pb[:, k:k + sl], wfun(k), ab[:, 0:sl],
                                                               ALU.mult, ALU.add), reads=[pb, ab], writes=[ab])
                act(dst[:, s0:s0 + sl], ab[:, 0:sl], func, [ab], [dst])
                op("dve", lambda e: e.tensor_copy(hb[:, cidx, :], pb[:, sl:sl + hw]), reads=[pb], writes=[hb])

        def bc_sum(srcs_fn, nsrc, c0, c1):
            p = nps()
            for i in range(nsrc):
                sbuf, sap = srcs_fn(i)
                op("pe", lambda e: e.matmul(p[:, 0:c1 - c0], lhsT=onesb[:], rhs=sap, start=(i == 0), stop=(i == nsrc - 1)),
                   reads=[onesb, sbuf], writes=[p], inc=(i == nsrc - 1))
            return p

        def rstd_from(p, w, scale, eps, dst):
            act(st_tmp[:, 0:w], p[:, 0:w], AF.Sqrt, [p], [st_tmp], bias=eps_col(eps), scale=scale)
            op("dve", lambda e: e.reciprocal(dst[:, 0:w], st_tmp[:, 0:w]), reads=[st_tmp], writes=[dst])

        eps_cols = {}

        def eps_col(v):
            if v not in eps_cols:
                b = K.sb("epsc%d" % len(eps_cols), [128, 1], F32)
                op("dve", lambda e: e.memset(b[:], float(v)), writes=[b])
                eps_cols[v] = b
            return eps_cols[v][:, 0:1]

        eps_col(RMS_EPS)
        eps_col(LN_EPS)

        def gemm_g(wspecs, srcs, tile, evac, m=128):
            wbs = [(wget(s), s[3]) for s in wspecs]
            for (c0, c1) in tile["pieces"]:
                p = nps()
                kk = 0
                nk_tot = sum(n for _, n in wbs)
                for wb, nk in wbs:
                    for k in range(nk):
                        src = srcs[kk]
                        last = (kk == nk_tot - 1)
                        op("pe", lambda e: e.matmul(p[0:m, 0:c1 - c0], lhsT=wb[:, k, 0:m], rhs=src[:, c0:c1],
                                                    start=(kk == 0), stop=last),
                           reads=[wb, src], writes=[p], inc=last)
                        kk += 1
                evac(p, c0, c1)
                yield

        HS = []
        for s_ in range(2):
            HS.append(dict(
                qs=tf[4 * s_ + 0], fk=tf[4 * s_ + 1], ln=tf[4 * s_ + 2], G=tf[4 * s_ + 3],
                vT=tb[4 * s_ + 0], gs=tb[4 * s_ + 1], qt=tb[4 * s_ + 2], kt=tb[4 * s_ + 3],
                S=K.sb("ShgS%d" % s_, [128, 128], F32), rr=K.sb("rrS%d" % s_, [128, 3, 16], F32),
                rrd=K.sb("rrdS%d" % s_, [128, 3, 16], F32),
                ATb=[K.sb("ATbS%d_%d" % (s_, i), [64, 64], BF16) for i in range(2)],
                vtok=[K.sb("vtokS%d_%d" % (s_, i), [64, 128], BF16) for i in range(2)],
                ktok=[K.sb("ktokS%d_%d" % (s_, i), [64, 128], BF16) for i in range(2)],
                Spb=[K.sb("SpbS%d_%d" % (s_, i), [128, 128], BF16) for i in range(2)],
                rstd=K.sb("rstdS%d" % s_, [128, 512], F32)))

        def hg_prep(tile, l, h, B):
            T = tile["w"]
            qs, fk, lnf, GnS = B["qs"], B["fk"], B["ln"], B["G"]
            Dd, E2 = B["ln"], B["G"]
            vT, gs, qt, kt = B["vT"], B["gs"], B["qt"], B["kt"]
            rr_, rrd_ = B["rr"], B["rrd"]
            specs = [("in", l, 0, NKC, o + h * 128, 128) for o in (O_Q, O_F, O_I, O_G)]
            yield from gemm_g([specs[0]], xT, tile, lambda p, c0, c1: act(qs[:, c0:c1], p[:, 0:c1 - c0], AF.Silu, [p], [qs]))
            yield from gemm_g([specs[1]], xT, tile, lambda p, c0, c1: act(fk[:, c0:c1], p[:, 0:c1 - c0], AF.Sigmoid, [p], [fk]))
            yield from gemm_g([specs[2]], xT, tile, lambda p, c0, c1: op("act", lambda e: e.copy(vT[:, c0:c1], p[:, 0:c1 - c0]),
                                                                         reads=[p], writes=[vT]))
            yield from gemm_g([specs[3]], xT, tile, lambda p, c0, c1: act(gs[:, c0:c1], p[:, 0:c1 - c0], AF.Silu, [p], [gs]))
            op("dve", lambda e: e.tensor_scalar(fk[:, 0:T], fk[:, 0:T], pv_oml[:, l, h:h + 1], pv_lb[:, l, h:h + 1],
                                                ALU.mult, ALU.add), reads=[fk, pv_oml, pv_lb], writes=[fk])
            act(lnf[:, 0:T], fk[:, 0:T], AF.Ln, [fk], [lnf])
            op("dve", lambda e: e.tensor_scalar(fk[:, 0:T], fk[:, 0:T], -1.0, 1.0, ALU.mult, ALU.add), reads=[fk], writes=[fk])
            op("dve", lambda e: e.memset(GnS[:, 0:1], 0.0), writes=[GnS])
            op("dve", lambda e: e.tensor_tensor_scan(GnS[:, 1:T + 1], one_col[:, 0:1].to_broadcast([128, T]), lnf[:, 0:T],
                                                     0.0, ALU.mult, ALU.add), reads=[one_col, lnf], writes=[GnS])
            yield
            runs = []
            for ci, (a, L, sq) in enumerate(tile["chunks"]):
                if runs and runs[-1][2] == L and runs[-1][0] + runs[-1][1] * L == a:
                    runs[-1][1] += 1
                else:
                    runs.append([a, 1, L, ci])
            for (a, n, L, ci0) in runs:
                hl = L // 2
                gv = lambda off: GnS[:, a + off:a + off + n * L].rearrange("p (j l) -> p j l", l=L)
                ref = gv(hl)[:, :, 0:1]
                sta = gv(0)[:, :, 0:1]
                end = gv(L)[:, :, 0:1]
                op("dve", lambda e: e.tensor_tensor(Dd[:, a:a + n * L].rearrange("p (j l) -> p j l", l=L), gv(1),
                                                    ref.to_broadcast([128, n, L]), ALU.subtract), reads=[GnS], writes=[Dd])
                r3 = lambda k: rrd_[:, k, ci0:ci0 + n].unsqueeze(2)
                op("dve", lambda e: e.tensor_tensor(r3(0), ref, sta, ALU.subtract), reads=[GnS], writes=[rrd_])
                op("dve", lambda e: e.tensor_tensor(r3(1), end, ref, ALU.subtract), reads=[GnS], writes=[rrd_])
                op("dve", lambda e: e.tensor_tensor(r3(2), end, sta, ALU.subtract), reads=[GnS], writes=[rrd_])
            nchk = len(tile["chunks"])
            act(rr_[:, :, 0:nchk], rrd_[:, :, 0:nchk], AF.Exp, [rrd_], [rr_])
            act(E2[:, 0:T], Dd[:, 0:T], AF.Exp, [Dd], [E2], scale=-1.0)
            act(Dd[:, 0:T], Dd[:, 0:T], AF.Exp, [Dd], [Dd])
            yield
            op("dve", lambda e: e.tensor_tensor(qt[:, 0:T], qs[:, 0:T], Dd[:, 0:T], ALU.mult), reads=[qs, Dd], writes=[qt])
            op("dve", lambda e: e.tensor_tensor(kt[:, 0:T], fk[:, 0:T], E2[:, 0:T], ALU.mult), reads=[fk, E2], writes=[kt])
            yield

        def hg_loop(tile, l, h, B, first_tile, last_tile):
            T = tile["w"]
            oT = B["qs"]
            vT, gs, qt, kt = B["vT"], B["gs"], B["qt"], B["kt"]
            rr_ = B["rr"]
            S = B["S"]
            cur_seq = None
            for ci, (a, L, sq) in enumerate(tile["chunks"]):
                if sq != cur_seq:
                    if cur_seq is not None:
                        store_hg(l, h, cur_seq, last_tile, S)
                    if sq == 0:
                        if first_tile:
                            op("dve", lambda e: e.memset(S[:], 0.0), writes=[S])
                        else:
                            K.dma(S[:], hg_c[l, h, :, :], reads=[hg_c], writes=[S])
                    else:
                        K.dma(S[:], st_hg[l, h, :, :], writes=[S])
                    cur_seq = sq
                i2 = ci % 2
                ATb_, vtok_, ktok_, Spb_ = B["ATb"][i2], B["vtok"][i2], B["ktok"][i2], B["Spb"][i2]
                p1 = nps()
                op("pe", lambda e: e.matmul(p1[0:L, 0:L], lhsT=kt[:, a:a + L], rhs=qt[:, a:a + L], start=True, stop=True),
                   reads=[kt, qt], writes=[p1])
                p2 = nps()
                p2b = p2[:, 0:256].bitcast(BF16)
                op("pe", lambda e: e.transpose(p2b[0:L, 0:128], vT[:, a:a + L], identb[:]), reads=[vT, identb], writes=[p2])
                p3 = nps()
                p3b = p3[:, 0:256].bitcast(BF16)
                op("pe", lambda e: e.transpose(p3b[0:L, 0:128], kt[:, a:a + L], identb[:]), reads=[kt, identb], writes=[p3])
                op("dve", lambda e: e.tensor_tensor(ATb_[0:L, 0:L], p1[0:L, 0:L], trif[0:L, 0:L], ALU.mult),
                   reads=[p1, trif], writes=[ATb_])
                op("act", lambda e: e.copy(vtok_[0:L, :], p2b[0:L, 0:128]), reads=[p2], writes=[vtok_])
                op("act", lambda e: e.copy(ktok_[0:L, :], p3b[0:L, 0:128]), reads=[p3], writes=[ktok_])
                op("dve", lambda e: e.tensor_scalar(Spb_[:], S[:], rr_[:, 0, ci:ci + 1], None, ALU.mult),
                   reads=[S, rr_], writes=[Spb_])
                yield
                p4 = nps()
                op("pe", lambda e: e.matmul(p4[:, 0:L], lhsT=vtok_[0:L, :], rhs=ATb_[0:L, 0:L], start=True, stop=False),
                   reads=[vtok_, ATb_], writes=[p4], inc=False)
                op("pe", lambda e: e.matmul(p4[:, 0:L], lhsT=Spb_[:], rhs=qt[:, a:a + L], start=False, stop=True),
                   reads=[Spb_, qt], writes=[p4])
                p5 = nps()
                op("pe", lambda e: e.matmul(p5[:, 0:128], lhsT=ktok_[0:L, :], rhs=vtok_[0:L, :], start=True, stop=True),
                   reads=[ktok_, vtok_], writes=[p5])
                op("act", lambda e: e.copy(oT[:, a:a + L], p4[:, 0:L]), reads=[p4], writes=[oT])
                op("dve", lambda e: e.tensor_scalar(S[:], S[:], rr_[:, 2, ci:ci + 1], None, ALU.mult), reads=[S, rr_], writes=[S])
                op("dve", lambda e: e.scalar_tensor_tensor(S[:], p5[:, 0:128], rr_[:, 1, ci:ci + 1], S[:], ALU.mult, ALU.add),
                   reads=[p5, rr_, S], writes=[S])
                yield
            store_hg(l, h, cur_seq, last_tile, S)
            sq_b = B["kt"]
            rstd_ = B["rstd"]
            act(sq_b[:, 0:T], oT[:, 0:T], AF.Square, [oT], [sq_b])
            yield
            for (c0, c1) in tile["pieces"]:
                w = c1 - c0
                p = bc_sum(lambda i: (sq_b, sq_b[:, c0:c1]), 1, c0, c1)
                rstd_from(p, w, 1.0 / 128, RMS_EPS, rstd_)
                yield
                op("dve", lambda e: e.scalar_tensor_tensor(oT[:, c0:c1], oT[:, c0:c1], pv_hgnw[:, l, h:h + 1], rstd_[:, 0:w],
                                                           ALU.mult, ALU.mult), reads=[oT, pv_hgnw, rstd_], writes=[oT])
                op("dve", lambda e: e.tensor_tensor(on[h][:, c0:c1], oT[:, c0:c1], gs[:, c0:c1], ALU.mult),
                   reads=[oT, gs], writes=[on[h]])
                yield

        def hgrn_all(tile, l, first_tile, last_tile):
            for _ in hg_prep(tile, l, 0, HS[0]):
                pass
            for h in range(HG_H):
                A = hg_loop(tile, l, h, HS[h % 2], first_tile, last_tile)
                Bg = hg_prep(tile, l, h + 1, HS[(h + 1) % 2]) if h + 1 < HG_H else None
                a_alive, b_alive = True, Bg is not None
                while a_alive or b_alive:
                    if a_alive:
                        try:
                            next(A)
                        except StopIteration:
                            a_alive = False
                    if b_alive:
                        try:
                            next(Bg)
                        except StopIteration:
                            b_alive = False

        def store_hg(l, h, sq, last_tile, S):
            if sq == 0:
                if last_tile:
                    K.dma(hg_out[l, 0, h, :, :], S[:], reads=[S], writes=[dram_hg])
                else:
                    K.dma(hg_c[l, h, :, :], S[:], reads=[S], writes=[hg_c])
            else:
                K.dma(hg_out[l, 1, h, :, :], S[:], reads=[S], writes=[dram_hg])

        def gated_proj(tile, l, srcs, gate_off, wname, accumulate):
            for c in range(16):
                sg = tf[0]
                gemm([("in", l, 0, NKC, gate_off + c * 128, 128)], xT, tile,
                     lambda p, c0, c1: act(sg[:, c0:c1], p[:, 0:c1 - c0], AF.Sigmoid, [p], [sg]))

                def ev(p, c0, c1):
                    if not accumulate:
                        op("dve", lambda e: e.tensor_tensor(mg[c][:, c0:c1], p[:, 0:c1 - c0], sg[:, c0:c1], ALU.mult),
                           reads=[p, sg], writes=[mg[c]])
                    else:
                        op("dve", lambda e: e.tensor_tensor(sg[:, c0:c1], p[:, 0:c1 - c0], sg[:, c0:c1], ALU.mult),
                           reads=[p, sg], writes=[sg])
                        op("dve", lambda e: e.tensor_tensor(mg[c][:, c0:c1], mg[c][:, c0:c1], sg[:, c0:c1], ALU.add),
                           reads=[mg[c], sg], writes=[mg[c]])
                gemm([(wname, l, 0, NKC, c * 128, 128)], srcs, tile, ev)

        def ss_load(l, g, sq, first_tile):
            H = Hss1
            if sq == 0:
                if first_tile:
                    op("dve", lambda e: e.memset(H[:], 0.0), writes=[H])
                else:
                    K.dma(H[:], ss_c[l, g, :, :], reads=[ss_c], writes=[H])
            else:
                for pr in range(4):
                    hh = (g * 4 + pr) * 2
                    K.dma(sttok[:], st_ssm[l, hh:hh + 2, :, :].rearrange("h p n -> (h p) n"), writes=[sttok])
                    p = nps()
                    op("pe", lambda e: e.transpose(p[:, 0:128], sttok[:], identf[:]), reads=[sttok, identf], writes=[p])
                    op("dve", lambda e: e.tensor_copy(H[:, pr * 128:(pr + 1) * 128], p[:, 0:128]), reads=[p], writes=[H])
            op("act", lambda e: e.copy(Hsb1[:], H[:]), reads=[H], writes=[Hsb1])

        def ss_store(l, g, sq, last_tile):
            H = Hss1
            if sq == 0 and not last_tile:
                K.dma(ss_c[l, g, :, :], H[:], reads=[H], writes=[ss_c])
                return
            for pr in range(4):
                hh = (g * 4 + pr) * 2
                p = nps()
                op("pe", lambda e: e.transpose(p[:, 0:128], H[:, pr * 128:(pr + 1) * 128], identf[:]),
                   reads=[H, identf], writes=[p])
                op("dve", lambda e: e.tensor_copy(sttok[:], p[:, 0:128]), reads=[p], writes=[sttok])
                K.dma(ssm_out[l, sq, hh:hh + 2, :, :].rearrange("h p n -> (h p) n"), sttok[:], reads=[sttok], writes=[dram_ss])

        def ssd(tile, l, first_tile, last_tile):
            T = tile["w"]
            chunks = tile["chunks"]
            wdt = ("in", l, 0, NKC, O_DT, 32)

            def ev_dt(p, c0, c1):
                act(dtT[0:32, c0:c1], p[0:32, 0:c1 - c0], AF.Exp, [p], [dtT], bias=hv["dtb"][:, l:l + 1])
            gemm([wdt], xT, tile, ev_dt, m=32)
            act(dtT[0:32, 0:T], dtT[0:32, 0:T], AF.Ln, [dtT], [dtT], bias=one_col[0:32, 0:1])
            op("dve", lambda e: e.tensor_scalar(adtT[0:32, 0:T], dtT[0:32, 0:T], hvA[:, l:l + 1], None, ALU.mult),
               reads=[dtT, hvA], writes=[adtT])
            if stop_at == "ssd_dt0":
                raise _Stop()
            for ci, (a, L, sq) in enumerate(chunks):
                p = nps()
                op("pe", lambda e: e.matmul(p[0:L, 0:32], lhsT=dtT[0:32, a:a + L], rhs=identf[0:32, 0:32], start=True, stop=True),
                   reads=[dtT, identf], writes=[p], inc=False)
                op("pe", lambda e: e.matmul(p[0:L, 32:64], lhsT=adtT[0:32, a:a + L], rhs=identf[0:32, 0:32], start=True, stop=True),
                   reads=[adtT, identf], writes=[p])
                op("dve", lambda e: e.tensor_copy(dta[0:L, ci, :], p[0:L, 0:64]), reads=[p], writes=[dta])
                p2 = nps()
                op("pe", lambda e: e.matmul(p2[0:L, 0:32], lhsT=trif[0:L, 0:L], rhs=dta[0:L, ci, 32:64], start=True, stop=True),
                   reads=[trif, dta], writes=[p2])
                p2x = nps()
                op("pe", lambda e: e.matmul(p2x[:, 0:32], lhsT=onesf[0:L, :], rhs=dta[0:L, ci, 32:64], start=True, stop=True),
                   reads=[onesf, dta], writes=[p2x])
                op("dve", lambda e: e.tensor_copy(gtok_all[0:L, ci, :], p2[0:L, 0:32]), reads=[p2], writes=[gtok_all])
                op("dve", lambda e: e.tensor_copy(gendS[:], p2x[:, 0:32]), reads=[p2x], writes=[gendS])
                act(dec_tok[0:L, ci, :], gtok_all[0:L, ci, :], AF.Exp, [gtok_all], [dec_tok], scale=-1.0)
                act(rtot_bc[:, ci, :], gendS[:], AF.Exp, [gendS], [rtot_bc], scale=-1.0)
                op("dve", lambda e: e.tensor_tensor(wtmp[0:L, :], gtok_all[0:L, ci, :], gendS[0:L, :], ALU.subtract),
                   reads=[gtok_all, gendS], writes=[wtmp])
                act(wtmp[0:L, :], wtmp[0:L, :], AF.Exp, [wtmp], [wtmp])
                op("dve", lambda e: e.tensor_tensor(w_tok[0:L, ci, :], wtmp[0:L, :], dta[0:L, ci, 0:32], ALU.mult),
                   reads=[wtmp, dta], writes=[w_tok])


        SSB = [dict(BT=tb[0], CT=tb[1], xc=tb[2:6], sz=tb[6:10]),
               dict(BT=hm[32], CT=hm[33], xc=hm[34:38], sz=hm[38:42])]

        def ssd_prep(tile, l, g, SB):
            BT, CT, xc, sz = SB["BT"], SB["CT"], SB["xc"], SB["sz"]
            pre = tf[8]
            for which, dstb in ((0, BT), (1, CT)):
                cidx = 16 + which * 4 + g
                yield from gemm_g([("in", l, 0, NKC, O_XBC + 2048 + which * 512 + g * 128, 128)], xT, tile,
                                  lambda p, c0, c1: seg_evac(tile, p, c0, c1, pre, 3))
                conv_apply(tile, pre, halo_x[l], halo_xs[l], cidx, lambda k: pv_cvw[:, l, k, cidx:cidx + 1],
                           pv_cvb[:, l, cidx:cidx + 1], 4, dstb, AF.Silu)
                yield
            for pr in range(4):
                cc = g * 4 + pr
                yield from gemm_g([("in", l, 0, NKC, O_XBC + cc * 128, 128)], xT, tile,
                                  lambda p, c0, c1: seg_evac(tile, p, c0, c1, pre, 3))
                conv_apply(tile, pre, halo_x[l], halo_xs[l], cc, lambda k: pv_cvw[:, l, k, cc:cc + 1],
                           pv_cvb[:, l, cc:cc + 1], 4, xc[pr], AF.Silu)
                yield
                yield from gemm_g([("in", l, 0, NKC, O_Z + cc * 128, 128)], xT, tile,
                                  lambda p, c0, c1: act(sz[pr][:, c0:c1], p[:, 0:c1 - c0], AF.Silu, [p], [sz[pr]]))

        def ssd_chunks(tile, l, g, SB, first_tile, last_tile):
            BT, CT, xc, sz = SB["BT"], SB["CT"], SB["xc"], SB["sz"]
            chunks = tile["chunks"]
            cur_seq = None
            H, Hb = Hss1, Hsb1
            for ci, (a, L, sq) in enumerate(chunks):
                if sq != cur_seq:
                    if cur_seq is not None:
                        ss_store(l, g, cur_seq, last_tile)
                    ss_load(l, g, sq, first_tile)
                    cur_seq = sq
                L8 = 8 * L
                gs8 = slice(g * 8, (g + 1) * 8)
                px = nps()
                pxb = px[:, 0:256].bitcast(BF16)
                for pr in range(4):
                    op("pe", lambda e: e.transpose(pxb[0:L, pr * 128:(pr + 1) * 128], xc[pr][:, a:a + L], identb[:]),
                       reads=[xc[pr], identb], writes=[px], inc=(pr == 3))
                op("dve", lambda e: e.tensor_tensor(xdt[0:L, :].rearrange("s (h p) -> s h p", p=64),
                                                    pxb[0:L, 0:512].rearrange("s (h p) -> s h p", p=64),
                                                    dta[0:L, ci, gs8].unsqueeze(2).to_broadcast([L, 8, 64]), ALU.mult),
                   reads=[px, dta], writes=[xdt])
                op("dve", lambda e: e.tensor_tensor(xw[0:L, :].rearrange("s (h p) -> s h p", p=64),
                                                    pxb[0:L, 0:512].rearrange("s (h p) -> s h p", p=64),
                                                    w_tok[0:L, ci, gs8].unsqueeze(2).to_broadcast([L, 8, 64]), ALU.mult),
                   reads=[px, w_tok], writes=[xw])
                pB = nps()
                pBb = pB[:, 0:256].bitcast(BF16)
                op("pe", lambda e: e.transpose(pBb[0:L, 0:128], BT[:, a:a + L], identb[:]), reads=[BT, identb], writes=[pB])
                op("act", lambda e: e.copy(Btok[0:L, :], pBb[0:L, 0:128]), reads=[pB], writes=[Btok])
                pC = nps()
                op("pe", lambda e: e.matmul(pC[0:L, 0:L], lhsT=BT[:, a:a + L], rhs=CT[:, a:a + L], start=True, stop=True),
                   reads=[BT, CT], writes=[pC])
                op("act", lambda e: e.copy(cbt[0:L, 0:L], pC[0:L, 0:L]), reads=[pC], writes=[cbt])
                adt8 = dta[0:L, ci, 32 + g * 8:32 + (g + 1) * 8].unsqueeze(2).to_broadcast([L, 8, L])
                op("dve", lambda e: e.tensor_tensor(R1[0:L, 0:L8].rearrange("s (h t) -> s h t", t=L), adt8,
                                                    trif[0:L, 0:L].unsqueeze(1).to_broadcast([L, 8, L]), ALU.mult),
                   reads=[dta, trif], writes=[R1])
                yield
                pL = nps()
                bigm = cb["bigm%d" % L]
                op("pe", lambda e: e.matmul(pL[0:L, 0:L8], lhsT=onesf[0:L, 0:L], rhs=R1[0:L, 0:L8], start=True, stop=False),
                   reads=[onesf, R1], writes=[pL], inc=False)
                op("pe", lambda e: e.matmul(pL[0:L, 0:L8], lhsT=identb[0:L, 0:L], rhs=bigmb[L][0:L, 0:L8], start=False, stop=True),
                   reads=[identb, bigmb[L]], writes=[pL])
                op("dve", lambda e: e.tensor_tensor(LT[0:L, 0:L8].rearrange("s (h t) -> s h t", t=L),
                                                    pL[0:L, 0:L8].rearrange("s (h t) -> s h t", t=L),
                                                    gtok_all[0:L, ci, gs8].unsqueeze(2).to_broadcast([L, 8, L]), ALU.subtract),
                   reads=[pL, gtok_all], writes=[LT])
                act(LT[0:L, 0:L8], LT[0:L, 0:L8], AF.Exp, [LT], [LT], scale=-1.0)
                op("dve", lambda e: e.tensor_tensor(scT[0:L, 0:L8].rearrange("s (h t) -> s h t", t=L),
                                                    LT[0:L, 0:L8].rearrange("s (h t) -> s h t", t=L),
                                                    cbt[0:L, 0:L].unsqueeze(1).to_broadcast([L, 8, L]), ALU.mult),
                   reads=[LT, cbt], writes=[scT])
                yield
                pY1 = nps()
                for hh in range(8):
                    op("pe", lambda e: e.matmul(pY1[0:L, hh * 64:(hh + 1) * 64], lhsT=scT[0:L, hh * L:(hh + 1) * L],
                                                rhs=xdt[0:L, hh * 64:(hh + 1) * 64], start=True, stop=True),
                       reads=[scT, xdt], writes=[pY1], inc=(hh == 7))
                pY2 = nps()
                op("pe", lambda e: e.matmul(pY2[0:L, 0:512], lhsT=CT[:, a:a + L], rhs=Hb[:, 0:512], start=True, stop=True),
                   reads=[CT, Hb], writes=[pY2])
                op("dve", lambda e: e.tensor_tensor(ytmp[0:L, :].rearrange("s (h p) -> s h p", p=64),
                                                    pY2[0:L, 0:512].rearrange("s (h p) -> s h p", p=64),
                                                    dec_tok[0:L, ci, gs8].unsqueeze(2).to_broadcast([L, 8, 64]), ALU.mult),
                   reads=[pY2, dec_tok], writes=[ytmp])
                op("dve", lambda e: e.tensor_tensor(ytok[0:L, :], pY1[0:L, 0:512], ytmp[0:L, :], ALU.add),
                   reads=[pY1, ytmp], writes=[ytok])
                yield
                pT = nps()
                pTb = pT[:, 0:256].bitcast(BF16)
                for pr in range(4):
                    op("pe", lambda e: e.transpose(pTb[:, pr * 64:pr * 64 + L], ytok[0:L, pr * 128:(pr + 1) * 128],
                                                   identb[0:L, 0:L]), reads=[ytok, identb], writes=[pT], inc=(pr == 3))
                for pr in range(4):
                    op("act", lambda e: e.copy(yT4[pr][:, a:a + L], pTb[:, pr * 64:pr * 64 + L]), reads=[pT], writes=[yT4[pr]])
                pH = nps()
                op("pe", lambda e: e.matmul(pH[:, 0:512], lhsT=Btok[0:L, :], rhs=xw[0:L, :], start=True, stop=True),
                   reads=[Btok, xw], writes=[pH])
                op("dve", lambda e: e.tensor_tensor(H[:].rearrange("n (h p) -> n h p", p=64),
                                                    H[:].rearrange("n (h p) -> n h p", p=64),
                                                    rtot_bc[:, ci, gs8].unsqueeze(2).to_broadcast([128, 8, 64]), ALU.mult),
                   reads=[H, rtot_bc], writes=[H])
                op("dve", lambda e: e.tensor_tensor(H[:], H[:], pH[:, 0:512], ALU.add), reads=[H, pH], writes=[H])
                op("act", lambda e: e.copy(Hb[:], H[:]), reads=[H], writes=[Hb])
                yield
            ss_store(l, g, cur_seq, last_tile)

        def ssd_post(tile, l, g, SB):
            T = tile["w"]
            xc, sz = SB["xc"], SB["sz"]
            for pr in range(4):
                cc = g * 4 + pr
                yv = yT4[pr][:, 0:T]
                op("dve", lambda e: e.scalar_tensor_tensor(yv, xc[pr][:, 0:T], pv_dsk[:, l, cc:cc + 1], yv, ALU.mult, ALU.add),
                   reads=[xc[pr], pv_dsk, yT4[pr]], writes=[yT4[pr]])
                op("dve", lambda e: e.tensor_tensor(yv, yv, sz[pr][:, 0:T], ALU.mult), reads=[yT4[pr], sz[pr]], writes=[yT4[pr]])
                act(xc[pr][:, 0:T], yv, AF.Square, [yT4[pr]], [xc[pr]])
            for (c0, c1) in tile["pieces"]:
                w = c1 - c0
                p = bc_sum(lambda i: (xc[i], xc[i][:, c0:c1]), 4, c0, c1)
                rstd_from(p, w, 1.0 / 512, RMS_EPS, st_rstd)
                for pr in range(4):
                    cc = g * 4 + pr
                    op("dve", lambda e: e.scalar_tensor_tensor(on[cc][:, c0:c1], yT4[pr][:, c0:c1], pv_snw[:, l, cc:cc + 1],
                                                               st_rstd[:, 0:w], ALU.mult, ALU.mult),
                       reads=[yT4[pr], pv_snw, st_rstd], writes=[on[cc]])

        def gated_proj_g(tile, l, srcs, gate_off, wname, sg):
            for c in range(16):
                yield from gemm_g([("in", l, 0, NKC, gate_off + c * 128, 128)], xT, tile,
                                  lambda p, c0, c1: act(sg[:, c0:c1], p[:, 0:c1 - c0], AF.Sigmoid, [p], [sg]))
                yield from gemm_g([(wname, l, 0, NKC, c * 128, 128)], srcs, tile,
                                  lambda p, c0, c1: op("dve", lambda e: e.tensor_tensor(mg[c][:, c0:c1], p[:, 0:c1 - c0], sg[:, c0:c1],
                                                                                       ALU.mult), reads=[p, sg], writes=[mg[c]]))

        def chain_g(*gens):
            for g_ in gens:
                if g_ is not None:
                    yield from g_

        def interleave(A, Bg):
            a_alive, b_alive = A is not None, Bg is not None
            while a_alive or b_alive:
                if a_alive:
                    try:
                        next(A)
                    except StopIteration:
                        a_alive = False
                if b_alive:
                    try:
                        next(Bg)
                    except StopIteration:
                        b_alive = False

        def ssd_all(tile, l, first_tile, last_tile):
            interleave(ssd_prep(tile, l, 0, SSB[0]), None)
            for g in range(SSM_G):
                A = ssd_chunks(tile, l, g, SSB[g % 2], first_tile, last_tile)
                nxt = ssd_prep(tile, l, g + 1, SSB[(g + 1) % 2]) if g + 1 < SSM_G else None
                pa = gated_proj_g(tile, l, on, O_GA, "pa", tf[6]) if g == 0 else None
                interleave(A, chain_g(nxt, pa))
                if g == 0:
                    dump("d_mga", mg, tile, l)
                ssd_post(tile, l, g, SSB[g % 2])

        def layer_norm(tile, gvec, bvec, l, emit_out):
            T = tile["w"]
            for (c0, c1) in tile["pieces"]:
                w = c1 - c0
                p1 = bc_sum(lambda i: (xT[i], xT[i][:, c0:c1]), 16, c0, c1)
                p2 = nps()
                for c in range(16):
                    sb_ = sqr[c % 2]
                    act(sb_[:, 0:w], xT[c][:, c0:c1], AF.Square, [xT[c]], [sb_])
                    op("pe", lambda e: e.matmul(p2[:, 0:w], lhsT=onesb[:], rhs=sb_[:, 0:w], start=(c == 0), stop=(c == 15)),
                       reads=[onesb, sb_], writes=[p2], inc=True)
                op("dve", lambda e: e.tensor_scalar(st_mean[:, 0:w], p1[:, 0:w], 1.0 / D, None, ALU.mult), reads=[p1], writes=[st_mean])
                op("dve", lambda e: e.tensor_tensor(st_tmp[:, 0:w], st_mean[:, 0:w], st_mean[:, 0:w], ALU.mult),
                   reads=[st_mean], writes=[st_tmp])
                op("dve", lambda e: e.scalar_tensor_tensor(st_tmp[:, 0:w], p2[:, 0:w], 1.0 / D, st_tmp[:, 0:w], ALU.mult, ALU.subtract),
                   reads=[p2, st_tmp], writes=[st_tmp])
                act(st_tmp[:, 0:w], st_tmp[:, 0:w], AF.Sqrt, [st_tmp], [st_tmp], bias=eps_col(LN_EPS))
                op("dve", lambda e: e.reciprocal(st_rstd[:, 0:w], st_tmp[:, 0:w]), reads=[st_tmp], writes=[st_rstd])
                for c in range(16):
                    t32 = tf[0] if c % 2 == 0 else tf[1]
                    op("dve", lambda e: e.tensor_tensor(t32[:, 0:w], xT[c][:, c0:c1], st_mean[:, 0:w], ALU.subtract),
                       reads=[xT[c], st_mean], writes=[t32])
                    op("dve", lambda e: e.tensor_tensor(t32[:, 0:w], t32[:, 0:w], st_rstd[:, 0:w], ALU.mult),
                       reads=[t32, st_rstd], writes=[t32])
                    if not emit_out:
                        act(xT[c][:, c0:c1], t32[:, 0:w], AF.Identity, [t32, gvec, bvec], [xT[c]],
                            scale=gvec[:, l, c:c + 1], bias=bvec[:, l, c:c + 1])
                    else:
                        yb = tf[2 + (c % 4)]
                        act(yb[:, 0:w], t32[:, 0:w], AF.Identity, [t32, gvec, bvec], [yb],
                            scale=gvec[:, l, c:c + 1], bias=bvec[:, l, c:c + 1])
                        nb = (w + 127) // 128
                        for bi in range(nb):
                            nr = min(128, w - bi * 128)
                            if c % 4 == 0 and bi == 0:
                                pass
                            pt = nps()
                            op("pe", lambda e: e.transpose(pt[0:nr, 0:128], yb[:, bi * 128:bi * 128 + nr], identf[:]),
                               reads=[yb, identf], writes=[pt])
                            ob = ost[ost_i[0] % 4]
                            ost_i[0] += 1
                            op("dve", lambda e: e.tensor_copy(ob[0:nr, :], pt[0:nr, 0:128]), reads=[pt], writes=[ob])
                            r0 = tile["r0"] + c0 + bi * 128
                            K.dma(y_out[r0:r0 + nr, c * 128:(c + 1) * 128], ob[0:nr, :], reads=[ob], writes=[dram_y])

        ost = [K.sb("ost%d" % i, [128, 128], F32) for i in range(4)]
        ost_i = [0]
        xld = [K.sb("xld%d" % i, [128, 512], F32) for i in range(2)]
        xld_i = [0]

        def dump(nm, bufs, tile, l):
            if not debug or l != 0 or tile["idx"] != 0:
                return
            for i, b in enumerate(bufs):
                K.dma(dbg[nm][i, :, 0:tile["w"]], b[:, 0:tile["w"]], reads=[b])

        def mixer(tile, l, first_tile, last_tile):
            if stop_at in ("load", "load_dma", "load_1"):
                raise _Stop()
            hgrn_all(tile, l, first_tile, last_tile)
            dump("d_on", on, tile, l)
            ssd(tile, l, first_tile, last_tile)
            ssd_all(tile, l, first_tile, last_tile)
            dump("d_yn", on, tile, l)
            gated_proj(tile, l, on, O_GB, "pb", True)
            dump("d_mg", mg, tile, l)
            for c in range(16):
                gemm([("out", l, 0, NKC, c * 128, 128)], mg, tile,
                     lambda p, c0, c1: op("dve", lambda e: e.scalar_tensor_tensor(xT[c][:, c0:c1], xT[c][:, c0:c1], ALPHA,
                                                                                  p[:, 0:c1 - c0], ALU.mult, ALU.add),
                                          reads=[xT[c], p], writes=[xT[c]]))
            layer_norm(tile, pv_l1g, pv_l1b, l, False)
            dump("d_xmid", xT, tile, l)
            if stop_at == "mixer":
                raise _Stop()

        def ffn(tile, l, last_layer):
            T = tile["w"]
            pre = tf[8]
            ga = tf[7]
            for j in range(NFF):
                gemm([("up", l, 0, NKC, j * 128, 128)], xT, tile, lambda p, c0, c1: seg_evac(tile, p, c0, c1, pre, 2))
                conv_apply(tile, pre, halo_f[l], halo_fs[l], j, lambda k: pv_fcw[:, l, k, j:j + 1], pv_fcb[:, l, j:j + 1], 3,
                           ga, AF.Gelu)
                gemm([("up", l, 0, NKC, DFF + j * 128, 128)], xT, tile,
                     lambda p, c0, c1: op("dve", lambda e: e.tensor_tensor(hm[j][:, c0:c1], p[:, 0:c1 - c0], ga[:, c0:c1], ALU.mult),
                                          reads=[p, ga], writes=[hm[j]]))
            for c in range(16):
                gemm([("dn", l, 0, 16, c * 128, 128), ("dn", l, 16, 16, c * 128, 128), ("dn", l, 32, 12, c * 128, 128)], hm, tile,
                     lambda p, c0, c1: op("dve", lambda e: e.scalar_tensor_tensor(xT[c][:, c0:c1], xT[c][:, c0:c1], ALPHA,
                                                                                  p[:, 0:c1 - c0], ALU.mult, ALU.add),
                                          reads=[xT[c], p], writes=[xT[c]]))
            dump("d_hm", hm, tile, l)
            dump("d_v2", xT, tile, l)
            layer_norm(tile, pv_l2g, pv_l2b, l, last_layer)

        if stop_at == "setup":
            tiles = []
        for tile in tiles:
            T = tile["w"]
            ti = tile["idx"]
            first_tile = (ti == 0)
            last_tile = (ti == NTILES - 1)
            nblk = (T + 127) // 128
            for bi in range(nblk):
                r0 = bi * 128
                nr = min(128, T - r0)
                for q4 in range(4):
                    xtok = xld[xld_i[0] % 2]
                    xld_i[0] += 1
                    K.dma(xtok[0:nr, :], xin[tile["r0"] + r0: tile["r0"] + r0 + nr, q4 * 512:(q4 + 1) * 512], writes=[xtok])
                    p = nps()
                    for j in range(4):
                        op("pe", lambda e: e.transpose(p[:, j * 128: j * 128 + nr], xtok[0:nr, j * 128:(j + 1) * 128],
                                                       identf[0:nr, 0:nr]), reads=[xtok, identf], writes=[p], inc=(j == 3))
                    for j in range(4):
                        cidx = q4 * 4 + j
                        if j % 2:
                            op("dve", lambda e: e.tensor_copy(xT[cidx][:, r0:r0 + nr], p[:, j * 128:j * 128 + nr]), reads=[p], writes=[xT[cidx]])
                        else:
                            op("dve", lambda e: e.tensor_copy(xT[cidx][:, r0:r0 + nr], p[:, j * 128:j * 128 + nr]), reads=[p], writes=[xT[cidx]])
            try:
                for l in range(nlayers):
                    mixer(tile, l, first_tile, last_tile)
                    ffn(tile, l, l == nlayers - 1)
            except _Stop:
                break
        for l in range(nlayers):
            for k in range(3):
                K.dma(conv_out[l, 0, k, :].rearrange("(c p) -> p c", p=128), halo_x[l][:, :, k], reads=[halo_x[l]], writes=[dram_cv],
                      allow_slow_non_contiguous=True)
                K.dma(conv_out[l, 1, k, :].rearrange("(c p) -> p c", p=128), halo_xs[l][:, :, k], reads=[halo_xs[l]], writes=[dram_cv],
                      allow_slow_non_contiguous=True)
            for k in range(2):
                K.dma(ffn_out[l, 0, k, :].rearrange("(c p) -> p c", p=128), halo_f[l][:, :, k], reads=[halo_f[l]], writes=[dram_ff],
                      allow_slow_non_contiguous=True)
                K.dma(ffn_out[l, 1, k, :].rearrange("(c p) -> p c", p=128), halo_fs[l][:, :, k], reads=[halo_fs[l]], writes=[dram_ff],
                      allow_slow_non_contiguous=True)
        K.finish([dram_y, dram_hg, dram_ss, dram_cv, dram_ff])
        print("instructions:", K.n_ins, "waits:", K.n_wait, "sems:", K.nsem, flush=True)
    return nc


_CACHE = {}


def kernel(**inp):
    f32 = lambda a: np.ascontiguousarray(np.asarray(a, dtype=np.float32))
    xp = f32(inp["x_prompt"])
    xs = f32(inp["x_sample"])
    meta = f32(inp["meta_tokens"])
    if "nc" not in _CACHE:
        _CACHE["nc"] = build_program()
    nc = _CACHE["nc"]
    consts = build_consts()
    wnames = ["hgrn_lb_logits", "hgrn_norm_w", "ssm_conv_w", "ssm_conv_b", "ssm_dt_bias", "ssm_a_log",
              "ssm_d", "ssm_norm_w", "ln1_g", "ln1_b", "ffn_conv_w", "ffn_conv_b", "ln2_g", "ln2_b"]
    shared = {n: f32(inp[n]) for n in wnames}
    shared["wst"] = build_wstream(inp)
    for k, v in consts.items():
        shared["c_" + k] = v
    in_maps = []
    for c in range(8):
        b = c % 2
        if c < 2:
            xin = np.concatenate([meta, xp[b, 0:TP], xs[c], xp[b, TP:]], axis=0)
        else:
            xin = np.zeros((NTOK, D), np.float32)
            xin[N_META + TP:N_META + TP + DEC_SEQ] = xs[c]
        m = dict(shared)
        m["xin"] = np.ascontiguousarray(xin)
        m["st_hg"] = f32(inp["state_hgrn"][:, c])
        m["st_ssm"] = f32(inp["state_ssm"][:, c])
        m["st_conv"] = f32(inp["state_ssm_conv"][:, c])
        m["st_ffn"] = f32(inp["state_ffn_conv"][:, c])
        in_maps.append(m)
    res = run_bass_kernel_spmd(nc, in_maps, core_ids=list(range(8)))
    R = res.results
    ys = [r["y"] for r in R]
    y_prompt = np.stack([np.concatenate([ys[b][N_META:N_META + TP], ys[b][N_META + TP + DEC_SEQ:]], axis=0) for b in range(2)])
    y_sample = np.stack([ys[c][N_META + TP:N_META + TP + DEC_SEQ] for c in range(8)])
    hg_p = np.stack([R[b]["hg_o"][:, 0] for b in range(2)], axis=1)
    ssm_p = np.stack([R[b]["ssm_o"][:, 0] for b in range(2)], axis=1)
    conv_p = np.stack([R[b]["conv_o"][:, 0] for b in range(2)], axis=1)
    ffn_p = np.stack([R[b]["ffn_o"][:, 0] for b in range(2)], axis=1)
    hg_s = np.stack([R[c]["hg_o"][:, 1] for c in range(8)], axis=1)
    ssm_s = np.stack([R[c]["ssm_o"][:, 1] for c in range(8)], axis=1)
    conv_s = np.stack([R[c]["conv_o"][:, 1] for c in range(8)], axis=1)
    ffn_s = np.stack([R[c]["ffn_o"][:, 1] for c in range(8)], axis=1)
    outs = (y_prompt, y_sample, hg_p, ssm_p, conv_p, ffn_p, hg_s, ssm_s, conv_s, ffn_s)
    return tuple(np.ascontiguousarray(o, dtype=np.float32) for o in outs)
DBG_NAMES = ["d_on", "d_mga", "d_yn", "d_mg", "d_xmid", "d_v2", "d_hm"]
```
